# Optimizing a Trainium2 kernel written in Bass

```python
import jax
import jax.numpy as jnp
from jax import lax
import numpy as np

D_MODEL = 1024
BATCH = 2
SEQ = 8192
DEPTH = 2

CHUNK = 64
N_AB = (DEPTH + 1) // 2
N_C = DEPTH // 2
RMS_EPS = 1e-6
RG_WIDTH = D_MODEL
RG_BLOCKS = 16
RG_BLOCK = RG_WIDTH // RG_BLOCKS
RG_CONV = 4
RG_C = 8.0
HG_HEADS = 8
HG_DK = D_MODEL // HG_HEADS
HG_DV = D_MODEL // HG_HEADS
HG_KW = HG_HEADS * HG_DK
HG_VW = HG_HEADS * HG_DV
AB_SPLITS = (RG_WIDTH, RG_WIDTH, HG_KW, HG_KW, HG_VW, HG_VW)
AB_IN = sum(AB_SPLITS)
AB_OUT = RG_WIDTH + HG_VW
RW_HEAD = 64
RW_HEADS = D_MODEL // RW_HEAD
RW_WIDTH = RW_HEADS * RW_HEAD
RW_DECAY_LORA = 64
RW_AAA_LORA = 64
RW_GN_EPS = 64e-5

kernel_name = "hybrid_rglru_hgrn2_rwkv7_trunk"


def _f32(t):
    return t.astype(jnp.float32)


def rms_norm(x, gain):
    xf = _f32(x)
    return xf * lax.rsqrt(jnp.mean(xf * xf, axis=-1, keepdims=True) + RMS_EPS) * _f32(gain)


def _linear_scan_combine(left, right):
    a_l, b_l = left
    a_r, b_r = right
    return a_l * a_r, a_r * b_l + b_r


def rglru_mixer(xb, conv_w, conv_b, w_a, b_a, w_x, b_x, lam):
    bsz, seq, _ = xb.shape
    xc = lax.conv_general_dilated(xb, conv_w[:, None, :], window_strides=(1,),
                                  padding=[(RG_CONV - 1, 0)],
                                  dimension_numbers=("NWC", "WIO", "NWC"),
                                  feature_group_count=RG_WIDTH) + conv_b
    xblk = xc.reshape(bsz, seq, RG_BLOCKS, RG_BLOCK)
    gate_r = jax.nn.sigmoid(jnp.einsum("bsni,nij->bsnj", xblk, w_a).reshape(bsz, seq, RG_WIDTH) + b_a)
    gate_i = jax.nn.sigmoid(jnp.einsum("bsni,nij->bsnj", xblk, w_x).reshape(bsz, seq, RG_WIDTH) + b_x)
    log_a = -RG_C * gate_r * jax.nn.softplus(-lam)
    a = jnp.exp(log_a)
    b = jnp.sqrt(-jnp.expm1(2.0 * log_a)) * (gate_i * xc)
    _, h = lax.associative_scan(_linear_scan_combine, (a, b), axis=1)
    return h


def hgrn2_mixer(q, f_raw, v, lb):
    bsz, seq, _ = q.shape
    n_chunks = seq // CHUNK
    q = jax.nn.silu(q)
    log_f = jnp.logaddexp(jnp.log(lb), jnp.log1p(-lb) + jax.nn.log_sigmoid(f_raw))
    k = -jnp.expm1(log_f)

    def to_chunks(t, d):
        return t.reshape(bsz, n_chunks, CHUNK, HG_HEADS, d).transpose(1, 0, 3, 2, 4)

    qc = to_chunks(q, HG_DK)
    kc = to_chunks(k, HG_DK)
    vc = to_chunks(v, HG_DV)
    gc = jnp.cumsum(to_chunks(log_f, HG_DK), axis=3)
    causal = jnp.tril(jnp.ones((CHUNK, CHUNK), dtype=bool))[None, None, :, :, None]

    def chunk_step(state, inp):
        q_c, k_c, v_c, g_c = inp
        rel = jnp.where(causal, g_c[:, :, :, None, :] - g_c[:, :, None, :, :], -jnp.inf)
        scores = jnp.einsum("bhtd,bhsd,bhtsd->bhts", q_c, k_c, jnp.exp(rel))
        out = (jnp.einsum("bhts,bhsv->bhtv", scores, v_c)
               + jnp.einsum("bhtd,bhdv->bhtv", q_c * jnp.exp(g_c), state))
        g_end = g_c[:, :, -1, :]
        new_state = (state * jnp.exp(g_end)[..., None]
                     + jnp.einsum("bhsd,bhsv->bhdv", k_c * jnp.exp(g_end[:, :, None, :] - g_c), v_c))
        return new_state, out

    state0 = jnp.zeros((bsz, HG_HEADS, HG_DK, HG_DV), jnp.float32)
    _, oc = lax.scan(chunk_step, state0, (qc, kc, vc, gc))
    return oc.transpose(1, 0, 3, 2, 4).reshape(bsz, seq, HG_HEADS, HG_DV)


def rglru_hgrn2_layer(u, w_in, w_out, conv_w, conv_b, w_a, b_a, w_x, b_x, lam, lb, g_norm):
    bsz, seq, _ = u.shape
    w_in, w_out, conv_w, conv_b, w_a, b_a, w_x, b_x, lam, g_norm = map(
        _f32, (w_in, w_out, conv_w, conv_b, w_a, b_a, w_x, b_x, lam, g_norm))
    proj = jnp.einsum("bsd,de->bse", u, w_in)
    rg_x, rg_g, hg_q, hg_f, hg_i, hg_g = jnp.split(proj, np.cumsum(AB_SPLITS)[:-1].tolist(), axis=-1)
    h = rglru_mixer(rg_x, conv_w, conv_b, w_a, b_a, w_x, b_x, lam)
    o = hgrn2_mixer(hg_q, hg_f, hg_i, lb)
    o = o * lax.rsqrt(jnp.mean(o * o, axis=-1, keepdims=True) + RMS_EPS) * g_norm
    merged = jnp.concatenate([h * jax.nn.silu(rg_g),
                              o.reshape(bsz, seq, HG_VW) * jax.nn.silu(hg_g)], axis=-1)
    return merged @ w_out


def rwkv7_layer(u, mu, w_in, w_out, w0, w1, w2, a0, a1, a2, k_k, k_a, r_k, ln_w, ln_b):
    bsz, seq, _ = u.shape
    mu, w_in, w_out, w0, w1, w2, a0, a1, a2, k_k, k_a, r_k, ln_w, ln_b = map(
        _f32, (mu, w_in, w_out, w0, w1, w2, a0, a1, a2, k_k, k_a, r_k, ln_w, ln_b))
    shifted = jnp.pad(u, ((0, 0), (1, 0), (0, 0)))[:, :-1]
    mixes = u[None] + (shifted - u)[None] * mu[:, None, None, :]
    r, k, v, g = jnp.einsum("pbsd,pdw->pbsw", mixes[:4], w_in)
    w_log = -jax.nn.softplus(-(w0 + jnp.tanh(mixes[4] @ w1) @ w2)) - 0.5
    decay = jnp.exp(-jnp.exp(w_log))
    iclr = jax.nn.sigmoid(a0 + (mixes[5] @ a1) @ a2)

    def heads(t):
        return t.reshape(bsz, seq, RW_HEADS, RW_HEAD)

    kk = heads(k * k_k)
    kk = kk / jnp.maximum(jnp.linalg.norm(kk, axis=-1, keepdims=True), 1e-12)
    k = heads(k * (1.0 + (iclr - 1.0) * k_a))
    r, v, decay, iclr = heads(r), heads(v), heads(decay), heads(iclr)

    def step(state, inp):
        r_t, w_t, k_t, v_t, al_t, be_t = inp
        sa = jnp.einsum("bhvk,bhk->bhv", state, al_t)
        state = (state * w_t[:, :, None, :] + sa[..., None] * be_t[:, :, None, :]
                 + v_t[..., None] * k_t[:, :, None, :])
        return state, jnp.einsum("bhvk,bhk->bhv", state, r_t)

    def time_major(t):
        return t.transpose(1, 0, 2, 3)

    state0 = jnp.zeros((bsz, RW_HEADS, RW_HEAD, RW_HEAD), jnp.float32)
    xs = (time_major(r), time_major(decay), time_major(k), time_major(v),
          time_major(-kk), time_major(kk * iclr))
    _, y = lax.scan(step, state0, xs)
    y = time_major(y)
    mean = jnp.mean(y, axis=-1, keepdims=True)
    var = jnp.mean(jnp.square(y - mean), axis=-1, keepdims=True)
    y = ((y - mean) * lax.rsqrt(var + RW_GN_EPS)).reshape(bsz, seq, RW_WIDTH) * ln_w + ln_b
    bonus = jnp.sum(r * k * r_k, axis=-1, keepdims=True) * v
    y = y + bonus.reshape(bsz, seq, RW_WIDTH)
    return (y * jax.nn.silu(g)) @ w_out


def setup_inputs(seed: int = 0) -> dict:
    key = jax.random.key(seed)
    ks = iter(jax.random.split(key, 40))

    def nrm(shape, scale):
        return scale * jax.random.normal(next(ks), shape, jnp.float32)

    def unif(shape, lo, hi):
        return jax.random.uniform(next(ks), shape, jnp.float32, lo, hi)

    x = nrm((BATCH, SEQ, D_MODEL), 1.0)
    pre_norm = 1.0 + nrm((DEPTH, D_MODEL), 0.02)
    post_norm = 1.0 + nrm((DEPTH, D_MODEL), 0.02)
    ab_w_in = nrm((N_AB, D_MODEL, AB_IN), D_MODEL ** -0.5)
    ab_w_out = nrm((N_AB, AB_OUT, D_MODEL), AB_OUT ** -0.5)
    rg_conv_w = nrm((N_AB, RG_CONV, RG_WIDTH), RG_CONV ** -0.5)
    rg_conv_b = nrm((N_AB, RG_WIDTH), 0.01)
    rg_w_a = nrm((N_AB, RG_BLOCKS, RG_BLOCK, RG_BLOCK), RG_BLOCK ** -0.5)
    rg_b_a = nrm((N_AB, RG_WIDTH), 0.01)
    rg_w_x = nrm((N_AB, RG_BLOCKS, RG_BLOCK, RG_BLOCK), RG_BLOCK ** -0.5)
    rg_b_x = nrm((N_AB, RG_WIDTH), 0.01)
    a_pow_c = unif((N_AB, RG_WIDTH), 0.9, 0.999)
    a_base = a_pow_c ** (1.0 / RG_C)
    rg_lambda = jnp.log(a_base) - jnp.log1p(-a_base)
    hg_lower_bounds = nrm((DEPTH + 1, HG_KW), 0.1)
    hg_out_norm = 1.0 + nrm((N_AB, HG_DV), 0.02)
    rw_mu = unif((N_C, 6, D_MODEL), 0.0, 1.0)
    rw_w_in = nrm((N_C, 4, D_MODEL, RW_WIDTH), D_MODEL ** -0.5)
    rw_w_out = nrm((N_C, RW_WIDTH, D_MODEL), RW_WIDTH ** -0.5)
    rw_w0 = unif((N_C, RW_WIDTH), -6.0, -1.0)
    rw_w1 = nrm((N_C, D_MODEL, RW_DECAY_LORA), D_MODEL ** -0.5)
    rw_w2 = nrm((N_C, RW_DECAY_LORA, RW_WIDTH), 0.1 * RW_DECAY_LORA ** -0.5)
    rw_a0 = nrm((N_C, RW_WIDTH), 0.1)
    rw_a1 = nrm((N_C, D_MODEL, RW_AAA_LORA), D_MODEL ** -0.5)
    rw_a2 = nrm((N_C, RW_AAA_LORA, RW_WIDTH), 0.1 * RW_AAA_LORA ** -0.5)
    rw_k_k = 0.85 + nrm((N_C, RW_WIDTH), 0.02)
    rw_k_a = 1.0 + nrm((N_C, RW_WIDTH), 0.02)
    rw_r_k = nrm((N_C, RW_HEADS, RW_HEAD), 0.1)
    rw_ln_w = 1.0 + nrm((N_C, RW_WIDTH), 0.02)
    rw_ln_b = nrm((N_C, RW_WIDTH), 0.01)
    return {"x": x, "pre_norm": pre_norm, "post_norm": post_norm,
            "ab_w_in": ab_w_in, "ab_w_out": ab_w_out,
            "rg_conv_w": rg_conv_w, "rg_conv_b": rg_conv_b,
            "rg_w_a": rg_w_a, "rg_b_a": rg_b_a, "rg_w_x": rg_w_x, "rg_b_x": rg_b_x,
            "rg_lambda": rg_lambda, "hg_lower_bounds": hg_lower_bounds, "hg_out_norm": hg_out_norm,
            "rw_mu": rw_mu, "rw_w_in": rw_w_in, "rw_w_out": rw_w_out,
            "rw_w0": rw_w0, "rw_w1": rw_w1, "rw_w2": rw_w2,
            "rw_a0": rw_a0, "rw_a1": rw_a1, "rw_a2": rw_a2,
            "rw_k_k": rw_k_k, "rw_k_a": rw_k_a, "rw_r_k": rw_r_k,
            "rw_ln_w": rw_ln_w, "rw_ln_b": rw_ln_b}


def reference(x, pre_norm, post_norm, ab_w_in, ab_w_out, rg_conv_w, rg_conv_b, rg_w_a, rg_b_a,
              rg_w_x, rg_b_x, rg_lambda, hg_lower_bounds, hg_out_norm, rw_mu, rw_w_in, rw_w_out,
              rw_w0, rw_w1, rw_w2, rw_a0, rw_a1, rw_a2, rw_k_k, rw_k_a, rw_r_k, rw_ln_w, rw_ln_b):
    lb_table = jnp.cumsum(jax.nn.softmax(_f32(hg_lower_bounds), axis=0), axis=0)
    for layer in range(DEPTH):
        j = layer // 2
        u = rms_norm(x, pre_norm[layer])
        if layer % 2 == 0:
            mix = rglru_hgrn2_layer(u, ab_w_in[j], ab_w_out[j], rg_conv_w[j], rg_conv_b[j],
                                    rg_w_a[j], rg_b_a[j], rg_w_x[j], rg_b_x[j], rg_lambda[j],
                                    lb_table[layer + 1] - lb_table[0], hg_out_norm[j])
        else:
            mix = rwkv7_layer(u, rw_mu[j], rw_w_in[j], rw_w_out[j], rw_w0[j], rw_w1[j], rw_w2[j],
                              rw_a0[j], rw_a1[j], rw_a2[j], rw_k_k[j], rw_k_a[j], rw_r_k[j],
                              rw_ln_w[j], rw_ln_b[j])
        x = x + rms_norm(mix, post_norm[layer]).astype(x.dtype)
    return x
```

```python
import os
import numpy as np
from contextlib import ExitStack
import concourse.bass as bass
import concourse.mybir as mybir
from concourse.bass_utils import run_bass_kernel_spmd

F32 = mybir.dt.float32
BF16 = mybir.dt.bfloat16
AF = mybir.ActivationFunctionType
ALU = mybir.AluOpType

D = 1024
EPS = 1e-6
GN_EPS = 64e-5
TT = 512
NSUB = 4
NCH = 8


class Tok:
    __slots__ = ("name", "w", "r", "dsem", "dcount", "q")

    def __init__(self, name):
        self.q = None
        self.name = name
        self.w = None
        self.r = {}
        self.dsem = None
        self.dcount = 0


class Eng:
    def __init__(self, name, sem):
        self.name = name
        self.sem = sem
        self.count = 0
        self.waited = {}
        self.prog = []


class KB:
    ENGS = ("tensor", "vector", "scalar", "gpsimd", "sync")

    def __init__(self, nc):
        self.nc = nc
        self.es = ExitStack()
        self.engs = {n: Eng(n, self.sem("e_" + n)) for n in self.ENGS}
        self.ntok = 0

    def sem(self, name):
        return self.es.enter_context(self.nc.semaphore(name))

    def sb(self, name, shape, dt):
        return self.es.enter_context(self.nc.sbuf_tensor("s_" + name, list(shape), dt))

    def ps(self, name, shape, dt):
        return self.es.enter_context(self.nc.psum_tensor("p_" + name, list(shape), dt))

    def tok(self, name=None):
        self.ntok += 1
        return Tok(name or f"t{self.ntok}")

    def _deps(self, en, R, W):
        e = self.engs[en]
        deps = []
        for t in R:
            if t.w is not None:
                deps.append(t.w)
        for t in W:
            if t.w is not None:
                deps.append(t.w)
            deps.extend(t.r.values())
        waits = {}
        noself = os.environ.get("MK_NOSELF", "0") == "1"
        for (sem, val, src) in deps:
            if src == "tensor" and en == "tensor":
                continue
            if noself and src == en:
                continue
            key = id(sem)
            if e.waited.get(key, 0) < val:
                e.waited[key] = val
                waits[key] = (sem, val)
        return list(waits.values())

    def emit(self, en, fn, R=(), W=()):
        e = self.engs[en]
        waits = self._deps(en, R, W)
        e.count += 1
        me = (e.sem, e.count, en)
        e.prog.append((waits, fn, (e.sem, 1)))
        for t in R:
            t.r[id(e.sem)] = me
        for t in W:
            t.w = me
            t.r = {}

    def dma(self, q, out, in_, tok, R=(), W=()):
        if out.dtype != in_.dtype:
            q = "gpsimd"
        if tok.q is None:
            tok.q = q
        elif tok.q != q:
            assert out.dtype == in_.dtype or tok.q == "gpsimd", tok.name
            q = tok.q
        e = self.engs[q]
        waits = self._deps(q, R, W)
        if tok.dsem is None:
            tok.dsem = self.sem("d_" + tok.name)
        tok.dcount += 16
        me = (tok.dsem, tok.dcount, "dma")
        e.prog.append((waits, lambda eng: eng.dma_start(out=out, in_=in_), (tok.dsem, 16)))
        for t in R:
            t.r[id(tok.dsem)] = me
        for t in W:
            t.w = me
            t.r = {}

    def collective(self, bin_, bout, tin, tout, name):
        e = self.engs["gpsimd"]
        waits = self._deps("gpsimd", [tin], [tout])
        if getattr(self, "cc_sem", None) is None:
            self.cc_sem = self.sem("cc_all")
            self.cc_count = 0
        sem = self.cc_sem
        self.cc_count += 1
        me = (sem, self.cc_count, "cc")
        groups = [[0, 1, 2, 3], [4, 5, 6, 7]]
        e.prog.append((waits, lambda eng: eng.collective_compute(
            "AllGather", ALU.bypass, replica_groups=groups, ins=[bin_.ap().opt()], outs=[bout.ap().opt()]), (sem, 1)))
        tin.r[id(sem)] = me
        tout.w = me
        tout.r = {}

    def absorb(self, owner, toks):
        for t in toks:
            deps = list(t.r.values()) + ([t.w] if t.w is not None else [])
            for v in deps:
                k = id(v[0])
                if k not in owner.r or owner.r[k][1] < v[1]:
                    owner.r[k] = v

    def wait_all(self, en, toks):
        e = self.engs[en]
        waits = self._deps(en, [], toks)
        e.prog.append((waits, None, None))

    def replay(self):
        nc = self.nc
        with nc.Block() as block:
            for en in self.ENGS:
                prog = self.engs[en].prog

                def body(eng, prog=prog):
                    for waits, fn, inc in prog:
                        for s, v in waits:
                            eng.wait_ge(s, v)
                        if fn is not None:
                            ins = fn(eng)
                            if inc is not None:
                                ins.then_inc(inc[0], inc[1])

                getattr(block, en)(body)

    def mm(self, out, lhsT, rhs, start, stop, R, W):
        self.emit("tensor", lambda e: e.matmul(out, lhsT, rhs, start=start, stop=stop), R, W)

    def tr(self, out, in_, ident, R, W):
        self.emit("tensor", lambda e: e.transpose(out, in_, ident), R, W)

    def act(self, out, in_, func, R, W, bias=None, scale=1.0, accum=None):
        kw = {}
        if bias is not None:
            kw["bias"] = bias
        if accum is not None:
            kw["accum_out"] = accum
        self.emit("scalar", lambda e: e.activation(out, in_, func, scale=scale, **kw), R, W)

    def ts(self, out, in0, s1, s2, op0, op1, R, W, en="vector"):
        if s2 is None:
            self.emit(en, lambda e: e.tensor_scalar(out, in0, s1, None, op0), R, W)
        else:
            self.emit(en, lambda e: e.tensor_scalar(out, in0, s1, s2, op0, op1), R, W)

    def tt(self, out, in0, in1, op, R, W, en="vector"):
        self.emit(en, lambda e: e.tensor_tensor(out, in0, in1, op), R, W)

    def stt(self, out, in0, scalar, in1, op0, op1, R, W, en="vector"):
        self.emit(en, lambda e: e.scalar_tensor_tensor(out, in0, scalar, in1, op0, op1), R, W)

    def scan(self, out, d0, d1, init, op0, op1, R, W):
        self.emit("vector", lambda e: e.tensor_tensor_scan(out, d0, d1, init, op0, op1), R, W)

    def recip(self, out, in_, R, W):
        self.emit("vector", lambda e: e.reciprocal(out, in_), R, W)

    def copy(self, out, in_, R, W, en="vector"):
        self.emit(en, lambda e: e.tensor_copy(out, in_), R, W)

    def memset(self, ap, val, W, en="vector"):
        self.emit(en, lambda e: e.memset(ap, val), (), W)


class Buf:
    def __init__(self, ap, tok):
        self.ap = ap
        self.tok = tok

    def __getitem__(self, idx):
        return self.ap[idx]


class Prog:
    def __init__(self, nc, TS, do_l0=True, do_l1=True, comm=False):
        self.comm = comm
        self.nc = nc
        self.TS = TS
        self.NT = TS // TT
        self.kb = KB(nc)
        self.dram = {}
        self.do_l0 = do_l0
        self.do_l1 = do_l1

    def din(self, name, shape, dt=F32):
        t = self.nc.dram_tensor(name, list(shape), dt, kind="ExternalInput").ap()
        self.dram[name] = t
        return t

    def dout(self, name, shape, dt=F32):
        t = self.nc.dram_tensor(name, list(shape), dt, kind="ExternalOutput").ap()
        self.dram[name] = t
        return t

    def sbuf(self, name, shape, dt):
        return Buf(self.kb.sb(name, shape, dt), self.kb.tok(name))

    def psum(self, name, shape, dt):
        return Buf(self.kb.ps(name, shape, dt), self.kb.tok(name))

    def alloc_common(self):
        kb = self.kb
        self.xs = [Buf(None, kb.tok(f"xs{g}")) for g in range(NSUB)]
        self.xs_t = kb.sb("xs", [128, NSUB, D], F32)
        for g in range(NSUB):
            self.xs[g].ap = self.xs_t[:, g, :]
        self.psb = [self.psum(f"psb{i}", [128, 1024], BF16) for i in range(2)]
        self.psf = [self.psum(f"psf{i}", [128, 512], F32) for i in range(6)]
        self.ident = self.sbuf("ident", [128, 128], BF16)
        self.maskU = self.sbuf("maskU", [128, 64], BF16)
        self.f = [self.sbuf(f"f{i}", [128, 516], F32) for i in range(12)]
        self.b = [self.sbuf(f"b{i}", [128, 512], BF16) for i in range(12)]
        self.sm = [self.sbuf(f"sm{i}", [128, 16], F32) for i in range(14)]
        self.xn = self.sbuf("xn", [128, 1024], BF16)
        self.uT = self.sbuf("uT", [128, 8, 516], BF16)
        self.merged = self.sbuf("merged", [128, 16, 512], BF16)
        self.wout0 = self.sbuf("wout0", [128, 16, 1024], BF16)
        self.wout1 = self.sbuf("wout1", [128, 8, 1024], BF16)
        self.NWS = 4
        self.wslot = [self.sbuf(f"ws{i}", [128, 8, 128], BF16) for i in range(self.NWS)]
        self.wi = 0
        self.cst = self.sbuf("cst", [128, 4], F32)
        self.postbc0 = self.sbuf("postbc0", [128, 1024], F32)
        self.postbc1 = self.sbuf("postbc1", [128, 1024], F32)
        self.dmaq = 0

    def q(self):
        self.dmaq += 1
        return "sync" if self.dmaq % 2 else "gpsimd"

    def load_w(self, src_ap):
        slots = getattr(self, "wslots_active", None) or self.wslot
        s = slots[self.wi % len(slots)]
        self.wi += 1
        self.kb.dma(self.q(), s.ap[:, :, :], src_ap.rearrange("(kc p) n -> p kc n", p=128), s.tok, W=[s.tok])
        return s

    def consts(self):
        kb = self.kb
        d = self.dram
        kb.dma("sync", self.ident.ap[:, :], d["ident"][:, :], self.ident.tok, W=[self.ident.tok])
        kb.dma("sync", self.maskU.ap[:, :], d["maskU"][:, :], self.maskU.tok, W=[self.maskU.tok])
        kb.memset(self.cst.ap[:, 0:1], EPS, [self.cst.tok])
        kb.memset(self.cst.ap[:, 1:2], 1.0, [self.cst.tok])
        kb.memset(self.cst.ap[:, 2:3], GN_EPS, [self.cst.tok])

    def load_x(self, tile, xs=None):
        kb = self.kb
        x = self.dram["x"]
        xs = xs or self.xs
        for sub in range(NSUB):
            g = tile * NSUB + sub
            kb.dma("sync", xs[sub].ap, x[g * 128:(g + 1) * 128, :], xs[sub].tok, W=[xs[sub].tok])

    def store_out(self, tile):
        kb = self.kb
        o = self.dram["out"]
        for sub in range(NSUB):
            g = tile * NSUB + sub
            kb.dma("sync", o[g * 128:(g + 1) * 128, :], self.xs[sub].ap, self.xs[sub].tok, R=[self.xs[sub].tok])

    def norm_T(self, tile, off):
        kb = self.kb
        ss, rms, rstd = self.sm[0], self.sm[1], self.sm[2]
        xn2 = getattr(self, "norm_xn2", None)
        xns = [self.xn, xn2 if xn2 is not None else self.xn]
        for sub in range(NSUB):
            xin = self.xs[sub]
            kb.act(self.xn.ap[:, :], xin.ap, AF.Square, R=[xin.tok], W=[self.xn.tok, ss.tok],
                   accum=ss.ap[:, sub:sub + 1])
        kb.act(rms.ap[:, 0:NSUB], ss.ap[:, 0:NSUB], AF.Sqrt, R=[ss.tok, self.cst.tok], W=[rms.tok],
               scale=1.0 / D, bias=self.cst.ap[:, 0:1])
        kb.recip(rstd.ap[:, 0:NSUB], rms.ap[:, 0:NSUB], R=[rms.tok], W=[rstd.tok])
        for sub in range(NSUB):
            xin = self.xs[sub]
            xn = xns[sub % 2]
            kb.ts(xn.ap[:, 0:1024], xin.ap, rstd.ap[:, sub:sub + 1], None, ALU.mult, None,
                  R=[xin.tok, rstd.tok], W=[xn.tok])
            pb = self.psb[sub % 2]
            for kc in range(8):
                kb.tr(pb.ap[:, kc * 128:(kc + 1) * 128], xn.ap[:, kc * 128:(kc + 1) * 128],
                      self.ident.ap[:, :], R=[xn.tok, self.ident.tok], W=[pb.tok])
            kb.tt(self.uT.ap[:, :, off + sub * 128: off + (sub + 1) * 128],
                  pb.ap[:, :].rearrange("p (k t) -> p k t", t=128),
                  self.gain.ap[:, self.gain_off:self.gain_off + 8].unsqueeze(2).broadcast_to([128, 8, 128]),
                  ALU.mult, R=[pb.tok, self.gain.tok], W=[self.uT.tok])

    def out_proj(self, tile, ncb, wout, postbc):
        kb = self.kb
        ss, rms, rstd = self.sm[3], self.sm[4], self.sm[5]
        tmpo = self.f[0]
        junk = self.b[0]
        for sub in range(NSUB):
            g = sub
            pss = [self.psf[0], self.psf[1]] if sub % 2 == 0 else [self.psf[2], self.psf[3]]
            for dh in range(2):
                for cb in range(ncb):
                    kb.mm(pss[dh].ap[:, :], self.merged.ap[:, cb, sub * 128:(sub + 1) * 128],
                          wout.ap[:, cb, dh * 512:(dh + 1) * 512], cb == 0, cb == ncb - 1,
                          R=[self.merged.tok, wout.tok], W=[pss[dh].tok])
            for dh in range(2):
                kb.act(junk.ap[:, 0:512], pss[dh].ap[:, :], AF.Square, R=[pss[dh].tok],
                       W=[junk.tok, ss.tok], accum=ss.ap[:, dh:dh + 1])
            kb.tt(ss.ap[:, 2:3], ss.ap[:, 0:1], ss.ap[:, 1:2], ALU.add, R=[ss.tok], W=[ss.tok])
            kb.act(rms.ap[:, 0:1], ss.ap[:, 2:3], AF.Sqrt, R=[ss.tok, self.cst.tok], W=[rms.tok],
                   scale=1.0 / D, bias=self.cst.ap[:, 0:1])
            kb.recip(rstd.ap[:, 0:1], rms.ap[:, 0:1], R=[rms.tok], W=[rstd.tok])
            for dh in range(2):
                kb.stt(tmpo.ap[:, 0:512], pss[dh].ap[:, :], rstd.ap[:, 0:1],
                       postbc.ap[:, dh * 512:(dh + 1) * 512], ALU.mult, ALU.mult,
                       R=[pss[dh].tok, rstd.tok, postbc.tok], W=[tmpo.tok])
                kb.tt(self.xs[g].ap[:, dh * 512:(dh + 1) * 512], self.xs[g].ap[:, dh * 512:(dh + 1) * 512],
                      tmpo.ap[:, 0:512], ALU.add, R=[tmpo.tok, self.xs[g].tok], W=[self.xs[g].tok])

    PV0_COLS = 104

    def l0_setup(self):
        kb = self.kb
        d = self.dram
        self.pv0 = self.sbuf("pv0", [128, self.PV0_COLS], F32)
        self.pd0 = self.sbuf("pd0", [128, 64], F32)
        self.WA = self.sbuf("WA", [128, 8, 128], BF16)
        self.WX = self.sbuf("WX", [128, 8, 128], BF16)
        self.st0 = self.sbuf("st0", [128, 1056], F32)
        self.rmask = self.sbuf("rmask", [128, 512], BF16)
        kb.dma("sync", self.pv0.ap[:, :], d["pv0"][:, :], self.pv0.tok, W=[self.pv0.tok])
        kb.dma("gpsimd", self.WA.ap[:, :, :], d["wa_bd"][:, :, :], self.WA.tok, W=[self.WA.tok])
        kb.dma("gpsimd", self.WX.ap[:, :, :], d["wx_bd"][:, :, :], self.WX.tok, W=[self.WX.tok])
        if self.comm:
            kb.memset(self.st0.ap[:, :], 0.0, [self.st0.tok])
        else:
            kb.dma("sync", self.st0.ap[:, :], d["st0_in"][:, :], self.st0.tok, W=[self.st0.tok])
        kb.dma("sync", self.rmask.ap[:, :], d["rmask"][:, :], self.rmask.tok, W=[self.rmask.tok])
        pv, pd = self.pv0, self.pd0
        kb.act(pd.ap[:, 32:40], pv.ap[:, 56:64], AF.Exp, R=[pv.tok], W=[pd.tok], scale=-1.0)
        kb.act(pd.ap[:, 32:40], pd.ap[:, 32:40], AF.Ln, R=[pd.tok, self.cst.tok], W=[pd.tok],
               bias=self.cst.ap[:, 1:2])
        kb.ts(pd.ap[:, 0:8], pd.ap[:, 32:40], -8.0, None, ALU.mult, None, R=[pd.tok], W=[pd.tok])
        kb.ts(pd.ap[:, 8:16], pd.ap[:, 32:40], -16.0, None, ALU.mult, None, R=[pd.tok], W=[pd.tok])
        kb.act(pd.ap[:, 32:56], pv.ap[:, 64:88], AF.Exp, R=[pv.tok], W=[pd.tok])
        kb.tt(pd.ap[:, 56:64], pd.ap[:, 32:40], pd.ap[:, 40:48], ALU.add, R=[pd.tok], W=[pd.tok])
        kb.tt(pd.ap[:, 56:64], pd.ap[:, 56:64], pd.ap[:, 48:56], ALU.add, R=[pd.tok], W=[pd.tok])
        kb.recip(pd.ap[:, 56:64], pd.ap[:, 56:64], R=[pd.tok], W=[pd.tok])
        kb.tt(pd.ap[:, 16:24], pd.ap[:, 40:48], pd.ap[:, 56:64], ALU.mult, R=[pd.tok], W=[pd.tok])
        kb.ts(pd.ap[:, 24:32], pd.ap[:, 16:24], -1.0, 1.0, ALU.mult, ALU.add, R=[pd.tok], W=[pd.tok])

    def load_wout0(self, cb):
        w = self.dram["w_out0"].rearrange("(cb p) n -> p cb n", p=128)
        self.kb.dma("gpsimd", self.wout0.ap[:, cb, :], w[:, cb, :], self.wout0.tok, W=[self.wout0.tok])

    def load_wout1(self, cb):
        w = self.dram["w_out1"].rearrange("(cb p) n -> p cb n", p=128)
        self.kb.dma("gpsimd", self.wout1.ap[:, cb, :], w[:, cb, :], self.wout1.tok, W=[self.wout1.tok])

    def l0_begin(self):
        kb = self.kb
        d = self.dram
        if not self.comm:
            for cb in range(16):
                self.load_wout0(cb)
        kb.dma("sync", self.postbc0.ap[:, :], d["post0_bc"][:, :], self.postbc0.tok, W=[self.postbc0.tok])

    def l0_rg_proj(self, cb, p1=False, banks=None):
        kb = self.kb
        d = self.dram
        px, pg = banks if banks is not None else (self.psf[2], self.psf[3])
        GW = 256 if self.comm else 1024
        wx = self.load_w(d["w_in0"][:, cb * 128:(cb + 1) * 128])
        for kc in range(8):
            kb.mm(px.ap[:, :], wx.ap[:, kc, :], self.uT.ap[:, kc, 0:512], kc == 0, kc == 7,
                  R=[wx.tok, self.uT.tok], W=[px.tok])
        if not p1:
            wg = self.load_w(d["w_in0"][:, GW + cb * 128:GW + (cb + 1) * 128])
            for kc in range(8):
                kb.mm(pg.ap[:, :], wg.ap[:, kc, :], self.uT.ap[:, kc, 0:512], kc == 0, kc == 7,
                      R=[wg.tok, self.uT.tok], W=[pg.tok])

    def l0_rg(self, cb, p1=False, after_evac=None):
        kb = self.kb
        d = self.dram
        pv, pd = self.pv0, self.pd0
        f, b = self.f, self.b
        xr, xc, r, ig, a, a2, gi, bb, h, sg = f[1], f[2], f[3], f[4], f[5], f[6], f[7], f[8], f[9], f[10]
        xcb = b[0]
        px, pg, pa, pi = self.psf[2], self.psf[3], self.psf[4], self.psf[5]
        st = self.st0
        kb.copy(xr.ap[:, 0:3], st.ap[:, cb * 3:cb * 3 + 3], R=[st.tok], W=[xr.tok])
        kb.act(xr.ap[:, 3:515], px.ap[:, :], AF.Copy, R=[px.tok], W=[xr.tok])
        if not p1:
            kb.act(sg.ap[:, 0:512], pg.ap[:, :], AF.Silu, R=[pg.tok], W=[sg.tok])
        if after_evac is not None:
            after_evac()
        kb.copy(st.ap[:, cb * 3:cb * 3 + 3], xr.ap[:, 512:515], R=[xr.tok], W=[st.tok])
        kb.ts(xc.ap[:, 0:512], xr.ap[:, 3:515], pv.ap[:, 24 + cb:25 + cb], pv.ap[:, 32 + cb:33 + cb],
              ALU.mult, ALU.add, R=[xr.tok, pv.tok], W=[xc.tok])
        for k in (2, 1, 0):
            kb.stt(xc.ap[:, 0:512], xr.ap[:, k:k + 512], pv.ap[:, k * 8 + cb:k * 8 + cb + 1], xc.ap[:, 0:512],
                   ALU.mult, ALU.add, R=[xr.tok, pv.tok, xc.tok], W=[xc.tok])
        kb.act(xcb.ap[:, :], xc.ap[:, 0:512], AF.Copy, R=[xc.tok], W=[xcb.tok])
        kb.mm(pa.ap[:, :], self.WA.ap[:, cb, :], xcb.ap[:, :], True, True, R=[self.WA.tok, xcb.tok], W=[pa.tok])
        kb.mm(pi.ap[:, :], self.WX.ap[:, cb, :], xcb.ap[:, :], True, True, R=[self.WX.tok, xcb.tok], W=[pi.tok])
        if p1:
            rs_ = self.sm[13]
            kb.act(r.ap[:, 0:512], pa.ap[:, :], AF.Sigmoid, R=[pa.tok, pv.tok], W=[r.tok, rs_.tok],
                   bias=pv.ap[:, 40 + cb:41 + cb], accum=rs_.ap[:, 0:1])
            kb.tt(self.dec0.ap[:, cb:cb + 1], self.dec0.ap[:, cb:cb + 1], rs_.ap[:, 0:1], ALU.add,
                  R=[rs_.tok, self.dec0.tok], W=[self.dec0.tok])
        else:
            kb.act(r.ap[:, 0:512], pa.ap[:, :], AF.Sigmoid, R=[pa.tok, pv.tok], W=[r.tok], bias=pv.ap[:, 40 + cb:41 + cb])
        kb.act(ig.ap[:, 0:512], pi.ap[:, :], AF.Sigmoid, R=[pi.tok, pv.tok], W=[ig.tok],
               bias=pv.ap[:, 48 + cb:49 + cb])
        kb.act(a.ap[:, 0:512], r.ap[:, 0:512], AF.Exp, R=[r.tok, pd.tok], W=[a.tok], scale=pd.ap[:, cb:cb + 1])
        kb.act(a2.ap[:, 0:512], r.ap[:, 0:512], AF.Exp, R=[r.tok, pd.tok], W=[a2.tok],
               scale=pd.ap[:, 8 + cb:9 + cb])
        kb.act(a2.ap[:, 0:512], a2.ap[:, 0:512], AF.Relu, R=[a2.tok, self.cst.tok], W=[a2.tok], scale=-1.0,
               bias=self.cst.ap[:, 1:2])
        kb.act(a2.ap[:, 0:512], a2.ap[:, 0:512], AF.Sqrt, R=[a2.tok], W=[a2.tok])
        kb.tt(gi.ap[:, 0:512], ig.ap[:, 0:512], xc.ap[:, 0:512], ALU.mult, R=[ig.tok, xc.tok], W=[gi.tok])
        kb.tt(bb.ap[:, 0:512], a2.ap[:, 0:512], gi.ap[:, 0:512], ALU.mult, R=[a2.tok, gi.tok], W=[bb.tok])
        kb.scan(h.ap[:, 0:512], a.ap[:, 0:512], bb.ap[:, 0:512], st.ap[:, 24 + cb:25 + cb], ALU.mult, ALU.add,
                R=[a.tok, bb.tok, st.tok], W=[h.tok])
        kb.copy(st.ap[:, 24 + cb:25 + cb], h.ap[:, 511:512], R=[h.tok], W=[st.tok])
        if p1:
            return
        kb.tt(self.merged.ap[:, cb, :], h.ap[:, 0:512], sg.ap[:, 0:512], ALU.mult, R=[h.tok, sg.tok],
              W=[self.merged.tok])

    def l0_rg2(self):
        kb = self.kb
        pv, pd = self.pv0, self.pd0
        st = self.st0
        f = self.f
        setA = dict(t=[f[i] for i in range(1, 11)], xcb=self.b[0],
                    ps=[self.psf[2], self.psf[3], self.psf[4], self.psf[5]])
        psb0 = Buf(self.psb[0].ap[:, :].bitcast(F32), self.psb[0].tok)
        psb1 = Buf(self.psb[1].ap[:, :].bitcast(F32), self.psb[1].tok)
        setB = dict(t=self.rgB_f, xcb=self.rgB_xcb, ps=[self.psf[0], self.psf[1], psb0, psb1])
        U = [setA, setB]
        for u in range(2):
            self.l0_rg_proj(u, False, (U[u]["ps"][0], U[u]["ps"][1]))

        def T(u, i):
            return U[u]["t"][i]
        for u in range(2):
            px, pg, pa, pi = U[u]["ps"]
            kb.copy(T(u, 0).ap[:, 0:3], st.ap[:, u * 3:u * 3 + 3], R=[st.tok], W=[T(u, 0).tok])
            kb.act(T(u, 0).ap[:, 3:515], px.ap[:, 0:512], AF.Copy, R=[px.tok], W=[T(u, 0).tok])
        for u in range(2):
            kb.copy(st.ap[:, u * 3:u * 3 + 3], T(u, 0).ap[:, 512:515], R=[T(u, 0).tok], W=[st.tok])
            kb.ts(T(u, 1).ap[:, 0:512], T(u, 0).ap[:, 3:515], pv.ap[:, 24 + u:25 + u], pv.ap[:, 32 + u:33 + u],
                  ALU.mult, ALU.add, R=[T(u, 0).tok, pv.tok], W=[T(u, 1).tok])
        for k in (2, 1, 0):
            for u in range(2):
                kb.stt(T(u, 1).ap[:, 0:512], T(u, 0).ap[:, k:k + 512], pv.ap[:, k * 8 + u:k * 8 + u + 1],
                       T(u, 1).ap[:, 0:512], ALU.mult, ALU.add, R=[T(u, 0).tok, pv.tok, T(u, 1).tok], W=[T(u, 1).tok])
        for u in range(2):
            xcb = U[u]["xcb"]
            kb.act(xcb.ap[:, 0:512], T(u, 1).ap[:, 0:512], AF.Copy, R=[T(u, 1).tok], W=[xcb.tok])
        for u in range(2):
            px, pg, pa, pi = U[u]["ps"]
            xcb = U[u]["xcb"]
            kb.mm(pa.ap[:, 0:512], self.WA.ap[:, u, :], xcb.ap[:, 0:512], True, True, R=[self.WA.tok, xcb.tok], W=[pa.tok])
            kb.mm(pi.ap[:, 0:512], self.WX.ap[:, u, :], xcb.ap[:, 0:512], True, True, R=[self.WX.tok, xcb.tok], W=[pi.tok])
        for u in range(2):
            px, pg, pa, pi = U[u]["ps"]
            kb.act(T(u, 2).ap[:, 0:512], pa.ap[:, 0:512], AF.Sigmoid, R=[pa.tok, pv.tok], W=[T(u, 2).tok],
                   bias=pv.ap[:, 40 + u:41 + u])
            kb.act(T(u, 3).ap[:, 0:512], pi.ap[:, 0:512], AF.Sigmoid, R=[pi.tok, pv.tok], W=[T(u, 3).tok],
                   bias=pv.ap[:, 48 + u:49 + u])
            kb.act(T(u, 9).ap[:, 0:512], pg.ap[:, 0:512], AF.Sigmoid, R=[pg.tok], W=[T(u, 9).tok])
            kb.tt(T(u, 9).ap[:, 0:512], T(u, 9).ap[:, 0:512], pg.ap[:, 0:512], ALU.mult, R=[T(u, 9).tok, pg.tok],
                  W=[T(u, 9).tok])
        for u in range(2):
            kb.act(T(u, 4).ap[:, 0:512], T(u, 2).ap[:, 0:512], AF.Exp, R=[T(u, 2).tok, pd.tok], W=[T(u, 4).tok],
                   scale=pd.ap[:, u:u + 1])
            kb.act(T(u, 5).ap[:, 0:512], T(u, 2).ap[:, 0:512], AF.Exp, R=[T(u, 2).tok, pd.tok], W=[T(u, 5).tok],
                   scale=pd.ap[:, 8 + u:9 + u])
        for u in range(2):
            kb.tt(T(u, 6).ap[:, 0:512], T(u, 3).ap[:, 0:512], T(u, 1).ap[:, 0:512], ALU.mult,
                  R=[T(u, 3).tok, T(u, 1).tok], W=[T(u, 6).tok])
            kb.act(T(u, 5).ap[:, 0:512], T(u, 5).ap[:, 0:512], AF.Relu, R=[T(u, 5).tok, self.cst.tok], W=[T(u, 5).tok],
                   scale=-1.0, bias=self.cst.ap[:, 1:2])
        for u in range(2):
            kb.act(T(u, 5).ap[:, 0:512], T(u, 5).ap[:, 0:512], AF.Sqrt, R=[T(u, 5).tok], W=[T(u, 5).tok])
        for u in range(2):
            kb.tt(T(u, 7).ap[:, 0:512], T(u, 5).ap[:, 0:512], T(u, 6).ap[:, 0:512], ALU.mult,
                  R=[T(u, 5).tok, T(u, 6).tok], W=[T(u, 7).tok])
            kb.scan(T(u, 8).ap[:, 0:512], T(u, 4).ap[:, 0:512], T(u, 7).ap[:, 0:512], st.ap[:, 24 + u:25 + u], ALU.mult,
                    ALU.add, R=[T(u, 4).tok, T(u, 7).tok, st.tok], W=[T(u, 8).tok])
            kb.copy(st.ap[:, 24 + u:25 + u], T(u, 8).ap[:, 511:512], R=[T(u, 8).tok], W=[st.tok])
        for u in range(2):
            kb.tt(self.merged.ap[:, u, :], T(u, 8).ap[:, 0:512], T(u, 9).ap[:, 0:512], ALU.mult,
                  R=[T(u, 8).tok, T(u, 9).tok], W=[self.merged.tok])

    def l0_hg_proj(self, hh, p1=False):
        kb = self.kb
        d = self.dram
        pq, pf, pg, pvv = self.psf[2], self.psf[3], self.psf[4], self.psf[5]
        GW = 256 if self.comm else 1024
        wf = self.load_w(d["w_in0"][:, 3 * GW + hh * 128:3 * GW + (hh + 1) * 128])
        wi = self.load_w(d["w_in0"][:, 4 * GW + hh * 128:4 * GW + (hh + 1) * 128])
        plist = [(wf, pf)]
        if not p1:
            wq = self.load_w(d["w_in0"][:, 2 * GW + hh * 128:2 * GW + (hh + 1) * 128])
            wg = self.load_w(d["w_in0"][:, 5 * GW + hh * 128:5 * GW + (hh + 1) * 128])
            plist += [(wq, pq), (wg, pg)]
        for (w_, p_) in plist:
            for kc in range(8):
                kb.mm(p_.ap[:, :], w_.ap[:, kc, :], self.uT.ap[:, kc, 0:512], kc == 0, kc == 7,
                      R=[w_.tok, self.uT.tok], W=[p_.tok])
        for sub in range(NSUB):
            for kc in range(8):
                kb.mm(pvv.ap[:, sub * 128:(sub + 1) * 128], self.uT.ap[:, kc, sub * 128:(sub + 1) * 128],
                      wi.ap[:, kc, :], kc == 0, kc == 7, R=[wi.tok, self.uT.tok], W=[pvv.tok])

    def l0_hg(self, hh, p1=False, after_evac=None):
        kb = self.kb
        d = self.dram
        pv, pd = self.pv0, self.pd0
        f, b, sm = self.f, self.b, self.sm
        q, sg, sf, lf, kv, g, gm, Ep, Em, tmp, Sp = f[1], f[2], f[3], f[4], f[5], f[6], f[7], f[8], f[9], f[10], f[11]
        qt, kt, vsb, ktT, scT, Spb, on = b[1], b[2], b[3], b[4], b[5], b[6], b[7]
        emid, ech, ss2, rms2, rstd2 = sm[6], sm[7], sm[8], sm[9], sm[10]
        pq, pf, pg, pvv, po, pm = self.psf[2], self.psf[3], self.psf[4], self.psf[5], self.psf[0], self.psf[1]
        st = self.st0
        S = st.ap[:, 32 + hh * 128:32 + (hh + 1) * 128]
        if not p1:
            kb.act(q.ap[:, 0:512], pq.ap[:, :], AF.Silu, R=[pq.tok], W=[q.tok])
            kb.act(sg.ap[:, 0:512], pg.ap[:, :], AF.Silu, R=[pg.tok], W=[sg.tok])
        kb.act(sf.ap[:, 0:512], pf.ap[:, :], AF.Sigmoid, R=[pf.tok], W=[sf.tok])
        kb.act(vsb.ap[:, :], pvv.ap[:, :], AF.Copy, R=[pvv.tok], W=[vsb.tok])
        if after_evac is not None:
            after_evac()
        kb.ts(sf.ap[:, 0:512], sf.ap[:, 0:512], pd.ap[:, 24 + hh:25 + hh], pd.ap[:, 16 + hh:17 + hh], ALU.mult, ALU.add,
              R=[sf.tok, pd.tok], W=[sf.tok])
        if p1:
            ls_ = self.sm[13]
            kb.act(lf.ap[:, 0:512], sf.ap[:, 0:512], AF.Ln, R=[sf.tok], W=[lf.tok, ls_.tok], accum=ls_.ap[:, 0:1])
            kb.tt(self.dec0.ap[:, 8 + hh:9 + hh], self.dec0.ap[:, 8 + hh:9 + hh], ls_.ap[:, 0:1], ALU.add,
                  R=[ls_.tok, self.dec0.tok], W=[self.dec0.tok])
        else:
            kb.act(lf.ap[:, 0:512], sf.ap[:, 0:512], AF.Ln, R=[sf.tok], W=[lf.tok])
        kb.ts(kv.ap[:, 0:512], sf.ap[:, 0:512], -1.0, 1.0, ALU.mult, ALU.add, R=[sf.tok], W=[kv.tok])
        kb.scan(g.ap[:, 0:512], self.rmask.ap[:, :], lf.ap[:, 0:512], 0.0, ALU.mult, ALU.add,
                R=[self.rmask.tok, lf.tok], W=[g.tok])
        g3 = g.ap[:, 0:512].rearrange("p (c t) -> p c t", t=64)
        gm3 = gm.ap[:, 0:512].rearrange("p (c t) -> p c t", t=64)
        Ep3 = Ep.ap[:, 0:512].rearrange("p (c t) -> p c t", t=64)
        kb.tt(gm3, g3, g3[:, :, 31:32].broadcast_to([128, NCH, 64]), ALU.subtract, R=[g.tok], W=[gm.tok])
        kb.act(Ep.ap[:, 0:512], gm.ap[:, 0:512], AF.Exp, R=[gm.tok], W=[Ep.tok])
        kb.act(Em.ap[:, 0:512], gm.ap[:, 0:512], AF.Exp, R=[gm.tok], W=[Em.tok], scale=-1.0)
        kb.act(emid.ap[:, 0:NCH].unsqueeze(2), g3[:, :, 31:32], AF.Exp, R=[g.tok], W=[emid.tok])
        if not p1:
            kb.tt(qt.ap[:, :], q.ap[:, 0:512], Ep.ap[:, 0:512], ALU.mult, R=[q.tok, Ep.tok], W=[qt.tok])
        kb.tt(kt.ap[:, :], kv.ap[:, 0:512], Em.ap[:, 0:512], ALU.mult, R=[kv.tok, Em.tok], W=[kt.tok])
        kb.tt(ech.ap[:, 0:NCH - 1].unsqueeze(2), Ep3[:, 0:NCH - 1, 63:64], emid.ap[:, 1:NCH].unsqueeze(2), ALU.mult,
              R=[Ep.tok, emid.tok], W=[ech.tok])
        kb.copy(ech.ap[:, NCH - 1:NCH], Ep.ap[:, 511:512], R=[Ep.tok], W=[ech.tok])
        kte = b[8]
        kb.tt(kte.ap[:, :].rearrange("p (c t) -> p c t", t=64), kt.ap[:, :].rearrange("p (c t) -> p c t", t=64),
              ech.ap[:, 0:NCH].unsqueeze(2).broadcast_to([128, NCH, 64]), ALU.mult, R=[kt.tok, ech.tok], W=[kte.tok])
        pbt = self.psb[0]
        for sub in range(NSUB):
            kb.tr(pbt.ap[:, sub * 128:(sub + 1) * 128], kte.ap[:, sub * 128:(sub + 1) * 128], self.ident.ap[:, :],
                  R=[kte.tok, self.ident.tok], W=[pbt.tok])
        kb.copy(ktT.ap[:, :], pbt.ap[:, 0:512], R=[pbt.tok], W=[ktT.tok])
        dbank = [self.psb[0].ap[:, :].bitcast(F32), self.psb[1].ap[:, :].bitcast(F32)]
        for c in range(NCH):
            sub, half = divmod(c, 2)
            p0 = 64 * half
            kb.mm(dbank[half][:, sub * 128:(sub + 1) * 128], ktT.ap[p0:p0 + 64, sub * 128:(sub + 1) * 128],
                  vsb.ap[p0:p0 + 64, sub * 128:(sub + 1) * 128], True, True, R=[ktT.tok, vsb.tok], W=[self.psb[half].tok])
        SpF = [f[10], f[10], f[10], f[10], f[11], f[11], f[11], f[11]]
        SpA = [b[9], b[9], b[9], b[9], b[10], b[10], b[10], b[10]]
        kb.act(SpF[0].ap[:, 0:128], S, AF.Copy, R=[st.tok, emid.tok], W=[SpF[0].tok], scale=emid.ap[:, 0:1])
        for c in range(NCH):
            sub, half = divmod(c, 2)
            dl = dbank[half][:, sub * 128:(sub + 1) * 128]
            dt_ = self.psb[half].tok
            ec = ech.ap[:, c:c + 1]
            cur = SpF[c].ap[:, (c % 4) * 128:(c % 4 + 1) * 128]
            if c < NCH - 1:
                n_ = c + 1
                kb.stt(SpF[n_].ap[:, (n_ % 4) * 128:(n_ % 4 + 1) * 128], cur, ec, dl, ALU.mult, ALU.add,
                       R=[SpF[c].tok, ech.tok, dt_], W=[SpF[n_].tok])
            else:
                kb.stt(S, cur, ec, dl, ALU.mult, ALU.add, R=[SpF[c].tok, ech.tok, dt_], W=[st.tok])
            if not p1 and c % 4 == 3:
                kb.act(SpA[c].ap[:, :], SpF[c].ap[:, 0:512], AF.Copy, R=[SpF[c].tok], W=[SpA[c].tok])
        if not p1:
            for sub in range(NSUB):
                for half in range(2):
                    c = sub * 2 + half
                    p0 = 64 * half
                    kb.mm(pm.ap[p0:p0 + 64, 0:64], kt.ap[:, c * 64:(c + 1) * 64], qt.ap[:, c * 64:(c + 1) * 64], True, True,
                          R=[kt.tok, qt.tok], W=[pm.tok])
                kb.tt(scT.ap[:, 0:64], pm.ap[:, 0:64], self.maskU.ap[:, :], ALU.mult, R=[pm.tok, self.maskU.tok],
                      W=[scT.tok])
                for half in range(2):
                    c = sub * 2 + half
                    p0 = 64 * half
                    kb.mm(po.ap[p0:p0 + 64, sub * 128:(sub + 1) * 128], scT.ap[p0:p0 + 64, 0:64],
                          vsb.ap[p0:p0 + 64, sub * 128:(sub + 1) * 128], True, False, R=[scT.tok, vsb.tok], W=[po.tok])
                    kb.mm(po.ap[p0:p0 + 64, sub * 128:(sub + 1) * 128], qt.ap[:, c * 64:(c + 1) * 64],
                          SpA[c].ap[:, (c % 4) * 128:(c % 4 + 1) * 128], False, True, R=[qt.tok, SpA[c].tok], W=[po.tok])
        if p1:
            return
        for sub in range(NSUB):
            kb.act(self.b[0].ap[:, 0:128], po.ap[:, sub * 128:(sub + 1) * 128], AF.Square, R=[po.tok],
                   W=[self.b[0].tok, ss2.tok], accum=ss2.ap[:, sub:sub + 1])
        kb.act(rms2.ap[:, 0:4], ss2.ap[:, 0:4], AF.Sqrt, R=[ss2.tok, self.cst.tok], W=[rms2.tok], scale=1.0 / 128,
               bias=self.cst.ap[:, 0:1])
        kb.recip(rstd2.ap[:, 0:4], rms2.ap[:, 0:4], R=[rms2.tok], W=[rstd2.tok])
        pbo = self.psb[1]
        for sub in range(NSUB):
            kb.ts(on.ap[:, sub * 128:(sub + 1) * 128], po.ap[:, sub * 128:(sub + 1) * 128], rstd2.ap[:, sub:sub + 1], None,
                  ALU.mult, None, R=[po.tok, rstd2.tok], W=[on.tok])
        for sub in range(NSUB):
            kb.tr(pbo.ap[:, sub * 128:(sub + 1) * 128], on.ap[:, sub * 128:(sub + 1) * 128], self.ident.ap[:, :],
                  R=[on.tok, self.ident.tok], W=[pbo.tok])
        kb.stt(self.merged.ap[:, 8 + hh, :], pbo.ap[:, 0:512], pv.ap[:, 96:97], sg.ap[:, 0:512], ALU.mult, ALU.mult,
               R=[pbo.tok, pv.tok, sg.tok], W=[self.merged.tok])

    def l0_hg2(self, after_proj=None):
        kb = self.kb
        d = self.dram
        pv, pd = self.pv0, self.pd0
        f, b, sm = self.f, self.b, self.sm
        st = self.st0
        psf = self.psf
        A = dict(q=f[1], sg=f[2], sf=f[3], lf=f[4], kv=f[5], g=f[6], gm=f[7], Ep=f[8], Em=f[9], SpF=[f[10], f[11]],
                 qt=b[1], kt=b[2], vsb=b[3], ktT=b[4], scT=b[5], on=b[7], kte=b[8], SpA=[b[9], b[10]], junk=b[0],
                 emid=sm[6], ech=sm[7], ss2=sm[8], rms2=sm[9], rstd2=sm[10],
                 pbt=self.psb[0], dl=[psf[2], psf[3]], po=psf[0])
        Bf, Bb, Bs = self.hgB_f, self.hgB_b, self.hgB_sm
        B = dict(q=Bf[0], sg=Bf[1], sf=Bf[2], lf=Bf[3], kv=Bf[4], g=Bf[5], gm=Bf[6], Ep=Bf[7], Em=Bf[8], SpF=[Bf[9], Bf[10]],
                 qt=Bb[0], kt=Bb[1], vsb=Bb[2], ktT=Bb[3], scT=Bb[4], on=Bb[5], kte=Bb[6], SpA=[Bb[7], Bb[8]], junk=Bb[6],
                 emid=Bs[0], ech=Bs[1], ss2=Bs[2], rms2=Bs[3], rstd2=Bs[4],
                 pbt=self.psb[1], dl=[psf[4], psf[5]], po=psf[1])
        U = [A, B]
        pq, pf, pg, pvv = psf[2], psf[3], psf[4], psf[5]
        for u in range(2):
            X = U[u]
            self.l0_hg_proj(u)
            kb.act(X["q"].ap[:, 0:512], pq.ap[:, :], AF.Sigmoid, R=[pq.tok], W=[X["q"].tok])
            kb.act(X["sg"].ap[:, 0:512], pg.ap[:, :], AF.Sigmoid, R=[pg.tok], W=[X["sg"].tok])
            kb.act(X["sf"].ap[:, 0:512], pf.ap[:, :], AF.Sigmoid, R=[pf.tok], W=[X["sf"].tok])
            kb.act(X["vsb"].ap[:, 0:512], pvv.ap[:, :], AF.Copy, R=[pvv.tok], W=[X["vsb"].tok])
            kb.tt(X["q"].ap[:, 0:512], X["q"].ap[:, 0:512], pq.ap[:, :], ALU.mult, R=[X["q"].tok, pq.tok], W=[X["q"].tok])
            kb.tt(X["sg"].ap[:, 0:512], X["sg"].ap[:, 0:512], pg.ap[:, :], ALU.mult, R=[X["sg"].tok, pg.tok],
                  W=[X["sg"].tok])
        if after_proj is not None:
            after_proj()

        def each():
            return [(u, U[u]) for u in range(2)]
        for u, X in each():
            kb.ts(X["sf"].ap[:, 0:512], X["sf"].ap[:, 0:512], pd.ap[:, 24 + u:25 + u], pd.ap[:, 16 + u:17 + u], ALU.mult,
                  ALU.add, R=[X["sf"].tok, pd.tok], W=[X["sf"].tok])
        for u, X in each():
            kb.act(X["lf"].ap[:, 0:512], X["sf"].ap[:, 0:512], AF.Ln, R=[X["sf"].tok], W=[X["lf"].tok])
        for u, X in each():
            kb.ts(X["kv"].ap[:, 0:512], X["sf"].ap[:, 0:512], -1.0, 1.0, ALU.mult, ALU.add, R=[X["sf"].tok], W=[X["kv"].tok])
            kb.scan(X["g"].ap[:, 0:512], self.rmask.ap[:, :], X["lf"].ap[:, 0:512], 0.0, ALU.mult, ALU.add,
                    R=[self.rmask.tok, X["lf"].tok], W=[X["g"].tok])
        for u, X in each():
            g3 = X["g"].ap[:, 0:512].rearrange("p (c t) -> p c t", t=64)
            gm3 = X["gm"].ap[:, 0:512].rearrange("p (c t) -> p c t", t=64)
            kb.tt(gm3, g3, g3[:, :, 31:32].broadcast_to([128, NCH, 64]), ALU.subtract, R=[X["g"].tok], W=[X["gm"].tok])
        for u, X in each():
            g3 = X["g"].ap[:, 0:512].rearrange("p (c t) -> p c t", t=64)
            kb.act(X["Ep"].ap[:, 0:512], X["gm"].ap[:, 0:512], AF.Exp, R=[X["gm"].tok], W=[X["Ep"].tok])
            kb.act(X["Em"].ap[:, 0:512], X["gm"].ap[:, 0:512], AF.Exp, R=[X["gm"].tok], W=[X["Em"].tok], scale=-1.0)
            kb.act(X["emid"].ap[:, 0:NCH].unsqueeze(2), g3[:, :, 31:32], AF.Exp, R=[X["g"].tok], W=[X["emid"].tok])
        for u, X in each():
            Ep3 = X["Ep"].ap[:, 0:512].rearrange("p (c t) -> p c t", t=64)
            kb.tt(X["qt"].ap[:, 0:512], X["q"].ap[:, 0:512], X["Ep"].ap[:, 0:512], ALU.mult, R=[X["q"].tok, X["Ep"].tok],
                  W=[X["qt"].tok])
            kb.tt(X["kt"].ap[:, 0:512], X["kv"].ap[:, 0:512], X["Em"].ap[:, 0:512], ALU.mult, R=[X["kv"].tok, X["Em"].tok],
                  W=[X["kt"].tok])
            kb.tt(X["ech"].ap[:, 0:NCH - 1].unsqueeze(2), Ep3[:, 0:NCH - 1, 63:64], X["emid"].ap[:, 1:NCH].unsqueeze(2),
                  ALU.mult, R=[X["Ep"].tok, X["emid"].tok], W=[X["ech"].tok])
            kb.copy(X["ech"].ap[:, NCH - 1:NCH], X["Ep"].ap[:, 511:512], R=[X["Ep"].tok], W=[X["ech"].tok])
            kb.tt(X["kte"].ap[:, 0:512].rearrange("p (c t) -> p c t", t=64),
                  X["kt"].ap[:, 0:512].rearrange("p (c t) -> p c t", t=64),
                  X["ech"].ap[:, 0:NCH].unsqueeze(2).broadcast_to([128, NCH, 64]), ALU.mult, R=[X["kt"].tok, X["ech"].tok],
                  W=[X["kte"].tok])
        for u, X in each():
            pbt = X["pbt"]
            for sub in range(NSUB):
                kb.tr(pbt.ap[:, sub * 128:(sub + 1) * 128], X["kte"].ap[:, sub * 128:(sub + 1) * 128], self.ident.ap[:, :],
                      R=[X["kte"].tok, self.ident.tok], W=[pbt.tok])
        for u, X in each():
            kb.copy(X["ktT"].ap[:, 0:512], X["pbt"].ap[:, 0:512], R=[X["pbt"].tok], W=[X["ktT"].tok])
        for u, X in each():
            for c in range(NCH):
                sub, half = divmod(c, 2)
                p0 = 64 * half
                bk = X["dl"][half]
                kb.mm(bk.ap[:, sub * 128:(sub + 1) * 128], X["ktT"].ap[p0:p0 + 64, sub * 128:(sub + 1) * 128],
                      X["vsb"].ap[p0:p0 + 64, sub * 128:(sub + 1) * 128], True, True, R=[X["ktT"].tok, X["vsb"].tok],
                      W=[bk.tok])
        for u, X in each():
            S = st.ap[:, 32 + u * 128:32 + (u + 1) * 128]
            kb.act(X["SpF"][0].ap[:, 0:128], S, AF.Copy, R=[st.tok, X["emid"].tok], W=[X["SpF"][0].tok],
                   scale=X["emid"].ap[:, 0:1])
        for c in range(NCH):
            sub, half = divmod(c, 2)
            for u, X in each():
                S = st.ap[:, 32 + u * 128:32 + (u + 1) * 128]
                bk = X["dl"][half]
                dl = bk.ap[:, sub * 128:(sub + 1) * 128]
                ec = X["ech"].ap[:, c:c + 1]
                cf = X["SpF"][c // 4]
                cur = cf.ap[:, (c % 4) * 128:(c % 4 + 1) * 128]
                if c < NCH - 1:
                    n_ = c + 1
                    nf = X["SpF"][n_ // 4]
                    kb.stt(nf.ap[:, (n_ % 4) * 128:(n_ % 4 + 1) * 128], cur, ec, dl, ALU.mult, ALU.add,
                           R=[cf.tok, X["ech"].tok, bk.tok], W=[nf.tok])
                else:
                    kb.stt(S, cur, ec, dl, ALU.mult, ALU.add, R=[cf.tok, X["ech"].tok, bk.tok], W=[st.tok])
                if c % 4 == 3:
                    kb.act(X["SpA"][c // 4].ap[:, 0:512], cf.ap[:, 0:512], AF.Copy, R=[cf.tok], W=[X["SpA"][c // 4].tok])
        for sub in range(NSUB):
            for u, X in each():
                pmf = X["pbt"].ap[:, :].bitcast(F32)
                for half in range(2):
                    c = sub * 2 + half
                    p0 = 64 * half
                    kb.mm(pmf[p0:p0 + 64, 0:64], X["kt"].ap[:, c * 64:(c + 1) * 64], X["qt"].ap[:, c * 64:(c + 1) * 64], True,
                          True, R=[X["kt"].tok, X["qt"].tok], W=[X["pbt"].tok])
            for u, X in each():
                pmf = X["pbt"].ap[:, :].bitcast(F32)
                kb.tt(X["scT"].ap[:, 0:64], pmf[:, 0:64], self.maskU.ap[:, :], ALU.mult, R=[X["pbt"].tok, self.maskU.tok],
                      W=[X["scT"].tok])
            for u, X in each():
                po = X["po"]
                for half in range(2):
                    c = sub * 2 + half
                    p0 = 64 * half
                    kb.mm(po.ap[p0:p0 + 64, sub * 128:(sub + 1) * 128], X["scT"].ap[p0:p0 + 64, 0:64],
                          X["vsb"].ap[p0:p0 + 64, sub * 128:(sub + 1) * 128], True, False, R=[X["scT"].tok, X["vsb"].tok],
                          W=[po.tok])
                    kb.mm(po.ap[p0:p0 + 64, sub * 128:(sub + 1) * 128], X["qt"].ap[:, c * 64:(c + 1) * 64],
                          X["SpA"][c // 4].ap[:, (c % 4) * 128:(c % 4 + 1) * 128], False, True,
                          R=[X["qt"].tok, X["SpA"][c // 4].tok], W=[po.tok])
        for u, X in each():
            po = X["po"]
            for sub in range(NSUB):
                kb.act(X["junk"].ap[:, 0:128], po.ap[:, sub * 128:(sub + 1) * 128], AF.Square, R=[po.tok],
                       W=[X["junk"].tok, X["ss2"].tok], accum=X["ss2"].ap[:, sub:sub + 1])
        for u, X in each():
            kb.act(X["rms2"].ap[:, 0:4], X["ss2"].ap[:, 0:4], AF.Sqrt, R=[X["ss2"].tok, self.cst.tok], W=[X["rms2"].tok],
                   scale=1.0 / 128, bias=self.cst.ap[:, 0:1])
        for u, X in each():
            kb.recip(X["rstd2"].ap[:, 0:4], X["rms2"].ap[:, 0:4], R=[X["rms2"].tok], W=[X["rstd2"].tok])
            for sub in range(NSUB):
                kb.ts(X["on"].ap[:, sub * 128:(sub + 1) * 128], X["po"].ap[:, sub * 128:(sub + 1) * 128],
                      X["rstd2"].ap[:, sub:sub + 1], None, ALU.mult, None, R=[X["po"].tok, X["rstd2"].tok], W=[X["on"].tok])
        for u, X in each():
            pbo = X["pbt"]
            for sub in range(NSUB):
                kb.tr(pbo.ap[:, sub * 128:(sub + 1) * 128], X["on"].ap[:, sub * 128:(sub + 1) * 128], self.ident.ap[:, :],
                      R=[X["on"].tok, self.ident.tok], W=[pbo.tok])
        for u, X in each():
            kb.stt(self.merged.ap[:, 8 + u, :], X["pbt"].ap[:, 0:512], pv.ap[:, 96:97], X["sg"].ap[:, 0:512], ALU.mult,
                   ALU.mult, R=[X["pbt"].tok, pv.tok, X["sg"].tok], W=[self.merged.tok])

    def l0_end(self):
        kb = self.kb
        kb.dma("sync", self.dram["st0_out"][:, :], self.st0.ap[:, :], self.st0.tok, R=[self.st0.tok])

    def layer0_tile(self, tile, p1=False):
        self.gain = self.pv0
        self.gain_off = 88
        self.norm_T(tile, 0)
        self.l0_rg_proj(0, p1)
        for cb in range(8):
            self.l0_rg(cb, p1, (lambda c=cb: self.l0_rg_proj(c + 1, p1)) if cb < 7 else None)
        self.l0_hg_proj(0, p1)
        for hh in range(8):
            self.l0_hg(hh, p1, (lambda h_=hh: self.l0_hg_proj(h_ + 1, p1)) if hh < 7 else None)
        if not p1:
            self.out_proj(tile, 16, self.wout0, self.postbc0)

    PV1_COLS = 112
    DECAY_C = -0.6065306597126334

    def l1_setup(self):
        kb = self.kb
        d = self.dram
        self.pv1 = self.sbuf("pv1", [128, self.PV1_COLS], F32)
        self.pd1 = self.sbuf("pd1", [128, 8], F32)
        self.mix = [self.sbuf(f"mix{i}", [128, 8, 512], BF16) for i in range(4)]
        self.w1b = self.sbuf("w1b", [128, 8, 64], BF16)
        self.a1b = self.sbuf("a1b", [128, 8, 64], BF16)
        self.w2b = self.sbuf("w2b", [64, 1024], BF16)
        self.a2b = self.sbuf("a2b", [64, 1024], BF16)
        self.hid = self.sbuf("hid", [64, 1024], BF16)
        self.M3 = self.sbuf("M3", [128, 384], BF16)
        self.M2 = self.sbuf("M2", [128, 256], BF16)
        self.onesbd = self.sbuf("onesbd", [128, 128], BF16)
        self.st1 = self.sbuf("st1", [128, 520], F32)
        self.SC = self.sbuf("SC", [128, 8, 384], BF16)
        self.PPh = [self.sbuf(f"PP{i}", [128, 512], BF16) for i in range(2)]
        self.smb = self.sbuf("smb", [128, 384], BF16)
        if not self.do_l0:
            self.rmask = self.sbuf("rmask", [128, 512], BF16)
            kb.dma("sync", self.rmask.ap[:, :], d["rmask"][:, :], self.rmask.tok, W=[self.rmask.tok])
        if self.comm:
            kb.memset(self.st1.ap[:, :], 0.0, [self.st1.tok])
        else:
            kb.dma("sync", self.st1.ap[:, :], d["st1_in"][:, :], self.st1.tok, W=[self.st1.tok])
        for (buf, nm) in ((self.pv1, "pv1"), (self.M3, "M3"), (self.M2, "M2"),
                          (self.postbc1, "post1_bc"), (self.onesbd, "onesbd")):
            kb.dma("sync", buf.ap[:, :], d[nm][:, :], buf.tok, W=[buf.tok])
        ncol = 256 if self.comm else D
        kb.dma("gpsimd", self.w2b.ap[:, 0:ncol], d["rw_w2"][:, :], self.w2b.tok, W=[self.w2b.tok])
        kb.dma("gpsimd", self.a2b.ap[:, 0:ncol], d["rw_a2"][:, :], self.a2b.tok, W=[self.a2b.tok])
        kb.dma("gpsimd", self.w1b.ap[:, :, :], d["rw_w1"].rearrange("(kc p) n -> p kc n", p=128), self.w1b.tok,
               W=[self.w1b.tok])
        kb.dma("gpsimd", self.a1b.ap[:, :, :], d["rw_a1"].rearrange("(kc p) n -> p kc n", p=128), self.a1b.tok,
               W=[self.a1b.tok])
        if not self.comm:
            for cb in range(8):
                self.load_wout1(cb)
        kb.memset(self.smb.ap[:, :], 0.0, [self.smb.tok])
        kb.ts(self.pd1.ap[:, 0:8], self.pv1.ap[:, 72:80], -1.0, 1.0, ALU.mult, ALU.add, R=[self.pv1.tok],
              W=[self.pd1.tok])

    def l1_end(self):
        kb = self.kb
        kb.dma("sync", self.dram["st1_out"][:, :], self.st1.ap[:, :], self.st1.tok, R=[self.st1.tok])

    def l1_mix(self, p, dst):
        for _ in self.l1_mix_g(p, dst):
            pass

    def l1_mix_g(self, p, dst):
        kb = self.kb
        diff = self.merged
        pool_set = tuple(int(c) for c in os.environ.get("MK_POOLMIX", "").split(",") if c)
        en = "gpsimd" if p in pool_set else "vector"
        for kc in range(8):
            kb.stt(dst.ap[:, kc, :], diff.ap[:, 8 + kc, :], self.pv1.ap[:, p * 8 + kc:p * 8 + kc + 1],
                   self.uT.ap[:, kc, 1:513], ALU.mult, ALU.add, R=[diff.tok, self.pv1.tok, self.uT.tok], W=[dst.tok],
                   en=en)
            yield

    def layer1_tile(self, tile, p1=False):
        kb = self.kb
        st1 = self.st1
        self.gain = self.pv1
        self.gain_off = 104
        for kc in range(8):
            kb.copy(self.uT.ap[:, kc, 0:1], st1.ap[:, 512 + kc:513 + kc], R=[st1.tok], W=[self.uT.tok])
        self.norm_T(tile, 1)
        for kc in range(8):
            kb.copy(st1.ap[:, 512 + kc:513 + kc], self.uT.ap[:, kc, 512:513], R=[self.uT.tok], W=[st1.tok])
        kb.tt(self.merged.ap[:, 8:16, :], self.uT.ap[:, :, 0:512], self.uT.ap[:, :, 1:513], ALU.subtract,
              R=[self.uT.tok], W=[self.merged.tok])
        self.l1_mix(4, self.mix[0])
        ph = self.psf[0]
        for kc in range(8):
            kb.mm(ph.ap[0:64, :], self.w1b.ap[:, kc, :], self.mix[0].ap[:, kc, :], kc == 0, kc == 7,
                  R=[self.w1b.tok, self.mix[0].tok], W=[ph.tok])
        kb.act(self.hid.ap[:, 0:512], ph.ap[0:64, :], AF.Tanh, R=[ph.tok], W=[self.hid.tok])
        self.l1_mix(5, self.mix[1])
        ph = self.psf[1]
        for kc in range(8):
            kb.mm(ph.ap[0:64, :], self.a1b.ap[:, kc, :], self.mix[1].ap[:, kc, :], kc == 0, kc == 7,
                  R=[self.a1b.tok, self.mix[1].tok], W=[ph.tok])
        kb.act(self.hid.ap[:, 512:1024], ph.ap[0:64, :], AF.Copy, R=[ph.tok], W=[self.hid.tok])
        for p_ in ((1, 2) if p1 else range(4)):
            self.l1_mix(p_, self.mix[p_])
        for hp in range(8):
            self.l1_hp(hp, p1)
        if not p1:
            self.out_proj(tile, 8, self.wout1, self.postbc1)

    def l1_hp(self, hp, p1=False, filler=None):
        kb = self.kb
        d = self.dram
        pv, pd = self.pv1, self.pd1
        f, b, sm = self.f, self.b, self.sm
        C = self.DECAY_C
        bq, r, k, sgm, icl, cs, Epos, Eneg, Eexc, kkn, t1, sg = (f[0], f[1], f[2], f[3], f[4], f[5], f[6], f[7],
                                                                 f[8], f[9], f[10], f[11])
        sq, at, rt, kt, bt, ktm, btm, vsb, rkr, yn, NN, NNT = (b[0], b[1], b[2], b[3], b[4], b[5], b[6], b[7],
                                                               b[8], b[9], b[10], b[11])
        psf = self.psf
        st1 = self.st1
        cols = slice(hp * 128, (hp + 1) * 128)
        pr, pk, pg, pvv, pw, pa = psf[2], psf[3], psf[4], psf[5], psf[0], psf[1]
        wk = self.load_w(d["rw_w_in"][1, :, cols])
        wv = self.load_w(d["rw_w_in"][2, :, cols])
        plist = [(wk, self.mix[1], pk)]
        if not p1:
            wr = self.load_w(d["rw_w_in"][0, :, cols])
            wg = self.load_w(d["rw_w_in"][3, :, cols])
            plist += [(wr, self.mix[0], pr), (wg, self.mix[3], pg)]
        for (w_, m_, p_) in plist:
            for kc in range(8):
                kb.mm(p_.ap[:, :], w_.ap[:, kc, :], m_.ap[:, kc, :], kc == 0, kc == 7, R=[w_.tok, m_.tok], W=[p_.tok])
        for sub in range(NSUB):
            for kc in range(8):
                kb.mm(pvv.ap[:, sub * 128:(sub + 1) * 128], self.mix[2].ap[:, kc, sub * 128:(sub + 1) * 128],
                      wv.ap[:, kc, :], kc == 0, kc == 7, R=[wv.tok, self.mix[2].tok], W=[pvv.tok])
        kb.mm(pw.ap[:, :], self.w2b.ap[0:64, cols], self.hid.ap[0:64, 0:512], True, True,
              R=[self.w2b.tok, self.hid.tok], W=[pw.tok])
        kb.mm(pa.ap[:, :], self.a2b.ap[0:64, cols], self.hid.ap[0:64, 512:1024], True, True,
              R=[self.a2b.tok, self.hid.tok], W=[pa.tok])
        if not p1:
            kb.act(r.ap[:, 0:512], pr.ap[:, :], AF.Copy, R=[pr.tok], W=[r.tok])
            kb.act(sg.ap[:, 0:512], pg.ap[:, :], AF.Silu, R=[pg.tok], W=[sg.tok])
        kb.act(k.ap[:, 0:512], pk.ap[:, :], AF.Copy, R=[pk.tok], W=[k.tok])
        kb.act(sq.ap[:, :], pk.ap[:, :], AF.Square, R=[pk.tok, pv.tok], W=[sq.tok], scale=pv.ap[:, 64 + hp:65 + hp])
        kb.act(vsb.ap[:, :], pvv.ap[:, :], AF.Copy, R=[pvv.tok], W=[vsb.tok])
        kb.act(sgm.ap[:, 0:512], pw.ap[:, :], AF.Sigmoid, R=[pw.tok, pv.tok], W=[sgm.tok], bias=pv.ap[:, 48 + hp:49 + hp])
        kb.act(icl.ap[:, 0:512], pa.ap[:, :], AF.Sigmoid, R=[pa.tok, pv.tok], W=[icl.tok], bias=pv.ap[:, 56 + hp:57 + hp])
        pn = psf[0]
        kb.mm(pn.ap[:, :], self.onesbd.ap[:, :], sq.ap[:, :], True, True, R=[self.onesbd.tok, sq.tok], W=[pn.tok])
        kb.act(t1.ap[:, 0:512], pn.ap[:, :], AF.Sqrt, R=[pn.tok], W=[t1.tok])
        kb.ts(t1.ap[:, 0:512], t1.ap[:, 0:512], 1e-12, None, ALU.max, None, R=[t1.tok], W=[t1.tok])
        kb.recip(t1.ap[:, 0:512], t1.ap[:, 0:512], R=[t1.tok], W=[t1.tok])
        kb.stt(kkn.ap[:, 0:512], k.ap[:, 0:512], pv.ap[:, 64 + hp:65 + hp], t1.ap[:, 0:512], ALU.mult, ALU.mult,
               R=[k.tok, pv.tok, t1.tok], W=[kkn.tok])
        kb.scan(cs.ap[:, 0:512], self.rmask.ap[:, :], sgm.ap[:, 0:512], 0.0, ALU.mult, ALU.add,
                R=[self.rmask.tok, sgm.tok], W=[cs.tok])
        kb.tt(Eexc.ap[:, 0:512], cs.ap[:, 0:512], sgm.ap[:, 0:512], ALU.subtract, R=[cs.tok, sgm.tok], W=[Eexc.tok])
        kb.act(Epos.ap[:, 0:512], cs.ap[:, 0:512], AF.Exp, R=[cs.tok], W=[Epos.tok], scale=C)
        kb.act(Eneg.ap[:, 0:512], cs.ap[:, 0:512], AF.Exp, R=[cs.tok], W=[Eneg.tok], scale=-C)
        kb.act(Eexc.ap[:, 0:512], Eexc.ap[:, 0:512], AF.Exp, R=[Eexc.tok], W=[Eexc.tok], scale=C)
        kb.ts(t1.ap[:, 0:512], icl.ap[:, 0:512], pv.ap[:, 72 + hp:73 + hp], pd.ap[:, hp:hp + 1], ALU.mult, ALU.add,
              R=[icl.tok, pv.tok, pd.tok], W=[t1.tok])
        kb.tt(k.ap[:, 0:512], k.ap[:, 0:512], t1.ap[:, 0:512], ALU.mult, R=[k.tok, t1.tok], W=[k.tok])
        kb.tt(bq.ap[:, 0:512], kkn.ap[:, 0:512], icl.ap[:, 0:512], ALU.mult, R=[kkn.tok, icl.tok], W=[bq.tok])
        kb.tt(kt.ap[:, :], k.ap[:, 0:512], Eneg.ap[:, 0:512], ALU.mult, R=[k.tok, Eneg.tok], W=[kt.tok])
        kb.tt(bt.ap[:, :], bq.ap[:, 0:512], Eneg.ap[:, 0:512], ALU.mult, R=[bq.tok, Eneg.tok], W=[bt.tok])
        kb.stt(at.ap[:, :], kkn.ap[:, 0:512], -1.0, Eexc.ap[:, 0:512], ALU.mult, ALU.mult, R=[kkn.tok, Eexc.tok],
               W=[at.tok])
        pbon = psf[1]
        if not p1:
            kb.tt(rt.ap[:, :], r.ap[:, 0:512], Epos.ap[:, 0:512], ALU.mult, R=[r.tok, Epos.tok], W=[rt.tok])
            kb.stt(rkr.ap[:, :], r.ap[:, 0:512], pv.ap[:, 80 + hp:81 + hp], k.ap[:, 0:512], ALU.mult, ALU.mult,
                   R=[r.tok, pv.tok, k.tok], W=[rkr.tok])
            kb.mm(pbon.ap[:, :], self.onesbd.ap[:, :], rkr.ap[:, :], True, True, R=[self.onesbd.tok, rkr.tok],
                  W=[pbon.tok])
            kb.act(r.ap[:, 0:512], pbon.ap[:, :], AF.Copy, R=[pbon.tok], W=[r.tok])
        Epos3 = Epos.ap[:, 0:512].rearrange("p (c t) -> p c t", t=64)
        wend_bc = Epos3[:, :, 63:64].broadcast_to([128, NCH, 64])
        kb.tt(yn.ap[:, :].rearrange("p (c t) -> p c t", t=64), kt.ap[:, :].rearrange("p (c t) -> p c t", t=64), wend_bc,
              ALU.mult, R=[kt.tok, Epos.tok], W=[yn.tok])
        for sub in range(NSUB):
            kb.tr(self.psb[0].ap[:, sub * 128:(sub + 1) * 128], yn.ap[:, sub * 128:(sub + 1) * 128], self.ident.ap[:, :],
                  R=[yn.tok, self.ident.tok], W=[self.psb[0].tok])
        kb.tt(yn.ap[:, :].rearrange("p (c t) -> p c t", t=64), bt.ap[:, :].rearrange("p (c t) -> p c t", t=64), wend_bc,
              ALU.mult, R=[bt.tok, Epos.tok], W=[yn.tok])
        for sub in range(NSUB):
            kb.tr(self.psb[1].ap[:, sub * 128:(sub + 1) * 128], yn.ap[:, sub * 128:(sub + 1) * 128], self.ident.ap[:, :],
                  R=[yn.tok, self.ident.tok], W=[self.psb[1].tok])
        kb.copy(ktm.ap[:, :], self.psb[0].ap[:, 0:512], R=[self.psb[0].tok], W=[ktm.tok])
        kb.act(btm.ap[:, :], self.psb[1].ap[:, 0:512], AF.Copy, R=[self.psb[1].tok], W=[btm.tok])
        NNs = [(NN, NNT), (rkr, sq)]
        for hl in range(2):
            rs = slice(64 * hl, 64 * hl + 64)
            nn, nnt = NNs[hl]
            for sub in range(NSUB):
                pi = hl * 4 + sub
                X, Y = (psf[0], psf[2]) if pi % 2 == 0 else (psf[3], psf[4])
                tc = slice(sub * 128, (sub + 1) * 128)
                kb.mm(X.ap[:, 0:128], kt.ap[rs, tc], at.ap[rs, tc], True, True, R=[kt.tok, at.tok], W=[X.tok])
                if not p1:
                    kb.mm(X.ap[:, 128:256], kt.ap[rs, tc], rt.ap[rs, tc], True, True, R=[kt.tok, rt.tok], W=[X.tok])
                    kb.mm(X.ap[:, 256:384], bt.ap[rs, tc], rt.ap[rs, tc], True, True, R=[bt.tok, rt.tok], W=[X.tok])
                kb.mm(Y.ap[:, 0:128], bt.ap[rs, tc], at.ap[rs, tc], True, True, R=[bt.tok, at.tok], W=[Y.tok])
                kb.mm(Y.ap[:, 128:256], at.ap[rs, tc], bt.ap[rs, tc], True, True, R=[bt.tok, at.tok], W=[Y.tok])
                nsc = 128 if p1 else 384
                kb.tt(self.SC.ap[:, pi, 0:nsc], X.ap[:, 0:nsc], self.M3.ap[:, 0:nsc], ALU.mult, R=[X.tok, self.M3.tok],
                      W=[self.SC.tok])
                kb.tt(nn.ap[:, tc], Y.ap[:, 0:128], self.M2.ap[:, 0:128], ALU.mult, R=[Y.tok, self.M2.tok], W=[nn.tok])
                kb.tt(nnt.ap[:, tc], Y.ap[:, 128:256], self.M2.ap[:, 128:256], ALU.mult, R=[Y.tok, self.M2.tok],
                      W=[nnt.tok])
            for j in range(4):
                kb.tt(self.PPh[hl].ap[:, j * 128:(j + 1) * 128], nn.ap[:, j * 128:(j + 1) * 128], self.ident.ap[:, :],
                      ALU.add, R=[nn.tok, self.ident.tok], W=[self.PPh[hl].tok])
        ABC = [(psf[3], psf[4], psf[5]), (psf[0], psf[2], psf[1])]
        for lvl in range(5):
            last = lvl == 4
            for hl in range(2):
                nn, nnt = NNs[hl]
                A_, B_, C_ = ABC[hl]
                for j in range(4):
                    tc = slice(j * 128, (j + 1) * 128)
                    if not last:
                        kb.mm(A_.ap[:, tc], nnt.ap[:, tc], nn.ap[:, tc], True, True, R=[nn.tok, nnt.tok], W=[A_.tok])
                    kb.mm(B_.ap[:, tc], nn.ap[:, tc], nnt.ap[:, tc], True, True, R=[nn.tok, nnt.tok], W=[B_.tok])
            for hl in range(2):
                nn, nnt = NNs[hl]
                A_, B_, C_ = ABC[hl]
                kb.act(nnt.ap[:, :], B_.ap[:, :], AF.Copy, R=[B_.tok], W=[nnt.tok])
                if not last:
                    kb.copy(nn.ap[:, :], A_.ap[:, :], R=[A_.tok], W=[nn.tok])
            for hl in range(2):
                nn, nnt = NNs[hl]
                A_, B_, C_ = ABC[hl]
                for j in range(4):
                    tc = slice(j * 128, (j + 1) * 128)
                    kb.mm(C_.ap[:, tc], nnt.ap[:, tc], self.PPh[hl].ap[:, tc], True, True, R=[nnt.tok, self.PPh[hl].tok],
                          W=[C_.tok])
            for hl in range(2):
                A_, B_, C_ = ABC[hl]
                kb.tt(self.PPh[hl].ap[:, :], C_.ap[:, :], self.PPh[hl].ap[:, :], ALU.add, R=[C_.tok, self.PPh[hl].tok],
                      W=[self.PPh[hl].tok])
        yps, XU, dH = psf[0], psf[2], psf[3]
        if p1:
            VW = 128
            smb, smbt = self.smbA, self.wout0.tok
            Hs, stt_ = self.stA[:, hp, :], self.wout0.tok
        else:
            VW = 64
            smb, smbt = self.smb.ap, self.smb.tok
            Hs, stt_ = st1.ap[:, hp * 64:(hp + 1) * 64], st1.tok
        ho = 4 * VW
        tmpH = t1
        for hl in range(2):
            rs = slice(64 * hl, 64 * hl + 64)
            kb.act(smb[rs, ho + hl * VW:ho + (hl + 1) * VW], Hs[rs, :], AF.Copy, R=[stt_], W=[smbt])
        for c in range(NCH):
            sub, half = divmod(c, 2)
            p0 = 64 * half
            ps_ = slice(p0, p0 + 64)
            cc = slice(c * 64, (c + 1) * 64)
            for hl in range(2):
                vv = slice(sub * 128 + hl * 64, sub * 128 + hl * 64 + 64)
                pi = hl * 4 + sub
                kb.mm(XU.ap[ps_, hl * VW:(hl + 1) * VW], at.ap[:, cc], smb[:, ho + hl * VW:ho + (hl + 1) * VW], True, False,
                      R=[at.tok, smbt], W=[XU.tok])
                kb.mm(XU.ap[ps_, hl * VW:hl * VW + 64], self.SC.ap[:, pi, p0:p0 + 64], vsb.ap[:, vv], False, True,
                      R=[self.SC.tok, vsb.tok], W=[XU.tok])
            kb.act(smb[ps_, 0:2 * VW], XU.ap[ps_, 0:2 * VW], AF.Copy, R=[XU.tok], W=[smbt])
            for hl in range(2):
                kb.mm(XU.ap[ps_, 2 * VW + hl * VW:2 * VW + (hl + 1) * VW],
                      self.PPh[hl].ap[:, sub * 128 + p0:sub * 128 + p0 + 64], smb[:, hl * VW:(hl + 1) * VW], True, True,
                      R=[self.PPh[hl].tok, smbt], W=[XU.tok])
            kb.copy(smb[ps_, 2 * VW:4 * VW], XU.ap[ps_, 2 * VW:4 * VW], R=[XU.tok], W=[smbt])
            for hl in range(2):
                rs = slice(64 * hl, 64 * hl + 64)
                vv = slice(sub * 128 + hl * 64, sub * 128 + hl * 64 + 64)
                kb.mm(dH.ap[rs, 0:VW], btm.ap[ps_, vv], smb[ps_, 2 * VW + hl * VW:2 * VW + (hl + 1) * VW], True, False,
                      R=[btm.tok, smbt], W=[dH.tok])
                kb.mm(dH.ap[rs, 0:64], ktm.ap[ps_, vv], vsb.ap[ps_, vv], False, True, R=[ktm.tok, vsb.tok], W=[dH.tok])
            if not p1:
                for hl in range(2):
                    us = slice(2 * VW + hl * VW, 2 * VW + hl * VW + 64)
                    vv = slice(sub * 128 + hl * 64, sub * 128 + hl * 64 + 64)
                    pi = hl * 4 + sub
                    kb.mm(yps.ap[ps_, vv], rt.ap[:, cc], smb[:, ho + hl * VW:ho + (hl + 1) * VW], True, False,
                          R=[rt.tok, smbt], W=[yps.tok])
                    kb.mm(yps.ap[ps_, vv], self.SC.ap[:, pi, 256 + p0:256 + p0 + 64], smb[:, us], False, False,
                          R=[self.SC.tok, smbt], W=[yps.tok])
                    kb.mm(yps.ap[ps_, vv], self.SC.ap[:, pi, 128 + p0:128 + p0 + 64], vsb.ap[:, vv], False, True,
                          R=[self.SC.tok, vsb.tok], W=[yps.tok])
            for hl in range(2):
                rs = slice(64 * hl, 64 * hl + 64)
                kb.stt(smb[rs, ho + hl * VW:ho + (hl + 1) * VW], Hs[rs, :], Epos.ap[rs, c * 64 + 63:c * 64 + 64],
                       dH.ap[rs, 0:VW], ALU.mult, ALU.add, R=[stt_, Epos.tok, dH.tok], W=[smbt])
            kb.stt(Hs, Hs, Epos.ap[:, c * 64 + 63:c * 64 + 64], dH.ap[:, 0:VW], ALU.mult, ALU.add,
                   R=[stt_, Epos.tok, dH.tok], W=[stt_])
            if filler is not None:
                filler()
        if p1:
            return
        s1, s2, mean, msq, var, rms, rstd = sm[6], sm[7], sm[8], sm[9], sm[10], sm[11], sm[12]
        sqy = t1
        y3 = yps.ap[:, :].rearrange("p (g v) -> p g v", v=64)
        kb.emit("vector", lambda e: e.reduce_sum(s1.ap[:, 0:8], y3, mybir.AxisListType.X), R=[yps.tok], W=[s1.tok])
        kb.act(sqy.ap[:, 0:512], yps.ap[:, :], AF.Square, R=[yps.tok], W=[sqy.tok])
        sq3 = sqy.ap[:, 0:512].rearrange("p (g v) -> p g v", v=64)
        kb.emit("vector", lambda e: e.reduce_sum(s2.ap[:, 0:8], sq3, mybir.AxisListType.X), R=[sqy.tok], W=[s2.tok])
        kb.ts(mean.ap[:, 0:8], s1.ap[:, 0:8], 1.0 / 64, None, ALU.mult, None, R=[s1.tok], W=[mean.tok])
        kb.tt(msq.ap[:, 0:8], mean.ap[:, 0:8], mean.ap[:, 0:8], ALU.mult, R=[mean.tok], W=[msq.tok])
        kb.stt(var.ap[:, 0:8], s2.ap[:, 0:8], 1.0 / 64, msq.ap[:, 0:8], ALU.mult, ALU.subtract, R=[s2.tok, msq.tok],
               W=[var.tok])
        kb.act(rms.ap[:, 0:8], var.ap[:, 0:8], AF.Sqrt, R=[var.tok, self.cst.tok], W=[rms.tok], bias=self.cst.ap[:, 2:3])
        kb.recip(rstd.ap[:, 0:8], rms.ap[:, 0:8], R=[rms.tok], W=[rstd.tok])
        for g in range(8):
            kb.ts(yn.ap[:, g * 64:(g + 1) * 64], yps.ap[:, g * 64:(g + 1) * 64], mean.ap[:, g:g + 1], rstd.ap[:, g:g + 1],
                  ALU.subtract, ALU.mult, R=[yps.tok, mean.tok, rstd.tok], W=[yn.tok])
        for sub in range(NSUB):
            kb.tr(self.psb[0].ap[:, sub * 128:(sub + 1) * 128], yn.ap[:, sub * 128:(sub + 1) * 128], self.ident.ap[:, :],
                  R=[yn.tok, self.ident.tok], W=[self.psb[0].tok])
        for sub in range(NSUB):
            kb.tr(self.psb[1].ap[:, sub * 128:(sub + 1) * 128], vsb.ap[:, sub * 128:(sub + 1) * 128], self.ident.ap[:, :],
                  R=[vsb.tok, self.ident.tok], W=[self.psb[1].tok])
        vT = sq
        kb.act(vT.ap[:, :], self.psb[1].ap[:, 0:512], AF.Copy, R=[self.psb[1].tok], W=[vT.tok])
        bv = k
        kb.tt(bv.ap[:, 0:512], r.ap[:, 0:512], vT.ap[:, :], ALU.mult, R=[r.tok, vT.tok], W=[bv.tok])
        kb.stt(sgm.ap[:, 0:512], self.psb[0].ap[:, 0:512], pv.ap[:, 88 + hp:89 + hp], bv.ap[:, 0:512], ALU.mult, ALU.add,
               R=[self.psb[0].tok, pv.tok, bv.tok], W=[sgm.tok])
        kb.stt(self.merged.ap[:, hp, :], sgm.ap[:, 0:512], pv.ap[:, 96 + hp:97 + hp], sg.ap[:, 0:512], ALU.add, ALU.mult,
               R=[sgm.tok, pv.tok, sg.tok], W=[self.merged.tok])

    def cc_gather(self, parts, W, name):
        kb = self.kb
        bin_ = self.nc.dram_tensor(name + "_i", [128, W], F32)
        bout = self.nc.dram_tensor(name + "_o", [512, W], F32)
        tin, tout = kb.tok(name + "_i"), kb.tok(name + "_o")
        off = 0
        for (ap, tok, w) in parts:
            kb.dma("gpsimd", bin_.ap()[:, off:off + w], ap, tin, R=[tok], W=[tin])
            off += w
        kb.collective(bin_, bout, tin, tout, name)
        G = self.merged.ap[:, :, :].rearrange("p a b -> p (a b)").bitcast(F32)
        for r in range(3):
            kb.dma("sync", G[:, r * W:(r + 1) * W], bout.ap()[r * 128:(r + 1) * 128, :], self.merged.tok, R=[tout],
                   W=[self.merged.tok])
        return G

    def comm_setup(self):
        kb = self.kb
        d = self.dram
        self.dec0 = self.sbuf("dec0", [128, 16], F32)
        self.msk = self.sbuf("msk", [128, 16], F32)
        self.eye2 = self.sbuf("eye2", [128, 64], F32)
        self.ul = self.sbuf("ul", [128, 8], F32)
        self.up0 = self.sbuf("up0", [128, 8], F32)
        self.dd = self.sbuf("dd", [128, 16], F32)
        kb.dma("sync", self.msk.ap[:, :], d["msk"][:, :], self.msk.tok, W=[self.msk.tok])
        kb.dma("sync", self.eye2.ap[:, :], d["eye2"][:, :], self.eye2.tok, W=[self.eye2.tok])
        kb.memset(self.dec0.ap[:, :], 0.0, [self.dec0.tok])
        w0f = self.wout0.ap[:, :, :].rearrange("p a b -> p (a b)")
        self.stA = w0f.bitcast(F32)[:, 0:1024].rearrange("p (a b) -> p a b", b=128)
        self.smbA = w0f[:, 2048:2048 + 768]
        self.rgB_f = []
        for i in range(4):
            flat = self.mix[i].ap[:, :, :].rearrange("p a b -> p (a b)").bitcast(F32)
            for j in range(3 if i < 3 else 1):
                self.rgB_f.append(Buf(flat[:, j * 516:(j + 1) * 516], kb.tok(f"rgB{i}_{j}")))
        flatb = self.mix[3].ap[:, :, :].rearrange("p a b -> p (a b)")
        self.rgB_xcb = Buf(flatb[:, 1032:1032 + 512], kb.tok("rgBx"))
        w1b_ = self.wout1.ap[:, :, :].rearrange("p a b -> p (a b)")
        self.hgB_b = [Buf(w1b_[:, i * 512:(i + 1) * 512], kb.tok(f"hgBb{i}")) for i in range(9)]
        self.hgB_f = list(self.rgB_f) + [Buf(w1b_.bitcast(F32)[:, 2304:2820], kb.tok("hgBf10"))]
        self.hgB_sm = [self.sbuf(f"smB{i}", [128, 16], F32) for i in range(5)]
        w0b = self.wout0.ap[:, :, :].rearrange("p a b -> p (a b)")
        w0f = w0b.bitcast(F32)
        self.l1B = dict(
            b=[Buf(w0b[:, i * 512:(i + 1) * 512], kb.tok(f"l1Bb{i}")) for i in range(5)],
            SC=Buf(w0b[:, 2560:5632].rearrange("p (a b) -> p a b", b=384), kb.tok("l1Bsc")),
            PPh=[Buf(w0b[:, 5632 + i * 512:5632 + (i + 1) * 512], kb.tok(f"l1Bpp{i}")) for i in range(2)],
            smb=Buf(w0b[:, 6656:7040], kb.tok("l1Bsmb")),
            f=[Buf(w0f[:, 3520 + i * 516:3520 + (i + 1) * 516], kb.tok(f"l1Bf{i}")) for i in range(3)],
        )
        self.l1E = dict(
            b=[Buf(w0b[:, 10136 + i * 512:10136 + (i + 1) * 512], kb.tok(f"l1Eb{i}")) for i in range(2)],
            f=[Buf(w0f[:, 5580 + i * 516:5580 + (i + 1) * 516], kb.tok(f"l1Ef{i}")) for i in range(3)],
            sm=[self.sbuf(f"smE{i}", [128, 16], F32) for i in range(7)],
        )
        self.st1h = [kb.tok("st1h0"), kb.tok("st1h1")]
        for t_ in self.st1h:
            t_.w = self.st1.tok.w
        self.x1buf = self.nc.dram_tensor("x1buf", [self.TS, D], F32)
        self.x1tok = [kb.tok(f"x1t{t}") for t in range(self.NT)]

    def comm_l0_prefix(self, G):
        kb = self.kb
        st, dd, msk, pd = self.st0, self.dd, self.msk, self.pd0
        mt = self.merged.tok
        W = 1072
        kb.memset(st.ap[:, :], 0.0, [st.tok])
        for j in range(3):
            Gj = G[:, j * W:(j + 1) * W]
            mj = msk.ap[:, j:j + 1]
            ej = msk.ap[:, 4 + j:5 + j]
            kb.tt(dd.ap[:, 0:8], Gj[:, 1056:1064], pd.ap[:, 0:8], ALU.mult, R=[mt, pd.tok], W=[dd.tok])
            kb.act(dd.ap[:, 0:8], dd.ap[:, 0:8], AF.Exp, R=[dd.tok], W=[dd.tok])
            kb.act(dd.ap[:, 8:16], Gj[:, 1064:1072], AF.Exp, R=[mt], W=[dd.tok])
            kb.ts(dd.ap[:, 0:16], dd.ap[:, 0:16], -1.0, None, ALU.add, None, R=[dd.tok], W=[dd.tok])
            kb.ts(dd.ap[:, 0:16], dd.ap[:, 0:16], mj, None, ALU.mult, None, R=[dd.tok, msk.tok], W=[dd.tok])
            kb.ts(dd.ap[:, 0:16], dd.ap[:, 0:16], 1.0, None, ALU.add, None, R=[dd.tok], W=[dd.tok])
            kb.stt(st.ap[:, 0:24], Gj[:, 0:24], ej, st.ap[:, 0:24], ALU.mult, ALU.add, R=[mt, msk.tok, st.tok], W=[st.tok])
            kb.tt(st.ap[:, 24:32], st.ap[:, 24:32], dd.ap[:, 0:8], ALU.mult, R=[st.tok, dd.tok], W=[st.tok])
            kb.stt(st.ap[:, 24:32], Gj[:, 24:32], mj, st.ap[:, 24:32], ALU.mult, ALU.add, R=[mt, msk.tok, st.tok],
                   W=[st.tok])
            for h in range(8):
                kb.ts(st.ap[:, 32 + h * 128:32 + (h + 1) * 128], st.ap[:, 32 + h * 128:32 + (h + 1) * 128],
                      dd.ap[:, 8 + h:9 + h], None, ALU.mult, None, R=[st.tok, dd.tok], W=[st.tok])
            kb.stt(st.ap[:, 32:1056], Gj[:, 32:1056], mj, st.ap[:, 32:1056], ALU.mult, ALU.add,
                   R=[mt, msk.tok, st.tok], W=[st.tok])

    def comm_uprev(self, G):
        kb = self.kb
        st1, msk = self.st1, self.msk
        mt = self.merged.tok
        kb.memset(self.up0.ap[:, :], 0.0, [self.up0.tok])
        for j in range(3):
            kb.stt(self.up0.ap[:, 0:8], G[:, j * 8:(j + 1) * 8], msk.ap[:, 4 + j:5 + j], self.up0.ap[:, 0:8], ALU.mult,
                   ALU.add, R=[mt, msk.tok, self.up0.tok], W=[self.up0.tok])

    def comm_l1_prefix(self, G):
        kb = self.kb
        st1, msk = self.st1, self.msk
        mt = self.merged.tok
        Mbd, Xb, Hb = self.b[0], self.b[1], self.b[2]
        t = self.f[1]
        ps = self.psf[0]
        kb.memset(st1.ap[:, 0:512], 0.0, [st1.tok])
        kb.memset(Mbd.ap[:, 0:128], 0.0, [Mbd.tok])
        for j in range(3):
            mj = msk.ap[:, j:j + 1]
            for hp in range(8):
                Gh = G[:, j * 1024 + hp * 128:j * 1024 + (hp + 1) * 128]
                H = st1.ap[:, hp * 64:(hp + 1) * 64]
                kb.copy(Mbd.ap[0:64, 0:64], Gh[0:64, 64:128], R=[mt], W=[Mbd.tok])
                kb.copy(Mbd.ap[64:128, 64:128], Gh[64:128, 64:128], R=[mt], W=[Mbd.tok])
                kb.tr(self.psb[0].ap[:, 0:128], Mbd.ap[:, 0:128], self.ident.ap[:, :], R=[Mbd.tok, self.ident.tok],
                      W=[self.psb[0].tok])
                kb.act(Xb.ap[:, 0:128], self.psb[0].ap[:, 0:128], AF.Copy, R=[self.psb[0].tok], W=[Xb.tok])
                kb.act(Hb.ap[:, 0:64], H, AF.Copy, R=[st1.tok], W=[Hb.tok])
                kb.mm(ps.ap[:, 0:64], Xb.ap[:, 0:128], Hb.ap[:, 0:64], True, True, R=[Xb.tok, Hb.tok], W=[ps.tok])
                kb.tt(t.ap[:, 0:64], ps.ap[:, 0:64], Gh[:, 0:64], ALU.add, R=[ps.tok, mt], W=[t.tok])
                kb.tt(t.ap[:, 0:64], t.ap[:, 0:64], H, ALU.subtract, R=[t.tok, st1.tok], W=[t.tok])
                kb.stt(H, t.ap[:, 0:64], mj, H, ALU.mult, ALU.add, R=[t.tok, msk.tok, st1.tok], W=[st1.tok])

    def cc_gather_dram(self, bin_, bout, tin, name):
        kb = self.kb
        tout = kb.tok(name + "_o")
        kb.collective(bin_, bout, tin, tout, name)
        return tout

    def l1_ctx(self, hp, setB):
        from types import SimpleNamespace as NS
        f, b = self.f, self.b
        cx = NS(hp=hp)
        (cx.bq, cx.r, cx.k, cx.sgm, cx.icl, cx.cs, cx.Epos, cx.Eneg, cx.Eexc, cx.kkn, cx.t1, cx.sg) = (
            f[0], f[1], f[2], f[3], f[4], f[5], f[6], f[7], f[8], f[9], f[10], f[11])
        (cx.sq, cx.at, cx.rt, cx.kt, cx.bt, cx.ktm, cx.btm, cx.vsb, cx.rkr, cx.yn, cx.NN, cx.NNT) = (
            b[0], b[1], b[2], b[3], b[4], b[5], b[6], b[7], b[8], b[9], b[10], b[11])
        cx.SC, cx.PPh, cx.smb = self.SC, self.PPh, self.smb
        cx.yps, cx.XU, cx.xo, cx.dH, cx.do = self.psf[0], self.psf[2], 0, self.psf[3], 0
        if setB:
            B = self.l1B
            cx.at, cx.rt, cx.ktm, cx.btm, cx.vsb = B["b"]
            cx.Epos, cx.sg, cx.r = B["f"]
            cx.SC, cx.PPh, cx.smb = B["SC"], B["PPh"], B["smb"]
            cx.yps, cx.xo, cx.do = self.psf[1], 256, 64
        cx.Hs = self.st1.ap[:, hp * 64:(hp + 1) * 64]
        cx.Ht = self.st1h[hp]
        sm = self.sm
        cx.e_sm = [sm[6], sm[7], sm[8], sm[9], sm[10], sm[11], sm[12]]
        cx.e_yn, cx.e_vT, cx.e_sqy, cx.e_bv, cx.e_tmp = cx.yn, cx.sq, cx.t1, cx.k, cx.sgm
        cx.e_pT = [Buf(self.psb[0].ap[:, 0:512], self.psb[0].tok), Buf(self.psb[1].ap[:, 0:512], self.psb[1].tok)]
        if setB:
            E = self.l1E
            cx.e_sm = E["sm"]
            cx.e_yn, cx.e_vT = E["b"]
            cx.e_sqy, cx.e_bv, cx.e_tmp = E["f"]
            cx.e_pT = [Buf(self.psf[4].ap[:, :].bitcast(BF16)[:, 0:512], self.psf[4].tok),
                       Buf(self.psf[5].ap[:, :].bitcast(BF16)[:, 0:512], self.psf[5].tok)]
        return cx

    def l1_pre(self, cx):
        kb = self.kb
        d = self.dram
        pv, pd = self.pv1, self.pd1
        C = self.DECAY_C
        hp = cx.hp
        bq, r, k, sgm, icl, cs, Epos, Eneg, Eexc, kkn, t1, sg = (cx.bq, cx.r, cx.k, cx.sgm, cx.icl, cx.cs, cx.Epos,
                                                                 cx.Eneg, cx.Eexc, cx.kkn, cx.t1, cx.sg)
        sq, at, rt, kt, bt, ktm, btm, vsb, rkr, yn, NN, NNT = (cx.sq, cx.at, cx.rt, cx.kt, cx.bt, cx.ktm, cx.btm,
                                                               cx.vsb, cx.rkr, cx.yn, cx.NN, cx.NNT)
        psf = self.psf
        cols = slice(hp * 128, (hp + 1) * 128)
        pr, pk, pg, pvv, pw, pa = psf[2], psf[3], psf[4], psf[5], psf[0], psf[1]
        wk = self.load_w(d["rw_w_in"][1, :, cols])
        wv = self.load_w(d["rw_w_in"][2, :, cols])
        wr = self.load_w(d["rw_w_in"][0, :, cols])
        wg = self.load_w(d["rw_w_in"][3, :, cols])
        for (w_, m_, p_) in ((wk, self.mix[1], pk), (wr, self.mix[0], pr), (wg, self.mix[3], pg)):
            for kc in range(8):
                kb.mm(p_.ap[:, :], w_.ap[:, kc, :], m_.ap[:, kc, :], kc == 0, kc == 7, R=[w_.tok, m_.tok], W=[p_.tok])
        for sub in range(NSUB):
            for kc in range(8):
                kb.mm(pvv.ap[:, sub * 128:(sub + 1) * 128], self.mix[2].ap[:, kc, sub * 128:(sub + 1) * 128],
                      wv.ap[:, kc, :], kc == 0, kc == 7, R=[wv.tok, self.mix[2].tok], W=[pvv.tok])
        kb.mm(pw.ap[:, :], self.w2b.ap[0:64, cols], self.hid.ap[0:64, 0:512], True, True,
              R=[self.w2b.tok, self.hid.tok], W=[pw.tok])
        kb.mm(pa.ap[:, :], self.a2b.ap[0:64, cols], self.hid.ap[0:64, 512:1024], True, True,
              R=[self.a2b.tok, self.hid.tok], W=[pa.tok])
        kb.copy(r.ap[:, 0:512], pr.ap[:, :], R=[pr.tok], W=[r.tok])
        kb.act(sg.ap[:, 0:512], pg.ap[:, :], AF.Sigmoid, R=[pg.tok], W=[sg.tok])
        kb.tt(sg.ap[:, 0:512], sg.ap[:, 0:512], pg.ap[:, :], ALU.mult, R=[sg.tok, pg.tok], W=[sg.tok])
        kb.copy(k.ap[:, 0:512], pk.ap[:, :], R=[pk.tok], W=[k.tok])
        kb.act(sq.ap[:, :], pk.ap[:, :], AF.Square, R=[pk.tok, pv.tok], W=[sq.tok], scale=pv.ap[:, 64 + hp:65 + hp])
        kb.copy(vsb.ap[:, 0:512], pvv.ap[:, :], R=[pvv.tok], W=[vsb.tok])
        kb.act(sgm.ap[:, 0:512], pw.ap[:, :], AF.Sigmoid, R=[pw.tok, pv.tok], W=[sgm.tok], bias=pv.ap[:, 48 + hp:49 + hp])
        kb.act(icl.ap[:, 0:512], pa.ap[:, :], AF.Sigmoid, R=[pa.tok, pv.tok], W=[icl.tok], bias=pv.ap[:, 56 + hp:57 + hp])
        pn = psf[0]
        kb.mm(pn.ap[:, :], self.onesbd.ap[:, :], sq.ap[:, :], True, True, R=[self.onesbd.tok, sq.tok], W=[pn.tok])
        kb.act(t1.ap[:, 0:512], pn.ap[:, :], AF.Sqrt, R=[pn.tok], W=[t1.tok])
        kb.ts(t1.ap[:, 0:512], t1.ap[:, 0:512], 1e-12, None, ALU.max, None, R=[t1.tok], W=[t1.tok])
        kb.recip(t1.ap[:, 0:512], t1.ap[:, 0:512], R=[t1.tok], W=[t1.tok])
        kb.stt(kkn.ap[:, 0:512], k.ap[:, 0:512], pv.ap[:, 64 + hp:65 + hp], t1.ap[:, 0:512], ALU.mult, ALU.mult,
               R=[k.tok, pv.tok, t1.tok], W=[kkn.tok])
        kb.scan(cs.ap[:, 0:512], self.rmask.ap[:, :], sgm.ap[:, 0:512], 0.0, ALU.mult, ALU.add,
                R=[self.rmask.tok, sgm.tok], W=[cs.tok])
        kb.tt(Eexc.ap[:, 0:512], cs.ap[:, 0:512], sgm.ap[:, 0:512], ALU.subtract, R=[cs.tok, sgm.tok], W=[Eexc.tok])
        kb.act(Epos.ap[:, 0:512], cs.ap[:, 0:512], AF.Exp, R=[cs.tok], W=[Epos.tok], scale=C)
        kb.act(Eneg.ap[:, 0:512], cs.ap[:, 0:512], AF.Exp, R=[cs.tok], W=[Eneg.tok], scale=-C)
        kb.act(Eexc.ap[:, 0:512], Eexc.ap[:, 0:512], AF.Exp, R=[Eexc.tok], W=[Eexc.tok], scale=C)
        kb.ts(t1.ap[:, 0:512], icl.ap[:, 0:512], pv.ap[:, 72 + hp:73 + hp], pd.ap[:, hp:hp + 1], ALU.mult, ALU.add,
              R=[icl.tok, pv.tok, pd.tok], W=[t1.tok])
        kb.tt(k.ap[:, 0:512], k.ap[:, 0:512], t1.ap[:, 0:512], ALU.mult, R=[k.tok, t1.tok], W=[k.tok])
        kb.tt(bq.ap[:, 0:512], kkn.ap[:, 0:512], icl.ap[:, 0:512], ALU.mult, R=[kkn.tok, icl.tok], W=[bq.tok])
        kb.tt(kt.ap[:, :], k.ap[:, 0:512], Eneg.ap[:, 0:512], ALU.mult, R=[k.tok, Eneg.tok], W=[kt.tok])
        kb.tt(bt.ap[:, :], bq.ap[:, 0:512], Eneg.ap[:, 0:512], ALU.mult, R=[bq.tok, Eneg.tok], W=[bt.tok])
        kb.stt(at.ap[:, 0:512], kkn.ap[:, 0:512], -1.0, Eexc.ap[:, 0:512], ALU.mult, ALU.mult, R=[kkn.tok, Eexc.tok],
               W=[at.tok])
        pbon = psf[1]
        kb.tt(rt.ap[:, 0:512], r.ap[:, 0:512], Epos.ap[:, 0:512], ALU.mult, R=[r.tok, Epos.tok], W=[rt.tok])
        kb.stt(rkr.ap[:, :], r.ap[:, 0:512], pv.ap[:, 80 + hp:81 + hp], k.ap[:, 0:512], ALU.mult, ALU.mult,
               R=[r.tok, pv.tok, k.tok], W=[rkr.tok])
        kb.mm(pbon.ap[:, :], self.onesbd.ap[:, :], rkr.ap[:, :], True, True, R=[self.onesbd.tok, rkr.tok], W=[pbon.tok])
        kb.act(r.ap[:, 0:512], pbon.ap[:, :], AF.Copy, R=[pbon.tok], W=[r.tok])
        Epos3 = Epos.ap[:, 0:512].rearrange("p (c t) -> p c t", t=64)
        wend_bc = Epos3[:, :, 63:64].broadcast_to([128, NCH, 64])
        kb.tt(yn.ap[:, :].rearrange("p (c t) -> p c t", t=64), kt.ap[:, :].rearrange("p (c t) -> p c t", t=64), wend_bc,
              ALU.mult, R=[kt.tok, Epos.tok], W=[yn.tok])
        for sub in range(NSUB):
            kb.tr(self.psb[0].ap[:, sub * 128:(sub + 1) * 128], yn.ap[:, sub * 128:(sub + 1) * 128], self.ident.ap[:, :],
                  R=[yn.tok, self.ident.tok], W=[self.psb[0].tok])
        kb.tt(yn.ap[:, :].rearrange("p (c t) -> p c t", t=64), bt.ap[:, :].rearrange("p (c t) -> p c t", t=64), wend_bc,
              ALU.mult, R=[bt.tok, Epos.tok], W=[yn.tok])
        for sub in range(NSUB):
            kb.tr(self.psb[1].ap[:, sub * 128:(sub + 1) * 128], yn.ap[:, sub * 128:(sub + 1) * 128], self.ident.ap[:, :],
                  R=[yn.tok, self.ident.tok], W=[self.psb[1].tok])
        kb.copy(ktm.ap[:, 0:512], self.psb[0].ap[:, 0:512], R=[self.psb[0].tok], W=[ktm.tok])
        kb.act(btm.ap[:, 0:512], self.psb[1].ap[:, 0:512], AF.Copy, R=[self.psb[1].tok], W=[btm.tok])
        SC, PPh = cx.SC, cx.PPh
        NNs = [(NN, NNT), (rkr, sq)]
        for hl in range(2):
            rs = slice(64 * hl, 64 * hl + 64)
            nn, nnt = NNs[hl]
            for sub in range(NSUB):
                pi = hl * 4 + sub
                X, Y = (psf[0], psf[2]) if pi % 2 == 0 else (psf[3], psf[4])
                tc = slice(sub * 128, (sub + 1) * 128)
                kb.mm(X.ap[:, 0:128], kt.ap[rs, tc], at.ap[rs, tc], True, True, R=[kt.tok, at.tok], W=[X.tok])
                kb.mm(X.ap[:, 128:256], kt.ap[rs, tc], rt.ap[rs, tc], True, True, R=[kt.tok, rt.tok], W=[X.tok])
                kb.mm(X.ap[:, 256:384], bt.ap[rs, tc], rt.ap[rs, tc], True, True, R=[bt.tok, rt.tok], W=[X.tok])
                kb.mm(Y.ap[:, 0:128], bt.ap[rs, tc], at.ap[rs, tc], True, True, R=[bt.tok, at.tok], W=[Y.tok])
                kb.mm(Y.ap[:, 128:256], at.ap[rs, tc], bt.ap[rs, tc], True, True, R=[bt.tok, at.tok], W=[Y.tok])
                kb.tt(SC.ap[:, pi, 0:384], X.ap[:, 0:384], self.M3.ap[:, 0:384], ALU.mult, R=[X.tok, self.M3.tok],
                      W=[SC.tok])
                kb.tt(nn.ap[:, tc], Y.ap[:, 0:128], self.M2.ap[:, 0:128], ALU.mult, R=[Y.tok, self.M2.tok], W=[nn.tok])
                kb.tt(nnt.ap[:, tc], Y.ap[:, 128:256], self.M2.ap[:, 128:256], ALU.mult, R=[Y.tok, self.M2.tok],
                      W=[nnt.tok])
            for j in range(4):
                kb.tt(PPh[hl].ap[:, j * 128:(j + 1) * 128], nn.ap[:, j * 128:(j + 1) * 128], self.ident.ap[:, :],
                      ALU.add, R=[nn.tok, self.ident.tok], W=[PPh[hl].tok])
        ABC = [(psf[3], psf[4], psf[5]), (psf[0], psf[2], psf[1])]
        for lvl in range(5):
            last = lvl == 4
            for hl in range(2):
                nn, nnt = NNs[hl]
                A_, B_, C_ = ABC[hl]
                for j in range(4):
                    tc = slice(j * 128, (j + 1) * 128)
                    if not last:
                        kb.mm(A_.ap[:, tc], nnt.ap[:, tc], nn.ap[:, tc], True, True, R=[nn.tok, nnt.tok], W=[A_.tok])
                    kb.mm(B_.ap[:, tc], nn.ap[:, tc], nnt.ap[:, tc], True, True, R=[nn.tok, nnt.tok], W=[B_.tok])
            for hl in range(2):
                nn, nnt = NNs[hl]
                A_, B_, C_ = ABC[hl]
                kb.act(nnt.ap[:, :], B_.ap[:, :], AF.Copy, R=[B_.tok], W=[nnt.tok])
                if not last:
                    kb.copy(nn.ap[:, :], A_.ap[:, :], R=[A_.tok], W=[nn.tok])
            for hl in range(2):
                nn, nnt = NNs[hl]
                A_, B_, C_ = ABC[hl]
                for j in range(4):
                    tc = slice(j * 128, (j + 1) * 128)
                    kb.mm(C_.ap[:, tc], nnt.ap[:, tc], PPh[hl].ap[:, tc], True, True, R=[nnt.tok, PPh[hl].tok],
                          W=[C_.tok])
            for hl in range(2):
                A_, B_, C_ = ABC[hl]
                kb.tt(PPh[hl].ap[:, 0:512], C_.ap[:, :], PPh[hl].ap[:, 0:512], ALU.add, R=[C_.tok, PPh[hl].tok],
                      W=[PPh[hl].tok])

    def l1_chain(self, cxs):
        kb = self.kb
        VW = 64
        ho = 4 * VW
        for cx in cxs:
            for hl in range(2):
                rs = slice(64 * hl, 64 * hl + 64)
                kb.act(cx.smb.ap[rs, ho + hl * VW:ho + (hl + 1) * VW], cx.Hs[rs, :], AF.Copy, R=[cx.Ht], W=[cx.smb.tok])
        for c in range(NCH):
            sub, half = divmod(c, 2)
            p0 = 64 * half
            ps_ = slice(p0, p0 + 64)
            cc = slice(c * 64, (c + 1) * 64)
            for cx in cxs:
                smb, XU, xo = cx.smb, cx.XU, cx.xo
                for hl in range(2):
                    vv = slice(sub * 128 + hl * 64, sub * 128 + hl * 64 + 64)
                    pi = hl * 4 + sub
                    kb.mm(XU.ap[ps_, xo + hl * VW:xo + (hl + 1) * VW], cx.at.ap[:, cc],
                          smb.ap[:, ho + hl * VW:ho + (hl + 1) * VW], True, False, R=[cx.at.tok, smb.tok], W=[XU.tok])
                    kb.mm(XU.ap[ps_, xo + hl * VW:xo + hl * VW + 64], cx.SC.ap[:, pi, p0:p0 + 64], cx.vsb.ap[:, vv], False,
                          True, R=[cx.SC.tok, cx.vsb.tok], W=[XU.tok])
            for cx in cxs:
                kb.act(cx.smb.ap[ps_, 0:2 * VW], cx.XU.ap[ps_, cx.xo:cx.xo + 2 * VW], AF.Copy, R=[cx.XU.tok],
                       W=[cx.smb.tok])
            for cx in cxs:
                smb, XU, xo = cx.smb, cx.XU, cx.xo
                for hl in range(2):
                    kb.mm(XU.ap[ps_, xo + 2 * VW + hl * VW:xo + 2 * VW + (hl + 1) * VW],
                          cx.PPh[hl].ap[:, sub * 128 + p0:sub * 128 + p0 + 64], smb.ap[:, hl * VW:(hl + 1) * VW], True, True,
                          R=[cx.PPh[hl].tok, smb.tok], W=[XU.tok])
            for cx in cxs:
                kb.copy(cx.smb.ap[ps_, 2 * VW:4 * VW], cx.XU.ap[ps_, cx.xo + 2 * VW:cx.xo + 4 * VW], R=[cx.XU.tok],
                        W=[cx.smb.tok])
            for cx in cxs:
                smb, dH, do = cx.smb, cx.dH, cx.do
                for hl in range(2):
                    rs = slice(64 * hl, 64 * hl + 64)
                    vv = slice(sub * 128 + hl * 64, sub * 128 + hl * 64 + 64)
                    kb.mm(dH.ap[rs, do:do + VW], cx.btm.ap[ps_, vv], smb.ap[ps_, 2 * VW + hl * VW:2 * VW + (hl + 1) * VW], True,
                          False, R=[cx.btm.tok, smb.tok], W=[dH.tok])
                    kb.mm(dH.ap[rs, do:do + 64], cx.ktm.ap[ps_, vv], cx.vsb.ap[ps_, vv], False, True,
                          R=[cx.ktm.tok, cx.vsb.tok], W=[dH.tok])
            for cx in cxs:
                smb, yps = cx.smb, cx.yps
                for hl in range(2):
                    us = slice(2 * VW + hl * VW, 2 * VW + hl * VW + 64)
                    vv = slice(sub * 128 + hl * 64, sub * 128 + hl * 64 + 64)
                    pi = hl * 4 + sub
                    kb.mm(yps.ap[ps_, vv], cx.rt.ap[:, cc], smb.ap[:, ho + hl * VW:ho + (hl + 1) * VW], True, False,
                          R=[cx.rt.tok, smb.tok], W=[yps.tok])
                    kb.mm(yps.ap[ps_, vv], cx.SC.ap[:, pi, 256 + p0:256 + p0 + 64], smb.ap[:, us], False, False,
                          R=[cx.SC.tok, smb.tok], W=[yps.tok])
                    kb.mm(yps.ap[ps_, vv], cx.SC.ap[:, pi, 128 + p0:128 + p0 + 64], cx.vsb.ap[:, vv], False, True,
                          R=[cx.SC.tok, cx.vsb.tok], W=[yps.tok])
            for cx in cxs:
                smb, dH, do = cx.smb, cx.dH, cx.do
                for hl in range(2):
                    rs = slice(64 * hl, 64 * hl + 64)
                    kb.stt(smb.ap[rs, ho + hl * VW:ho + (hl + 1) * VW], cx.Hs[rs, :], cx.Epos.ap[rs, c * 64 + 63:c * 64 + 64],
                           dH.ap[rs, do:do + VW], ALU.mult, ALU.add, R=[cx.Ht, cx.Epos.tok, dH.tok], W=[smb.tok])
            for cx in cxs:
                kb.stt(cx.Hs, cx.Hs, cx.Epos.ap[:, c * 64 + 63:c * 64 + 64], cx.dH.ap[:, cx.do:cx.do + VW], ALU.mult,
                       ALU.add, R=[cx.Ht, cx.Epos.tok, cx.dH.tok], W=[cx.Ht])

    def l1_epi(self, cx):
        kb = self.kb
        pv = self.pv1
        sm = self.sm
        hp = cx.hp
        yps, yn, vsb, sg, r = cx.yps, cx.yn, cx.vsb, cx.sg, cx.r
        s1, s2, mean, msq, var, rms, rstd = sm[6], sm[7], sm[8], sm[9], sm[10], sm[11], sm[12]
        sqy = cx.t1
        y3 = yps.ap[:, :].rearrange("p (g v) -> p g v", v=64)
        kb.emit("vector", lambda e: e.reduce_sum(s1.ap[:, 0:8], y3, mybir.AxisListType.X), R=[yps.tok], W=[s1.tok])
        kb.act(sqy.ap[:, 0:512], yps.ap[:, :], AF.Square, R=[yps.tok], W=[sqy.tok])
        sq3 = sqy.ap[:, 0:512].rearrange("p (g v) -> p g v", v=64)
        kb.emit("vector", lambda e: e.reduce_sum(s2.ap[:, 0:8], sq3, mybir.AxisListType.X), R=[sqy.tok], W=[s2.tok])
        kb.ts(mean.ap[:, 0:8], s1.ap[:, 0:8], 1.0 / 64, None, ALU.mult, None, R=[s1.tok], W=[mean.tok])
        kb.tt(msq.ap[:, 0:8], mean.ap[:, 0:8], mean.ap[:, 0:8], ALU.mult, R=[mean.tok], W=[msq.tok])
        kb.stt(var.ap[:, 0:8], s2.ap[:, 0:8], 1.0 / 64, msq.ap[:, 0:8], ALU.mult, ALU.subtract, R=[s2.tok, msq.tok],
               W=[var.tok])
        kb.act(rms.ap[:, 0:8], var.ap[:, 0:8], AF.Sqrt, R=[var.tok, self.cst.tok], W=[rms.tok], bias=self.cst.ap[:, 2:3])
        kb.recip(rstd.ap[:, 0:8], rms.ap[:, 0:8], R=[rms.tok], W=[rstd.tok])
        for g in range(8):
            kb.ts(yn.ap[:, g * 64:(g + 1) * 64], yps.ap[:, g * 64:(g + 1) * 64], mean.ap[:, g:g + 1], rstd.ap[:, g:g + 1],
                  ALU.subtract, ALU.mult, R=[yps.tok, mean.tok, rstd.tok], W=[yn.tok])
        for sub in range(NSUB):
            kb.tr(self.psb[0].ap[:, sub * 128:(sub + 1) * 128], yn.ap[:, sub * 128:(sub + 1) * 128], self.ident.ap[:, :],
                  R=[yn.tok, self.ident.tok], W=[self.psb[0].tok])
        for sub in range(NSUB):
            kb.tr(self.psb[1].ap[:, sub * 128:(sub + 1) * 128], vsb.ap[:, sub * 128:(sub + 1) * 128], self.ident.ap[:, :],
                  R=[vsb.tok, self.ident.tok], W=[self.psb[1].tok])
        vT = cx.sq
        kb.act(vT.ap[:, :], self.psb[1].ap[:, 0:512], AF.Copy, R=[self.psb[1].tok], W=[vT.tok])
        bv = cx.k
        kb.tt(bv.ap[:, 0:512], r.ap[:, 0:512], vT.ap[:, :], ALU.mult, R=[r.tok, vT.tok], W=[bv.tok])
        kb.stt(cx.sgm.ap[:, 0:512], self.psb[0].ap[:, 0:512], pv.ap[:, 88 + hp:89 + hp], bv.ap[:, 0:512], ALU.mult, ALU.add,
               R=[self.psb[0].tok, pv.tok, bv.tok], W=[cx.sgm.tok])
        kb.stt(self.merged.ap[:, hp, :], cx.sgm.ap[:, 0:512], pv.ap[:, 96 + hp:97 + hp], sg.ap[:, 0:512], ALU.add, ALU.mult,
               R=[cx.sgm.tok, pv.tok, sg.tok], W=[self.merged.tok])

    def l1_epi2(self, cxs):
        kb = self.kb
        pv = self.pv1
        X = mybir.AxisListType.X
        for cx in cxs:
            s1 = cx.e_sm[0]
            y3 = cx.yps.ap[:, :].rearrange("p (g v) -> p g v", v=64)
            kb.emit("vector", lambda e, s1=s1, y3=y3: e.reduce_sum(s1.ap[:, 0:8], y3, X), R=[cx.yps.tok], W=[s1.tok])
            kb.act(cx.e_sqy.ap[:, 0:512], cx.yps.ap[:, :], AF.Square, R=[cx.yps.tok], W=[cx.e_sqy.tok])
        for cx in cxs:
            s1, s2, mean, msq, var, rms, rstd = cx.e_sm
            sq3 = cx.e_sqy.ap[:, 0:512].rearrange("p (g v) -> p g v", v=64)
            kb.emit("vector", lambda e, s2=s2, sq3=sq3: e.reduce_sum(s2.ap[:, 0:8], sq3, X), R=[cx.e_sqy.tok], W=[s2.tok])
            kb.ts(mean.ap[:, 0:8], s1.ap[:, 0:8], 1.0 / 64, None, ALU.mult, None, R=[s1.tok], W=[mean.tok])
            kb.tt(msq.ap[:, 0:8], mean.ap[:, 0:8], mean.ap[:, 0:8], ALU.mult, R=[mean.tok], W=[msq.tok])
            kb.stt(var.ap[:, 0:8], s2.ap[:, 0:8], 1.0 / 64, msq.ap[:, 0:8], ALU.mult, ALU.subtract, R=[s2.tok, msq.tok],
                   W=[var.tok])
        for cx in cxs:
            s1, s2, mean, msq, var, rms, rstd = cx.e_sm
            kb.act(rms.ap[:, 0:8], var.ap[:, 0:8], AF.Sqrt, R=[var.tok, self.cst.tok], W=[rms.tok], bias=self.cst.ap[:, 2:3])
        for cx in cxs:
            s1, s2, mean, msq, var, rms, rstd = cx.e_sm
            kb.recip(rstd.ap[:, 0:8], rms.ap[:, 0:8], R=[rms.tok], W=[rstd.tok])
            for g in range(8):
                kb.ts(cx.e_yn.ap[:, g * 64:(g + 1) * 64], cx.yps.ap[:, g * 64:(g + 1) * 64], mean.ap[:, g:g + 1],
                      rstd.ap[:, g:g + 1], ALU.subtract, ALU.mult, R=[cx.yps.tok, mean.tok, rstd.tok], W=[cx.e_yn.tok])
        for cx in cxs:
            for sub in range(NSUB):
                kb.tr(cx.e_pT[0].ap[:, sub * 128:(sub + 1) * 128], cx.e_yn.ap[:, sub * 128:(sub + 1) * 128],
                      self.ident.ap[:, :], R=[cx.e_yn.tok, self.ident.tok], W=[cx.e_pT[0].tok])
            for sub in range(NSUB):
                kb.tr(cx.e_pT[1].ap[:, sub * 128:(sub + 1) * 128], cx.vsb.ap[:, sub * 128:(sub + 1) * 128],
                      self.ident.ap[:, :], R=[cx.vsb.tok, self.ident.tok], W=[cx.e_pT[1].tok])
        for cx in cxs:
            kb.act(cx.e_vT.ap[:, 0:512], cx.e_pT[1].ap[:, 0:512], AF.Copy, R=[cx.e_pT[1].tok], W=[cx.e_vT.tok])
        for cx in cxs:
            kb.tt(cx.e_bv.ap[:, 0:512], cx.r.ap[:, 0:512], cx.e_vT.ap[:, 0:512], ALU.mult, R=[cx.r.tok, cx.e_vT.tok],
                  W=[cx.e_bv.tok])
            kb.stt(cx.e_tmp.ap[:, 0:512], cx.e_pT[0].ap[:, 0:512], pv.ap[:, 88 + cx.hp:89 + cx.hp], cx.e_bv.ap[:, 0:512],
                   ALU.mult, ALU.add, R=[cx.e_pT[0].tok, pv.tok, cx.e_bv.tok], W=[cx.e_tmp.tok])
            kb.stt(self.merged.ap[:, cx.hp, :], cx.e_tmp.ap[:, 0:512], pv.ap[:, 96 + cx.hp:97 + cx.hp], cx.sg.ap[:, 0:512],
                   ALU.add, ALU.mult, R=[cx.e_tmp.tok, pv.tok, cx.sg.tok], W=[self.merged.tok])

    def l1_front_load(self, T, u1all, u1tok):
        kb = self.kb
        st1 = self.st1
        r, tl = divmod(T, self.NT)
        up = st1.ap[:, 512:520].unsqueeze(2)
        kb.copy(self.uT.ap[:, :, 0:1], up, R=[st1.tok], W=[self.uT.tok])
        src = u1all[tl].ap()[r * 128:(r + 1) * 128, :].rearrange("p (k t) -> p k t", k=8)
        kb.dma("sync", self.uT.ap[:, :, 1:513], src, self.uT.tok, R=[u1tok[tl]], W=[self.uT.tok])
        kb.copy(up, self.uT.ap[:, :, 512:513], R=[self.uT.tok], W=[st1.tok])

    def l1_front_rest(self, T):
        kb = self.kb
        diff = self.merged
        kb.tt(diff.ap[:, 8:16, :], self.uT.ap[:, :, 0:512], self.uT.ap[:, :, 1:513], ALU.subtract,
              R=[self.uT.tok], W=[diff.tok])
        for (wb, ws, ph, fn, lo) in ((self.w1b, self.w1s, self.psf[4], AF.Tanh, 0),
                                     (self.a1b, self.a1s, self.psf[5], AF.Copy, 512)):
            for kc in range(8):
                kb.mm(ph.ap[0:64, :], wb.ap[:, kc, :], self.uT.ap[:, kc, 1:513], kc == 0, False,
                      R=[wb.tok, self.uT.tok], W=[ph.tok])
            for kc in range(8):
                kb.mm(ph.ap[0:64, :], ws.ap[:, kc, :], diff.ap[:, 8 + kc, :], False, kc == 7,
                      R=[ws.tok, diff.tok], W=[ph.tok])
            if fn == AF.Tanh:
                tf = self.f[0]
                kb.act(tf.ap[0:64, 0:512], ph.ap[0:64, :], AF.Sigmoid, R=[ph.tok], W=[tf.tok], scale=2.0)
                kb.ts(self.hid.ap[:, lo:lo + 512], tf.ap[0:64, 0:512], 2.0, -1.0, ALU.mult, ALU.add, R=[tf.tok],
                      W=[self.hid.tok])
            else:
                kb.act(self.hid.ap[:, lo:lo + 512], ph.ap[0:64, :], fn, R=[ph.tok], W=[self.hid.tok])
        for p_ in range(4):
            self.l1_mix(p_, self.mix[p_])

    def layer1_tile_hp(self, T, NG, u1all, u1tok, mgloc, mgtok, after_pre=None):
        kb = self.kb
        NT = self.NT
        r, tl = divmod(T, NT)
        if T == 0:
            self.l1_front_load(0, u1all, u1tok)
            self.l1_front_rest(0)
            if NG > 1:
                self.l1_front_load(1, u1all, u1tok)
        cxA = self.l1_ctx(0, False)
        cxB = self.l1_ctx(1, True)
        self.l1_pre(cxA)
        self.l1_pre(cxB)
        if after_pre is not None:
            after_pre()
        self.l1_chain([cxA, cxB])
        self.l1_epi2([cxA, cxB])
        dst = mgloc[T].ap().rearrange("p (j t) -> p j t", j=2)
        kb.dma("sync", dst, self.merged.ap[:, 0:2, :], mgtok[T], R=[self.merged.tok], W=[mgtok[T]])
        if T + 1 < NG:
            self.l1_front_rest(T + 1)
            if T + 2 < NG:
                self.l1_front_load(T + 2, u1all, u1tok)

    def build_comm(self):
        kb = self.kb
        NT = self.NT
        NG = 4 * NT
        self.declare()
        self.alloc_common()
        self.consts()
        self.l0_setup()
        self.l0_begin()
        self.l1_setup()
        self.comm_setup()
        u1loc = [self.nc.dram_tensor(f"u1loc{t}", [128, 8 * 512], BF16) for t in range(NT)]
        u1all = [self.nc.dram_tensor(f"u1all{t}", [512, 8 * 512], BF16) for t in range(NT)]
        mgloc = [self.nc.dram_tensor(f"mgloc{q}", [128, 2 * 512], BF16) for q in range(NG)]
        mgall = [self.nc.dram_tensor(f"mgall{q}", [512, 2 * 512], BF16) for q in range(NG)]
        u1t = [kb.tok(f"u1loc{t}") for t in range(NT)]
        mgt = [kb.tok(f"mgloc{q}") for q in range(NG)]
        u1a = [None] * NT
        mga = [None] * NG
        u0loc = [self.nc.dram_tensor(f"u0loc{t}", [128, 8 * 512], BF16) for t in range(NT)]
        u0all = [self.nc.dram_tensor(f"u0all{t}", [512, 8 * 512], BF16) for t in range(NT)]
        NCK = NG
        m0loc = [self.nc.dram_tensor(f"m0loc{c}", [128, 4 * 512], BF16) for c in range(NCK)]
        m0all = [self.nc.dram_tensor(f"m0all{c}", [512, 4 * 512], BF16) for c in range(NCK)]
        u0t = [kb.tok(f"u0loc{t}") for t in range(NT)]
        m0t = [kb.tok(f"m0loc{c}") for c in range(NCK)]
        u0a = [None] * NT
        m0a = [None] * NCK
        w1f_ = self.wout1.ap[:, :, :].rearrange("p a b -> p (a b)").bitcast(F32)
        xsA = self.xs
        xsB = [Buf(w1f_[:, i * 1024:(i + 1) * 1024], kb.tok(f"xsB{i}")) for i in range(NSUB)]
        xbufs = [xsA, xsB]
        m0f_ = self.mix[0].ap[:, :, :].rearrange("p a b -> p (a b)")
        xn2_p0 = Buf(m0f_[:, 0:1024], kb.tok("xn2p0"))
        self.norm_xn2 = xn2_p0
        self.load_x(0, xsA)
        if NT > 1:
            self.load_x(1, xsB)
        for tile in range(NT):
            self.xs = xbufs[tile % 2]
            self.gain = self.pv0
            self.gain_off = 88
            self.norm_T(tile, 0)
            dst = u0loc[tile].ap().rearrange("p (k t) -> p k t", k=8)
            kb.dma("sync", dst, self.uT.ap[:, :, 0:512], u0t[tile], R=[self.uT.tok], W=[u0t[tile]])
            u0a[tile] = self.cc_gather_dram(u0loc[tile], u0all[tile], u0t[tile], f"g0u{tile}")
            if tile + 2 < NT:
                self.load_x(tile + 2, xbufs[tile % 2])
        self.xs = xsA
        self.norm_xn2 = None
        for bf in self.rgB_f[0:3]:
            kb.absorb(bf.tok, [xn2_p0.tok])
        for bf in self.hgB_b + [self.hgB_f[10]]:
            kb.absorb(bf.tok, [b_.tok for b_ in xsB])
        kb.absorb(self.wout1.tok, [b_.tok for b_ in xsB])
        scf = self.SC.ap[:, :, :].rearrange("p a b -> p (a b)")
        m3f = self.mix[3].ap[:, :, :].rearrange("p a b -> p (a b)")
        xslots = [Buf(scf[:, i * 1024:(i + 1) * 1024].rearrange("p (a b) -> p a b", b=128), kb.tok(f"wsx{i}"))
                  for i in range(3)]
        xslots += [Buf(m3f[:, 2048 + i * 1024:2048 + (i + 1) * 1024].rearrange("p (a b) -> p a b", b=128),
                       kb.tok(f"wsx{3 + i}")) for i in range(2)]
        self.wslots_active = self.wslot + xslots[0:4]
        pend = []
        def p1_load(T_):
            r_, tl_ = divmod(T_, NT)
            src_ = u0all[tl_].ap()[r_ * 128:(r_ + 1) * 128, :].rearrange("p (k t) -> p k t", k=8)
            kb.dma("sync", self.uT.ap[:, :, 0:512], src_, self.uT.tok, R=[u0a[tl_]], W=[self.uT.tok])

        p1_load(0)
        for T in range(NG):
            r, tl = divmod(T, NT)
            self.l0_rg2()
            self.l0_hg2((lambda T_=T: p1_load(T_ + 1)) if T + 1 < NG else None)
            nblk = -(-16 // NG)
            for cbw in range(T * nblk, min(16, (T + 1) * nblk)):
                self.load_wout0(cbw)
            ck = T
            dst = m0loc[ck].ap().rearrange("p (j t) -> p j t", j=4)
            kb.dma("sync", dst[:, 0:2, :], self.merged.ap[:, 0:2, :], m0t[ck], R=[self.merged.tok], W=[m0t[ck]])
            kb.dma("sync", dst[:, 2:4, :], self.merged.ap[:, 8:10, :], m0t[ck], R=[self.merged.tok], W=[m0t[ck]])
            for ck_ in pend:
                m0a[ck_] = self.cc_gather_dram(m0loc[ck_], m0all[ck_], m0t[ck_], f"g0m{ck_}")
            pend = [ck]
        for ck_ in pend:
            m0a[ck_] = self.cc_gather_dram(m0loc[ck_], m0all[ck_], m0t[ck_], f"g0m{ck_}")
        self.wslots_active = None
        kb.absorb(self.SC.tok, [b_.tok for b_ in xslots[0:3]])
        kb.absorb(self.mix[3].tok, [xslots[3].tok])
        for i in range(4):
            kb.absorb(self.mix[i].tok, [bf.tok for bf in self.rgB_f] + [self.rgB_xcb.tok])
        kb.absorb(self.wout1.tok, [bf.tok for bf in self.hgB_b] + [self.hgB_f[10].tok])
        Lp = [(self.mix[0], self.mix[1]), (self.mix[2], self.mix[3])]
        xn2_p2 = Buf(self.wout1.ap[:, :, :].rearrange("p a b -> p (a b)")[:, 0:1024], kb.tok("xn2p2"))
        xn2_p2.tok.r = dict(self.wout1.tok.r)
        xn2_p2.tok.w = self.wout1.tok.w
        self.norm_xn2 = xn2_p2

        def p2_load(tile, q):
            La, Lb = Lp[q % 2]
            ck = q * NT + tile
            for r in range(4):
                src = m0all[ck].ap()[r * 128:(r + 1) * 128, :].rearrange("p (j t) -> p j t", j=4)
                qn = "sync" if q % 2 == 0 else "scalar"
                kb.dma(qn, La.ap[:, 2 * r:2 * r + 2, :], src[:, 0:2, :], La.tok, R=[m0a[ck]], W=[La.tok])
                kb.dma(qn, Lb.ap[:, 2 * r:2 * r + 2, :], src[:, 2:4, :], Lb.tok, R=[m0a[ck]], W=[Lb.tok])

        def p2_select(tile, q):
            mq = self.msk.ap[:, 8 + q:9 + q]
            for (Lx, lo) in ((Lp[q % 2][0], 0), (Lp[q % 2][1], 8)):
                if q == 0:
                    kb.ts(self.merged.ap[:, lo:lo + 8, :], Lx.ap[:, :, :], mq, None, ALU.mult, None,
                          R=[Lx.tok, self.msk.tok], W=[self.merged.tok])
                else:
                    kb.stt(self.merged.ap[:, lo:lo + 8, :], Lx.ap[:, :, :], mq, self.merged.ap[:, lo:lo + 8, :], ALU.mult,
                           ALU.add, R=[Lx.tok, self.msk.tok, self.merged.tok], W=[self.merged.tok])

        p2_load(0, 0)
        p2_load(0, 1)
        for tile in range(NT):
            self.load_x(tile)
            p2_select(tile, 0)
            p2_load(tile, 2)
            p2_select(tile, 1)
            p2_load(tile, 3)
            p2_select(tile, 2)
            p2_select(tile, 3)
            if tile + 1 < NT:
                p2_load(tile + 1, 0)
                p2_load(tile + 1, 1)
            self.out_proj(tile, 16, self.wout0, self.postbc0)
            for sub in range(NSUB):
                g = tile * NSUB + sub
                kb.dma("sync", self.x1buf.ap()[g * 128:(g + 1) * 128, :], self.xs[sub].ap, self.x1tok[tile],
                       R=[self.xs[sub].tok], W=[self.x1tok[tile]])
            self.gain = self.pv1
            self.gain_off = 104
            self.norm_T(tile, 1)
            dst = u1loc[tile].ap().rearrange("p (k t) -> p k t", k=8)
            kb.dma("sync", dst, self.uT.ap[:, :, 1:513], u1t[tile], R=[self.uT.tok], W=[u1t[tile]])
            u1a[tile] = self.cc_gather_dram(u1loc[tile], u1all[tile], u1t[tile], f"gu{tile}")
        self.norm_xn2 = None
        kb.absorb(self.wout1.tok, [xn2_p2.tok])
        carved = (self.l1B["b"] + [self.l1B["SC"], self.l1B["smb"]] + self.l1B["PPh"] + self.l1B["f"]
                  + self.l1E["b"] + self.l1E["f"])
        for bf in carved:
            bf.tok.r = dict(self.wout0.tok.r)
            bf.tok.w = self.wout0.tok.w
        kb.memset(self.l1B["smb"].ap, 0.0, [self.l1B["smb"].tok])
        self.w1s = Buf(self.WA.ap[:, :, :].rearrange("p a b -> p (a b)")[:, 0:512].rearrange("p (a b) -> p a b", b=64),
                       kb.tok("w1s"))
        self.a1s = Buf(self.WX.ap[:, :, :].rearrange("p a b -> p (a b)")[:, 0:512].rearrange("p (a b) -> p a b", b=64),
                       kb.tok("a1s"))
        for (ws, wb, own, c0) in ((self.w1s, self.w1b, self.WA, 32), (self.a1s, self.a1b, self.WX, 40)):
            ws.tok.r = dict(own.tok.r)
            ws.tok.w = own.tok.w
            kb.tt(ws.ap, wb.ap[:, :, :], self.pv1.ap[:, c0:c0 + 8].unsqueeze(2).broadcast_to([128, 8, 64]), ALU.mult,
                  R=[wb.tok, self.pv1.tok], W=[ws.tok])
        def p4_load(tl, q):
            L = self.mix[2 * (q % 2)]
            for r in range(4):
                Tg = q * NT + tl
                src = mgall[Tg].ap()[r * 128:(r + 1) * 128, :].rearrange("p (j t) -> p j t", j=2)
                kb.dma("sync" if q % 2 == 0 else "scalar", L.ap[:, 2 * r:2 * r + 2, :], src, L.tok, R=[mga[Tg]],
                       W=[L.tok])

        def p4_first():
            p4_load(0, 0)
            p4_load(0, 1)

        pend = []
        for T in range(NG):
            self.layer1_tile_hp(T, NG, u1all, u1a, mgloc, mgt, p4_first if T == NG - 1 else None)
            nblk = -(-8 // NG)
            for cbw in range(T * nblk, min(8, (T + 1) * nblk)):
                self.load_wout1(cbw)
            for q_ in pend:
                mga[q_] = self.cc_gather_dram(mgloc[q_], mgall[q_], mgt[q_], f"gm{q_}")
            pend = [T]
        for q_ in pend:
            mga[q_] = self.cc_gather_dram(mgloc[q_], mgall[q_], mgt[q_], f"gm{q_}")
        def p4_select(tl, q):
            L = self.mix[2 * (q % 2)]
            mq = self.msk.ap[:, 8 + q:9 + q]
            if q == 0:
                kb.ts(self.merged.ap[:, 0:8, :], L.ap[:, :, :], mq, None, ALU.mult, None, R=[L.tok, self.msk.tok],
                      W=[self.merged.tok])
            else:
                kb.stt(self.merged.ap[:, 0:8, :], L.ap[:, :, :], mq, self.merged.ap[:, 0:8, :], ALU.mult, ALU.add,
                       R=[L.tok, self.msk.tok, self.merged.tok], W=[self.merged.tok])

        for tl in range(NT):
            self.load_x1(tl)
            p4_select(tl, 0)
            p4_load(tl, 2)
            p4_select(tl, 1)
            p4_load(tl, 3)
            p4_select(tl, 2)
            p4_select(tl, 3)
            if tl + 1 < NT:
                p4_load(tl + 1, 0)
                p4_load(tl + 1, 1)
            self.out_proj(tl, 8, self.wout1, self.postbc1)
            self.store_out(tl)
        toks = [b.tok for b in self.xs]
        self.kb.wait_all("sync", toks)
        self.kb.replay()

    def load_x1(self, tile):
        kb = self.kb
        for sub in range(NSUB):
            g = tile * NSUB + sub
            kb.dma("sync", self.xs[sub].ap, self.x1buf.ap()[g * 128:(g + 1) * 128, :], self.xs[sub].tok,
                   R=[self.x1tok[tile]], W=[self.xs[sub].tok])

    def declare(self):
        TS = self.TS
        self.din("x", [TS, D])
        self.dout("out", [TS, D])
        self.din("ident", [128, 128])
        self.din("maskU", [128, 64])
        if self.comm:
            self.din("msk", [128, 16])
            self.din("eye2", [128, 64])
        if self.do_l0:
            self.din("rmask", [128, 512])
            self.din("pv0", [128, self.PV0_COLS])
            self.din("wa_bd", [128, 8, 128])
            self.din("wx_bd", [128, 8, 128])
            self.din("w_in0", [D, 1536 if self.comm else 6144])
            self.din("w_out0", [2048, D])
            self.din("post0_bc", [128, D])
            if not self.comm:
                self.din("st0_in", [128, 1056])
                self.dout("st0_out", [128, 1056])
        if self.do_l1:
            if not self.do_l0:
                self.din("rmask", [128, 512])
            self.din("pv1", [128, self.PV1_COLS])
            self.din("M3", [128, 384])
            self.din("M2", [128, 256])
            self.din("onesbd", [128, 128])
            self.din("rw_w_in", [4, D, 256 if self.comm else D])
            self.din("w_out1", [D, D])
            self.din("rw_w1", [D, 64])
            self.din("rw_a1", [D, 64])
            self.din("rw_w2", [64, 256 if self.comm else D])
            self.din("rw_a2", [64, 256 if self.comm else D])
            self.din("post1_bc", [128, D])
            if not self.comm:
                self.din("st1_in", [128, 520])
                self.dout("st1_out", [128, 520])

    def build(self):
        self.declare()
        self.alloc_common()
        self.consts()
        if self.do_l0:
            self.l0_setup()
            self.l0_begin()
        if self.do_l1:
            self.l1_setup()
        for tile in range(self.NT):
            self.load_x(tile)
            if self.do_l0:
                self.layer0_tile(tile)
            if self.do_l1:
                self.layer1_tile(tile)
            self.store_out(tile)
        toks = [b.tok for b in self.xs]
        if self.do_l0:
            self.l0_end()
            toks.append(self.st0.tok)
        if self.do_l1:
            self.l1_end()
            toks.append(self.st1.tok)
        self.kb.wait_all("sync", toks)
        self.kb.replay()


def build_program(TS, do_l0=True, do_l1=True, comm=False):
    nc = bass.Bass("TRN2", target_bir_lowering=False)
    p = Prog(nc, TS, do_l0, do_l1, comm)
    if comm:
        p.build_comm()
    else:
        p.build()
    return nc


def fm(v):
    return np.ascontiguousarray(np.asarray(v, np.float32).reshape(8, 128).T)


def host_consts():
    p = np.arange(128)[:, None]
    t = np.arange(64)[None, :]
    c = {}
    c["ident"] = np.eye(128, dtype=np.float32)
    c["maskU"] = ((p % 64) <= t).astype(np.float32)
    rm = np.ones((128, 512), np.float32)
    rm[:, ::64] = 0.0
    c["rmask"] = rm
    return c


def host_l0(inp):
    o = {}
    pv = np.zeros((128, Prog.PV0_COLS), np.float32)
    cw = np.asarray(inp["rg_conv_w"][0], np.float32)
    for k in range(4):
        pv[:, k * 8:(k + 1) * 8] = fm(cw[k])
    pv[:, 32:40] = fm(inp["rg_conv_b"][0])
    pv[:, 40:48] = fm(inp["rg_b_a"][0])
    pv[:, 48:56] = fm(inp["rg_b_x"][0])
    pv[:, 56:64] = fm(inp["rg_lambda"][0])
    for r in range(3):
        pv[:, 64 + r * 8:64 + (r + 1) * 8] = fm(inp["hg_lower_bounds"][r])
    pv[:, 88:96] = fm(inp["pre_norm"][0])
    pv[:, 96] = np.asarray(inp["hg_out_norm"][0], np.float32)
    o["pv0"] = pv
    for nm, key in (("wa_bd", "rg_w_a"), ("wx_bd", "rg_w_x")):
        w = np.asarray(inp[key][0], np.float32)
        bd = np.zeros((128, 8, 128), np.float32)
        for cb in range(8):
            bd[0:64, cb, 0:64] = w[2 * cb]
            bd[64:128, cb, 64:128] = w[2 * cb + 1]
        o[nm] = bd
    o["w_in0"] = np.ascontiguousarray(np.asarray(inp["ab_w_in"][0], np.float32))
    o["w_out0"] = np.ascontiguousarray(np.asarray(inp["ab_w_out"][0], np.float32))
    o["post0_bc"] = np.ascontiguousarray(np.broadcast_to(np.asarray(inp["post_norm"][0], np.float32)[None, :], (128, D)))
    return o


def host_l1(inp):
    o = {}
    pv = np.zeros((128, Prog.PV1_COLS), np.float32)
    mu = np.asarray(inp["rw_mu"][0], np.float32)
    for p_ in range(6):
        pv[:, p_ * 8:(p_ + 1) * 8] = fm(mu[p_])
    pv[:, 48:56] = fm(inp["rw_w0"][0])
    pv[:, 56:64] = fm(inp["rw_a0"][0])
    pv[:, 64:72] = fm(inp["rw_k_k"][0])
    pv[:, 72:80] = fm(inp["rw_k_a"][0])
    pv[:, 80:88] = fm(np.asarray(inp["rw_r_k"][0], np.float32).reshape(-1))
    pv[:, 88:96] = fm(inp["rw_ln_w"][0])
    pv[:, 96:104] = fm(inp["rw_ln_b"][0])
    pv[:, 104:112] = fm(inp["pre_norm"][1])
    o["pv1"] = pv
    j = np.arange(128)[:, None]
    t = np.arange(128)[None, :]
    same = (j // 64) == (t // 64)
    MS = (same & (j < t)).astype(np.float32)
    MI = (same & (j <= t)).astype(np.float32)
    o["M3"] = np.ascontiguousarray(np.concatenate([MS, MI, MI], axis=1))
    o["M2"] = np.ascontiguousarray(np.concatenate([MS, MS.T], axis=1))
    o["onesbd"] = same.astype(np.float32)
    o["rw_w_in"] = np.ascontiguousarray(np.asarray(inp["rw_w_in"][0], np.float32))
    o["w_out1"] = np.ascontiguousarray(np.asarray(inp["rw_w_out"][0], np.float32))
    for nm in ("rw_w1", "rw_a1", "rw_w2", "rw_a2"):
        o[nm] = np.ascontiguousarray(np.asarray(inp[nm][0], np.float32))
    o["post1_bc"] = np.ascontiguousarray(np.broadcast_to(np.asarray(inp["post_norm"][1], np.float32)[None, :], (128, D)))
    return o


N_CORES = 8
SEG = 2048
MODE = os.environ.get("MK_MODE", "comm")
_NC_CACHE = {}


def _program(TS, comm=False):
    if (TS, comm) not in _NC_CACHE:
        _NC_CACHE[(TS, comm)] = build_program(TS, True, True, comm)
    return _NC_CACHE[(TS, comm)]


def host_comm(s, l1=None):
    m = np.zeros((128, 16), np.float32)
    for j in range(3):
        m[:, j] = 1.0 if j < s else 0.0
        m[:, 4 + j] = 1.0 if j == s - 1 else 0.0
    for q in range(4):
        m[:, 8 + q] = 1.0 if q == s else 0.0
    o = {"msk": m, "eye2": np.ascontiguousarray(np.tile(np.eye(64, dtype=np.float32), (2, 1)))}
    if l1 is not None:
        w = l1["w_in0"]
        o["w_in0"] = np.ascontiguousarray(np.concatenate(
            [w[:, g * 1024 + 256 * s:g * 1024 + 256 * s + 256] for g in range(6)], axis=1))
        pv0 = l1["pv0"].copy()
        for base in (0, 8, 16, 24, 32, 40, 48, 56, 64, 72, 80):
            pv0[:, base:base + 2] = l1["pv0"][:, base + 2 * s:base + 2 * s + 2]
        o["pv0"] = pv0
        for nm in ("wa_bd", "wx_bd"):
            bd = l1[nm].copy()
            bd[:, 0:2, :] = l1[nm][:, 2 * s:2 * s + 2, :]
            o[nm] = bd
        c0 = 256 * s
        o["rw_w_in"] = np.ascontiguousarray(l1["rw_w_in"][:, :, c0:c0 + 256])
        o["rw_w2"] = np.ascontiguousarray(l1["rw_w2"][:, c0:c0 + 256])
        o["rw_a2"] = np.ascontiguousarray(l1["rw_a2"][:, c0:c0 + 256])
        pv = l1["pv1"].copy()
        for base in (48, 56, 64, 72, 80, 88, 96):
            pv[:, base:base + 2] = l1["pv1"][:, base + 2 * s:base + 2 * s + 2]
        o["pv1"] = pv
    return o


def kernel(**inp):
    inp = {k: np.asarray(v) for k, v in inp.items()}
    x = np.asarray(inp["x"], np.float32)
    B, S, _ = x.shape
    base = {}
    base.update(host_consts())
    base.update(host_l0(inp))
    base.update(host_l1(inp))
    if MODE == "comm":
        nc = _program(SEG, True)
        maps = []
        for c in range(N_CORES):
            b, sgi = divmod(c, 4)
            m = dict(base)
            m.update(host_comm(sgi, base))
            m["x"] = np.ascontiguousarray(x[b, sgi * SEG:(sgi + 1) * SEG])
            maps.append(m)
        res = run_bass_kernel_spmd(nc, maps, core_ids=list(range(N_CORES)))
        out = np.empty((B, S, D), np.float32)
        for c in range(N_CORES):
            b, sgi = divmod(c, 4)
            out[b, sgi * SEG:(sgi + 1) * SEG] = res.results[c]["out"]
        return out
    if MODE == "fused":
        nc = _program(S)
        maps = []
        for c in range(N_CORES):
            m = dict(base)
            m["x"] = np.ascontiguousarray(x[c // 4])
            m["st0_in"] = np.zeros((128, 1056), np.float32)
            m["st1_in"] = np.zeros((128, 520), np.float32)
            maps.append(m)
        res = run_bass_kernel_spmd(nc, maps, core_ids=list(range(N_CORES)))
        out = np.empty((B, S, D), np.float32)
        for c in range(N_CORES):
            b, s = divmod(c, 4)
            out[b, s * SEG:(s + 1) * SEG] = res.results[c]["out"][s * SEG:(s + 1) * SEG]
        return out
    nc = _program(SEG)
    nseg = S // SEG
    st0 = [np.zeros((128, 1056), np.float32) for _ in range(N_CORES)]
    st1 = [np.zeros((128, 520), np.float32) for _ in range(N_CORES)]
    res = None
    for launch in range(nseg):
        maps = []
        for c in range(N_CORES):
            b, s = divmod(c, nseg)
            m = dict(base)
            m["x"] = np.ascontiguousarray(x[b, s * SEG:(s + 1) * SEG])
            m["st0_in"] = st0[c]
            m["st1_in"] = st1[c]
            maps.append(m)
        res = run_bass_kernel_spmd(nc, maps, core_ids=list(range(N_CORES)))
        for c in range(N_CORES):
            b, s = divmod(c, nseg)
            if s + 1 < nseg:
                st0[c + 1] = np.ascontiguousarray(res.results[c]["st0_out"])
                st1[c + 1] = np.ascontiguousarray(res.results[c]["st1_out"])
    out = np.empty((B, S, D), np.float32)
    for c in range(N_CORES):
        b, s = divmod(c, nseg)
        out[b, s * SEG:(s + 1) * SEG] = res.results[c]["out"]
    return out
```

```python
import os
import numpy as np
from contextlib import ExitStack
import concourse.bass as bass
import concourse.mybir as mybir
from concourse.bass_utils import run_bass_kernel_spmd

F32 = mybir.dt.float32
BF16 = mybir.dt.bfloat16
AF = mybir.ActivationFunctionType
ALU = mybir.AluOpType

D = 1024
EPS = 1e-6
GN_EPS = 64e-5
TT = 512
NSUB = 4
NCH = 8


class Tok:
    __slots__ = ("name", "w", "r", "dsem", "dcount", "q")

    def __init__(self, name):
        self.q = None
        self.name = name
        self.w = None
        self.r = {}
        self.dsem = None
        self.dcount = 0


class Eng:
    def __init__(self, name, sem):
        self.name = name
        self.sem = sem
        self.count = 0
        self.waited = {}
        self.prog = []


class KB:
    ENGS = ("tensor", "vector", "scalar", "gpsimd", "sync")

    def __init__(self, nc):
        self.nc = nc
        self.es = ExitStack()
        self.engs = {n: Eng(n, self.sem("e_" + n)) for n in self.ENGS}
        self.ntok = 0

    def sem(self, name):
        return self.es.enter_context(self.nc.semaphore(name))

    def sb(self, name, shape, dt):
        return self.es.enter_context(self.nc.sbuf_tensor("s_" + name, list(shape), dt))

    def ps(self, name, shape, dt):
        return self.es.enter_context(self.nc.psum_tensor("p_" + name, list(shape), dt))

    def tok(self, name=None):
        self.ntok += 1
        return Tok(name or f"t{self.ntok}")

    def _deps(self, en, R, W):
        e = self.engs[en]
        deps = []
        for t in R:
            if t.w is not None:
                deps.append(t.w)
        for t in W:
            if t.w is not None:
                deps.append(t.w)
            deps.extend(t.r.values())
        waits = {}
        noself = os.environ.get("MK_NOSELF", "0") == "1"
        for (sem, val, src) in deps:
            if src == "tensor" and en == "tensor":
                continue
            if noself and src == en:
                continue
            key = id(sem)
            if e.waited.get(key, 0) < val:
                e.waited[key] = val
                waits[key] = (sem, val)
        return list(waits.values())

    def emit(self, en, fn, R=(), W=()):
        e = self.engs[en]
        waits = self._deps(en, R, W)
        e.count += 1
        me = (e.sem, e.count, en)
        e.prog.append((waits, fn, (e.sem, 1)))
        for t in R:
            t.r[id(e.sem)] = me
        for t in W:
            t.w = me
            t.r = {}

    def dma(self, q, out, in_, tok, R=(), W=()):
        if out.dtype != in_.dtype:
            q = "gpsimd"
        if tok.q is None:
            tok.q = q
        elif tok.q != q:
            assert out.dtype == in_.dtype or tok.q == "gpsimd", tok.name
            q = tok.q
        e = self.engs[q]
        waits = self._deps(q, R, W)
        if tok.dsem is None:
            tok.dsem = self.sem("d_" + tok.name)
        tok.dcount += 16
        me = (tok.dsem, tok.dcount, "dma")
        e.prog.append((waits, lambda eng: eng.dma_start(out=out, in_=in_), (tok.dsem, 16)))
        for t in R:
            t.r[id(tok.dsem)] = me
        for t in W:
            t.w = me
            t.r = {}

    def collective(self, bin_, bout, tin, tout, name):
        e = self.engs["gpsimd"]
        waits = self._deps("gpsimd", [tin], [tout])
        if getattr(self, "cc_sem", None) is None:
            self.cc_sem = self.sem("cc_all")
            self.cc_count = 0
        sem = self.cc_sem
        self.cc_count += 1
        me = (sem, self.cc_count, "cc")
        groups = [[0, 1, 2, 3], [4, 5, 6, 7]]
        e.prog.append((waits, lambda eng: eng.collective_compute(
            "AllGather", ALU.bypass, replica_groups=groups, ins=[bin_.ap().opt()], outs=[bout.ap().opt()]), (sem, 1)))
        tin.r[id(sem)] = me
        tout.w = me
        tout.r = {}

    def absorb(self, owner, toks):
        for t in toks:
            deps = list(t.r.values()) + ([t.w] if t.w is not None else [])
            for v in deps:
                k = id(v[0])
                if k not in owner.r or owner.r[k][1] < v[1]:
                    owner.r[k] = v

    def wait_all(self, en, toks):
        e = self.engs[en]
        waits = self._deps(en, [], toks)
        e.prog.append((waits, None, None))

    def replay(self):
        nc = self.nc
        with nc.Block() as block:
            for en in self.ENGS:
                prog = self.engs[en].prog

                def body(eng, prog=prog):
                    for waits, fn, inc in prog:
                        for s, v in waits:
                            eng.wait_ge(s, v)
                        if fn is not None:
                            ins = fn(eng)
                            if inc is not None:
                                ins.then_inc(inc[0], inc[1])

                getattr(block, en)(body)

    def mm(self, out, lhsT, rhs, start, stop, R, W):
        self.emit("tensor", lambda e: e.matmul(out, lhsT, rhs, start=start, stop=stop), R, W)

    def tr(self, out, in_, ident, R, W):
        self.emit("tensor", lambda e: e.transpose(out, in_, ident), R, W)

    def act(self, out, in_, func, R, W, bias=None, scale=1.0, accum=None):
        kw = {}
        if bias is not None:
            kw["bias"] = bias
        if accum is not None:
            kw["accum_out"] = accum
        self.emit("scalar", lambda e: e.activation(out, in_, func, scale=scale, **kw), R, W)

    def ts(self, out, in0, s1, s2, op0, op1, R, W, en="vector"):
        if s2 is None:
            self.emit(en, lambda e: e.tensor_scalar(out, in0, s1, None, op0), R, W)
        else:
            self.emit(en, lambda e: e.tensor_scalar(out, in0, s1, s2, op0, op1), R, W)

    def tt(self, out, in0, in1, op, R, W, en="vector"):
        self.emit(en, lambda e: e.tensor_tensor(out, in0, in1, op), R, W)

    def stt(self, out, in0, scalar, in1, op0, op1, R, W, en="vector"):
        self.emit(en, lambda e: e.scalar_tensor_tensor(out, in0, scalar, in1, op0, op1), R, W)

    def scan(self, out, d0, d1, init, op0, op1, R, W):
        self.emit("vector", lambda e: e.tensor_tensor_scan(out, d0, d1, init, op0, op1), R, W)

    def recip(self, out, in_, R, W):
        self.emit("vector", lambda e: e.reciprocal(out, in_), R, W)

    def copy(self, out, in_, R, W, en="vector"):
        self.emit(en, lambda e: e.tensor_copy(out, in_), R, W)

    def memset(self, ap, val, W, en="vector"):
        self.emit(en, lambda e: e.memset(ap, val), (), W)


class Buf:
    def __init__(self, ap, tok):
        self.ap = ap
        self.tok = tok

    def __getitem__(self, idx):
        return self.ap[idx]


class Prog:
    def __init__(self, nc, TS, do_l0=True, do_l1=True, comm=False):
        self.comm = comm
        self.nc = nc
        self.TS = TS
        self.NT = TS // TT
        self.kb = KB(nc)
        self.dram = {}
        self.do_l0 = do_l0
        self.do_l1 = do_l1

    def din(self, name, shape, dt=F32):
        t = self.nc.dram_tensor(name, list(shape), dt, kind="ExternalInput").ap()
        self.dram[name] = t
        return t

    def dout(self, name, shape, dt=F32):
        t = self.nc.dram_tensor(name, list(shape), dt, kind="ExternalOutput").ap()
        self.dram[name] = t
        return t

    def sbuf(self, name, shape, dt):
        return Buf(self.kb.sb(name, shape, dt), self.kb.tok(name))

    def psum(self, name, shape, dt):
        return Buf(self.kb.ps(name, shape, dt), self.kb.tok(name))

    def alloc_common(self):
        kb = self.kb
        self.xs = [Buf(None, kb.tok(f"xs{g}")) for g in range(NSUB)]
        self.xs_t = kb.sb("xs", [128, NSUB, D], F32)
        for g in range(NSUB):
            self.xs[g].ap = self.xs_t[:, g, :]
        self.psb = [self.psum(f"psb{i}", [128, 1024], BF16) for i in range(2)]
        self.psf = [self.psum(f"psf{i}", [128, 512], F32) for i in range(6)]
        self.ident = self.sbuf("ident", [128, 128], BF16)
        self.maskU = self.sbuf("maskU", [128, 64], BF16)
        self.f = [self.sbuf(f"f{i}", [128, 516], F32) for i in range(12)]
        self.b = [self.sbuf(f"b{i}", [128, 512], BF16) for i in range(12)]
        self.sm = [self.sbuf(f"sm{i}", [128, 16], F32) for i in range(14)]
        self.xn = self.sbuf("xn", [128, 1024], BF16)
        self.uT = self.sbuf("uT", [128, 8, 516], BF16)
        self.merged = self.sbuf("merged", [128, 16, 512], BF16)
        self.wout0 = self.sbuf("wout0", [128, 16, 1024], BF16)
        self.wout1 = self.sbuf("wout1", [128, 8, 1024], BF16)
        self.NWS = 4
        self.wslot = [self.sbuf(f"ws{i}", [128, 8, 128], BF16) for i in range(self.NWS)]
        self.wi = 0
        self.cst = self.sbuf("cst", [128, 4], F32)
        self.postbc0 = self.sbuf("postbc0", [128, 1024], F32)
        self.postbc1 = self.sbuf("postbc1", [128, 1024], F32)
        self.dmaq = 0

    def q(self):
        self.dmaq += 1
        return "sync" if self.dmaq % 2 else "gpsimd"

    def load_w(self, src_ap):
        slots = getattr(self, "wslots_active", None) or self.wslot
        s = slots[self.wi % len(slots)]
        self.wi += 1
        self.kb.dma(self.q(), s.ap[:, :, :], src_ap.rearrange("(kc p) n -> p kc n", p=128), s.tok, W=[s.tok])
        return s

    def consts(self):
        kb = self.kb
        d = self.dram
        kb.dma("sync", self.ident.ap[:, :], d["ident"][:, :], self.ident.tok, W=[self.ident.tok])
        kb.dma("sync", self.maskU.ap[:, :], d["maskU"][:, :], self.maskU.tok, W=[self.maskU.tok])
        kb.memset(self.cst.ap[:, 0:1], EPS, [self.cst.tok])
        kb.memset(self.cst.ap[:, 1:2], 1.0, [self.cst.tok])
        kb.memset(self.cst.ap[:, 2:3], GN_EPS, [self.cst.tok])

    def load_x(self, tile, xs=None):
        kb = self.kb
        x = self.dram["x"]
        xs = xs or self.xs
        for sub in range(NSUB):
            g = tile * NSUB + sub
            kb.dma("sync", xs[sub].ap, x[g * 128:(g + 1) * 128, :], xs[sub].tok, W=[xs[sub].tok])

    def store_out(self, tile):
        kb = self.kb
        o = self.dram["out"]
        for sub in range(NSUB):
            g = tile * NSUB + sub
            kb.dma("sync", o[g * 128:(g + 1) * 128, :], self.xs[sub].ap, self.xs[sub].tok, R=[self.xs[sub].tok])

    def norm_T(self, tile, off):
        kb = self.kb
        ss, rms, rstd = self.sm[0], self.sm[1], self.sm[2]
        xn2 = getattr(self, "norm_xn2", None)
        xns = [self.xn, xn2 if xn2 is not None else self.xn]
        for sub in range(NSUB):
            xin = self.xs[sub]
            kb.act(self.xn.ap[:, :], xin.ap, AF.Square, R=[xin.tok], W=[self.xn.tok, ss.tok],
                   accum=ss.ap[:, sub:sub + 1])
        kb.act(rms.ap[:, 0:NSUB], ss.ap[:, 0:NSUB], AF.Sqrt, R=[ss.tok, self.cst.tok], W=[rms.tok],
               scale=1.0 / D, bias=self.cst.ap[:, 0:1])
        kb.recip(rstd.ap[:, 0:NSUB], rms.ap[:, 0:NSUB], R=[rms.tok], W=[rstd.tok])
        for sub in range(NSUB):
            xin = self.xs[sub]
            xn = xns[sub % 2]
            kb.ts(xn.ap[:, 0:1024], xin.ap, rstd.ap[:, sub:sub + 1], None, ALU.mult, None,
                  R=[xin.tok, rstd.tok], W=[xn.tok])
            pb = self.psb[sub % 2]
            for kc in range(8):
                kb.tr(pb.ap[:, kc * 128:(kc + 1) * 128], xn.ap[:, kc * 128:(kc + 1) * 128],
                      self.ident.ap[:, :], R=[xn.tok, self.ident.tok], W=[pb.tok])
            kb.tt(self.uT.ap[:, :, off + sub * 128: off + (sub + 1) * 128],
                  pb.ap[:, :].rearrange("p (k t) -> p k t", t=128),
                  self.gain.ap[:, self.gain_off:self.gain_off + 8].unsqueeze(2).broadcast_to([128, 8, 128]),
                  ALU.mult, R=[pb.tok, self.gain.tok], W=[self.uT.tok])

    def out_proj(self, tile, ncb, wout, postbc):
        kb = self.kb
        ss, rms, rstd = self.sm[3], self.sm[4], self.sm[5]
        tmpo = self.f[0]
        junk = self.b[0]
        for sub in range(NSUB):
            g = sub
            pss = [self.psf[0], self.psf[1]] if sub % 2 == 0 else [self.psf[2], self.psf[3]]
            for dh in range(2):
                for cb in range(ncb):
                    kb.mm(pss[dh].ap[:, :], self.merged.ap[:, cb, sub * 128:(sub + 1) * 128],
                          wout.ap[:, cb, dh * 512:(dh + 1) * 512], cb == 0, cb == ncb - 1,
                          R=[self.merged.tok, wout.tok], W=[pss[dh].tok])
            for dh in range(2):
                kb.act(junk.ap[:, 0:512], pss[dh].ap[:, :], AF.Square, R=[pss[dh].tok],
                       W=[junk.tok, ss.tok], accum=ss.ap[:, dh:dh + 1])
            kb.tt(ss.ap[:, 2:3], ss.ap[:, 0:1], ss.ap[:, 1:2], ALU.add, R=[ss.tok], W=[ss.tok])
            kb.act(rms.ap[:, 0:1], ss.ap[:, 2:3], AF.Sqrt, R=[ss.tok, self.cst.tok], W=[rms.tok],
                   scale=1.0 / D, bias=self.cst.ap[:, 0:1])
            kb.recip(rstd.ap[:, 0:1], rms.ap[:, 0:1], R=[rms.tok], W=[rstd.tok])
            for dh in range(2):
                kb.stt(tmpo.ap[:, 0:512], pss[dh].ap[:, :], rstd.ap[:, 0:1],
                       postbc.ap[:, dh * 512:(dh + 1) * 512], ALU.mult, ALU.mult,
                       R=[pss[dh].tok, rstd.tok, postbc.tok], W=[tmpo.tok])
                kb.tt(self.xs[g].ap[:, dh * 512:(dh + 1) * 512], self.xs[g].ap[:, dh * 512:(dh + 1) * 512],
                      tmpo.ap[:, 0:512], ALU.add, R=[tmpo.tok, self.xs[g].tok], W=[self.xs[g].tok])

    PV0_COLS = 104

    def l0_setup(self):
        kb = self.kb
        d = self.dram
        self.pv0 = self.sbuf("pv0", [128, self.PV0_COLS], F32)
        self.pd0 = self.sbuf("pd0", [128, 64], F32)
        self.WA = self.sbuf("WA", [128, 8, 128], BF16)
        self.WX = self.sbuf("WX", [128, 8, 128], BF16)
        self.st0 = self.sbuf("st0", [128, 1056], F32)
        self.rmask = self.sbuf("rmask", [128, 512], BF16)
        kb.dma("sync", self.pv0.ap[:, :], d["pv0"][:, :], self.pv0.tok, W=[self.pv0.tok])
        kb.dma("gpsimd", self.WA.ap[:, :, :], d["wa_bd"][:, :, :], self.WA.tok, W=[self.WA.tok])
        kb.dma("gpsimd", self.WX.ap[:, :, :], d["wx_bd"][:, :, :], self.WX.tok, W=[self.WX.tok])
        if self.comm:
            kb.memset(self.st0.ap[:, :], 0.0, [self.st0.tok])
        else:
            kb.dma("sync", self.st0.ap[:, :], d["st0_in"][:, :], self.st0.tok, W=[self.st0.tok])
        kb.dma("sync", self.rmask.ap[:, :], d["rmask"][:, :], self.rmask.tok, W=[self.rmask.tok])
        pv, pd = self.pv0, self.pd0
        kb.act(pd.ap[:, 32:40], pv.ap[:, 56:64], AF.Exp, R=[pv.tok], W=[pd.tok], scale=-1.0)
        kb.act(pd.ap[:, 32:40], pd.ap[:, 32:40], AF.Ln, R=[pd.tok, self.cst.tok], W=[pd.tok],
               bias=self.cst.ap[:, 1:2])
        kb.ts(pd.ap[:, 0:8], pd.ap[:, 32:40], -8.0, None, ALU.mult, None, R=[pd.tok], W=[pd.tok])
        kb.ts(pd.ap[:, 8:16], pd.ap[:, 32:40], -16.0, None, ALU.mult, None, R=[pd.tok], W=[pd.tok])
        kb.act(pd.ap[:, 32:56], pv.ap[:, 64:88], AF.Exp, R=[pv.tok], W=[pd.tok])
        kb.tt(pd.ap[:, 56:64], pd.ap[:, 32:40], pd.ap[:, 40:48], ALU.add, R=[pd.tok], W=[pd.tok])
        kb.tt(pd.ap[:, 56:64], pd.ap[:, 56:64], pd.ap[:, 48:56], ALU.add, R=[pd.tok], W=[pd.tok])
        kb.recip(pd.ap[:, 56:64], pd.ap[:, 56:64], R=[pd.tok], W=[pd.tok])
        kb.tt(pd.ap[:, 16:24], pd.ap[:, 40:48], pd.ap[:, 56:64], ALU.mult, R=[pd.tok], W=[pd.tok])
        kb.ts(pd.ap[:, 24:32], pd.ap[:, 16:24], -1.0, 1.0, ALU.mult, ALU.add, R=[pd.tok], W=[pd.tok])

    def load_wout0(self, cb):
        w = self.dram["w_out0"].rearrange("(cb p) n -> p cb n", p=128)
        self.kb.dma("gpsimd", self.wout0.ap[:, cb, :], w[:, cb, :], self.wout0.tok, W=[self.wout0.tok])

    def load_wout1(self, cb):
        w = self.dram["w_out1"].rearrange("(cb p) n -> p cb n", p=128)
        self.kb.dma("gpsimd", self.wout1.ap[:, cb, :], w[:, cb, :], self.wout1.tok, W=[self.wout1.tok])

    def l0_begin(self):
        kb = self.kb
        d = self.dram
        if not self.comm:
            for cb in range(16):
                self.load_wout0(cb)
        kb.dma("sync", self.postbc0.ap[:, :], d["post0_bc"][:, :], self.postbc0.tok, W=[self.postbc0.tok])

    def l0_rg_proj(self, cb, p1=False, banks=None):
        kb = self.kb
        d = self.dram
        px, pg = banks if banks is not None else (self.psf[2], self.psf[3])
        GW = 256 if self.comm else 1024
        wx = self.load_w(d["w_in0"][:, cb * 128:(cb + 1) * 128])
        for kc in range(8):
            kb.mm(px.ap[:, :], wx.ap[:, kc, :], self.uT.ap[:, kc, 0:512], kc == 0, kc == 7,
                  R=[wx.tok, self.uT.tok], W=[px.tok])
        if not p1:
            wg = self.load_w(d["w_in0"][:, GW + cb * 128:GW + (cb + 1) * 128])
            for kc in range(8):
                kb.mm(pg.ap[:, :], wg.ap[:, kc, :], self.uT.ap[:, kc, 0:512], kc == 0, kc == 7,
                      R=[wg.tok, self.uT.tok], W=[pg.tok])

    def l0_rg(self, cb, p1=False, after_evac=None):
        kb = self.kb
        d = self.dram
        pv, pd = self.pv0, self.pd0
        f, b = self.f, self.b
        xr, xc, r, ig, a, a2, gi, bb, h, sg = f[1], f[2], f[3], f[4], f[5], f[6], f[7], f[8], f[9], f[10]
        xcb = b[0]
        px, pg, pa, pi = self.psf[2], self.psf[3], self.psf[4], self.psf[5]
        st = self.st0
        kb.copy(xr.ap[:, 0:3], st.ap[:, cb * 3:cb * 3 + 3], R=[st.tok], W=[xr.tok])
        kb.act(xr.ap[:, 3:515], px.ap[:, :], AF.Copy, R=[px.tok], W=[xr.tok])
        if not p1:
            kb.act(sg.ap[:, 0:512], pg.ap[:, :], AF.Silu, R=[pg.tok], W=[sg.tok])
        if after_evac is not None:
            after_evac()
        kb.copy(st.ap[:, cb * 3:cb * 3 + 3], xr.ap[:, 512:515], R=[xr.tok], W=[st.tok])
        kb.ts(xc.ap[:, 0:512], xr.ap[:, 3:515], pv.ap[:, 24 + cb:25 + cb], pv.ap[:, 32 + cb:33 + cb],
              ALU.mult, ALU.add, R=[xr.tok, pv.tok], W=[xc.tok])
        for k in (2, 1, 0):
            kb.stt(xc.ap[:, 0:512], xr.ap[:, k:k + 512], pv.ap[:, k * 8 + cb:k * 8 + cb + 1], xc.ap[:, 0:512],
                   ALU.mult, ALU.add, R=[xr.tok, pv.tok, xc.tok], W=[xc.tok])
        kb.act(xcb.ap[:, :], xc.ap[:, 0:512], AF.Copy, R=[xc.tok], W=[xcb.tok])
        kb.mm(pa.ap[:, :], self.WA.ap[:, cb, :], xcb.ap[:, :], True, True, R=[self.WA.tok, xcb.tok], W=[pa.tok])
        kb.mm(pi.ap[:, :], self.WX.ap[:, cb, :], xcb.ap[:, :], True, True, R=[self.WX.tok, xcb.tok], W=[pi.tok])
        if p1:
            rs_ = self.sm[13]
            kb.act(r.ap[:, 0:512], pa.ap[:, :], AF.Sigmoid, R=[pa.tok, pv.tok], W=[r.tok, rs_.tok],
                   bias=pv.ap[:, 40 + cb:41 + cb], accum=rs_.ap[:, 0:1])
            kb.tt(self.dec0.ap[:, cb:cb + 1], self.dec0.ap[:, cb:cb + 1], rs_.ap[:, 0:1], ALU.add,
                  R=[rs_.tok, self.dec0.tok], W=[self.dec0.tok])
        else:
            kb.act(r.ap[:, 0:512], pa.ap[:, :], AF.Sigmoid, R=[pa.tok, pv.tok], W=[r.tok], bias=pv.ap[:, 40 + cb:41 + cb])
        kb.act(ig.ap[:, 0:512], pi.ap[:, :], AF.Sigmoid, R=[pi.tok, pv.tok], W=[ig.tok],
               bias=pv.ap[:, 48 + cb:49 + cb])
        kb.act(a.ap[:, 0:512], r.ap[:, 0:512], AF.Exp, R=[r.tok, pd.tok], W=[a.tok], scale=pd.ap[:, cb:cb + 1])
        kb.act(a2.ap[:, 0:512], r.ap[:, 0:512], AF.Exp, R=[r.tok, pd.tok], W=[a2.tok],
               scale=pd.ap[:, 8 + cb:9 + cb])
        kb.act(a2.ap[:, 0:512], a2.ap[:, 0:512], AF.Relu, R=[a2.tok, self.cst.tok], W=[a2.tok], scale=-1.0,
               bias=self.cst.ap[:, 1:2])
        kb.act(a2.ap[:, 0:512], a2.ap[:, 0:512], AF.Sqrt, R=[a2.tok], W=[a2.tok])
        kb.tt(gi.ap[:, 0:512], ig.ap[:, 0:512], xc.ap[:, 0:512], ALU.mult, R=[ig.tok, xc.tok], W=[gi.tok])
        kb.tt(bb.ap[:, 0:512], a2.ap[:, 0:512], gi.ap[:, 0:512], ALU.mult, R=[a2.tok, gi.tok], W=[bb.tok])
        kb.scan(h.ap[:, 0:512], a.ap[:, 0:512], bb.ap[:, 0:512], st.ap[:, 24 + cb:25 + cb], ALU.mult, ALU.add,
                R=[a.tok, bb.tok, st.tok], W=[h.tok])
        kb.copy(st.ap[:, 24 + cb:25 + cb], h.ap[:, 511:512], R=[h.tok], W=[st.tok])
        if p1:
            return
        kb.tt(self.merged.ap[:, cb, :], h.ap[:, 0:512], sg.ap[:, 0:512], ALU.mult, R=[h.tok, sg.tok],
              W=[self.merged.tok])

    def l0_rg2(self):
        kb = self.kb
        pv, pd = self.pv0, self.pd0
        st = self.st0
        f = self.f
        setA = dict(t=[f[i] for i in range(1, 11)], xcb=self.b[0],
                    ps=[self.psf[2], self.psf[3], self.psf[4], self.psf[5]])
        psb0 = Buf(self.psb[0].ap[:, :].bitcast(F32), self.psb[0].tok)
        psb1 = Buf(self.psb[1].ap[:, :].bitcast(F32), self.psb[1].tok)
        setB = dict(t=self.rgB_f, xcb=self.rgB_xcb, ps=[self.psf[0], self.psf[1], psb0, psb1])
        U = [setA, setB]
        for u in range(2):
            self.l0_rg_proj(u, False, (U[u]["ps"][0], U[u]["ps"][1]))

        def T(u, i):
            return U[u]["t"][i]
        for u in range(2):
            px, pg, pa, pi = U[u]["ps"]
            kb.copy(T(u, 0).ap[:, 0:3], st.ap[:, u * 3:u * 3 + 3], R=[st.tok], W=[T(u, 0).tok])
            kb.act(T(u, 0).ap[:, 3:515], px.ap[:, 0:512], AF.Copy, R=[px.tok], W=[T(u, 0).tok])
        for u in range(2):
            kb.copy(st.ap[:, u * 3:u * 3 + 3], T(u, 0).ap[:, 512:515], R=[T(u, 0).tok], W=[st.tok])
            kb.ts(T(u, 1).ap[:, 0:512], T(u, 0).ap[:, 3:515], pv.ap[:, 24 + u:25 + u], pv.ap[:, 32 + u:33 + u],
                  ALU.mult, ALU.add, R=[T(u, 0).tok, pv.tok], W=[T(u, 1).tok])
        for k in (2, 1, 0):
            for u in range(2):
                kb.stt(T(u, 1).ap[:, 0:512], T(u, 0).ap[:, k:k + 512], pv.ap[:, k * 8 + u:k * 8 + u + 1],
                       T(u, 1).ap[:, 0:512], ALU.mult, ALU.add, R=[T(u, 0).tok, pv.tok, T(u, 1).tok], W=[T(u, 1).tok])
        for u in range(2):
            xcb = U[u]["xcb"]
            kb.act(xcb.ap[:, 0:512], T(u, 1).ap[:, 0:512], AF.Copy, R=[T(u, 1).tok], W=[xcb.tok])
        for u in range(2):
            px, pg, pa, pi = U[u]["ps"]
            xcb = U[u]["xcb"]
            kb.mm(pa.ap[:, 0:512], self.WA.ap[:, u, :], xcb.ap[:, 0:512], True, True, R=[self.WA.tok, xcb.tok], W=[pa.tok])
            kb.mm(pi.ap[:, 0:512], self.WX.ap[:, u, :], xcb.ap[:, 0:512], True, True, R=[self.WX.tok, xcb.tok], W=[pi.tok])
        for u in range(2):
            px, pg, pa, pi = U[u]["ps"]
            kb.act(T(u, 2).ap[:, 0:512], pa.ap[:, 0:512], AF.Sigmoid, R=[pa.tok, pv.tok], W=[T(u, 2).tok],
                   bias=pv.ap[:, 40 + u:41 + u])
            kb.act(T(u, 3).ap[:, 0:512], pi.ap[:, 0:512], AF.Sigmoid, R=[pi.tok, pv.tok], W=[T(u, 3).tok],
                   bias=pv.ap[:, 48 + u:49 + u])
            kb.act(T(u, 9).ap[:, 0:512], pg.ap[:, 0:512], AF.Sigmoid, R=[pg.tok], W=[T(u, 9).tok])
            kb.tt(T(u, 9).ap[:, 0:512], T(u, 9).ap[:, 0:512], pg.ap[:, 0:512], ALU.mult, R=[T(u, 9).tok, pg.tok],
                  W=[T(u, 9).tok])
        for u in range(2):
            kb.act(T(u, 4).ap[:, 0:512], T(u, 2).ap[:, 0:512], AF.Exp, R=[T(u, 2).tok, pd.tok], W=[T(u, 4).tok],
                   scale=pd.ap[:, u:u + 1])
            kb.act(T(u, 5).ap[:, 0:512], T(u, 2).ap[:, 0:512], AF.Exp, R=[T(u, 2).tok, pd.tok], W=[T(u, 5).tok],
                   scale=pd.ap[:, 8 + u:9 + u])
        for u in range(2):
            kb.tt(T(u, 6).ap[:, 0:512], T(u, 3).ap[:, 0:512], T(u, 1).ap[:, 0:512], ALU.mult,
                  R=[T(u, 3).tok, T(u, 1).tok], W=[T(u, 6).tok])
            kb.act(T(u, 5).ap[:, 0:512], T(u, 5).ap[:, 0:512], AF.Relu, R=[T(u, 5).tok, self.cst.tok], W=[T(u, 5).tok],
                   scale=-1.0, bias=self.cst.ap[:, 1:2])
        for u in range(2):
            kb.act(T(u, 5).ap[:, 0:512], T(u, 5).ap[:, 0:512], AF.Sqrt, R=[T(u, 5).tok], W=[T(u, 5).tok])
        for u in range(2):
            kb.tt(T(u, 7).ap[:, 0:512], T(u, 5).ap[:, 0:512], T(u, 6).ap[:, 0:512], ALU.mult,
                  R=[T(u, 5).tok, T(u, 6).tok], W=[T(u, 7).tok])
            kb.scan(T(u, 8).ap[:, 0:512], T(u, 4).ap[:, 0:512], T(u, 7).ap[:, 0:512], st.ap[:, 24 + u:25 + u], ALU.mult,
                    ALU.add, R=[T(u, 4).tok, T(u, 7).tok, st.tok], W=[T(u, 8).tok])
            kb.copy(st.ap[:, 24 + u:25 + u], T(u, 8).ap[:, 511:512], R=[T(u, 8).tok], W=[st.tok])
        for u in range(2):
            kb.tt(self.merged.ap[:, u, :], T(u, 8).ap[:, 0:512], T(u, 9).ap[:, 0:512], ALU.mult,
                  R=[T(u, 8).tok, T(u, 9).tok], W=[self.merged.tok])

    def l0_hg_proj(self, hh, p1=False):
        kb = self.kb
        d = self.dram
        pq, pf, pg, pvv = self.psf[2], self.psf[3], self.psf[4], self.psf[5]
        GW = 256 if self.comm else 1024
        wf = self.load_w(d["w_in0"][:, 3 * GW + hh * 128:3 * GW + (hh + 1) * 128])
        wi = self.load_w(d["w_in0"][:, 4 * GW + hh * 128:4 * GW + (hh + 1) * 128])
        plist = [(wf, pf)]
        if not p1:
            wq = self.load_w(d["w_in0"][:, 2 * GW + hh * 128:2 * GW + (hh + 1) * 128])
            wg = self.load_w(d["w_in0"][:, 5 * GW + hh * 128:5 * GW + (hh + 1) * 128])
            plist += [(wq, pq), (wg, pg)]
        for (w_, p_) in plist:
            for kc in range(8):
                kb.mm(p_.ap[:, :], w_.ap[:, kc, :], self.uT.ap[:, kc, 0:512], kc == 0, kc == 7,
                      R=[w_.tok, self.uT.tok], W=[p_.tok])
        for sub in range(NSUB):
            for kc in range(8):
                kb.mm(pvv.ap[:, sub * 128:(sub + 1) * 128], self.uT.ap[:, kc, sub * 128:(sub + 1) * 128],
                      wi.ap[:, kc, :], kc == 0, kc == 7, R=[wi.tok, self.uT.tok], W=[pvv.tok])

    def l0_hg(self, hh, p1=False, after_evac=None):
        kb = self.kb
        d = self.dram
        pv, pd = self.pv0, self.pd0
        f, b, sm = self.f, self.b, self.sm
        q, sg, sf, lf, kv, g, gm, Ep, Em, tmp, Sp = f[1], f[2], f[3], f[4], f[5], f[6], f[7], f[8], f[9], f[10], f[11]
        qt, kt, vsb, ktT, scT, Spb, on = b[1], b[2], b[3], b[4], b[5], b[6], b[7]
        emid, ech, ss2, rms2, rstd2 = sm[6], sm[7], sm[8], sm[9], sm[10]
        pq, pf, pg, pvv, po, pm = self.psf[2], self.psf[3], self.psf[4], self.psf[5], self.psf[0], self.psf[1]
        st = self.st0
        S = st.ap[:, 32 + hh * 128:32 + (hh + 1) * 128]
        if not p1:
            kb.act(q.ap[:, 0:512], pq.ap[:, :], AF.Silu, R=[pq.tok], W=[q.tok])
            kb.act(sg.ap[:, 0:512], pg.ap[:, :], AF.Silu, R=[pg.tok], W=[sg.tok])
        kb.act(sf.ap[:, 0:512], pf.ap[:, :], AF.Sigmoid, R=[pf.tok], W=[sf.tok])
        kb.act(vsb.ap[:, :], pvv.ap[:, :], AF.Copy, R=[pvv.tok], W=[vsb.tok])
        if after_evac is not None:
            after_evac()
        kb.ts(sf.ap[:, 0:512], sf.ap[:, 0:512], pd.ap[:, 24 + hh:25 + hh], pd.ap[:, 16 + hh:17 + hh], ALU.mult, ALU.add,
              R=[sf.tok, pd.tok], W=[sf.tok])
        if p1:
            ls_ = self.sm[13]
            kb.act(lf.ap[:, 0:512], sf.ap[:, 0:512], AF.Ln, R=[sf.tok], W=[lf.tok, ls_.tok], accum=ls_.ap[:, 0:1])
            kb.tt(self.dec0.ap[:, 8 + hh:9 + hh], self.dec0.ap[:, 8 + hh:9 + hh], ls_.ap[:, 0:1], ALU.add,
                  R=[ls_.tok, self.dec0.tok], W=[self.dec0.tok])
        else:
            kb.act(lf.ap[:, 0:512], sf.ap[:, 0:512], AF.Ln, R=[sf.tok], W=[lf.tok])
        kb.ts(kv.ap[:, 0:512], sf.ap[:, 0:512], -1.0, 1.0, ALU.mult, ALU.add, R=[sf.tok], W=[kv.tok])
        kb.scan(g.ap[:, 0:512], self.rmask.ap[:, :], lf.ap[:, 0:512], 0.0, ALU.mult, ALU.add,
                R=[self.rmask.tok, lf.tok], W=[g.tok])
        g3 = g.ap[:, 0:512].rearrange("p (c t) -> p c t", t=64)
        gm3 = gm.ap[:, 0:512].rearrange("p (c t) -> p c t", t=64)
        Ep3 = Ep.ap[:, 0:512].rearrange("p (c t) -> p c t", t=64)
        kb.tt(gm3, g3, g3[:, :, 31:32].broadcast_to([128, NCH, 64]), ALU.subtract, R=[g.tok], W=[gm.tok])
        kb.act(Ep.ap[:, 0:512], gm.ap[:, 0:512], AF.Exp, R=[gm.tok], W=[Ep.tok])
        kb.act(Em.ap[:, 0:512], gm.ap[:, 0:512], AF.Exp, R=[gm.tok], W=[Em.tok], scale=-1.0)
        kb.act(emid.ap[:, 0:NCH].unsqueeze(2), g3[:, :, 31:32], AF.Exp, R=[g.tok], W=[emid.tok])
        if not p1:
            kb.tt(qt.ap[:, :], q.ap[:, 0:512], Ep.ap[:, 0:512], ALU.mult, R=[q.tok, Ep.tok], W=[qt.tok])
        kb.tt(kt.ap[:, :], kv.ap[:, 0:512], Em.ap[:, 0:512], ALU.mult, R=[kv.tok, Em.tok], W=[kt.tok])
        kb.tt(ech.ap[:, 0:NCH - 1].unsqueeze(2), Ep3[:, 0:NCH - 1, 63:64], emid.ap[:, 1:NCH].unsqueeze(2), ALU.mult,
              R=[Ep.tok, emid.tok], W=[ech.tok])
        kb.copy(ech.ap[:, NCH - 1:NCH], Ep.ap[:, 511:512], R=[Ep.tok], W=[ech.tok])
        kte = b[8]
        kb.tt(kte.ap[:, :].rearrange("p (c t) -> p c t", t=64), kt.ap[:, :].rearrange("p (c t) -> p c t", t=64),
              ech.ap[:, 0:NCH].unsqueeze(2).broadcast_to([128, NCH, 64]), ALU.mult, R=[kt.tok, ech.tok], W=[kte.tok])
        pbt = self.psb[0]
        for sub in range(NSUB):
            kb.tr(pbt.ap[:, sub * 128:(sub + 1) * 128], kte.ap[:, sub * 128:(sub + 1) * 128], self.ident.ap[:, :],
                  R=[kte.tok, self.ident.tok], W=[pbt.tok])
        kb.copy(ktT.ap[:, :], pbt.ap[:, 0:512], R=[pbt.tok], W=[ktT.tok])
        dbank = [self.psb[0].ap[:, :].bitcast(F32), self.psb[1].ap[:, :].bitcast(F32)]
        for c in range(NCH):
            sub, half = divmod(c, 2)
            p0 = 64 * half
            kb.mm(dbank[half][:, sub * 128:(sub + 1) * 128], ktT.ap[p0:p0 + 64, sub * 128:(sub + 1) * 128],
                  vsb.ap[p0:p0 + 64, sub * 128:(sub + 1) * 128], True, True, R=[ktT.tok, vsb.tok], W=[self.psb[half].tok])
        SpF = [f[10], f[10], f[10], f[10], f[11], f[11], f[11], f[11]]
        SpA = [b[9], b[9], b[9], b[9], b[10], b[10], b[10], b[10]]
        kb.act(SpF[0].ap[:, 0:128], S, AF.Copy, R=[st.tok, emid.tok], W=[SpF[0].tok], scale=emid.ap[:, 0:1])
        for c in range(NCH):
            sub, half = divmod(c, 2)
            dl = dbank[half][:, sub * 128:(sub + 1) * 128]
            dt_ = self.psb[half].tok
            ec = ech.ap[:, c:c + 1]
            cur = SpF[c].ap[:, (c % 4) * 128:(c % 4 + 1) * 128]
            if c < NCH - 1:
                n_ = c + 1
                kb.stt(SpF[n_].ap[:, (n_ % 4) * 128:(n_ % 4 + 1) * 128], cur, ec, dl, ALU.mult, ALU.add,
                       R=[SpF[c].tok, ech.tok, dt_], W=[SpF[n_].tok])
            else:
                kb.stt(S, cur, ec, dl, ALU.mult, ALU.add, R=[SpF[c].tok, ech.tok, dt_], W=[st.tok])
            if not p1 and c % 4 == 3:
                kb.act(SpA[c].ap[:, :], SpF[c].ap[:, 0:512], AF.Copy, R=[SpF[c].tok], W=[SpA[c].tok])
        if not p1:
            for sub in range(NSUB):
                for half in range(2):
                    c = sub * 2 + half
                    p0 = 64 * half
                    kb.mm(pm.ap[p0:p0 + 64, 0:64], kt.ap[:, c * 64:(c + 1) * 64], qt.ap[:, c * 64:(c + 1) * 64], True, True,
                          R=[kt.tok, qt.tok], W=[pm.tok])
                kb.tt(scT.ap[:, 0:64], pm.ap[:, 0:64], self.maskU.ap[:, :], ALU.mult, R=[pm.tok, self.maskU.tok],
                      W=[scT.tok])
                for half in range(2):
                    c = sub * 2 + half
                    p0 = 64 * half
                    kb.mm(po.ap[p0:p0 + 64, sub * 128:(sub + 1) * 128], scT.ap[p0:p0 + 64, 0:64],
                          vsb.ap[p0:p0 + 64, sub * 128:(sub + 1) * 128], True, False, R=[scT.tok, vsb.tok], W=[po.tok])
                    kb.mm(po.ap[p0:p0 + 64, sub * 128:(sub + 1) * 128], qt.ap[:, c * 64:(c + 1) * 64],
                          SpA[c].ap[:, (c % 4) * 128:(c % 4 + 1) * 128], False, True, R=[qt.tok, SpA[c].tok], W=[po.tok])
        if p1:
            return
        for sub in range(NSUB):
            kb.act(self.b[0].ap[:, 0:128], po.ap[:, sub * 128:(sub + 1) * 128], AF.Square, R=[po.tok],
                   W=[self.b[0].tok, ss2.tok], accum=ss2.ap[:, sub:sub + 1])
        kb.act(rms2.ap[:, 0:4], ss2.ap[:, 0:4], AF.Sqrt, R=[ss2.tok, self.cst.tok], W=[rms2.tok], scale=1.0 / 128,
               bias=self.cst.ap[:, 0:1])
        kb.recip(rstd2.ap[:, 0:4], rms2.ap[:, 0:4], R=[rms2.tok], W=[rstd2.tok])
        pbo = self.psb[1]
        for sub in range(NSUB):
            kb.ts(on.ap[:, sub * 128:(sub + 1) * 128], po.ap[:, sub * 128:(sub + 1) * 128], rstd2.ap[:, sub:sub + 1], None,
                  ALU.mult, None, R=[po.tok, rstd2.tok], W=[on.tok])
        for sub in range(NSUB):
            kb.tr(pbo.ap[:, sub * 128:(sub + 1) * 128], on.ap[:, sub * 128:(sub + 1) * 128], self.ident.ap[:, :],
                  R=[on.tok, self.ident.tok], W=[pbo.tok])
        kb.stt(self.merged.ap[:, 8 + hh, :], pbo.ap[:, 0:512], pv.ap[:, 96:97], sg.ap[:, 0:512], ALU.mult, ALU.mult,
               R=[pbo.tok, pv.tok, sg.tok], W=[self.merged.tok])

    def l0_hg2(self, after_proj=None):
        kb = self.kb
        d = self.dram
        pv, pd = self.pv0, self.pd0
        f, b, sm = self.f, self.b, self.sm
        st = self.st0
        psf = self.psf
        A = dict(q=f[1], sg=f[2], sf=f[3], lf=f[4], kv=f[5], g=f[6], gm=f[7], Ep=f[8], Em=f[9], SpF=[f[10], f[11]],
                 qt=b[1], kt=b[2], vsb=b[3], ktT=b[4], scT=b[5], on=b[7], kte=b[8], SpA=[b[9], b[10]], junk=b[0],
                 emid=sm[6], ech=sm[7], ss2=sm[8], rms2=sm[9], rstd2=sm[10],
                 pbt=self.psb[0], dl=[psf[2], psf[3]], po=psf[0])
        Bf, Bb, Bs = self.hgB_f, self.hgB_b, self.hgB_sm
        B = dict(q=Bf[0], sg=Bf[1], sf=Bf[2], lf=Bf[3], kv=Bf[4], g=Bf[5], gm=Bf[6], Ep=Bf[7], Em=Bf[8], SpF=[Bf[9], Bf[10]],
                 qt=Bb[0], kt=Bb[1], vsb=Bb[2], ktT=Bb[3], scT=Bb[4], on=Bb[5], kte=Bb[6], SpA=[Bb[7], Bb[8]], junk=Bb[6],
                 emid=Bs[0], ech=Bs[1], ss2=Bs[2], rms2=Bs[3], rstd2=Bs[4],
                 pbt=self.psb[1], dl=[psf[4], psf[5]], po=psf[1])
        U = [A, B]
        pq, pf, pg, pvv = psf[2], psf[3], psf[4], psf[5]
        for u in range(2):
            X = U[u]
            self.l0_hg_proj(u)
            kb.act(X["q"].ap[:, 0:512], pq.ap[:, :], AF.Sigmoid, R=[pq.tok], W=[X["q"].tok])
            kb.act(X["sg"].ap[:, 0:512], pg.ap[:, :], AF.Sigmoid, R=[pg.tok], W=[X["sg"].tok])
            kb.act(X["sf"].ap[:, 0:512], pf.ap[:, :], AF.Sigmoid, R=[pf.tok], W=[X["sf"].tok])
            kb.act(X["vsb"].ap[:, 0:512], pvv.ap[:, :], AF.Copy, R=[pvv.tok], W=[X["vsb"].tok])
            kb.tt(X["q"].ap[:, 0:512], X["q"].ap[:, 0:512], pq.ap[:, :], ALU.mult, R=[X["q"].tok, pq.tok], W=[X["q"].tok])
            kb.tt(X["sg"].ap[:, 0:512], X["sg"].ap[:, 0:512], pg.ap[:, :], ALU.mult, R=[X["sg"].tok, pg.tok],
                  W=[X["sg"].tok])
        if after_proj is not None:
            after_proj()

        def each():
            return [(u, U[u]) for u in range(2)]
        for u, X in each():
            kb.ts(X["sf"].ap[:, 0:512], X["sf"].ap[:, 0:512], pd.ap[:, 24 + u:25 + u], pd.ap[:, 16 + u:17 + u], ALU.mult,
                  ALU.add, R=[X["sf"].tok, pd.tok], W=[X["sf"].tok])
        for u, X in each():
            kb.act(X["lf"].ap[:, 0:512], X["sf"].ap[:, 0:512], AF.Ln, R=[X["sf"].tok], W=[X["lf"].tok])
        for u, X in each():
            kb.ts(X["kv"].ap[:, 0:512], X["sf"].ap[:, 0:512], -1.0, 1.0, ALU.mult, ALU.add, R=[X["sf"].tok], W=[X["kv"].tok])
            kb.scan(X["g"].ap[:, 0:512], self.rmask.ap[:, :], X["lf"].ap[:, 0:512], 0.0, ALU.mult, ALU.add,
                    R=[self.rmask.tok, X["lf"].tok], W=[X["g"].tok])
        for u, X in each():
            g3 = X["g"].ap[:, 0:512].rearrange("p (c t) -> p c t", t=64)
            gm3 = X["gm"].ap[:, 0:512].rearrange("p (c t) -> p c t", t=64)
            kb.tt(gm3, g3, g3[:, :, 31:32].broadcast_to([128, NCH, 64]), ALU.subtract, R=[X["g"].tok], W=[X["gm"].tok])
        for u, X in each():
            g3 = X["g"].ap[:, 0:512].rearrange("p (c t) -> p c t", t=64)
            kb.act(X["Ep"].ap[:, 0:512], X["gm"].ap[:, 0:512], AF.Exp, R=[X["gm"].tok], W=[X["Ep"].tok])
            kb.act(X["Em"].ap[:, 0:512], X["gm"].ap[:, 0:512], AF.Exp, R=[X["gm"].tok], W=[X["Em"].tok], scale=-1.0)
            kb.act(X["emid"].ap[:, 0:NCH].unsqueeze(2), g3[:, :, 31:32], AF.Exp, R=[X["g"].tok], W=[X["emid"].tok])
        for u, X in each():
            Ep3 = X["Ep"].ap[:, 0:512].rearrange("p (c t) -> p c t", t=64)
            kb.tt(X["qt"].ap[:, 0:512], X["q"].ap[:, 0:512], X["Ep"].ap[:, 0:512], ALU.mult, R=[X["q"].tok, X["Ep"].tok],
                  W=[X["qt"].tok])
            kb.tt(X["kt"].ap[:, 0:512], X["kv"].ap[:, 0:512], X["Em"].ap[:, 0:512], ALU.mult, R=[X["kv"].tok, X["Em"].tok],
                  W=[X["kt"].tok])
            kb.tt(X["ech"].ap[:, 0:NCH - 1].unsqueeze(2), Ep3[:, 0:NCH - 1, 63:64], X["emid"].ap[:, 1:NCH].unsqueeze(2),
                  ALU.mult, R=[X["Ep"].tok, X["emid"].tok], W=[X["ech"].tok])
            kb.copy(X["ech"].ap[:, NCH - 1:NCH], X["Ep"].ap[:, 511:512], R=[X["Ep"].tok], W=[X["ech"].tok])
            kb.tt(X["kte"].ap[:, 0:512].rearrange("p (c t) -> p c t", t=64),
                  X["kt"].ap[:, 0:512].rearrange("p (c t) -> p c t", t=64),
                  X["ech"].ap[:, 0:NCH].unsqueeze(2).broadcast_to([128, NCH, 64]), ALU.mult, R=[X["kt"].tok, X["ech"].tok],
                  W=[X["kte"].tok])
        for u, X in each():
            pbt = X["pbt"]
            for sub in range(NSUB):
                kb.tr(pbt.ap[:, sub * 128:(sub + 1) * 128], X["kte"].ap[:, sub * 128:(sub + 1) * 128], self.ident.ap[:, :],
                      R=[X["kte"].tok, self.ident.tok], W=[pbt.tok])
        for u, X in each():
            kb.copy(X["ktT"].ap[:, 0:512], X["pbt"].ap[:, 0:512], R=[X["pbt"].tok], W=[X["ktT"].tok])
        for u, X in each():
            for c in range(NCH):
                sub, half = divmod(c, 2)
                p0 = 64 * half
                bk = X["dl"][half]
                kb.mm(bk.ap[:, sub * 128:(sub + 1) * 128], X["ktT"].ap[p0:p0 + 64, sub * 128:(sub + 1) * 128],
                      X["vsb"].ap[p0:p0 + 64, sub * 128:(sub + 1) * 128], True, True, R=[X["ktT"].tok, X["vsb"].tok],
                      W=[bk.tok])
        for u, X in each():
            S = st.ap[:, 32 + u * 128:32 + (u + 1) * 128]
            kb.act(X["SpF"][0].ap[:, 0:128], S, AF.Copy, R=[st.tok, X["emid"].tok], W=[X["SpF"][0].tok],
                   scale=X["emid"].ap[:, 0:1])
        for c in range(NCH):
            sub, half = divmod(c, 2)
            for u, X in each():
                S = st.ap[:, 32 + u * 128:32 + (u + 1) * 128]
                bk = X["dl"][half]
                dl = bk.ap[:, sub * 128:(sub + 1) * 128]
                ec = X["ech"].ap[:, c:c + 1]
                cf = X["SpF"][c // 4]
                cur = cf.ap[:, (c % 4) * 128:(c % 4 + 1) * 128]
                if c < NCH - 1:
                    n_ = c + 1
                    nf = X["SpF"][n_ // 4]
                    kb.stt(nf.ap[:, (n_ % 4) * 128:(n_ % 4 + 1) * 128], cur, ec, dl, ALU.mult, ALU.add,
                           R=[cf.tok, X["ech"].tok, bk.tok], W=[nf.tok])
                else:
                    kb.stt(S, cur, ec, dl, ALU.mult, ALU.add, R=[cf.tok, X["ech"].tok, bk.tok], W=[st.tok])
                if c % 4 == 3:
                    kb.act(X["SpA"][c // 4].ap[:, 0:512], cf.ap[:, 0:512], AF.Copy, R=[cf.tok], W=[X["SpA"][c // 4].tok])
        for sub in range(NSUB):
            for u, X in each():
                pmf = X["pbt"].ap[:, :].bitcast(F32)
                for half in range(2):
                    c = sub * 2 + half
                    p0 = 64 * half
                    kb.mm(pmf[p0:p0 + 64, 0:64], X["kt"].ap[:, c * 64:(c + 1) * 64], X["qt"].ap[:, c * 64:(c + 1) * 64], True,
                          True, R=[X["kt"].tok, X["qt"].tok], W=[X["pbt"].tok])
            for u, X in each():
                pmf = X["pbt"].ap[:, :].bitcast(F32)
                kb.tt(X["scT"].ap[:, 0:64], pmf[:, 0:64], self.maskU.ap[:, :], ALU.mult, R=[X["pbt"].tok, self.maskU.tok],
                      W=[X["scT"].tok])
            for u, X in each():
                po = X["po"]
                for half in range(2):
                    c = sub * 2 + half
                    p0 = 64 * half
                    kb.mm(po.ap[p0:p0 + 64, sub * 128:(sub + 1) * 128], X["scT"].ap[p0:p0 + 64, 0:64],
                          X["vsb"].ap[p0:p0 + 64, sub * 128:(sub + 1) * 128], True, False, R=[X["scT"].tok, X["vsb"].tok],
                          W=[po.tok])
                    kb.mm(po.ap[p0:p0 + 64, sub * 128:(sub + 1) * 128], X["qt"].ap[:, c * 64:(c + 1) * 64],
                          X["SpA"][c // 4].ap[:, (c % 4) * 128:(c % 4 + 1) * 128], False, True,
                          R=[X["qt"].tok, X["SpA"][c // 4].tok], W=[po.tok])
        for u, X in each():
            po = X["po"]
            for sub in range(NSUB):
                kb.act(X["junk"].ap[:, 0:128], po.ap[:, sub * 128:(sub + 1) * 128], AF.Square, R=[po.tok],
                       W=[X["junk"].tok, X["ss2"].tok], accum=X["ss2"].ap[:, sub:sub + 1])
        for u, X in each():
            kb.act(X["rms2"].ap[:, 0:4], X["ss2"].ap[:, 0:4], AF.Sqrt, R=[X["ss2"].tok, self.cst.tok], W=[X["rms2"].tok],
                   scale=1.0 / 128, bias=self.cst.ap[:, 0:1])
        for u, X in each():
            kb.recip(X["rstd2"].ap[:, 0:4], X["rms2"].ap[:, 0:4], R=[X["rms2"].tok], W=[X["rstd2"].tok])
            for sub in range(NSUB):
                kb.ts(X["on"].ap[:, sub * 128:(sub + 1) * 128], X["po"].ap[:, sub * 128:(sub + 1) * 128],
                      X["rstd2"].ap[:, sub:sub + 1], None, ALU.mult, None, R=[X["po"].tok, X["rstd2"].tok], W=[X["on"].tok])
        for u, X in each():
            pbo = X["pbt"]
            for sub in range(NSUB):
                kb.tr(pbo.ap[:, sub * 128:(sub + 1) * 128], X["on"].ap[:, sub * 128:(sub + 1) * 128], self.ident.ap[:, :],
                      R=[X["on"].tok, self.ident.tok], W=[pbo.tok])
        for u, X in each():
            kb.stt(self.merged.ap[:, 8 + u, :], X["pbt"].ap[:, 0:512], pv.ap[:, 96:97], X["sg"].ap[:, 0:512], ALU.mult,
                   ALU.mult, R=[X["pbt"].tok, pv.tok, X["sg"].tok], W=[self.merged.tok])

    def l0_end(self):
        kb = self.kb
        kb.dma("sync", self.dram["st0_out"][:, :], self.st0.ap[:, :], self.st0.tok, R=[self.st0.tok])

    def layer0_tile(self, tile, p1=False):
        self.gain = self.pv0
        self.gain_off = 88
        self.norm_T(tile, 0)
        self.l0_rg_proj(0, p1)
        for cb in range(8):
            self.l0_rg(cb, p1, (lambda c=cb: self.l0_rg_proj(c + 1, p1)) if cb < 7 else None)
        self.l0_hg_proj(0, p1)
        for hh in range(8):
            self.l0_hg(hh, p1, (lambda h_=hh: self.l0_hg_proj(h_ + 1, p1)) if hh < 7 else None)
        if not p1:
            self.out_proj(tile, 16, self.wout0, self.postbc0)

    PV1_COLS = 112
    DECAY_C = -0.6065306597126334

    def l1_setup(self):
        kb = self.kb
        d = self.dram
        self.pv1 = self.sbuf("pv1", [128, self.PV1_COLS], F32)
        self.pd1 = self.sbuf("pd1", [128, 8], F32)
        self.mix = [self.sbuf(f"mix{i}", [128, 8, 512], BF16) for i in range(4)]
        self.w1b = self.sbuf("w1b", [128, 8, 64], BF16)
        self.a1b = self.sbuf("a1b", [128, 8, 64], BF16)
        self.w2b = self.sbuf("w2b", [64, 1024], BF16)
        self.a2b = self.sbuf("a2b", [64, 1024], BF16)
        self.hid = self.sbuf("hid", [64, 1024], BF16)
        self.M3 = self.sbuf("M3", [128, 384], BF16)
        self.M2 = self.sbuf("M2", [128, 256], BF16)
        self.onesbd = self.sbuf("onesbd", [128, 128], BF16)
        self.st1 = self.sbuf("st1", [128, 520], F32)
        self.SC = self.sbuf("SC", [128, 8, 384], BF16)
        self.PPh = [self.sbuf(f"PP{i}", [128, 512], BF16) for i in range(2)]
        self.smb = self.sbuf("smb", [128, 384], BF16)
        if not self.do_l0:
            self.rmask = self.sbuf("rmask", [128, 512], BF16)
            kb.dma("sync", self.rmask.ap[:, :], d["rmask"][:, :], self.rmask.tok, W=[self.rmask.tok])
        if self.comm:
            kb.memset(self.st1.ap[:, :], 0.0, [self.st1.tok])
        else:
            kb.dma("sync", self.st1.ap[:, :], d["st1_in"][:, :], self.st1.tok, W=[self.st1.tok])
        for (buf, nm) in ((self.pv1, "pv1"), (self.M3, "M3"), (self.M2, "M2"),
                          (self.postbc1, "post1_bc"), (self.onesbd, "onesbd")):
            kb.dma("sync", buf.ap[:, :], d[nm][:, :], buf.tok, W=[buf.tok])
        ncol = 256 if self.comm else D
        kb.dma("gpsimd", self.w2b.ap[:, 0:ncol], d["rw_w2"][:, :], self.w2b.tok, W=[self.w2b.tok])
        kb.dma("gpsimd", self.a2b.ap[:, 0:ncol], d["rw_a2"][:, :], self.a2b.tok, W=[self.a2b.tok])
        kb.dma("gpsimd", self.w1b.ap[:, :, :], d["rw_w1"].rearrange("(kc p) n -> p kc n", p=128), self.w1b.tok,
               W=[self.w1b.tok])
        kb.dma("gpsimd", self.a1b.ap[:, :, :], d["rw_a1"].rearrange("(kc p) n -> p kc n", p=128), self.a1b.tok,
               W=[self.a1b.tok])
        if not self.comm:
            for cb in range(8):
                self.load_wout1(cb)
        kb.memset(self.smb.ap[:, :], 0.0, [self.smb.tok])
        kb.ts(self.pd1.ap[:, 0:8], self.pv1.ap[:, 72:80], -1.0, 1.0, ALU.mult, ALU.add, R=[self.pv1.tok],
              W=[self.pd1.tok])

    def l1_end(self):
        kb = self.kb
        kb.dma("sync", self.dram["st1_out"][:, :], self.st1.ap[:, :], self.st1.tok, R=[self.st1.tok])

    def l1_mix(self, p, dst):
        for _ in self.l1_mix_g(p, dst):
            pass

    def l1_mix_g(self, p, dst):
        kb = self.kb
        diff = self.merged
        pool_set = tuple(int(c) for c in os.environ.get("MK_POOLMIX", "").split(",") if c)
        en = "gpsimd" if p in pool_set else "vector"
        for kc in range(8):
            kb.stt(dst.ap[:, kc, :], diff.ap[:, 8 + kc, :], self.pv1.ap[:, p * 8 + kc:p * 8 + kc + 1],
                   self.uT.ap[:, kc, 1:513], ALU.mult, ALU.add, R=[diff.tok, self.pv1.tok, self.uT.tok], W=[dst.tok],
                   en=en)
            yield

    def layer1_tile(self, tile, p1=False):
        kb = self.kb
        st1 = self.st1
        self.gain = self.pv1
        self.gain_off = 104
        for kc in range(8):
            kb.copy(self.uT.ap[:, kc, 0:1], st1.ap[:, 512 + kc:513 + kc], R=[st1.tok], W=[self.uT.tok])
        self.norm_T(tile, 1)
        for kc in range(8):
            kb.copy(st1.ap[:, 512 + kc:513 + kc], self.uT.ap[:, kc, 512:513], R=[self.uT.tok], W=[st1.tok])
        kb.tt(self.merged.ap[:, 8:16, :], self.uT.ap[:, :, 0:512], self.uT.ap[:, :, 1:513], ALU.subtract,
              R=[self.uT.tok], W=[self.merged.tok])
        self.l1_mix(4, self.mix[0])
        ph = self.psf[0]
        for kc in range(8):
            kb.mm(ph.ap[0:64, :], self.w1b.ap[:, kc, :], self.mix[0].ap[:, kc, :], kc == 0, kc == 7,
                  R=[self.w1b.tok, self.mix[0].tok], W=[ph.tok])
        kb.act(self.hid.ap[:, 0:512], ph.ap[0:64, :], AF.Tanh, R=[ph.tok], W=[self.hid.tok])
        self.l1_mix(5, self.mix[1])
        ph = self.psf[1]
        for kc in range(8):
            kb.mm(ph.ap[0:64, :], self.a1b.ap[:, kc, :], self.mix[1].ap[:, kc, :], kc == 0, kc == 7,
                  R=[self.a1b.tok, self.mix[1].tok], W=[ph.tok])
        kb.act(self.hid.ap[:, 512:1024], ph.ap[0:64, :], AF.Copy, R=[ph.tok], W=[self.hid.tok])
        for p_ in ((1, 2) if p1 else range(4)):
            self.l1_mix(p_, self.mix[p_])
        for hp in range(8):
            self.l1_hp(hp, p1)
        if not p1:
            self.out_proj(tile, 8, self.wout1, self.postbc1)

    def l1_hp(self, hp, p1=False, filler=None):
        kb = self.kb
        d = self.dram
        pv, pd = self.pv1, self.pd1
        f, b, sm = self.f, self.b, self.sm
        C = self.DECAY_C
        bq, r, k, sgm, icl, cs, Epos, Eneg, Eexc, kkn, t1, sg = (f[0], f[1], f[2], f[3], f[4], f[5], f[6], f[7],
                                                                 f[8], f[9], f[10], f[11])
        sq, at, rt, kt, bt, ktm, btm, vsb, rkr, yn, NN, NNT = (b[0], b[1], b[2], b[3], b[4], b[5], b[6], b[7],
                                                               b[8], b[9], b[10], b[11])
        psf = self.psf
        st1 = self.st1
        cols = slice(hp * 128, (hp + 1) * 128)
        pr, pk, pg, pvv, pw, pa = psf[2], psf[3], psf[4], psf[5], psf[0], psf[1]
        wk = self.load_w(d["rw_w_in"][1, :, cols])
        wv = self.load_w(d["rw_w_in"][2, :, cols])
        plist = [(wk, self.mix[1], pk)]
        if not p1:
            wr = self.load_w(d["rw_w_in"][0, :, cols])
            wg = self.load_w(d["rw_w_in"][3, :, cols])
            plist += [(wr, self.mix[0], pr), (wg, self.mix[3], pg)]
        for (w_, m_, p_) in plist:
            for kc in range(8):
                kb.mm(p_.ap[:, :], w_.ap[:, kc, :], m_.ap[:, kc, :], kc == 0, kc == 7, R=[w_.tok, m_.tok], W=[p_.tok])
        for sub in range(NSUB):
            for kc in range(8):
                kb.mm(pvv.ap[:, sub * 128:(sub + 1) * 128], self.mix[2].ap[:, kc, sub * 128:(sub + 1) * 128],
                      wv.ap[:, kc, :], kc == 0, kc == 7, R=[wv.tok, self.mix[2].tok], W=[pvv.tok])
        kb.mm(pw.ap[:, :], self.w2b.ap[0:64, cols], self.hid.ap[0:64, 0:512], True, True,
              R=[self.w2b.tok, self.hid.tok], W=[pw.tok])
        kb.mm(pa.ap[:, :], self.a2b.ap[0:64, cols], self.hid.ap[0:64, 512:1024], True, True,
              R=[self.a2b.tok, self.hid.tok], W=[pa.tok])
        if not p1:
            kb.act(r.ap[:, 0:512], pr.ap[:, :], AF.Copy, R=[pr.tok], W=[r.tok])
            kb.act(sg.ap[:, 0:512], pg.ap[:, :], AF.Silu, R=[pg.tok], W=[sg.tok])
        kb.act(k.ap[:, 0:512], pk.ap[:, :], AF.Copy, R=[pk.tok], W=[k.tok])
        kb.act(sq.ap[:, :], pk.ap[:, :], AF.Square, R=[pk.tok, pv.tok], W=[sq.tok], scale=pv.ap[:, 64 + hp:65 + hp])
        kb.act(vsb.ap[:, :], pvv.ap[:, :], AF.Copy, R=[pvv.tok], W=[vsb.tok])
        kb.act(sgm.ap[:, 0:512], pw.ap[:, :], AF.Sigmoid, R=[pw.tok, pv.tok], W=[sgm.tok], bias=pv.ap[:, 48 + hp:49 + hp])
        kb.act(icl.ap[:, 0:512], pa.ap[:, :], AF.Sigmoid, R=[pa.tok, pv.tok], W=[icl.tok], bias=pv.ap[:, 56 + hp:57 + hp])
        pn = psf[0]
        kb.mm(pn.ap[:, :], self.onesbd.ap[:, :], sq.ap[:, :], True, True, R=[self.onesbd.tok, sq.tok], W=[pn.tok])
        kb.act(t1.ap[:, 0:512], pn.ap[:, :], AF.Sqrt, R=[pn.tok], W=[t1.tok])
        kb.ts(t1.ap[:, 0:512], t1.ap[:, 0:512], 1e-12, None, ALU.max, None, R=[t1.tok], W=[t1.tok])
        kb.recip(t1.ap[:, 0:512], t1.ap[:, 0:512], R=[t1.tok], W=[t1.tok])
        kb.stt(kkn.ap[:, 0:512], k.ap[:, 0:512], pv.ap[:, 64 + hp:65 + hp], t1.ap[:, 0:512], ALU.mult, ALU.mult,
               R=[k.tok, pv.tok, t1.tok], W=[kkn.tok])
        kb.scan(cs.ap[:, 0:512], self.rmask.ap[:, :], sgm.ap[:, 0:512], 0.0, ALU.mult, ALU.add,
                R=[self.rmask.tok, sgm.tok], W=[cs.tok])
        kb.tt(Eexc.ap[:, 0:512], cs.ap[:, 0:512], sgm.ap[:, 0:512], ALU.subtract, R=[cs.tok, sgm.tok], W=[Eexc.tok])
        kb.act(Epos.ap[:, 0:512], cs.ap[:, 0:512], AF.Exp, R=[cs.tok], W=[Epos.tok], scale=C)
        kb.act(Eneg.ap[:, 0:512], cs.ap[:, 0:512], AF.Exp, R=[cs.tok], W=[Eneg.tok], scale=-C)
        kb.act(Eexc.ap[:, 0:512], Eexc.ap[:, 0:512], AF.Exp, R=[Eexc.tok], W=[Eexc.tok], scale=C)
        kb.ts(t1.ap[:, 0:512], icl.ap[:, 0:512], pv.ap[:, 72 + hp:73 + hp], pd.ap[:, hp:hp + 1], ALU.mult, ALU.add,
              R=[icl.tok, pv.tok, pd.tok], W=[t1.tok])
        kb.tt(k.ap[:, 0:512], k.ap[:, 0:512], t1.ap[:, 0:512], ALU.mult, R=[k.tok, t1.tok], W=[k.tok])
        kb.tt(bq.ap[:, 0:512], kkn.ap[:, 0:512], icl.ap[:, 0:512], ALU.mult, R=[kkn.tok, icl.tok], W=[bq.tok])
        kb.tt(kt.ap[:, :], k.ap[:, 0:512], Eneg.ap[:, 0:512], ALU.mult, R=[k.tok, Eneg.tok], W=[kt.tok])
        kb.tt(bt.ap[:, :], bq.ap[:, 0:512], Eneg.ap[:, 0:512], ALU.mult, R=[bq.tok, Eneg.tok], W=[bt.tok])
        kb.stt(at.ap[:, :], kkn.ap[:, 0:512], -1.0, Eexc.ap[:, 0:512], ALU.mult, ALU.mult, R=[kkn.tok, Eexc.tok],
               W=[at.tok])
        pbon = psf[1]
        if not p1:
            kb.tt(rt.ap[:, :], r.ap[:, 0:512], Epos.ap[:, 0:512], ALU.mult, R=[r.tok, Epos.tok], W=[rt.tok])
            kb.stt(rkr.ap[:, :], r.ap[:, 0:512], pv.ap[:, 80 + hp:81 + hp], k.ap[:, 0:512], ALU.mult, ALU.mult,
                   R=[r.tok, pv.tok, k.tok], W=[rkr.tok])
            kb.mm(pbon.ap[:, :], self.onesbd.ap[:, :], rkr.ap[:, :], True, True, R=[self.onesbd.tok, rkr.tok],
                  W=[pbon.tok])
            kb.act(r.ap[:, 0:512], pbon.ap[:, :], AF.Copy, R=[pbon.tok], W=[r.tok])
        Epos3 = Epos.ap[:, 0:512].rearrange("p (c t) -> p c t", t=64)
        wend_bc = Epos3[:, :, 63:64].broadcast_to([128, NCH, 64])
        kb.tt(yn.ap[:, :].rearrange("p (c t) -> p c t", t=64), kt.ap[:, :].rearrange("p (c t) -> p c t", t=64), wend_bc,
              ALU.mult, R=[kt.tok, Epos.tok], W=[yn.tok])
        for sub in range(NSUB):
            kb.tr(self.psb[0].ap[:, sub * 128:(sub + 1) * 128], yn.ap[:, sub * 128:(sub + 1) * 128], self.ident.ap[:, :],
                  R=[yn.tok, self.ident.tok], W=[self.psb[0].tok])
        kb.tt(yn.ap[:, :].rearrange("p (c t) -> p c t", t=64), bt.ap[:, :].rearrange("p (c t) -> p c t", t=64), wend_bc,
              ALU.mult, R=[bt.tok, Epos.tok], W=[yn.tok])
        for sub in range(NSUB):
            kb.tr(self.psb[1].ap[:, sub * 128:(sub + 1) * 128], yn.ap[:, sub * 128:(sub + 1) * 128], self.ident.ap[:, :],
                  R=[yn.tok, self.ident.tok], W=[self.psb[1].tok])
        kb.copy(ktm.ap[:, :], self.psb[0].ap[:, 0:512], R=[self.psb[0].tok], W=[ktm.tok])
        kb.act(btm.ap[:, :], self.psb[1].ap[:, 0:512], AF.Copy, R=[self.psb[1].tok], W=[btm.tok])
        NNs = [(NN, NNT), (rkr, sq)]
        for hl in range(2):
            rs = slice(64 * hl, 64 * hl + 64)
            nn, nnt = NNs[hl]
            for sub in range(NSUB):
                pi = hl * 4 + sub
                X, Y = (psf[0], psf[2]) if pi % 2 == 0 else (psf[3], psf[4])
                tc = slice(sub * 128, (sub + 1) * 128)
                kb.mm(X.ap[:, 0:128], kt.ap[rs, tc], at.ap[rs, tc], True, True, R=[kt.tok, at.tok], W=[X.tok])
                if not p1:
                    kb.mm(X.ap[:, 128:256], kt.ap[rs, tc], rt.ap[rs, tc], True, True, R=[kt.tok, rt.tok], W=[X.tok])
                    kb.mm(X.ap[:, 256:384], bt.ap[rs, tc], rt.ap[rs, tc], True, True, R=[bt.tok, rt.tok], W=[X.tok])
                kb.mm(Y.ap[:, 0:128], bt.ap[rs, tc], at.ap[rs, tc], True, True, R=[bt.tok, at.tok], W=[Y.tok])
                kb.mm(Y.ap[:, 128:256], at.ap[rs, tc], bt.ap[rs, tc], True, True, R=[bt.tok, at.tok], W=[Y.tok])
                nsc = 128 if p1 else 384
                kb.tt(self.SC.ap[:, pi, 0:nsc], X.ap[:, 0:nsc], self.M3.ap[:, 0:nsc], ALU.mult, R=[X.tok, self.M3.tok],
                      W=[self.SC.tok])
                kb.tt(nn.ap[:, tc], Y.ap[:, 0:128], self.M2.ap[:, 0:128], ALU.mult, R=[Y.tok, self.M2.tok], W=[nn.tok])
                kb.tt(nnt.ap[:, tc], Y.ap[:, 128:256], self.M2.ap[:, 128:256], ALU.mult, R=[Y.tok, self.M2.tok],
                      W=[nnt.tok])
            for j in range(4):
                kb.tt(self.PPh[hl].ap[:, j * 128:(j + 1) * 128], nn.ap[:, j * 128:(j + 1) * 128], self.ident.ap[:, :],
                      ALU.add, R=[nn.tok, self.ident.tok], W=[self.PPh[hl].tok])
        ABC = [(psf[3], psf[4], psf[5]), (psf[0], psf[2], psf[1])]
        for lvl in range(5):
            last = lvl == 4
            for hl in range(2):
                nn, nnt = NNs[hl]
                A_, B_, C_ = ABC[hl]
                for j in range(4):
                    tc = slice(j * 128, (j + 1) * 128)
                    if not last:
                        kb.mm(A_.ap[:, tc], nnt.ap[:, tc], nn.ap[:, tc], True, True, R=[nn.tok, nnt.tok], W=[A_.tok])
                    kb.mm(B_.ap[:, tc], nn.ap[:, tc], nnt.ap[:, tc], True, True, R=[nn.tok, nnt.tok], W=[B_.tok])
            for hl in range(2):
                nn, nnt = NNs[hl]
                A_, B_, C_ = ABC[hl]
                kb.act(nnt.ap[:, :], B_.ap[:, :], AF.Copy, R=[B_.tok], W=[nnt.tok])
                if not last:
                    kb.copy(nn.ap[:, :], A_.ap[:, :], R=[A_.tok], W=[nn.tok])
            for hl in range(2):
                nn, nnt = NNs[hl]
                A_, B_, C_ = ABC[hl]
                for j in range(4):
                    tc = slice(j * 128, (j + 1) * 128)
                    kb.mm(C_.ap[:, tc], nnt.ap[:, tc], self.PPh[hl].ap[:, tc], True, True, R=[nnt.tok, self.PPh[hl].tok],
                          W=[C_.tok])
            for hl in range(2):
                A_, B_, C_ = ABC[hl]
                kb.tt(self.PPh[hl].ap[:, :], C_.ap[:, :], self.PPh[hl].ap[:, :], ALU.add, R=[C_.tok, self.PPh[hl].tok],
                      W=[self.PPh[hl].tok])
        yps, XU, dH = psf[0], psf[2], psf[3]
        if p1:
            VW = 128
            smb, smbt = self.smbA, self.wout0.tok
            Hs, stt_ = self.stA[:, hp, :], self.wout0.tok
        else:
            VW = 64
            smb, smbt = self.smb.ap, self.smb.tok
            Hs, stt_ = st1.ap[:, hp * 64:(hp + 1) * 64], st1.tok
        ho = 4 * VW
        tmpH = t1
        for hl in range(2):
            rs = slice(64 * hl, 64 * hl + 64)
            kb.act(smb[rs, ho + hl * VW:ho + (hl + 1) * VW], Hs[rs, :], AF.Copy, R=[stt_], W=[smbt])
        for c in range(NCH):
            sub, half = divmod(c, 2)
            p0 = 64 * half
            ps_ = slice(p0, p0 + 64)
            cc = slice(c * 64, (c + 1) * 64)
            for hl in range(2):
                vv = slice(sub * 128 + hl * 64, sub * 128 + hl * 64 + 64)
                pi = hl * 4 + sub
                kb.mm(XU.ap[ps_, hl * VW:(hl + 1) * VW], at.ap[:, cc], smb[:, ho + hl * VW:ho + (hl + 1) * VW], True, False,
                      R=[at.tok, smbt], W=[XU.tok])
                kb.mm(XU.ap[ps_, hl * VW:hl * VW + 64], self.SC.ap[:, pi, p0:p0 + 64], vsb.ap[:, vv], False, True,
                      R=[self.SC.tok, vsb.tok], W=[XU.tok])
            kb.act(smb[ps_, 0:2 * VW], XU.ap[ps_, 0:2 * VW], AF.Copy, R=[XU.tok], W=[smbt])
            for hl in range(2):
                kb.mm(XU.ap[ps_, 2 * VW + hl * VW:2 * VW + (hl + 1) * VW],
                      self.PPh[hl].ap[:, sub * 128 + p0:sub * 128 + p0 + 64], smb[:, hl * VW:(hl + 1) * VW], True, True,
                      R=[self.PPh[hl].tok, smbt], W=[XU.tok])
            kb.copy(smb[ps_, 2 * VW:4 * VW], XU.ap[ps_, 2 * VW:4 * VW], R=[XU.tok], W=[smbt])
            for hl in range(2):
                rs = slice(64 * hl, 64 * hl + 64)
                vv = slice(sub * 128 + hl * 64, sub * 128 + hl * 64 + 64)
                kb.mm(dH.ap[rs, 0:VW], btm.ap[ps_, vv], smb[ps_, 2 * VW + hl * VW:2 * VW + (hl + 1) * VW], True, False,
                      R=[btm.tok, smbt], W=[dH.tok])
                kb.mm(dH.ap[rs, 0:64], ktm.ap[ps_, vv], vsb.ap[ps_, vv], False, True, R=[ktm.tok, vsb.tok], W=[dH.tok])
            if not p1:
                for hl in range(2):
                    us = slice(2 * VW + hl * VW, 2 * VW + hl * VW + 64)
                    vv = slice(sub * 128 + hl * 64, sub * 128 + hl * 64 + 64)
                    pi = hl * 4 + sub
                    kb.mm(yps.ap[ps_, vv], rt.ap[:, cc], smb[:, ho + hl * VW:ho + (hl + 1) * VW], True, False,
                          R=[rt.tok, smbt], W=[yps.tok])
                    kb.mm(yps.ap[ps_, vv], self.SC.ap[:, pi, 256 + p0:256 + p0 + 64], smb[:, us], False, False,
                          R=[self.SC.tok, smbt], W=[yps.tok])
                    kb.mm(yps.ap[ps_, vv], self.SC.ap[:, pi, 128 + p0:128 + p0 + 64], vsb.ap[:, vv], False, True,
                          R=[self.SC.tok, vsb.tok], W=[yps.tok])
            for hl in range(2):
                rs = slice(64 * hl, 64 * hl + 64)
                kb.stt(smb[rs, ho + hl * VW:ho + (hl + 1) * VW], Hs[rs, :], Epos.ap[rs, c * 64 + 63:c * 64 + 64],
                       dH.ap[rs, 0:VW], ALU.mult, ALU.add, R=[stt_, Epos.tok, dH.tok], W=[smbt])
            kb.stt(Hs, Hs, Epos.ap[:, c * 64 + 63:c * 64 + 64], dH.ap[:, 0:VW], ALU.mult, ALU.add,
                   R=[stt_, Epos.tok, dH.tok], W=[stt_])
            if filler is not None:
                filler()
        if p1:
            return
        s1, s2, mean, msq, var, rms, rstd = sm[6], sm[7], sm[8], sm[9], sm[10], sm[11], sm[12]
        sqy = t1
        y3 = yps.ap[:, :].rearrange("p (g v) -> p g v", v=64)
        kb.emit("vector", lambda e: e.reduce_sum(s1.ap[:, 0:8], y3, mybir.AxisListType.X), R=[yps.tok], W=[s1.tok])
        kb.act(sqy.ap[:, 0:512], yps.ap[:, :], AF.Square, R=[yps.tok], W=[sqy.tok])
        sq3 = sqy.ap[:, 0:512].rearrange("p (g v) -> p g v", v=64)
        kb.emit("vector", lambda e: e.reduce_sum(s2.ap[:, 0:8], sq3, mybir.AxisListType.X), R=[sqy.tok], W=[s2.tok])
        kb.ts(mean.ap[:, 0:8], s1.ap[:, 0:8], 1.0 / 64, None, ALU.mult, None, R=[s1.tok], W=[mean.tok])
        kb.tt(msq.ap[:, 0:8], mean.ap[:, 0:8], mean.ap[:, 0:8], ALU.mult, R=[mean.tok], W=[msq.tok])
        kb.stt(var.ap[:, 0:8], s2.ap[:, 0:8], 1.0 / 64, msq.ap[:, 0:8], ALU.mult, ALU.subtract, R=[s2.tok, msq.tok],
               W=[var.tok])
        kb.act(rms.ap[:, 0:8], var.ap[:, 0:8], AF.Sqrt, R=[var.tok, self.cst.tok], W=[rms.tok], bias=self.cst.ap[:, 2:3])
        kb.recip(rstd.ap[:, 0:8], rms.ap[:, 0:8], R=[rms.tok], W=[rstd.tok])
        for g in range(8):
            kb.ts(yn.ap[:, g * 64:(g + 1) * 64], yps.ap[:, g * 64:(g + 1) * 64], mean.ap[:, g:g + 1], rstd.ap[:, g:g + 1],
                  ALU.subtract, ALU.mult, R=[yps.tok, mean.tok, rstd.tok], W=[yn.tok])
        for sub in range(NSUB):
            kb.tr(self.psb[0].ap[:, sub * 128:(sub + 1) * 128], yn.ap[:, sub * 128:(sub + 1) * 128], self.ident.ap[:, :],
                  R=[yn.tok, self.ident.tok], W=[self.psb[0].tok])
        for sub in range(NSUB):
            kb.tr(self.psb[1].ap[:, sub * 128:(sub + 1) * 128], vsb.ap[:, sub * 128:(sub + 1) * 128], self.ident.ap[:, :],
                  R=[vsb.tok, self.ident.tok], W=[self.psb[1].tok])
        vT = sq
        kb.act(vT.ap[:, :], self.psb[1].ap[:, 0:512], AF.Copy, R=[self.psb[1].tok], W=[vT.tok])
        bv = k
        kb.tt(bv.ap[:, 0:512], r.ap[:, 0:512], vT.ap[:, :], ALU.mult, R=[r.tok, vT.tok], W=[bv.tok])
        kb.stt(sgm.ap[:, 0:512], self.psb[0].ap[:, 0:512], pv.ap[:, 88 + hp:89 + hp], bv.ap[:, 0:512], ALU.mult, ALU.add,
               R=[self.psb[0].tok, pv.tok, bv.tok], W=[sgm.tok])
        kb.stt(self.merged.ap[:, hp, :], sgm.ap[:, 0:512], pv.ap[:, 96 + hp:97 + hp], sg.ap[:, 0:512], ALU.add, ALU.mult,
               R=[sgm.tok, pv.tok, sg.tok], W=[self.merged.tok])

    def cc_gather(self, parts, W, name):
        kb = self.kb
        bin_ = self.nc.dram_tensor(name + "_i", [128, W], F32)
        bout = self.nc.dram_tensor(name + "_o", [512, W], F32)
        tin, tout = kb.tok(name + "_i"), kb.tok(name + "_o")
        off = 0
        for (ap, tok, w) in parts:
            kb.dma("gpsimd", bin_.ap()[:, off:off + w], ap, tin, R=[tok], W=[tin])
            off += w
        kb.collective(bin_, bout, tin, tout, name)
        G = self.merged.ap[:, :, :].rearrange("p a b -> p (a b)").bitcast(F32)
        for r in range(3):
            kb.dma("sync", G[:, r * W:(r + 1) * W], bout.ap()[r * 128:(r + 1) * 128, :], self.merged.tok, R=[tout],
                   W=[self.merged.tok])
        return G

    def comm_setup(self):
        kb = self.kb
        d = self.dram
        self.dec0 = self.sbuf("dec0", [128, 16], F32)
        self.msk = self.sbuf("msk", [128, 16], F32)
        self.eye2 = self.sbuf("eye2", [128, 64], F32)
        self.ul = self.sbuf("ul", [128, 8], F32)
        self.up0 = self.sbuf("up0", [128, 8], F32)
        self.dd = self.sbuf("dd", [128, 16], F32)
        kb.dma("sync", self.msk.ap[:, :], d["msk"][:, :], self.msk.tok, W=[self.msk.tok])
        kb.dma("sync", self.eye2.ap[:, :], d["eye2"][:, :], self.eye2.tok, W=[self.eye2.tok])
        kb.memset(self.dec0.ap[:, :], 0.0, [self.dec0.tok])
        w0f = self.wout0.ap[:, :, :].rearrange("p a b -> p (a b)")
        self.stA = w0f.bitcast(F32)[:, 0:1024].rearrange("p (a b) -> p a b", b=128)
        self.smbA = w0f[:, 2048:2048 + 768]
        self.rgB_f = []
        for i in range(4):
            flat = self.mix[i].ap[:, :, :].rearrange("p a b -> p (a b)").bitcast(F32)
            for j in range(3 if i < 3 else 1):
                self.rgB_f.append(Buf(flat[:, j * 516:(j + 1) * 516], kb.tok(f"rgB{i}_{j}")))
        flatb = self.mix[3].ap[:, :, :].rearrange("p a b -> p (a b)")
        self.rgB_xcb = Buf(flatb[:, 1032:1032 + 512], kb.tok("rgBx"))
        w1b_ = self.wout1.ap[:, :, :].rearrange("p a b -> p (a b)")
        self.hgB_b = [Buf(w1b_[:, i * 512:(i + 1) * 512], kb.tok(f"hgBb{i}")) for i in range(9)]
        self.hgB_f = list(self.rgB_f) + [Buf(w1b_.bitcast(F32)[:, 2304:2820], kb.tok("hgBf10"))]
        self.hgB_sm = [self.sbuf(f"smB{i}", [128, 16], F32) for i in range(5)]
        w0b = self.wout0.ap[:, :, :].rearrange("p a b -> p (a b)")
        w0f = w0b.bitcast(F32)
        self.l1B = dict(
            b=[Buf(w0b[:, i * 512:(i + 1) * 512], kb.tok(f"l1Bb{i}")) for i in range(5)],
            SC=Buf(w0b[:, 2560:5632].rearrange("p (a b) -> p a b", b=384), kb.tok("l1Bsc")),
            PPh=[Buf(w0b[:, 5632 + i * 512:5632 + (i + 1) * 512], kb.tok(f"l1Bpp{i}")) for i in range(2)],
            smb=Buf(w0b[:, 6656:7040], kb.tok("l1Bsmb")),
            f=[Buf(w0f[:, 3520 + i * 516:3520 + (i + 1) * 516], kb.tok(f"l1Bf{i}")) for i in range(3)],
        )
        self.l1E = dict(
            b=[Buf(w0b[:, 10136 + i * 512:10136 + (i + 1) * 512], kb.tok(f"l1Eb{i}")) for i in range(2)],
            f=[Buf(w0f[:, 5580 + i * 516:5580 + (i + 1) * 516], kb.tok(f"l1Ef{i}")) for i in range(3)],
            sm=[self.sbuf(f"smE{i}", [128, 16], F32) for i in range(7)],
        )
        self.st1h = [kb.tok("st1h0"), kb.tok("st1h1")]
        for t_ in self.st1h:
            t_.w = self.st1.tok.w
        self.x1buf = self.nc.dram_tensor("x1buf", [self.TS, D], F32)
        self.x1tok = [kb.tok(f"x1t{t}") for t in range(self.NT)]

    def comm_l0_prefix(self, G):
        kb = self.kb
        st, dd, msk, pd = self.st0, self.dd, self.msk, self.pd0
        mt = self.merged.tok
        W = 1072
        kb.memset(st.ap[:, :], 0.0, [st.tok])
        for j in range(3):
            Gj = G[:, j * W:(j + 1) * W]
            mj = msk.ap[:, j:j + 1]
            ej = msk.ap[:, 4 + j:5 + j]
            kb.tt(dd.ap[:, 0:8], Gj[:, 1056:1064], pd.ap[:, 0:8], ALU.mult, R=[mt, pd.tok], W=[dd.tok])
            kb.act(dd.ap[:, 0:8], dd.ap[:, 0:8], AF.Exp, R=[dd.tok], W=[dd.tok])
            kb.act(dd.ap[:, 8:16], Gj[:, 1064:1072], AF.Exp, R=[mt], W=[dd.tok])
            kb.ts(dd.ap[:, 0:16], dd.ap[:, 0:16], -1.0, None, ALU.add, None, R=[dd.tok], W=[dd.tok])
            kb.ts(dd.ap[:, 0:16], dd.ap[:, 0:16], mj, None, ALU.mult, None, R=[dd.tok, msk.tok], W=[dd.tok])
            kb.ts(dd.ap[:, 0:16], dd.ap[:, 0:16], 1.0, None, ALU.add, None, R=[dd.tok], W=[dd.tok])
            kb.stt(st.ap[:, 0:24], Gj[:, 0:24], ej, st.ap[:, 0:24], ALU.mult, ALU.add, R=[mt, msk.tok, st.tok], W=[st.tok])
            kb.tt(st.ap[:, 24:32], st.ap[:, 24:32], dd.ap[:, 0:8], ALU.mult, R=[st.tok, dd.tok], W=[st.tok])
            kb.stt(st.ap[:, 24:32], Gj[:, 24:32], mj, st.ap[:, 24:32], ALU.mult, ALU.add, R=[mt, msk.tok, st.tok],
                   W=[st.tok])
            for h in range(8):
                kb.ts(st.ap[:, 32 + h * 128:32 + (h + 1) * 128], st.ap[:, 32 + h * 128:32 + (h + 1) * 128],
                      dd.ap[:, 8 + h:9 + h], None, ALU.mult, None, R=[st.tok, dd.tok], W=[st.tok])
            kb.stt(st.ap[:, 32:1056], Gj[:, 32:1056], mj, st.ap[:, 32:1056], ALU.mult, ALU.add,
                   R=[mt, msk.tok, st.tok], W=[st.tok])

    def comm_uprev(self, G):
        kb = self.kb
        st1, msk = self.st1, self.msk
        mt = self.merged.tok
        kb.memset(self.up0.ap[:, :], 0.0, [self.up0.tok])
        for j in range(3):
            kb.stt(self.up0.ap[:, 0:8], G[:, j * 8:(j + 1) * 8], msk.ap[:, 4 + j:5 + j], self.up0.ap[:, 0:8], ALU.mult,
                   ALU.add, R=[mt, msk.tok, self.up0.tok], W=[self.up0.tok])

    def comm_l1_prefix(self, G):
        kb = self.kb
        st1, msk = self.st1, self.msk
        mt = self.merged.tok
        Mbd, Xb, Hb = self.b[0], self.b[1], self.b[2]
        t = self.f[1]
        ps = self.psf[0]
        kb.memset(st1.ap[:, 0:512], 0.0, [st1.tok])
        kb.memset(Mbd.ap[:, 0:128], 0.0, [Mbd.tok])
        for j in range(3):
            mj = msk.ap[:, j:j + 1]
            for hp in range(8):
                Gh = G[:, j * 1024 + hp * 128:j * 1024 + (hp + 1) * 128]
                H = st1.ap[:, hp * 64:(hp + 1) * 64]
                kb.copy(Mbd.ap[0:64, 0:64], Gh[0:64, 64:128], R=[mt], W=[Mbd.tok])
                kb.copy(Mbd.ap[64:128, 64:128], Gh[64:128, 64:128], R=[mt], W=[Mbd.tok])
                kb.tr(self.psb[0].ap[:, 0:128], Mbd.ap[:, 0:128], self.ident.ap[:, :], R=[Mbd.tok, self.ident.tok],
                      W=[self.psb[0].tok])
                kb.act(Xb.ap[:, 0:128], self.psb[0].ap[:, 0:128], AF.Copy, R=[self.psb[0].tok], W=[Xb.tok])
                kb.act(Hb.ap[:, 0:64], H, AF.Copy, R=[st1.tok], W=[Hb.tok])
                kb.mm(ps.ap[:, 0:64], Xb.ap[:, 0:128], Hb.ap[:, 0:64], True, True, R=[Xb.tok, Hb.tok], W=[ps.tok])
                kb.tt(t.ap[:, 0:64], ps.ap[:, 0:64], Gh[:, 0:64], ALU.add, R=[ps.tok, mt], W=[t.tok])
                kb.tt(t.ap[:, 0:64], t.ap[:, 0:64], H, ALU.subtract, R=[t.tok, st1.tok], W=[t.tok])
                kb.stt(H, t.ap[:, 0:64], mj, H, ALU.mult, ALU.add, R=[t.tok, msk.tok, st1.tok], W=[st1.tok])

    def cc_gather_dram(self, bin_, bout, tin, name):
        kb = self.kb
        tout = kb.tok(name + "_o")
        kb.collective(bin_, bout, tin, tout, name)
        return tout

    def l1_ctx(self, hp, setB):
        from types import SimpleNamespace as NS
        f, b = self.f, self.b
        cx = NS(hp=hp)
        (cx.bq, cx.r, cx.k, cx.sgm, cx.icl, cx.cs, cx.Epos, cx.Eneg, cx.Eexc, cx.kkn, cx.t1, cx.sg) = (
            f[0], f[1], f[2], f[3], f[4], f[5], f[6], f[7], f[8], f[9], f[10], f[11])
        (cx.sq, cx.at, cx.rt, cx.kt, cx.bt, cx.ktm, cx.btm, cx.vsb, cx.rkr, cx.yn, cx.NN, cx.NNT) = (
            b[0], b[1], b[2], b[3], b[4], b[5], b[6], b[7], b[8], b[9], b[10], b[11])
        cx.SC, cx.PPh, cx.smb = self.SC, self.PPh, self.smb
        cx.yps, cx.XU, cx.xo, cx.dH, cx.do = self.psf[0], self.psf[2], 0, self.psf[3], 0
        if setB:
            B = self.l1B
            cx.at, cx.rt, cx.ktm, cx.btm, cx.vsb = B["b"]
            cx.Epos, cx.sg, cx.r = B["f"]
            cx.SC, cx.PPh, cx.smb = B["SC"], B["PPh"], B["smb"]
            cx.yps, cx.xo, cx.do = self.psf[1], 256, 64
        cx.Hs = self.st1.ap[:, hp * 64:(hp + 1) * 64]
        cx.Ht = self.st1h[hp]
        sm = self.sm
        cx.e_sm = [sm[6], sm[7], sm[8], sm[9], sm[10], sm[11], sm[12]]
        cx.e_yn, cx.e_vT, cx.e_sqy, cx.e_bv, cx.e_tmp = cx.yn, cx.sq, cx.t1, cx.k, cx.sgm
        cx.e_pT = [Buf(self.psb[0].ap[:, 0:512], self.psb[0].tok), Buf(self.psb[1].ap[:, 0:512], self.psb[1].tok)]
        if setB:
            E = self.l1E
            cx.e_sm = E["sm"]
            cx.e_yn, cx.e_vT = E["b"]
            cx.e_sqy, cx.e_bv, cx.e_tmp = E["f"]
            cx.e_pT = [Buf(self.psf[4].ap[:, :].bitcast(BF16)[:, 0:512], self.psf[4].tok),
                       Buf(self.psf[5].ap[:, :].bitcast(BF16)[:, 0:512], self.psf[5].tok)]
        return cx

    def l1_pre(self, cx):
        kb = self.kb
        d = self.dram
        pv, pd = self.pv1, self.pd1
        C = self.DECAY_C
        hp = cx.hp
        bq, r, k, sgm, icl, cs, Epos, Eneg, Eexc, kkn, t1, sg = (cx.bq, cx.r, cx.k, cx.sgm, cx.icl, cx.cs, cx.Epos,
                                                                 cx.Eneg, cx.Eexc, cx.kkn, cx.t1, cx.sg)
        sq, at, rt, kt, bt, ktm, btm, vsb, rkr, yn, NN, NNT = (cx.sq, cx.at, cx.rt, cx.kt, cx.bt, cx.ktm, cx.btm,
                                                               cx.vsb, cx.rkr, cx.yn, cx.NN, cx.NNT)
        psf = self.psf
        cols = slice(hp * 128, (hp + 1) * 128)
        pr, pk, pg, pvv, pw, pa = psf[2], psf[3], psf[4], psf[5], psf[0], psf[1]
        wk = self.load_w(d["rw_w_in"][1, :, cols])
        wv = self.load_w(d["rw_w_in"][2, :, cols])
        wr = self.load_w(d["rw_w_in"][0, :, cols])
        wg = self.load_w(d["rw_w_in"][3, :, cols])
        for (w_, m_, p_) in ((wk, self.mix[1], pk), (wr, self.mix[0], pr), (wg, self.mix[3], pg)):
            for kc in range(8):
                kb.mm(p_.ap[:, :], w_.ap[:, kc, :], m_.ap[:, kc, :], kc == 0, kc == 7, R=[w_.tok, m_.tok], W=[p_.tok])
        for sub in range(NSUB):
            for kc in range(8):
                kb.mm(pvv.ap[:, sub * 128:(sub + 1) * 128], self.mix[2].ap[:, kc, sub * 128:(sub + 1) * 128],
                      wv.ap[:, kc, :], kc == 0, kc == 7, R=[wv.tok, self.mix[2].tok], W=[pvv.tok])
        kb.mm(pw.ap[:, :], self.w2b.ap[0:64, cols], self.hid.ap[0:64, 0:512], True, True,
              R=[self.w2b.tok, self.hid.tok], W=[pw.tok])
        kb.mm(pa.ap[:, :], self.a2b.ap[0:64, cols], self.hid.ap[0:64, 512:1024], True, True,
              R=[self.a2b.tok, self.hid.tok], W=[pa.tok])
        kb.act(r.ap[:, 0:512], pr.ap[:, :], AF.Copy, R=[pr.tok], W=[r.tok])
        kb.act(sg.ap[:, 0:512], pg.ap[:, :], AF.Sigmoid, R=[pg.tok], W=[sg.tok])
        kb.tt(sg.ap[:, 0:512], sg.ap[:, 0:512], pg.ap[:, :], ALU.mult, R=[sg.tok, pg.tok], W=[sg.tok])
        kb.act(k.ap[:, 0:512], pk.ap[:, :], AF.Copy, R=[pk.tok], W=[k.tok])
        kb.act(sq.ap[:, :], pk.ap[:, :], AF.Square, R=[pk.tok, pv.tok], W=[sq.tok], scale=pv.ap[:, 64 + hp:65 + hp])
        kb.act(vsb.ap[:, 0:512], pvv.ap[:, :], AF.Copy, R=[pvv.tok], W=[vsb.tok])
        kb.act(sgm.ap[:, 0:512], pw.ap[:, :], AF.Sigmoid, R=[pw.tok, pv.tok], W=[sgm.tok], bias=pv.ap[:, 48 + hp:49 + hp])
        kb.act(icl.ap[:, 0:512], pa.ap[:, :], AF.Sigmoid, R=[pa.tok, pv.tok], W=[icl.tok], bias=pv.ap[:, 56 + hp:57 + hp])
        pn = psf[0]
        kb.mm(pn.ap[:, :], self.onesbd.ap[:, :], sq.ap[:, :], True, True, R=[self.onesbd.tok, sq.tok], W=[pn.tok])
        kb.act(t1.ap[:, 0:512], pn.ap[:, :], AF.Sqrt, R=[pn.tok], W=[t1.tok])
        kb.ts(t1.ap[:, 0:512], t1.ap[:, 0:512], 1e-12, None, ALU.max, None, R=[t1.tok], W=[t1.tok])
        kb.recip(t1.ap[:, 0:512], t1.ap[:, 0:512], R=[t1.tok], W=[t1.tok])
        kb.stt(kkn.ap[:, 0:512], k.ap[:, 0:512], pv.ap[:, 64 + hp:65 + hp], t1.ap[:, 0:512], ALU.mult, ALU.mult,
               R=[k.tok, pv.tok, t1.tok], W=[kkn.tok])
        kb.scan(cs.ap[:, 0:512], self.rmask.ap[:, :], sgm.ap[:, 0:512], 0.0, ALU.mult, ALU.add,
                R=[self.rmask.tok, sgm.tok], W=[cs.tok])
        kb.tt(Eexc.ap[:, 0:512], cs.ap[:, 0:512], sgm.ap[:, 0:512], ALU.subtract, R=[cs.tok, sgm.tok], W=[Eexc.tok])
        kb.act(Epos.ap[:, 0:512], cs.ap[:, 0:512], AF.Exp, R=[cs.tok], W=[Epos.tok], scale=C)
        kb.act(Eneg.ap[:, 0:512], cs.ap[:, 0:512], AF.Exp, R=[cs.tok], W=[Eneg.tok], scale=-C)
        kb.act(Eexc.ap[:, 0:512], Eexc.ap[:, 0:512], AF.Exp, R=[Eexc.tok], W=[Eexc.tok], scale=C)
        kb.ts(t1.ap[:, 0:512], icl.ap[:, 0:512], pv.ap[:, 72 + hp:73 + hp], pd.ap[:, hp:hp + 1], ALU.mult, ALU.add,
              R=[icl.tok, pv.tok, pd.tok], W=[t1.tok])
        kb.tt(k.ap[:, 0:512], k.ap[:, 0:512], t1.ap[:, 0:512], ALU.mult, R=[k.tok, t1.tok], W=[k.tok])
        kb.tt(bq.ap[:, 0:512], kkn.ap[:, 0:512], icl.ap[:, 0:512], ALU.mult, R=[kkn.tok, icl.tok], W=[bq.tok])
        kb.tt(kt.ap[:, :], k.ap[:, 0:512], Eneg.ap[:, 0:512], ALU.mult, R=[k.tok, Eneg.tok], W=[kt.tok])
        kb.tt(bt.ap[:, :], bq.ap[:, 0:512], Eneg.ap[:, 0:512], ALU.mult, R=[bq.tok, Eneg.tok], W=[bt.tok])
        kb.stt(at.ap[:, 0:512], kkn.ap[:, 0:512], -1.0, Eexc.ap[:, 0:512], ALU.mult, ALU.mult, R=[kkn.tok, Eexc.tok],
               W=[at.tok])
        pbon = psf[1]
        kb.tt(rt.ap[:, 0:512], r.ap[:, 0:512], Epos.ap[:, 0:512], ALU.mult, R=[r.tok, Epos.tok], W=[rt.tok])
        kb.stt(rkr.ap[:, :], r.ap[:, 0:512], pv.ap[:, 80 + hp:81 + hp], k.ap[:, 0:512], ALU.mult, ALU.mult,
               R=[r.tok, pv.tok, k.tok], W=[rkr.tok])
        kb.mm(pbon.ap[:, :], self.onesbd.ap[:, :], rkr.ap[:, :], True, True, R=[self.onesbd.tok, rkr.tok], W=[pbon.tok])
        kb.act(r.ap[:, 0:512], pbon.ap[:, :], AF.Copy, R=[pbon.tok], W=[r.tok])
        Epos3 = Epos.ap[:, 0:512].rearrange("p (c t) -> p c t", t=64)
        wend_bc = Epos3[:, :, 63:64].broadcast_to([128, NCH, 64])
        kb.tt(yn.ap[:, :].rearrange("p (c t) -> p c t", t=64), kt.ap[:, :].rearrange("p (c t) -> p c t", t=64), wend_bc,
              ALU.mult, R=[kt.tok, Epos.tok], W=[yn.tok])
        for sub in range(NSUB):
            kb.tr(self.psb[0].ap[:, sub * 128:(sub + 1) * 128], yn.ap[:, sub * 128:(sub + 1) * 128], self.ident.ap[:, :],
                  R=[yn.tok, self.ident.tok], W=[self.psb[0].tok])
        kb.tt(yn.ap[:, :].rearrange("p (c t) -> p c t", t=64), bt.ap[:, :].rearrange("p (c t) -> p c t", t=64), wend_bc,
              ALU.mult, R=[bt.tok, Epos.tok], W=[yn.tok])
        for sub in range(NSUB):
            kb.tr(self.psb[1].ap[:, sub * 128:(sub + 1) * 128], yn.ap[:, sub * 128:(sub + 1) * 128], self.ident.ap[:, :],
                  R=[yn.tok, self.ident.tok], W=[self.psb[1].tok])
        kb.copy(ktm.ap[:, 0:512], self.psb[0].ap[:, 0:512], R=[self.psb[0].tok], W=[ktm.tok])
        kb.act(btm.ap[:, 0:512], self.psb[1].ap[:, 0:512], AF.Copy, R=[self.psb[1].tok], W=[btm.tok])
        SC, PPh = cx.SC, cx.PPh
        NNs = [(NN, NNT), (rkr, sq)]
        for hl in range(2):
            rs = slice(64 * hl, 64 * hl + 64)
            nn, nnt = NNs[hl]
            for sub in range(NSUB):
                pi = hl * 4 + sub
                X, Y = (psf[0], psf[2]) if pi % 2 == 0 else (psf[3], psf[4])
                tc = slice(sub * 128, (sub + 1) * 128)
                kb.mm(X.ap[:, 0:128], kt.ap[rs, tc], at.ap[rs, tc], True, True, R=[kt.tok, at.tok], W=[X.tok])
                kb.mm(X.ap[:, 128:256], kt.ap[rs, tc], rt.ap[rs, tc], True, True, R=[kt.tok, rt.tok], W=[X.tok])
                kb.mm(X.ap[:, 256:384], bt.ap[rs, tc], rt.ap[rs, tc], True, True, R=[bt.tok, rt.tok], W=[X.tok])
                kb.mm(Y.ap[:, 0:128], bt.ap[rs, tc], at.ap[rs, tc], True, True, R=[bt.tok, at.tok], W=[Y.tok])
                kb.mm(Y.ap[:, 128:256], at.ap[rs, tc], bt.ap[rs, tc], True, True, R=[bt.tok, at.tok], W=[Y.tok])
                kb.tt(SC.ap[:, pi, 0:384], X.ap[:, 0:384], self.M3.ap[:, 0:384], ALU.mult, R=[X.tok, self.M3.tok],
                      W=[SC.tok])
                kb.tt(nn.ap[:, tc], Y.ap[:, 0:128], self.M2.ap[:, 0:128], ALU.mult, R=[Y.tok, self.M2.tok], W=[nn.tok])
                kb.tt(nnt.ap[:, tc], Y.ap[:, 128:256], self.M2.ap[:, 128:256], ALU.mult, R=[Y.tok, self.M2.tok],
                      W=[nnt.tok])
            for j in range(4):
                kb.tt(PPh[hl].ap[:, j * 128:(j + 1) * 128], nn.ap[:, j * 128:(j + 1) * 128], self.ident.ap[:, :],
                      ALU.add, R=[nn.tok, self.ident.tok], W=[PPh[hl].tok])
        ABC = [(psf[3], psf[4], psf[5]), (psf[0], psf[2], psf[1])]
        for lvl in range(5):
            last = lvl == 4
            for hl in range(2):
                nn, nnt = NNs[hl]
                A_, B_, C_ = ABC[hl]
                for j in range(4):
                    tc = slice(j * 128, (j + 1) * 128)
                    if not last:
                        kb.mm(A_.ap[:, tc], nnt.ap[:, tc], nn.ap[:, tc], True, True, R=[nn.tok, nnt.tok], W=[A_.tok])
                    kb.mm(B_.ap[:, tc], nn.ap[:, tc], nnt.ap[:, tc], True, True, R=[nn.tok, nnt.tok], W=[B_.tok])
            for hl in range(2):
                nn, nnt = NNs[hl]
                A_, B_, C_ = ABC[hl]
                kb.act(nnt.ap[:, :], B_.ap[:, :], AF.Copy, R=[B_.tok], W=[nnt.tok])
                if not last:
                    kb.copy(nn.ap[:, :], A_.ap[:, :], R=[A_.tok], W=[nn.tok])
            for hl in range(2):
                nn, nnt = NNs[hl]
                A_, B_, C_ = ABC[hl]
                for j in range(4):
                    tc = slice(j * 128, (j + 1) * 128)
                    kb.mm(C_.ap[:, tc], nnt.ap[:, tc], PPh[hl].ap[:, tc], True, True, R=[nnt.tok, PPh[hl].tok],
                          W=[C_.tok])
            for hl in range(2):
                A_, B_, C_ = ABC[hl]
                kb.tt(PPh[hl].ap[:, 0:512], C_.ap[:, :], PPh[hl].ap[:, 0:512], ALU.add, R=[C_.tok, PPh[hl].tok],
                      W=[PPh[hl].tok])

    def l1_chain(self, cxs):
        kb = self.kb
        VW = 64
        ho = 4 * VW
        for cx in cxs:
            for hl in range(2):
                rs = slice(64 * hl, 64 * hl + 64)
                kb.act(cx.smb.ap[rs, ho + hl * VW:ho + (hl + 1) * VW], cx.Hs[rs, :], AF.Copy, R=[cx.Ht], W=[cx.smb.tok])
        for c in range(NCH):
            sub, half = divmod(c, 2)
            p0 = 64 * half
            ps_ = slice(p0, p0 + 64)
            cc = slice(c * 64, (c + 1) * 64)
            for cx in cxs:
                smb, XU, xo = cx.smb, cx.XU, cx.xo
                for hl in range(2):
                    vv = slice(sub * 128 + hl * 64, sub * 128 + hl * 64 + 64)
                    pi = hl * 4 + sub
                    kb.mm(XU.ap[ps_, xo + hl * VW:xo + (hl + 1) * VW], cx.at.ap[:, cc],
                          smb.ap[:, ho + hl * VW:ho + (hl + 1) * VW], True, False, R=[cx.at.tok, smb.tok], W=[XU.tok])
                    kb.mm(XU.ap[ps_, xo + hl * VW:xo + hl * VW + 64], cx.SC.ap[:, pi, p0:p0 + 64], cx.vsb.ap[:, vv], False,
                          True, R=[cx.SC.tok, cx.vsb.tok], W=[XU.tok])
            for cx in cxs:
                kb.act(cx.smb.ap[ps_, 0:2 * VW], cx.XU.ap[ps_, cx.xo:cx.xo + 2 * VW], AF.Copy, R=[cx.XU.tok],
                       W=[cx.smb.tok])
            for cx in cxs:
                smb, XU, xo = cx.smb, cx.XU, cx.xo
                for hl in range(2):
                    kb.mm(XU.ap[ps_, xo + 2 * VW + hl * VW:xo + 2 * VW + (hl + 1) * VW],
                          cx.PPh[hl].ap[:, sub * 128 + p0:sub * 128 + p0 + 64], smb.ap[:, hl * VW:(hl + 1) * VW], True, True,
                          R=[cx.PPh[hl].tok, smb.tok], W=[XU.tok])
            for cx in cxs:
                kb.copy(cx.smb.ap[ps_, 2 * VW:4 * VW], cx.XU.ap[ps_, cx.xo + 2 * VW:cx.xo + 4 * VW], R=[cx.XU.tok],
                        W=[cx.smb.tok])
            for cx in cxs:
                smb, dH, do = cx.smb, cx.dH, cx.do
                for hl in range(2):
                    rs = slice(64 * hl, 64 * hl + 64)
                    vv = slice(sub * 128 + hl * 64, sub * 128 + hl * 64 + 64)
                    kb.mm(dH.ap[rs, do:do + VW], cx.btm.ap[ps_, vv], smb.ap[ps_, 2 * VW + hl * VW:2 * VW + (hl + 1) * VW], True,
                          False, R=[cx.btm.tok, smb.tok], W=[dH.tok])
                    kb.mm(dH.ap[rs, do:do + 64], cx.ktm.ap[ps_, vv], cx.vsb.ap[ps_, vv], False, True,
                          R=[cx.ktm.tok, cx.vsb.tok], W=[dH.tok])
            for cx in cxs:
                smb, yps = cx.smb, cx.yps
                for hl in range(2):
                    us = slice(2 * VW + hl * VW, 2 * VW + hl * VW + 64)
                    vv = slice(sub * 128 + hl * 64, sub * 128 + hl * 64 + 64)
                    pi = hl * 4 + sub
                    kb.mm(yps.ap[ps_, vv], cx.rt.ap[:, cc], smb.ap[:, ho + hl * VW:ho + (hl + 1) * VW], True, False,
                          R=[cx.rt.tok, smb.tok], W=[yps.tok])
                    kb.mm(yps.ap[ps_, vv], cx.SC.ap[:, pi, 256 + p0:256 + p0 + 64], smb.ap[:, us], False, False,
                          R=[cx.SC.tok, smb.tok], W=[yps.tok])
                    kb.mm(yps.ap[ps_, vv], cx.SC.ap[:, pi, 128 + p0:128 + p0 + 64], cx.vsb.ap[:, vv], False, True,
                          R=[cx.SC.tok, cx.vsb.tok], W=[yps.tok])
            for cx in cxs:
                smb, dH, do = cx.smb, cx.dH, cx.do
                for hl in range(2):
                    rs = slice(64 * hl, 64 * hl + 64)
                    kb.stt(smb.ap[rs, ho + hl * VW:ho + (hl + 1) * VW], cx.Hs[rs, :], cx.Epos.ap[rs, c * 64 + 63:c * 64 + 64],
                           dH.ap[rs, do:do + VW], ALU.mult, ALU.add, R=[cx.Ht, cx.Epos.tok, dH.tok], W=[smb.tok])
            for cx in cxs:
                kb.stt(cx.Hs, cx.Hs, cx.Epos.ap[:, c * 64 + 63:c * 64 + 64], cx.dH.ap[:, cx.do:cx.do + VW], ALU.mult,
                       ALU.add, R=[cx.Ht, cx.Epos.tok, cx.dH.tok], W=[cx.Ht])

    def l1_epi(self, cx):
        kb = self.kb
        pv = self.pv1
        sm = self.sm
        hp = cx.hp
        yps, yn, vsb, sg, r = cx.yps, cx.yn, cx.vsb, cx.sg, cx.r
        s1, s2, mean, msq, var, rms, rstd = sm[6], sm[7], sm[8], sm[9], sm[10], sm[11], sm[12]
        sqy = cx.t1
        y3 = yps.ap[:, :].rearrange("p (g v) -> p g v", v=64)
        kb.emit("vector", lambda e: e.reduce_sum(s1.ap[:, 0:8], y3, mybir.AxisListType.X), R=[yps.tok], W=[s1.tok])
        kb.act(sqy.ap[:, 0:512], yps.ap[:, :], AF.Square, R=[yps.tok], W=[sqy.tok])
        sq3 = sqy.ap[:, 0:512].rearrange("p (g v) -> p g v", v=64)
        kb.emit("vector", lambda e: e.reduce_sum(s2.ap[:, 0:8], sq3, mybir.AxisListType.X), R=[sqy.tok], W=[s2.tok])
        kb.ts(mean.ap[:, 0:8], s1.ap[:, 0:8], 1.0 / 64, None, ALU.mult, None, R=[s1.tok], W=[mean.tok])
        kb.tt(msq.ap[:, 0:8], mean.ap[:, 0:8], mean.ap[:, 0:8], ALU.mult, R=[mean.tok], W=[msq.tok])
        kb.stt(var.ap[:, 0:8], s2.ap[:, 0:8], 1.0 / 64, msq.ap[:, 0:8], ALU.mult, ALU.subtract, R=[s2.tok, msq.tok],
               W=[var.tok])
        kb.act(rms.ap[:, 0:8], var.ap[:, 0:8], AF.Sqrt, R=[var.tok, self.cst.tok], W=[rms.tok], bias=self.cst.ap[:, 2:3])
        kb.recip(rstd.ap[:, 0:8], rms.ap[:, 0:8], R=[rms.tok], W=[rstd.tok])
        for g in range(8):
            kb.ts(yn.ap[:, g * 64:(g + 1) * 64], yps.ap[:, g * 64:(g + 1) * 64], mean.ap[:, g:g + 1], rstd.ap[:, g:g + 1],
                  ALU.subtract, ALU.mult, R=[yps.tok, mean.tok, rstd.tok], W=[yn.tok])
        for sub in range(NSUB):
            kb.tr(self.psb[0].ap[:, sub * 128:(sub + 1) * 128], yn.ap[:, sub * 128:(sub + 1) * 128], self.ident.ap[:, :],
                  R=[yn.tok, self.ident.tok], W=[self.psb[0].tok])
        for sub in range(NSUB):
            kb.tr(self.psb[1].ap[:, sub * 128:(sub + 1) * 128], vsb.ap[:, sub * 128:(sub + 1) * 128], self.ident.ap[:, :],
                  R=[vsb.tok, self.ident.tok], W=[self.psb[1].tok])
        vT = cx.sq
        kb.act(vT.ap[:, :], self.psb[1].ap[:, 0:512], AF.Copy, R=[self.psb[1].tok], W=[vT.tok])
        bv = cx.k
        kb.tt(bv.ap[:, 0:512], r.ap[:, 0:512], vT.ap[:, :], ALU.mult, R=[r.tok, vT.tok], W=[bv.tok])
        kb.stt(cx.sgm.ap[:, 0:512], self.psb[0].ap[:, 0:512], pv.ap[:, 88 + hp:89 + hp], bv.ap[:, 0:512], ALU.mult, ALU.add,
               R=[self.psb[0].tok, pv.tok, bv.tok], W=[cx.sgm.tok])
        kb.stt(self.merged.ap[:, hp, :], cx.sgm.ap[:, 0:512], pv.ap[:, 96 + hp:97 + hp], sg.ap[:, 0:512], ALU.add, ALU.mult,
               R=[cx.sgm.tok, pv.tok, sg.tok], W=[self.merged.tok])

    def l1_epi2(self, cxs):
        kb = self.kb
        pv = self.pv1
        X = mybir.AxisListType.X
        for cx in cxs:
            s1 = cx.e_sm[0]
            y3 = cx.yps.ap[:, :].rearrange("p (g v) -> p g v", v=64)
            kb.emit("vector", lambda e, s1=s1, y3=y3: e.reduce_sum(s1.ap[:, 0:8], y3, X), R=[cx.yps.tok], W=[s1.tok])
            kb.act(cx.e_sqy.ap[:, 0:512], cx.yps.ap[:, :], AF.Square, R=[cx.yps.tok], W=[cx.e_sqy.tok])
        for cx in cxs:
            s1, s2, mean, msq, var, rms, rstd = cx.e_sm
            sq3 = cx.e_sqy.ap[:, 0:512].rearrange("p (g v) -> p g v", v=64)
            kb.emit("vector", lambda e, s2=s2, sq3=sq3: e.reduce_sum(s2.ap[:, 0:8], sq3, X), R=[cx.e_sqy.tok], W=[s2.tok])
            kb.ts(mean.ap[:, 0:8], s1.ap[:, 0:8], 1.0 / 64, None, ALU.mult, None, R=[s1.tok], W=[mean.tok])
            kb.tt(msq.ap[:, 0:8], mean.ap[:, 0:8], mean.ap[:, 0:8], ALU.mult, R=[mean.tok], W=[msq.tok])
            kb.stt(var.ap[:, 0:8], s2.ap[:, 0:8], 1.0 / 64, msq.ap[:, 0:8], ALU.mult, ALU.subtract, R=[s2.tok, msq.tok],
                   W=[var.tok])
        for cx in cxs:
            s1, s2, mean, msq, var, rms, rstd = cx.e_sm
            kb.act(rms.ap[:, 0:8], var.ap[:, 0:8], AF.Sqrt, R=[var.tok, self.cst.tok], W=[rms.tok], bias=self.cst.ap[:, 2:3])
        for cx in cxs:
            s1, s2, mean, msq, var, rms, rstd = cx.e_sm
            kb.recip(rstd.ap[:, 0:8], rms.ap[:, 0:8], R=[rms.tok], W=[rstd.tok])
            y3_ = cx.yps.ap[:, :].rearrange("p (g v) -> p g v", v=64)
            t3_ = cx.e_sqy.ap[:, 0:512].rearrange("p (g v) -> p g v", v=64)
            n3_ = cx.e_yn.ap[:, 0:512].rearrange("p (g v) -> p g v", v=64)
            kb.tt(t3_, y3_, mean.ap[:, 0:8].unsqueeze(2).broadcast_to([128, 8, 64]), ALU.subtract,
                  R=[cx.yps.tok, mean.tok, s2.tok], W=[cx.e_sqy.tok])
            kb.tt(n3_, t3_, rstd.ap[:, 0:8].unsqueeze(2).broadcast_to([128, 8, 64]), ALU.mult,
                  R=[cx.e_sqy.tok, rstd.tok], W=[cx.e_yn.tok])
        for cx in cxs:
            for sub in range(NSUB):
                kb.tr(cx.e_pT[0].ap[:, sub * 128:(sub + 1) * 128], cx.e_yn.ap[:, sub * 128:(sub + 1) * 128],
                      self.ident.ap[:, :], R=[cx.e_yn.tok, self.ident.tok], W=[cx.e_pT[0].tok])
            for sub in range(NSUB):
                kb.tr(cx.e_pT[1].ap[:, sub * 128:(sub + 1) * 128], cx.vsb.ap[:, sub * 128:(sub + 1) * 128],
                      self.ident.ap[:, :], R=[cx.vsb.tok, self.ident.tok], W=[cx.e_pT[1].tok])
        for cx in cxs:
            kb.act(cx.e_vT.ap[:, 0:512], cx.e_pT[1].ap[:, 0:512], AF.Copy, R=[cx.e_pT[1].tok], W=[cx.e_vT.tok])
        for cx in cxs:
            kb.tt(cx.e_bv.ap[:, 0:512], cx.r.ap[:, 0:512], cx.e_vT.ap[:, 0:512], ALU.mult, R=[cx.r.tok, cx.e_vT.tok],
                  W=[cx.e_bv.tok])
            kb.stt(cx.e_tmp.ap[:, 0:512], cx.e_pT[0].ap[:, 0:512], pv.ap[:, 88 + cx.hp:89 + cx.hp], cx.e_bv.ap[:, 0:512],
                   ALU.mult, ALU.add, R=[cx.e_pT[0].tok, pv.tok, cx.e_bv.tok], W=[cx.e_tmp.tok])
            kb.stt(self.merged.ap[:, cx.hp, :], cx.e_tmp.ap[:, 0:512], pv.ap[:, 96 + cx.hp:97 + cx.hp], cx.sg.ap[:, 0:512],
                   ALU.add, ALU.mult, R=[cx.e_tmp.tok, pv.tok, cx.sg.tok], W=[self.merged.tok])

    def l1_front_load(self, T, u1all, u1tok):
        kb = self.kb
        st1 = self.st1
        r, tl = divmod(T, self.NT)
        up = st1.ap[:, 512:520].unsqueeze(2)
        kb.copy(self.uT.ap[:, :, 0:1], up, R=[st1.tok], W=[self.uT.tok])
        src = u1all[tl].ap()[r * 128:(r + 1) * 128, :].rearrange("p (k t) -> p k t", k=8)
        kb.dma("sync", self.uT.ap[:, :, 1:513], src, self.uT.tok, R=[u1tok[tl]], W=[self.uT.tok])
        kb.copy(up, self.uT.ap[:, :, 512:513], R=[self.uT.tok], W=[st1.tok])

    def l1_front_rest(self, T):
        kb = self.kb
        diff = self.merged
        kb.tt(diff.ap[:, 8:16, :], self.uT.ap[:, :, 0:512], self.uT.ap[:, :, 1:513], ALU.subtract,
              R=[self.uT.tok], W=[diff.tok])
        for (wb, ws, ph, fn, lo) in ((self.w1b, self.w1s, self.psf[4], AF.Tanh, 0),
                                     (self.a1b, self.a1s, self.psf[5], AF.Copy, 512)):
            for kc in range(8):
                kb.mm(ph.ap[0:64, :], wb.ap[:, kc, :], self.uT.ap[:, kc, 1:513], kc == 0, False,
                      R=[wb.tok, self.uT.tok], W=[ph.tok])
            for kc in range(8):
                kb.mm(ph.ap[0:64, :], ws.ap[:, kc, :], diff.ap[:, 8 + kc, :], False, kc == 7,
                      R=[ws.tok, diff.tok], W=[ph.tok])
            if fn == AF.Tanh:
                tf = self.f[0]
                kb.act(tf.ap[0:64, 0:512], ph.ap[0:64, :], AF.Sigmoid, R=[ph.tok], W=[tf.tok], scale=2.0)
                kb.ts(self.hid.ap[:, lo:lo + 512], tf.ap[0:64, 0:512], 2.0, -1.0, ALU.mult, ALU.add, R=[tf.tok],
                      W=[self.hid.tok])
            else:
                kb.act(self.hid.ap[:, lo:lo + 512], ph.ap[0:64, :], fn, R=[ph.tok], W=[self.hid.tok])
        for p_ in range(4):
            self.l1_mix(p_, self.mix[p_])

    def layer1_tile_hp(self, T, NG, u1all, u1tok, mgloc, mgtok, after_pre=None):
        kb = self.kb
        NT = self.NT
        r, tl = divmod(T, NT)
        if T == 0:
            self.l1_front_load(0, u1all, u1tok)
            self.l1_front_rest(0)
            if NG > 1:
                self.l1_front_load(1, u1all, u1tok)
        cxA = self.l1_ctx(0, False)
        cxB = self.l1_ctx(1, True)
        self.l1_pre(cxA)
        self.l1_pre(cxB)
        if after_pre is not None:
            after_pre()
        self.l1_chain([cxA, cxB])
        self.l1_epi2([cxA, cxB])
        dst = mgloc[T].ap().rearrange("p (j t) -> p j t", j=2)
        kb.dma("sync", dst, self.merged.ap[:, 0:2, :], mgtok[T], R=[self.merged.tok], W=[mgtok[T]])
        if T + 1 < NG:
            self.l1_front_rest(T + 1)
            if T + 2 < NG:
                self.l1_front_load(T + 2, u1all, u1tok)

    def build_comm(self):
        kb = self.kb
        NT = self.NT
        NG = 4 * NT
        self.declare()
        self.alloc_common()
        self.consts()
        self.l0_setup()
        self.l0_begin()
        self.l1_setup()
        self.comm_setup()
        u1loc = [self.nc.dram_tensor(f"u1loc{t}", [128, 8 * 512], BF16) for t in range(NT)]
        u1all = [self.nc.dram_tensor(f"u1all{t}", [512, 8 * 512], BF16) for t in range(NT)]
        mgloc = [self.nc.dram_tensor(f"mgloc{q}", [128, 2 * 512], BF16) for q in range(NG)]
        mgall = [self.nc.dram_tensor(f"mgall{q}", [512, 2 * 512], BF16) for q in range(NG)]
        u1t = [kb.tok(f"u1loc{t}") for t in range(NT)]
        mgt = [kb.tok(f"mgloc{q}") for q in range(NG)]
        u1a = [None] * NT
        mga = [None] * NG
        u0loc = [self.nc.dram_tensor(f"u0loc{t}", [128, 8 * 512], BF16) for t in range(NT)]
        u0all = [self.nc.dram_tensor(f"u0all{t}", [512, 8 * 512], BF16) for t in range(NT)]
        NCK = NG
        m0loc = [self.nc.dram_tensor(f"m0loc{c}", [128, 4 * 512], BF16) for c in range(NCK)]
        m0all = [self.nc.dram_tensor(f"m0all{c}", [512, 4 * 512], BF16) for c in range(NCK)]
        u0t = [kb.tok(f"u0loc{t}") for t in range(NT)]
        m0t = [kb.tok(f"m0loc{c}") for c in range(NCK)]
        u0a = [None] * NT
        m0a = [None] * NCK
        w1f_ = self.wout1.ap[:, :, :].rearrange("p a b -> p (a b)").bitcast(F32)
        xsA = self.xs
        xsB = [Buf(w1f_[:, i * 1024:(i + 1) * 1024], kb.tok(f"xsB{i}")) for i in range(NSUB)]
        xbufs = [xsA, xsB]
        m0f_ = self.mix[0].ap[:, :, :].rearrange("p a b -> p (a b)")
        xn2_p0 = Buf(m0f_[:, 0:1024], kb.tok("xn2p0"))
        self.norm_xn2 = xn2_p0
        self.load_x(0, xsA)
        if NT > 1:
            self.load_x(1, xsB)
        for tile in range(NT):
            self.xs = xbufs[tile % 2]
            self.gain = self.pv0
            self.gain_off = 88
            self.norm_T(tile, 0)
            dst = u0loc[tile].ap().rearrange("p (k t) -> p k t", k=8)
            kb.dma("sync", dst, self.uT.ap[:, :, 0:512], u0t[tile], R=[self.uT.tok], W=[u0t[tile]])
            u0a[tile] = self.cc_gather_dram(u0loc[tile], u0all[tile], u0t[tile], f"g0u{tile}")
            if tile + 2 < NT:
                self.load_x(tile + 2, xbufs[tile % 2])
        self.xs = xsA
        self.norm_xn2 = None
        for bf in self.rgB_f[0:3]:
            kb.absorb(bf.tok, [xn2_p0.tok])
        for bf in self.hgB_b + [self.hgB_f[10]]:
            kb.absorb(bf.tok, [b_.tok for b_ in xsB])
        kb.absorb(self.wout1.tok, [b_.tok for b_ in xsB])
        scf = self.SC.ap[:, :, :].rearrange("p a b -> p (a b)")
        m3f = self.mix[3].ap[:, :, :].rearrange("p a b -> p (a b)")
        xslots = [Buf(scf[:, i * 1024:(i + 1) * 1024].rearrange("p (a b) -> p a b", b=128), kb.tok(f"wsx{i}"))
                  for i in range(3)]
        xslots += [Buf(m3f[:, 2048 + i * 1024:2048 + (i + 1) * 1024].rearrange("p (a b) -> p a b", b=128),
                       kb.tok(f"wsx{3 + i}")) for i in range(2)]
        self.wslots_active = self.wslot + xslots[0:4]
        pend = []
        def p1_load(T_):
            r_, tl_ = divmod(T_, NT)
            src_ = u0all[tl_].ap()[r_ * 128:(r_ + 1) * 128, :].rearrange("p (k t) -> p k t", k=8)
            kb.dma("sync", self.uT.ap[:, :, 0:512], src_, self.uT.tok, R=[u0a[tl_]], W=[self.uT.tok])

        p1_load(0)
        for T in range(NG):
            r, tl = divmod(T, NT)
            self.l0_rg2()
            self.l0_hg2((lambda T_=T: p1_load(T_ + 1)) if T + 1 < NG else None)
            nblk = -(-16 // NG)
            for cbw in range(T * nblk, min(16, (T + 1) * nblk)):
                self.load_wout0(cbw)
            ck = T
            dst = m0loc[ck].ap().rearrange("p (j t) -> p j t", j=4)
            kb.dma("sync", dst[:, 0:2, :], self.merged.ap[:, 0:2, :], m0t[ck], R=[self.merged.tok], W=[m0t[ck]])
            kb.dma("sync", dst[:, 2:4, :], self.merged.ap[:, 8:10, :], m0t[ck], R=[self.merged.tok], W=[m0t[ck]])
            for ck_ in pend:
                m0a[ck_] = self.cc_gather_dram(m0loc[ck_], m0all[ck_], m0t[ck_], f"g0m{ck_}")
            pend = [ck]
        for ck_ in pend:
            m0a[ck_] = self.cc_gather_dram(m0loc[ck_], m0all[ck_], m0t[ck_], f"g0m{ck_}")
        self.wslots_active = None
        kb.absorb(self.SC.tok, [b_.tok for b_ in xslots[0:3]])
        kb.absorb(self.mix[3].tok, [xslots[3].tok])
        for i in range(4):
            kb.absorb(self.mix[i].tok, [bf.tok for bf in self.rgB_f] + [self.rgB_xcb.tok])
        kb.absorb(self.wout1.tok, [bf.tok for bf in self.hgB_b] + [self.hgB_f[10].tok])
        Lp = [(self.mix[0], self.mix[1]), (self.mix[2], self.mix[3])]
        xn2_p2 = Buf(self.wout1.ap[:, :, :].rearrange("p a b -> p (a b)")[:, 0:1024], kb.tok("xn2p2"))
        xn2_p2.tok.r = dict(self.wout1.tok.r)
        xn2_p2.tok.w = self.wout1.tok.w
        self.norm_xn2 = xn2_p2

        def p2_load(tile, q):
            La, Lb = Lp[q % 2]
            ck = q * NT + tile
            for r in range(4):
                src = m0all[ck].ap()[r * 128:(r + 1) * 128, :].rearrange("p (j t) -> p j t", j=4)
                qn = "sync" if q % 2 == 0 else "scalar"
                kb.dma(qn, La.ap[:, 2 * r:2 * r + 2, :], src[:, 0:2, :], La.tok, R=[m0a[ck]], W=[La.tok])
                kb.dma(qn, Lb.ap[:, 2 * r:2 * r + 2, :], src[:, 2:4, :], Lb.tok, R=[m0a[ck]], W=[Lb.tok])

        def p2_select(tile, q):
            mq = self.msk.ap[:, 8 + q:9 + q]
            for (Lx, lo) in ((Lp[q % 2][0], 0), (Lp[q % 2][1], 8)):
                if q == 0:
                    kb.ts(self.merged.ap[:, lo:lo + 8, :], Lx.ap[:, :, :], mq, None, ALU.mult, None,
                          R=[Lx.tok, self.msk.tok], W=[self.merged.tok])
                else:
                    kb.stt(self.merged.ap[:, lo:lo + 8, :], Lx.ap[:, :, :], mq, self.merged.ap[:, lo:lo + 8, :], ALU.mult,
                           ALU.add, R=[Lx.tok, self.msk.tok, self.merged.tok], W=[self.merged.tok])

        p2_load(0, 0)
        p2_load(0, 1)
        for tile in range(NT):
            self.load_x(tile)
            p2_select(tile, 0)
            p2_load(tile, 2)
            p2_select(tile, 1)
            p2_load(tile, 3)
            p2_select(tile, 2)
            p2_select(tile, 3)
            if tile + 1 < NT:
                p2_load(tile + 1, 0)
                p2_load(tile + 1, 1)
            self.out_proj(tile, 16, self.wout0, self.postbc0)
            for sub in range(NSUB):
                g = tile * NSUB + sub
                kb.dma("sync", self.x1buf.ap()[g * 128:(g + 1) * 128, :], self.xs[sub].ap, self.x1tok[tile],
                       R=[self.xs[sub].tok], W=[self.x1tok[tile]])
            self.gain = self.pv1
            self.gain_off = 104
            self.norm_T(tile, 1)
            dst = u1loc[tile].ap().rearrange("p (k t) -> p k t", k=8)
            kb.dma("sync", dst, self.uT.ap[:, :, 1:513], u1t[tile], R=[self.uT.tok], W=[u1t[tile]])
            u1a[tile] = self.cc_gather_dram(u1loc[tile], u1all[tile], u1t[tile], f"gu{tile}")
        self.norm_xn2 = None
        kb.absorb(self.wout1.tok, [xn2_p2.tok])
        carved = (self.l1B["b"] + [self.l1B["SC"], self.l1B["smb"]] + self.l1B["PPh"] + self.l1B["f"]
                  + self.l1E["b"] + self.l1E["f"])
        for bf in carved:
            bf.tok.r = dict(self.wout0.tok.r)
            bf.tok.w = self.wout0.tok.w
        kb.memset(self.l1B["smb"].ap, 0.0, [self.l1B["smb"].tok])
        self.w1s = Buf(self.WA.ap[:, :, :].rearrange("p a b -> p (a b)")[:, 0:512].rearrange("p (a b) -> p a b", b=64),
                       kb.tok("w1s"))
        self.a1s = Buf(self.WX.ap[:, :, :].rearrange("p a b -> p (a b)")[:, 0:512].rearrange("p (a b) -> p a b", b=64),
                       kb.tok("a1s"))
        for (ws, wb, own, c0) in ((self.w1s, self.w1b, self.WA, 32), (self.a1s, self.a1b, self.WX, 40)):
            ws.tok.r = dict(own.tok.r)
            ws.tok.w = own.tok.w
            kb.tt(ws.ap, wb.ap[:, :, :], self.pv1.ap[:, c0:c0 + 8].unsqueeze(2).broadcast_to([128, 8, 64]), ALU.mult,
                  R=[wb.tok, self.pv1.tok], W=[ws.tok])
        def p4_load(tl, q):
            L = self.mix[2 * (q % 2)]
            for r in range(4):
                Tg = q * NT + tl
                src = mgall[Tg].ap()[r * 128:(r + 1) * 128, :].rearrange("p (j t) -> p j t", j=2)
                kb.dma("sync" if q % 2 == 0 else "scalar", L.ap[:, 2 * r:2 * r + 2, :], src, L.tok, R=[mga[Tg]],
                       W=[L.tok])

        def p4_first():
            p4_load(0, 0)
            p4_load(0, 1)

        pend = []
        for T in range(NG):
            self.layer1_tile_hp(T, NG, u1all, u1a, mgloc, mgt, p4_first if T == NG - 1 else None)
            nblk = -(-8 // NG)
            for cbw in range(T * nblk, min(8, (T + 1) * nblk)):
                self.load_wout1(cbw)
            for q_ in pend:
                mga[q_] = self.cc_gather_dram(mgloc[q_], mgall[q_], mgt[q_], f"gm{q_}")
            pend = [T]
        for q_ in pend:
            mga[q_] = self.cc_gather_dram(mgloc[q_], mgall[q_], mgt[q_], f"gm{q_}")
        def p4_select(tl, q):
            L = self.mix[2 * (q % 2)]
            mq = self.msk.ap[:, 8 + q:9 + q]
            if q == 0:
                kb.ts(self.merged.ap[:, 0:8, :], L.ap[:, :, :], mq, None, ALU.mult, None, R=[L.tok, self.msk.tok],
                      W=[self.merged.tok])
            else:
                kb.stt(self.merged.ap[:, 0:8, :], L.ap[:, :, :], mq, self.merged.ap[:, 0:8, :], ALU.mult, ALU.add,
                       R=[L.tok, self.msk.tok, self.merged.tok], W=[self.merged.tok])

        for tl in range(NT):
            self.load_x1(tl)
            p4_select(tl, 0)
            p4_load(tl, 2)
            p4_select(tl, 1)
            p4_load(tl, 3)
            p4_select(tl, 2)
            p4_select(tl, 3)
            if tl + 1 < NT:
                p4_load(tl + 1, 0)
                p4_load(tl + 1, 1)
            self.out_proj(tl, 8, self.wout1, self.postbc1)
            self.store_out(tl)
        toks = [b.tok for b in self.xs]
        self.kb.wait_all("sync", toks)
        self.kb.replay()

    def load_x1(self, tile):
        kb = self.kb
        for sub in range(NSUB):
            g = tile * NSUB + sub
            kb.dma("sync", self.xs[sub].ap, self.x1buf.ap()[g * 128:(g + 1) * 128, :], self.xs[sub].tok,
                   R=[self.x1tok[tile]], W=[self.xs[sub].tok])

    def declare(self):
        TS = self.TS
        self.din("x", [TS, D])
        self.dout("out", [TS, D])
        self.din("ident", [128, 128])
        self.din("maskU", [128, 64])
        if self.comm:
            self.din("msk", [128, 16])
            self.din("eye2", [128, 64])
        if self.do_l0:
            self.din("rmask", [128, 512])
            self.din("pv0", [128, self.PV0_COLS])
            self.din("wa_bd", [128, 8, 128])
            self.din("wx_bd", [128, 8, 128])
            self.din("w_in0", [D, 1536 if self.comm else 6144])
            self.din("w_out0", [2048, D])
            self.din("post0_bc", [128, D])
            if not self.comm:
                self.din("st0_in", [128, 1056])
                self.dout("st0_out", [128, 1056])
        if self.do_l1:
            if not self.do_l0:
                self.din("rmask", [128, 512])
            self.din("pv1", [128, self.PV1_COLS])
            self.din("M3", [128, 384])
            self.din("M2", [128, 256])
            self.din("onesbd", [128, 128])
            self.din("rw_w_in", [4, D, 256 if self.comm else D])
            self.din("w_out1", [D, D])
            self.din("rw_w1", [D, 64])
            self.din("rw_a1", [D, 64])
            self.din("rw_w2", [64, 256 if self.comm else D])
            self.din("rw_a2", [64, 256 if self.comm else D])
            self.din("post1_bc", [128, D])
            if not self.comm:
                self.din("st1_in", [128, 520])
                self.dout("st1_out", [128, 520])

    def build(self):
        self.declare()
        self.alloc_common()
        self.consts()
        if self.do_l0:
            self.l0_setup()
            self.l0_begin()
        if self.do_l1:
            self.l1_setup()
        for tile in range(self.NT):
            self.load_x(tile)
            if self.do_l0:
                self.layer0_tile(tile)
            if self.do_l1:
                self.layer1_tile(tile)
            self.store_out(tile)
        toks = [b.tok for b in self.xs]
        if self.do_l0:
            self.l0_end()
            toks.append(self.st0.tok)
        if self.do_l1:
            self.l1_end()
            toks.append(self.st1.tok)
        self.kb.wait_all("sync", toks)
        self.kb.replay()


def build_program(TS, do_l0=True, do_l1=True, comm=False):
    nc = bass.Bass("TRN2", target_bir_lowering=False)
    p = Prog(nc, TS, do_l0, do_l1, comm)
    if comm:
        p.build_comm()
    else:
        p.build()
    return nc


def fm(v):
    return np.ascontiguousarray(np.asarray(v, np.float32).reshape(8, 128).T)


def host_consts():
    p = np.arange(128)[:, None]
    t = np.arange(64)[None, :]
    c = {}
    c["ident"] = np.eye(128, dtype=np.float32)
    c["maskU"] = ((p % 64) <= t).astype(np.float32)
    rm = np.ones((128, 512), np.float32)
    rm[:, ::64] = 0.0
    c["rmask"] = rm
    return c


def host_l0(inp):
    o = {}
    pv = np.zeros((128, Prog.PV0_COLS), np.float32)
    cw = np.asarray(inp["rg_conv_w"][0], np.float32)
    for k in range(4):
        pv[:, k * 8:(k + 1) * 8] = fm(cw[k])
    pv[:, 32:40] = fm(inp["rg_conv_b"][0])
    pv[:, 40:48] = fm(inp["rg_b_a"][0])
    pv[:, 48:56] = fm(inp["rg_b_x"][0])
    pv[:, 56:64] = fm(inp["rg_lambda"][0])
    for r in range(3):
        pv[:, 64 + r * 8:64 + (r + 1) * 8] = fm(inp["hg_lower_bounds"][r])
    pv[:, 88:96] = fm(inp["pre_norm"][0])
    pv[:, 96] = np.asarray(inp["hg_out_norm"][0], np.float32)
    o["pv0"] = pv
    for nm, key in (("wa_bd", "rg_w_a"), ("wx_bd", "rg_w_x")):
        w = np.asarray(inp[key][0], np.float32)
        bd = np.zeros((128, 8, 128), np.float32)
        for cb in range(8):
            bd[0:64, cb, 0:64] = w[2 * cb]
            bd[64:128, cb, 64:128] = w[2 * cb + 1]
        o[nm] = bd
    o["w_in0"] = np.ascontiguousarray(np.asarray(inp["ab_w_in"][0], np.float32))
    o["w_out0"] = np.ascontiguousarray(np.asarray(inp["ab_w_out"][0], np.float32))
    o["post0_bc"] = np.ascontiguousarray(np.broadcast_to(np.asarray(inp["post_norm"][0], np.float32)[None, :], (128, D)))
    return o


def host_l1(inp):
    o = {}
    pv = np.zeros((128, Prog.PV1_COLS), np.float32)
    mu = np.asarray(inp["rw_mu"][0], np.float32)
    for p_ in range(6):
        pv[:, p_ * 8:(p_ + 1) * 8] = fm(mu[p_])
    pv[:, 48:56] = fm(inp["rw_w0"][0])
    pv[:, 56:64] = fm(inp["rw_a0"][0])
    pv[:, 64:72] = fm(inp["rw_k_k"][0])
    pv[:, 72:80] = fm(inp["rw_k_a"][0])
    pv[:, 80:88] = fm(np.asarray(inp["rw_r_k"][0], np.float32).reshape(-1))
    pv[:, 88:96] = fm(inp["rw_ln_w"][0])
    pv[:, 96:104] = fm(inp["rw_ln_b"][0])
    pv[:, 104:112] = fm(inp["pre_norm"][1])
    o["pv1"] = pv
    j = np.arange(128)[:, None]
    t = np.arange(128)[None, :]
    same = (j // 64) == (t // 64)
    MS = (same & (j < t)).astype(np.float32)
    MI = (same & (j <= t)).astype(np.float32)
    o["M3"] = np.ascontiguousarray(np.concatenate([MS, MI, MI], axis=1))
    o["M2"] = np.ascontiguousarray(np.concatenate([MS, MS.T], axis=1))
    o["onesbd"] = same.astype(np.float32)
    o["rw_w_in"] = np.ascontiguousarray(np.asarray(inp["rw_w_in"][0], np.float32))
    o["w_out1"] = np.ascontiguousarray(np.asarray(inp["rw_w_out"][0], np.float32))
    for nm in ("rw_w1", "rw_a1", "rw_w2", "rw_a2"):
        o[nm] = np.ascontiguousarray(np.asarray(inp[nm][0], np.float32))
    o["post1_bc"] = np.ascontiguousarray(np.broadcast_to(np.asarray(inp["post_norm"][1], np.float32)[None, :], (128, D)))
    return o


N_CORES = 8
SEG = 2048
MODE = os.environ.get("MK_MODE", "comm")
_NC_CACHE = {}


def _program(TS, comm=False):
    if (TS, comm) not in _NC_CACHE:
        _NC_CACHE[(TS, comm)] = build_program(TS, True, True, comm)
    return _NC_CACHE[(TS, comm)]


def host_comm(s, l1=None):
    m = np.zeros((128, 16), np.float32)
    for j in range(3):
        m[:, j] = 1.0 if j < s else 0.0
        m[:, 4 + j] = 1.0 if j == s - 1 else 0.0
    for q in range(4):
        m[:, 8 + q] = 1.0 if q == s else 0.0
    o = {"msk": m, "eye2": np.ascontiguousarray(np.tile(np.eye(64, dtype=np.float32), (2, 1)))}
    if l1 is not None:
        w = l1["w_in0"]
        o["w_in0"] = np.ascontiguousarray(np.concatenate(
            [w[:, g * 1024 + 256 * s:g * 1024 + 256 * s + 256] for g in range(6)], axis=1))
        pv0 = l1["pv0"].copy()
        for base in (0, 8, 16, 24, 32, 40, 48, 56, 64, 72, 80):
            pv0[:, base:base + 2] = l1["pv0"][:, base + 2 * s:base + 2 * s + 2]
        o["pv0"] = pv0
        for nm in ("wa_bd", "wx_bd"):
            bd = l1[nm].copy()
            bd[:, 0:2, :] = l1[nm][:, 2 * s:2 * s + 2, :]
            o[nm] = bd
        c0 = 256 * s
        o["rw_w_in"] = np.ascontiguousarray(l1["rw_w_in"][:, :, c0:c0 + 256])
        o["rw_w2"] = np.ascontiguousarray(l1["rw_w2"][:, c0:c0 + 256])
        o["rw_a2"] = np.ascontiguousarray(l1["rw_a2"][:, c0:c0 + 256])
        pv = l1["pv1"].copy()
        for base in (48, 56, 64, 72, 80, 88, 96):
            pv[:, base:base + 2] = l1["pv1"][:, base + 2 * s:base + 2 * s + 2]
        o["pv1"] = pv
    return o


def kernel(**inp):
    inp = {k: np.asarray(v) for k, v in inp.items()}
    x = np.asarray(inp["x"], np.float32)
    B, S, _ = x.shape
    base = {}
    base.update(host_consts())
    base.update(host_l0(inp))
    base.update(host_l1(inp))
    if MODE == "comm":
        nc = _program(SEG, True)
        maps = []
        for c in range(N_CORES):
            b, sgi = divmod(c, 4)
            m = dict(base)
            m.update(host_comm(sgi, base))
            m["x"] = np.ascontiguousarray(x[b, sgi * SEG:(sgi + 1) * SEG])
            maps.append(m)
        res = run_bass_kernel_spmd(nc, maps, core_ids=list(range(N_CORES)))
        out = np.empty((B, S, D), np.float32)
        for c in range(N_CORES):
            b, sgi = divmod(c, 4)
            out[b, sgi * SEG:(sgi + 1) * SEG] = res.results[c]["out"]
        return out
    if MODE == "fused":
        nc = _program(S)
        maps = []
        for c in range(N_CORES):
            m = dict(base)
            m["x"] = np.ascontiguousarray(x[c // 4])
            m["st0_in"] = np.zeros((128, 1056), np.float32)
            m["st1_in"] = np.zeros((128, 520), np.float32)
            maps.append(m)
        res = run_bass_kernel_spmd(nc, maps, core_ids=list(range(N_CORES)))
        out = np.empty((B, S, D), np.float32)
        for c in range(N_CORES):
            b, s = divmod(c, 4)
            out[b, s * SEG:(s + 1) * SEG] = res.results[c]["out"][s * SEG:(s + 1) * SEG]
        return out
    nc = _program(SEG)
    nseg = S // SEG
    st0 = [np.zeros((128, 1056), np.float32) for _ in range(N_CORES)]
    st1 = [np.zeros((128, 520), np.float32) for _ in range(N_CORES)]
    res = None
    for launch in range(nseg):
        maps = []
        for c in range(N_CORES):
            b, s = divmod(c, nseg)
            m = dict(base)
            m["x"] = np.ascontiguousarray(x[b, s * SEG:(s + 1) * SEG])
            m["st0_in"] = st0[c]
            m["st1_in"] = st1[c]
            maps.append(m)
        res = run_bass_kernel_spmd(nc, maps, core_ids=list(range(N_CORES)))
        for c in range(N_CORES):
            b, s = divmod(c, nseg)
            if s + 1 < nseg:
                st0[c + 1] = np.ascontiguousarray(res.results[c]["st0_out"])
                st1[c + 1] = np.ascontiguousarray(res.results[c]["st1_out"])
    out = np.empty((B, S, D), np.float32)
    for c in range(N_CORES):
        b, s = divmod(c, nseg)
        out[b, s * SEG:(s + 1) * SEG] = res.results[c]["out"]
    return out
```

```python
import os
import numpy as np
from contextlib import ExitStack
import concourse.bass as bass
import concourse.mybir as mybir
from concourse.bass_utils import run_bass_kernel_spmd

F32 = mybir.dt.float32
BF16 = mybir.dt.bfloat16
AF = mybir.ActivationFunctionType
ALU = mybir.AluOpType

D = 1024
EPS = 1e-6
GN_EPS = 64e-5
TT = 512
NSUB = 4
NCH = 8


class Tok:
    __slots__ = ("name", "w", "r", "dsem", "dcount", "q")

    def __init__(self, name):
        self.q = None
        self.name = name
        self.w = None
        self.r = {}
        self.dsem = None
        self.dcount = 0


class Eng:
    def __init__(self, name, sem):
        self.name = name
        self.sem = sem
        self.count = 0
        self.waited = {}
        self.prog = []


class KB:
    ENGS = ("tensor", "vector", "scalar", "gpsimd", "sync")

    def __init__(self, nc):
        self.nc = nc
        self.es = ExitStack()
        self.engs = {n: Eng(n, self.sem("e_" + n)) for n in self.ENGS}
        self.ntok = 0

    def sem(self, name):
        return self.es.enter_context(self.nc.semaphore(name))

    def sb(self, name, shape, dt):
        return self.es.enter_context(self.nc.sbuf_tensor("s_" + name, list(shape), dt))

    def ps(self, name, shape, dt):
        return self.es.enter_context(self.nc.psum_tensor("p_" + name, list(shape), dt))

    def tok(self, name=None):
        self.ntok += 1
        return Tok(name or f"t{self.ntok}")

    def _deps(self, en, R, W):
        e = self.engs[en]
        deps = []
        for t in R:
            if t.w is not None:
                deps.append(t.w)
        for t in W:
            if t.w is not None:
                deps.append(t.w)
            deps.extend(t.r.values())
        waits = {}
        noself = os.environ.get("MK_NOSELF", "0") == "1"
        for (sem, val, src) in deps:
            if src == "tensor" and en == "tensor":
                continue
            if noself and src == en:
                continue
            key = id(sem)
            if e.waited.get(key, 0) < val:
                e.waited[key] = val
                waits[key] = (sem, val)
        return list(waits.values())

    def emit(self, en, fn, R=(), W=()):
        e = self.engs[en]
        waits = self._deps(en, R, W)
        e.count += 1
        me = (e.sem, e.count, en)
        e.prog.append((waits, fn, (e.sem, 1)))
        for t in R:
            t.r[id(e.sem)] = me
        for t in W:
            t.w = me
            t.r = {}

    def dma(self, q, out, in_, tok, R=(), W=()):
        if out.dtype != in_.dtype:
            q = "gpsimd"
        if tok.q is None:
            tok.q = q
        elif tok.q != q:
            assert out.dtype == in_.dtype or tok.q == "gpsimd", tok.name
            q = tok.q
        e = self.engs[q]
        waits = self._deps(q, R, W)
        if tok.dsem is None:
            tok.dsem = self.sem("d_" + tok.name)
        tok.dcount += 16
        me = (tok.dsem, tok.dcount, "dma")
        e.prog.append((waits, lambda eng: eng.dma_start(out=out, in_=in_), (tok.dsem, 16)))
        for t in R:
            t.r[id(tok.dsem)] = me
        for t in W:
            t.w = me
            t.r = {}

    def collective(self, bin_, bout, tin, tout, name):
        e = self.engs["gpsimd"]
        waits = self._deps("gpsimd", [tin], [tout])
        if getattr(self, "cc_sem", None) is None:
            self.cc_sem = self.sem("cc_all")
            self.cc_count = 0
        sem = self.cc_sem
        self.cc_count += 1
        me = (sem, self.cc_count, "cc")
        groups = [[0, 1, 2, 3], [4, 5, 6, 7]]
        e.prog.append((waits, lambda eng: eng.collective_compute(
            "AllGather", ALU.bypass, replica_groups=groups, ins=[bin_.ap().opt()], outs=[bout.ap().opt()]), (sem, 1)))
        tin.r[id(sem)] = me
        tout.w = me
        tout.r = {}

    def absorb(self, owner, toks):
        for t in toks:
            deps = list(t.r.values()) + ([t.w] if t.w is not None else [])
            for v in deps:
                k = id(v[0])
                if k not in owner.r or owner.r[k][1] < v[1]:
                    owner.r[k] = v

    def wait_all(self, en, toks):
        e = self.engs[en]
        waits = self._deps(en, [], toks)
        e.prog.append((waits, None, None))

    def replay(self):
        nc = self.nc
        with nc.Block() as block:
            for en in self.ENGS:
                prog = self.engs[en].prog

                def body(eng, prog=prog):
                    for waits, fn, inc in prog:
                        for s, v in waits:
                            eng.wait_ge(s, v)
                        if fn is not None:
                            ins = fn(eng)
                            if inc is not None:
                                ins.then_inc(inc[0], inc[1])

                getattr(block, en)(body)

    def mm(self, out, lhsT, rhs, start, stop, R, W):
        self.emit("tensor", lambda e: e.matmul(out, lhsT, rhs, start=start, stop=stop), R, W)

    def tr(self, out, in_, ident, R, W):
        self.emit("tensor", lambda e: e.transpose(out, in_, ident), R, W)

    def act(self, out, in_, func, R, W, bias=None, scale=1.0, accum=None):
        kw = {}
        if bias is not None:
            kw["bias"] = bias
        if accum is not None:
            kw["accum_out"] = accum
        self.emit("scalar", lambda e: e.activation(out, in_, func, scale=scale, **kw), R, W)

    def ts(self, out, in0, s1, s2, op0, op1, R, W, en="vector"):
        if s2 is None:
            self.emit(en, lambda e: e.tensor_scalar(out, in0, s1, None, op0), R, W)
        else:
            self.emit(en, lambda e: e.tensor_scalar(out, in0, s1, s2, op0, op1), R, W)

    def tt(self, out, in0, in1, op, R, W, en="vector"):
        self.emit(en, lambda e: e.tensor_tensor(out, in0, in1, op), R, W)

    def stt(self, out, in0, scalar, in1, op0, op1, R, W, en="vector"):
        self.emit(en, lambda e: e.scalar_tensor_tensor(out, in0, scalar, in1, op0, op1), R, W)

    def scan(self, out, d0, d1, init, op0, op1, R, W):
        self.emit("vector", lambda e: e.tensor_tensor_scan(out, d0, d1, init, op0, op1), R, W)

    def recip(self, out, in_, R, W):
        self.emit("vector", lambda e: e.reciprocal(out, in_), R, W)

    def copy(self, out, in_, R, W, en="vector"):
        self.emit(en, lambda e: e.tensor_copy(out, in_), R, W)

    def memset(self, ap, val, W, en="vector"):
        self.emit(en, lambda e: e.memset(ap, val), (), W)


class Buf:
    def __init__(self, ap, tok):
        self.ap = ap
        self.tok = tok

    def __getitem__(self, idx):
        return self.ap[idx]


class Prog:
    def __init__(self, nc, TS, do_l0=True, do_l1=True, comm=False):
        self.comm = comm
        self.nc = nc
        self.TS = TS
        self.NT = TS // TT
        self.kb = KB(nc)
        self.dram = {}
        self.do_l0 = do_l0
        self.do_l1 = do_l1

    def din(self, name, shape, dt=F32):
        t = self.nc.dram_tensor(name, list(shape), dt, kind="ExternalInput").ap()
        self.dram[name] = t
        return t

    def dout(self, name, shape, dt=F32):
        t = self.nc.dram_tensor(name, list(shape), dt, kind="ExternalOutput").ap()
        self.dram[name] = t
        return t

    def sbuf(self, name, shape, dt):
        return Buf(self.kb.sb(name, shape, dt), self.kb.tok(name))

    def psum(self, name, shape, dt):
        return Buf(self.kb.ps(name, shape, dt), self.kb.tok(name))

    def alloc_common(self):
        kb = self.kb
        self.xs = [Buf(None, kb.tok(f"xs{g}")) for g in range(NSUB)]
        self.xs_t = kb.sb("xs", [128, NSUB, D], F32)
        for g in range(NSUB):
            self.xs[g].ap = self.xs_t[:, g, :]
        self.psb = [self.psum(f"psb{i}", [128, 1024], BF16) for i in range(2)]
        self.psf = [self.psum(f"psf{i}", [128, 512], F32) for i in range(6)]
        self.ident = self.sbuf("ident", [128, 128], BF16)
        self.maskU = self.sbuf("maskU", [128, 64], BF16)
        self.f = [self.sbuf(f"f{i}", [128, 516], F32) for i in range(12)]
        self.b = [self.sbuf(f"b{i}", [128, 512], BF16) for i in range(12)]
        self.sm = [self.sbuf(f"sm{i}", [128, 16], F32) for i in range(14)]
        self.xn = self.sbuf("xn", [128, 1024], BF16)
        self.uT = self.sbuf("uT", [128, 8, 516], BF16)
        self.merged = self.sbuf("merged", [128, 16, 512], BF16)
        self.wout0 = self.sbuf("wout0", [128, 16, 1024], BF16)
        self.wout1 = self.sbuf("wout1", [128, 8, 1024], BF16)
        self.NWS = 4
        self.wslot = [self.sbuf(f"ws{i}", [128, 8, 128], BF16) for i in range(self.NWS)]
        self.wi = 0
        self.cst = self.sbuf("cst", [128, 4], F32)
        self.postbc0 = self.sbuf("postbc0", [128, 1024], F32)
        self.postbc1 = self.sbuf("postbc1", [128, 1024], F32)
        self.dmaq = 0

    def q(self):
        self.dmaq += 1
        return "sync" if self.dmaq % 2 else "gpsimd"

    def load_w(self, src_ap):
        slots = getattr(self, "wslots_active", None) or self.wslot
        s = slots[self.wi % len(slots)]
        self.wi += 1
        self.kb.dma(self.q(), s.ap[:, :, :], src_ap.rearrange("(kc p) n -> p kc n", p=128), s.tok, W=[s.tok])
        return s

    def consts(self):
        kb = self.kb
        d = self.dram
        kb.dma("sync", self.ident.ap[:, :], d["ident"][:, :], self.ident.tok, W=[self.ident.tok])
        kb.dma("sync", self.maskU.ap[:, :], d["maskU"][:, :], self.maskU.tok, W=[self.maskU.tok])
        kb.memset(self.cst.ap[:, 0:1], EPS, [self.cst.tok])
        kb.memset(self.cst.ap[:, 1:2], 1.0, [self.cst.tok])
        kb.memset(self.cst.ap[:, 2:3], GN_EPS, [self.cst.tok])

    def load_x(self, tile, xs=None):
        kb = self.kb
        x = self.dram["x"]
        xs = xs or self.xs
        for sub in range(NSUB):
            g = tile * NSUB + sub
            kb.dma("sync", xs[sub].ap, x[g * 128:(g + 1) * 128, :], xs[sub].tok, W=[xs[sub].tok])

    def store_out(self, tile):
        kb = self.kb
        o = self.dram["out"]
        for sub in range(NSUB):
            g = tile * NSUB + sub
            kb.dma("sync", o[g * 128:(g + 1) * 128, :], self.xs[sub].ap, self.xs[sub].tok, R=[self.xs[sub].tok])

    def norm_T(self, tile, off):
        kb = self.kb
        ss, rms, rstd = self.sm[0], self.sm[1], self.sm[2]
        xn2 = getattr(self, "norm_xn2", None)
        xns = [self.xn, xn2 if xn2 is not None else self.xn]
        for sub in range(NSUB):
            xin = self.xs[sub]
            kb.act(self.xn.ap[:, :], xin.ap, AF.Square, R=[xin.tok], W=[self.xn.tok, ss.tok],
                   accum=ss.ap[:, sub:sub + 1])
        kb.act(rms.ap[:, 0:NSUB], ss.ap[:, 0:NSUB], AF.Sqrt, R=[ss.tok, self.cst.tok], W=[rms.tok],
               scale=1.0 / D, bias=self.cst.ap[:, 0:1])
        kb.recip(rstd.ap[:, 0:NSUB], rms.ap[:, 0:NSUB], R=[rms.tok], W=[rstd.tok])
        for sub in range(NSUB):
            xin = self.xs[sub]
            xn = xns[sub % 2]
            kb.ts(xn.ap[:, 0:1024], xin.ap, rstd.ap[:, sub:sub + 1], None, ALU.mult, None,
                  R=[xin.tok, rstd.tok], W=[xn.tok])
            pb = self.psb[sub % 2]
            for kc in range(8):
                kb.tr(pb.ap[:, kc * 128:(kc + 1) * 128], xn.ap[:, kc * 128:(kc + 1) * 128],
                      self.ident.ap[:, :], R=[xn.tok, self.ident.tok], W=[pb.tok])
            kb.tt(self.uT.ap[:, :, off + sub * 128: off + (sub + 1) * 128],
                  pb.ap[:, :].rearrange("p (k t) -> p k t", t=128),
                  self.gain.ap[:, self.gain_off:self.gain_off + 8].unsqueeze(2).broadcast_to([128, 8, 128]),
                  ALU.mult, R=[pb.tok, self.gain.tok], W=[self.uT.tok])

    def out_proj(self, tile, ncb, wout, postbc):
        kb = self.kb
        ss, rms, rstd = self.sm[3], self.sm[4], self.sm[5]
        tmpo = self.f[0]
        junk = self.b[0]
        for sub in range(NSUB):
            g = sub
            pss = [self.psf[0], self.psf[1]] if sub % 2 == 0 else [self.psf[2], self.psf[3]]
            for dh in range(2):
                for cb in range(ncb):
                    kb.mm(pss[dh].ap[:, :], self.merged.ap[:, cb, sub * 128:(sub + 1) * 128],
                          wout.ap[:, cb, dh * 512:(dh + 1) * 512], cb == 0, cb == ncb - 1,
                          R=[self.merged.tok, wout.tok], W=[pss[dh].tok])
            for dh in range(2):
                kb.act(junk.ap[:, 0:512], pss[dh].ap[:, :], AF.Square, R=[pss[dh].tok],
                       W=[junk.tok, ss.tok], accum=ss.ap[:, dh:dh + 1])
            kb.tt(ss.ap[:, 2:3], ss.ap[:, 0:1], ss.ap[:, 1:2], ALU.add, R=[ss.tok], W=[ss.tok])
            kb.act(rms.ap[:, 0:1], ss.ap[:, 2:3], AF.Sqrt, R=[ss.tok, self.cst.tok], W=[rms.tok],
                   scale=1.0 / D, bias=self.cst.ap[:, 0:1])
            kb.recip(rstd.ap[:, 0:1], rms.ap[:, 0:1], R=[rms.tok], W=[rstd.tok])
            for dh in range(2):
                kb.stt(tmpo.ap[:, 0:512], pss[dh].ap[:, :], rstd.ap[:, 0:1],
                       postbc.ap[:, dh * 512:(dh + 1) * 512], ALU.mult, ALU.mult,
                       R=[pss[dh].tok, rstd.tok, postbc.tok], W=[tmpo.tok])
                kb.tt(self.xs[g].ap[:, dh * 512:(dh + 1) * 512], self.xs[g].ap[:, dh * 512:(dh + 1) * 512],
                      tmpo.ap[:, 0:512], ALU.add, R=[tmpo.tok, self.xs[g].tok], W=[self.xs[g].tok])

    PV0_COLS = 104

    def l0_setup(self):
        kb = self.kb
        d = self.dram
        self.pv0 = self.sbuf("pv0", [128, self.PV0_COLS], F32)
        self.pd0 = self.sbuf("pd0", [128, 64], F32)
        self.WA = self.sbuf("WA", [128, 8, 128], BF16)
        self.WX = self.sbuf("WX", [128, 8, 128], BF16)
        self.st0 = self.sbuf("st0", [128, 1056], F32)
        self.rmask = self.sbuf("rmask", [128, 512], BF16)
        kb.dma("sync", self.pv0.ap[:, :], d["pv0"][:, :], self.pv0.tok, W=[self.pv0.tok])
        kb.dma("gpsimd", self.WA.ap[:, :, :], d["wa_bd"][:, :, :], self.WA.tok, W=[self.WA.tok])
        kb.dma("gpsimd", self.WX.ap[:, :, :], d["wx_bd"][:, :, :], self.WX.tok, W=[self.WX.tok])
        if self.comm:
            kb.memset(self.st0.ap[:, :], 0.0, [self.st0.tok])
        else:
            kb.dma("sync", self.st0.ap[:, :], d["st0_in"][:, :], self.st0.tok, W=[self.st0.tok])
        kb.dma("sync", self.rmask.ap[:, :], d["rmask"][:, :], self.rmask.tok, W=[self.rmask.tok])
        pv, pd = self.pv0, self.pd0
        kb.act(pd.ap[:, 32:40], pv.ap[:, 56:64], AF.Exp, R=[pv.tok], W=[pd.tok], scale=-1.0)
        kb.act(pd.ap[:, 32:40], pd.ap[:, 32:40], AF.Ln, R=[pd.tok, self.cst.tok], W=[pd.tok],
               bias=self.cst.ap[:, 1:2])
        kb.ts(pd.ap[:, 0:8], pd.ap[:, 32:40], -8.0, None, ALU.mult, None, R=[pd.tok], W=[pd.tok])
        kb.ts(pd.ap[:, 8:16], pd.ap[:, 32:40], -16.0, None, ALU.mult, None, R=[pd.tok], W=[pd.tok])
        kb.act(pd.ap[:, 32:56], pv.ap[:, 64:88], AF.Exp, R=[pv.tok], W=[pd.tok])
        kb.tt(pd.ap[:, 56:64], pd.ap[:, 32:40], pd.ap[:, 40:48], ALU.add, R=[pd.tok], W=[pd.tok])
        kb.tt(pd.ap[:, 56:64], pd.ap[:, 56:64], pd.ap[:, 48:56], ALU.add, R=[pd.tok], W=[pd.tok])
        kb.recip(pd.ap[:, 56:64], pd.ap[:, 56:64], R=[pd.tok], W=[pd.tok])
        kb.tt(pd.ap[:, 16:24], pd.ap[:, 40:48], pd.ap[:, 56:64], ALU.mult, R=[pd.tok], W=[pd.tok])
        kb.ts(pd.ap[:, 24:32], pd.ap[:, 16:24], -1.0, 1.0, ALU.mult, ALU.add, R=[pd.tok], W=[pd.tok])

    def load_wout0(self, cb):
        w = self.dram["w_out0"].rearrange("(cb p) n -> p cb n", p=128)
        self.kb.dma("gpsimd", self.wout0.ap[:, cb, :], w[:, cb, :], self.wout0.tok, W=[self.wout0.tok])

    def load_wout1(self, cb):
        w = self.dram["w_out1"].rearrange("(cb p) n -> p cb n", p=128)
        self.kb.dma("gpsimd", self.wout1.ap[:, cb, :], w[:, cb, :], self.wout1.tok, W=[self.wout1.tok])

    def l0_begin(self):
        kb = self.kb
        d = self.dram
        if not self.comm:
            for cb in range(16):
                self.load_wout0(cb)
        kb.dma("sync", self.postbc0.ap[:, :], d["post0_bc"][:, :], self.postbc0.tok, W=[self.postbc0.tok])

    def l0_rg_proj(self, cb, p1=False, banks=None):
        kb = self.kb
        d = self.dram
        px, pg = banks if banks is not None else (self.psf[2], self.psf[3])
        GW = 256 if self.comm else 1024
        wx = self.load_w(d["w_in0"][:, cb * 128:(cb + 1) * 128])
        for kc in range(8):
            kb.mm(px.ap[:, :], wx.ap[:, kc, :], self.uT.ap[:, kc, 0:512], kc == 0, kc == 7,
                  R=[wx.tok, self.uT.tok], W=[px.tok])
        if not p1:
            wg = self.load_w(d["w_in0"][:, GW + cb * 128:GW + (cb + 1) * 128])
            for kc in range(8):
                kb.mm(pg.ap[:, :], wg.ap[:, kc, :], self.uT.ap[:, kc, 0:512], kc == 0, kc == 7,
                      R=[wg.tok, self.uT.tok], W=[pg.tok])

    def l0_rg(self, cb, p1=False, after_evac=None):
        kb = self.kb
        d = self.dram
        pv, pd = self.pv0, self.pd0
        f, b = self.f, self.b
        xr, xc, r, ig, a, a2, gi, bb, h, sg = f[1], f[2], f[3], f[4], f[5], f[6], f[7], f[8], f[9], f[10]
        xcb = b[0]
        px, pg, pa, pi = self.psf[2], self.psf[3], self.psf[4], self.psf[5]
        st = self.st0
        kb.copy(xr.ap[:, 0:3], st.ap[:, cb * 3:cb * 3 + 3], R=[st.tok], W=[xr.tok])
        kb.act(xr.ap[:, 3:515], px.ap[:, :], AF.Copy, R=[px.tok], W=[xr.tok])
        if not p1:
            kb.act(sg.ap[:, 0:512], pg.ap[:, :], AF.Silu, R=[pg.tok], W=[sg.tok])
        if after_evac is not None:
            after_evac()
        kb.copy(st.ap[:, cb * 3:cb * 3 + 3], xr.ap[:, 512:515], R=[xr.tok], W=[st.tok])
        kb.ts(xc.ap[:, 0:512], xr.ap[:, 3:515], pv.ap[:, 24 + cb:25 + cb], pv.ap[:, 32 + cb:33 + cb],
              ALU.mult, ALU.add, R=[xr.tok, pv.tok], W=[xc.tok])
        for k in (2, 1, 0):
            kb.stt(xc.ap[:, 0:512], xr.ap[:, k:k + 512], pv.ap[:, k * 8 + cb:k * 8 + cb + 1], xc.ap[:, 0:512],
                   ALU.mult, ALU.add, R=[xr.tok, pv.tok, xc.tok], W=[xc.tok])
        kb.act(xcb.ap[:, :], xc.ap[:, 0:512], AF.Copy, R=[xc.tok], W=[xcb.tok])
        kb.mm(pa.ap[:, :], self.WA.ap[:, cb, :], xcb.ap[:, :], True, True, R=[self.WA.tok, xcb.tok], W=[pa.tok])
        kb.mm(pi.ap[:, :], self.WX.ap[:, cb, :], xcb.ap[:, :], True, True, R=[self.WX.tok, xcb.tok], W=[pi.tok])
        if p1:
            rs_ = self.sm[13]
            kb.act(r.ap[:, 0:512], pa.ap[:, :], AF.Sigmoid, R=[pa.tok, pv.tok], W=[r.tok, rs_.tok],
                   bias=pv.ap[:, 40 + cb:41 + cb], accum=rs_.ap[:, 0:1])
            kb.tt(self.dec0.ap[:, cb:cb + 1], self.dec0.ap[:, cb:cb + 1], rs_.ap[:, 0:1], ALU.add,
                  R=[rs_.tok, self.dec0.tok], W=[self.dec0.tok])
        else:
            kb.act(r.ap[:, 0:512], pa.ap[:, :], AF.Sigmoid, R=[pa.tok, pv.tok], W=[r.tok], bias=pv.ap[:, 40 + cb:41 + cb])
        kb.act(ig.ap[:, 0:512], pi.ap[:, :], AF.Sigmoid, R=[pi.tok, pv.tok], W=[ig.tok],
               bias=pv.ap[:, 48 + cb:49 + cb])
        kb.act(a.ap[:, 0:512], r.ap[:, 0:512], AF.Exp, R=[r.tok, pd.tok], W=[a.tok], scale=pd.ap[:, cb:cb + 1])
        kb.act(a2.ap[:, 0:512], r.ap[:, 0:512], AF.Exp, R=[r.tok, pd.tok], W=[a2.tok],
               scale=pd.ap[:, 8 + cb:9 + cb])
        kb.act(a2.ap[:, 0:512], a2.ap[:, 0:512], AF.Relu, R=[a2.tok, self.cst.tok], W=[a2.tok], scale=-1.0,
               bias=self.cst.ap[:, 1:2])
        kb.act(a2.ap[:, 0:512], a2.ap[:, 0:512], AF.Sqrt, R=[a2.tok], W=[a2.tok])
        kb.tt(gi.ap[:, 0:512], ig.ap[:, 0:512], xc.ap[:, 0:512], ALU.mult, R=[ig.tok, xc.tok], W=[gi.tok])
        kb.tt(bb.ap[:, 0:512], a2.ap[:, 0:512], gi.ap[:, 0:512], ALU.mult, R=[a2.tok, gi.tok], W=[bb.tok])
        kb.scan(h.ap[:, 0:512], a.ap[:, 0:512], bb.ap[:, 0:512], st.ap[:, 24 + cb:25 + cb], ALU.mult, ALU.add,
                R=[a.tok, bb.tok, st.tok], W=[h.tok])
        kb.copy(st.ap[:, 24 + cb:25 + cb], h.ap[:, 511:512], R=[h.tok], W=[st.tok])
        if p1:
            return
        kb.tt(self.merged.ap[:, cb, :], h.ap[:, 0:512], sg.ap[:, 0:512], ALU.mult, R=[h.tok, sg.tok],
              W=[self.merged.tok])

    def l0_rg2(self):
        kb = self.kb
        pv, pd = self.pv0, self.pd0
        st = self.st0
        f = self.f
        setA = dict(t=[f[i] for i in range(1, 11)], xcb=self.b[0],
                    ps=[self.psf[2], self.psf[3], self.psf[4], self.psf[5]])
        psb0 = Buf(self.psb[0].ap[:, :].bitcast(F32), self.psb[0].tok)
        psb1 = Buf(self.psb[1].ap[:, :].bitcast(F32), self.psb[1].tok)
        setB = dict(t=self.rgB_f, xcb=self.rgB_xcb, ps=[self.psf[0], self.psf[1], psb0, psb1])
        U = [setA, setB]
        for u in range(2):
            self.l0_rg_proj(u, False, (U[u]["ps"][0], U[u]["ps"][1]))

        def T(u, i):
            return U[u]["t"][i]
        for u in range(2):
            px, pg, pa, pi = U[u]["ps"]
            kb.copy(T(u, 0).ap[:, 0:3], st.ap[:, u * 3:u * 3 + 3], R=[st.tok], W=[T(u, 0).tok])
            kb.act(T(u, 0).ap[:, 3:515], px.ap[:, 0:512], AF.Copy, R=[px.tok], W=[T(u, 0).tok])
        for u in range(2):
            kb.copy(st.ap[:, u * 3:u * 3 + 3], T(u, 0).ap[:, 512:515], R=[T(u, 0).tok], W=[st.tok])
            kb.ts(T(u, 1).ap[:, 0:512], T(u, 0).ap[:, 3:515], pv.ap[:, 24 + u:25 + u], pv.ap[:, 32 + u:33 + u],
                  ALU.mult, ALU.add, R=[T(u, 0).tok, pv.tok], W=[T(u, 1).tok])
        for k in (2, 1, 0):
            for u in range(2):
                kb.stt(T(u, 1).ap[:, 0:512], T(u, 0).ap[:, k:k + 512], pv.ap[:, k * 8 + u:k * 8 + u + 1],
                       T(u, 1).ap[:, 0:512], ALU.mult, ALU.add, R=[T(u, 0).tok, pv.tok, T(u, 1).tok], W=[T(u, 1).tok])
        for u in range(2):
            xcb = U[u]["xcb"]
            kb.act(xcb.ap[:, 0:512], T(u, 1).ap[:, 0:512], AF.Copy, R=[T(u, 1).tok], W=[xcb.tok])
        for u in range(2):
            px, pg, pa, pi = U[u]["ps"]
            xcb = U[u]["xcb"]
            kb.mm(pa.ap[:, 0:512], self.WA.ap[:, u, :], xcb.ap[:, 0:512], True, True, R=[self.WA.tok, xcb.tok], W=[pa.tok])
            kb.mm(pi.ap[:, 0:512], self.WX.ap[:, u, :], xcb.ap[:, 0:512], True, True, R=[self.WX.tok, xcb.tok], W=[pi.tok])
        for u in range(2):
            px, pg, pa, pi = U[u]["ps"]
            kb.act(T(u, 2).ap[:, 0:512], pa.ap[:, 0:512], AF.Sigmoid, R=[pa.tok, pv.tok], W=[T(u, 2).tok],
                   bias=pv.ap[:, 40 + u:41 + u])
            kb.act(T(u, 3).ap[:, 0:512], pi.ap[:, 0:512], AF.Sigmoid, R=[pi.tok, pv.tok], W=[T(u, 3).tok],
                   bias=pv.ap[:, 48 + u:49 + u])
            kb.act(T(u, 9).ap[:, 0:512], pg.ap[:, 0:512], AF.Sigmoid, R=[pg.tok], W=[T(u, 9).tok])
            kb.tt(T(u, 9).ap[:, 0:512], T(u, 9).ap[:, 0:512], pg.ap[:, 0:512], ALU.mult, R=[T(u, 9).tok, pg.tok],
                  W=[T(u, 9).tok])
        for u in range(2):
            kb.act(T(u, 4).ap[:, 0:512], T(u, 2).ap[:, 0:512], AF.Exp, R=[T(u, 2).tok, pd.tok], W=[T(u, 4).tok],
                   scale=pd.ap[:, u:u + 1])
            kb.act(T(u, 5).ap[:, 0:512], T(u, 2).ap[:, 0:512], AF.Exp, R=[T(u, 2).tok, pd.tok], W=[T(u, 5).tok],
                   scale=pd.ap[:, 8 + u:9 + u])
        for u in range(2):
            kb.tt(T(u, 6).ap[:, 0:512], T(u, 3).ap[:, 0:512], T(u, 1).ap[:, 0:512], ALU.mult,
                  R=[T(u, 3).tok, T(u, 1).tok], W=[T(u, 6).tok])
            kb.act(T(u, 5).ap[:, 0:512], T(u, 5).ap[:, 0:512], AF.Relu, R=[T(u, 5).tok, self.cst.tok], W=[T(u, 5).tok],
                   scale=-1.0, bias=self.cst.ap[:, 1:2])
        for u in range(2):
            kb.act(T(u, 5).ap[:, 0:512], T(u, 5).ap[:, 0:512], AF.Sqrt, R=[T(u, 5).tok], W=[T(u, 5).tok])
        for u in range(2):
            kb.tt(T(u, 7).ap[:, 0:512], T(u, 5).ap[:, 0:512], T(u, 6).ap[:, 0:512], ALU.mult,
                  R=[T(u, 5).tok, T(u, 6).tok], W=[T(u, 7).tok])
            kb.scan(T(u, 8).ap[:, 0:512], T(u, 4).ap[:, 0:512], T(u, 7).ap[:, 0:512], st.ap[:, 24 + u:25 + u], ALU.mult,
                    ALU.add, R=[T(u, 4).tok, T(u, 7).tok, st.tok], W=[T(u, 8).tok])
            kb.copy(st.ap[:, 24 + u:25 + u], T(u, 8).ap[:, 511:512], R=[T(u, 8).tok], W=[st.tok])
        for u in range(2):
            kb.tt(self.merged.ap[:, u, :], T(u, 8).ap[:, 0:512], T(u, 9).ap[:, 0:512], ALU.mult,
                  R=[T(u, 8).tok, T(u, 9).tok], W=[self.merged.tok])

    def l0_hg_proj(self, hh, p1=False):
        kb = self.kb
        d = self.dram
        pq, pf, pg, pvv = self.psf[2], self.psf[3], self.psf[4], self.psf[5]
        GW = 256 if self.comm else 1024
        wf = self.load_w(d["w_in0"][:, 3 * GW + hh * 128:3 * GW + (hh + 1) * 128])
        wi = self.load_w(d["w_in0"][:, 4 * GW + hh * 128:4 * GW + (hh + 1) * 128])
        plist = [(wf, pf)]
        if not p1:
            wq = self.load_w(d["w_in0"][:, 2 * GW + hh * 128:2 * GW + (hh + 1) * 128])
            wg = self.load_w(d["w_in0"][:, 5 * GW + hh * 128:5 * GW + (hh + 1) * 128])
            plist += [(wq, pq), (wg, pg)]
        for (w_, p_) in plist:
            for kc in range(8):
                kb.mm(p_.ap[:, :], w_.ap[:, kc, :], self.uT.ap[:, kc, 0:512], kc == 0, kc == 7,
                      R=[w_.tok, self.uT.tok], W=[p_.tok])
        for sub in range(NSUB):
            for kc in range(8):
                kb.mm(pvv.ap[:, sub * 128:(sub + 1) * 128], self.uT.ap[:, kc, sub * 128:(sub + 1) * 128],
                      wi.ap[:, kc, :], kc == 0, kc == 7, R=[wi.tok, self.uT.tok], W=[pvv.tok])

    def l0_hg(self, hh, p1=False, after_evac=None):
        kb = self.kb
        d = self.dram
        pv, pd = self.pv0, self.pd0
        f, b, sm = self.f, self.b, self.sm
        q, sg, sf, lf, kv, g, gm, Ep, Em, tmp, Sp = f[1], f[2], f[3], f[4], f[5], f[6], f[7], f[8], f[9], f[10], f[11]
        qt, kt, vsb, ktT, scT, Spb, on = b[1], b[2], b[3], b[4], b[5], b[6], b[7]
        emid, ech, ss2, rms2, rstd2 = sm[6], sm[7], sm[8], sm[9], sm[10]
        pq, pf, pg, pvv, po, pm = self.psf[2], self.psf[3], self.psf[4], self.psf[5], self.psf[0], self.psf[1]
        st = self.st0
        S = st.ap[:, 32 + hh * 128:32 + (hh + 1) * 128]
        if not p1:
            kb.act(q.ap[:, 0:512], pq.ap[:, :], AF.Silu, R=[pq.tok], W=[q.tok])
            kb.act(sg.ap[:, 0:512], pg.ap[:, :], AF.Silu, R=[pg.tok], W=[sg.tok])
        kb.act(sf.ap[:, 0:512], pf.ap[:, :], AF.Sigmoid, R=[pf.tok], W=[sf.tok])
        kb.act(vsb.ap[:, :], pvv.ap[:, :], AF.Copy, R=[pvv.tok], W=[vsb.tok])
        if after_evac is not None:
            after_evac()
        kb.ts(sf.ap[:, 0:512], sf.ap[:, 0:512], pd.ap[:, 24 + hh:25 + hh], pd.ap[:, 16 + hh:17 + hh], ALU.mult, ALU.add,
              R=[sf.tok, pd.tok], W=[sf.tok])
        if p1:
            ls_ = self.sm[13]
            kb.act(lf.ap[:, 0:512], sf.ap[:, 0:512], AF.Ln, R=[sf.tok], W=[lf.tok, ls_.tok], accum=ls_.ap[:, 0:1])
            kb.tt(self.dec0.ap[:, 8 + hh:9 + hh], self.dec0.ap[:, 8 + hh:9 + hh], ls_.ap[:, 0:1], ALU.add,
                  R=[ls_.tok, self.dec0.tok], W=[self.dec0.tok])
        else:
            kb.act(lf.ap[:, 0:512], sf.ap[:, 0:512], AF.Ln, R=[sf.tok], W=[lf.tok])
        kb.ts(kv.ap[:, 0:512], sf.ap[:, 0:512], -1.0, 1.0, ALU.mult, ALU.add, R=[sf.tok], W=[kv.tok])
        kb.scan(g.ap[:, 0:512], self.rmask.ap[:, :], lf.ap[:, 0:512], 0.0, ALU.mult, ALU.add,
                R=[self.rmask.tok, lf.tok], W=[g.tok])
        g3 = g.ap[:, 0:512].rearrange("p (c t) -> p c t", t=64)
        gm3 = gm.ap[:, 0:512].rearrange("p (c t) -> p c t", t=64)
        Ep3 = Ep.ap[:, 0:512].rearrange("p (c t) -> p c t", t=64)
        kb.tt(gm3, g3, g3[:, :, 31:32].broadcast_to([128, NCH, 64]), ALU.subtract, R=[g.tok], W=[gm.tok])
        kb.act(Ep.ap[:, 0:512], gm.ap[:, 0:512], AF.Exp, R=[gm.tok], W=[Ep.tok])
        kb.act(Em.ap[:, 0:512], gm.ap[:, 0:512], AF.Exp, R=[gm.tok], W=[Em.tok], scale=-1.0)
        kb.act(emid.ap[:, 0:NCH].unsqueeze(2), g3[:, :, 31:32], AF.Exp, R=[g.tok], W=[emid.tok])
        if not p1:
            kb.tt(qt.ap[:, :], q.ap[:, 0:512], Ep.ap[:, 0:512], ALU.mult, R=[q.tok, Ep.tok], W=[qt.tok])
        kb.tt(kt.ap[:, :], kv.ap[:, 0:512], Em.ap[:, 0:512], ALU.mult, R=[kv.tok, Em.tok], W=[kt.tok])
        kb.tt(ech.ap[:, 0:NCH - 1].unsqueeze(2), Ep3[:, 0:NCH - 1, 63:64], emid.ap[:, 1:NCH].unsqueeze(2), ALU.mult,
              R=[Ep.tok, emid.tok], W=[ech.tok])
        kb.copy(ech.ap[:, NCH - 1:NCH], Ep.ap[:, 511:512], R=[Ep.tok], W=[ech.tok])
        kte = b[8]
        kb.tt(kte.ap[:, :].rearrange("p (c t) -> p c t", t=64), kt.ap[:, :].rearrange("p (c t) -> p c t", t=64),
              ech.ap[:, 0:NCH].unsqueeze(2).broadcast_to([128, NCH, 64]), ALU.mult, R=[kt.tok, ech.tok], W=[kte.tok])
        pbt = self.psb[0]
        for sub in range(NSUB):
            kb.tr(pbt.ap[:, sub * 128:(sub + 1) * 128], kte.ap[:, sub * 128:(sub + 1) * 128], self.ident.ap[:, :],
                  R=[kte.tok, self.ident.tok], W=[pbt.tok])
        kb.copy(ktT.ap[:, :], pbt.ap[:, 0:512], R=[pbt.tok], W=[ktT.tok])
        dbank = [self.psb[0].ap[:, :].bitcast(F32), self.psb[1].ap[:, :].bitcast(F32)]
        for c in range(NCH):
            sub, half = divmod(c, 2)
            p0 = 64 * half
            kb.mm(dbank[half][:, sub * 128:(sub + 1) * 128], ktT.ap[p0:p0 + 64, sub * 128:(sub + 1) * 128],
                  vsb.ap[p0:p0 + 64, sub * 128:(sub + 1) * 128], True, True, R=[ktT.tok, vsb.tok], W=[self.psb[half].tok])
        SpF = [f[10], f[10], f[10], f[10], f[11], f[11], f[11], f[11]]
        SpA = [b[9], b[9], b[9], b[9], b[10], b[10], b[10], b[10]]
        kb.act(SpF[0].ap[:, 0:128], S, AF.Copy, R=[st.tok, emid.tok], W=[SpF[0].tok], scale=emid.ap[:, 0:1])
        for c in range(NCH):
            sub, half = divmod(c, 2)
            dl = dbank[half][:, sub * 128:(sub + 1) * 128]
            dt_ = self.psb[half].tok
            ec = ech.ap[:, c:c + 1]
            cur = SpF[c].ap[:, (c % 4) * 128:(c % 4 + 1) * 128]
            if c < NCH - 1:
                n_ = c + 1
                kb.stt(SpF[n_].ap[:, (n_ % 4) * 128:(n_ % 4 + 1) * 128], cur, ec, dl, ALU.mult, ALU.add,
                       R=[SpF[c].tok, ech.tok, dt_], W=[SpF[n_].tok])
            else:
                kb.stt(S, cur, ec, dl, ALU.mult, ALU.add, R=[SpF[c].tok, ech.tok, dt_], W=[st.tok])
            if not p1 and c % 4 == 3:
                kb.act(SpA[c].ap[:, :], SpF[c].ap[:, 0:512], AF.Copy, R=[SpF[c].tok], W=[SpA[c].tok])
        if not p1:
            for sub in range(NSUB):
                for half in range(2):
                    c = sub * 2 + half
                    p0 = 64 * half
                    kb.mm(pm.ap[p0:p0 + 64, 0:64], kt.ap[:, c * 64:(c + 1) * 64], qt.ap[:, c * 64:(c + 1) * 64], True, True,
                          R=[kt.tok, qt.tok], W=[pm.tok])
                kb.tt(scT.ap[:, 0:64], pm.ap[:, 0:64], self.maskU.ap[:, :], ALU.mult, R=[pm.tok, self.maskU.tok],
                      W=[scT.tok])
                for half in range(2):
                    c = sub * 2 + half
                    p0 = 64 * half
                    kb.mm(po.ap[p0:p0 + 64, sub * 128:(sub + 1) * 128], scT.ap[p0:p0 + 64, 0:64],
                          vsb.ap[p0:p0 + 64, sub * 128:(sub + 1) * 128], True, False, R=[scT.tok, vsb.tok], W=[po.tok])
                    kb.mm(po.ap[p0:p0 + 64, sub * 128:(sub + 1) * 128], qt.ap[:, c * 64:(c + 1) * 64],
                          SpA[c].ap[:, (c % 4) * 128:(c % 4 + 1) * 128], False, True, R=[qt.tok, SpA[c].tok], W=[po.tok])
        if p1:
            return
        for sub in range(NSUB):
            kb.act(self.b[0].ap[:, 0:128], po.ap[:, sub * 128:(sub + 1) * 128], AF.Square, R=[po.tok],
                   W=[self.b[0].tok, ss2.tok], accum=ss2.ap[:, sub:sub + 1])
        kb.act(rms2.ap[:, 0:4], ss2.ap[:, 0:4], AF.Sqrt, R=[ss2.tok, self.cst.tok], W=[rms2.tok], scale=1.0 / 128,
               bias=self.cst.ap[:, 0:1])
        kb.recip(rstd2.ap[:, 0:4], rms2.ap[:, 0:4], R=[rms2.tok], W=[rstd2.tok])
        pbo = self.psb[1]
        for sub in range(NSUB):
            kb.ts(on.ap[:, sub * 128:(sub + 1) * 128], po.ap[:, sub * 128:(sub + 1) * 128], rstd2.ap[:, sub:sub + 1], None,
                  ALU.mult, None, R=[po.tok, rstd2.tok], W=[on.tok])
        for sub in range(NSUB):
            kb.tr(pbo.ap[:, sub * 128:(sub + 1) * 128], on.ap[:, sub * 128:(sub + 1) * 128], self.ident.ap[:, :],
                  R=[on.tok, self.ident.tok], W=[pbo.tok])
        kb.stt(self.merged.ap[:, 8 + hh, :], pbo.ap[:, 0:512], pv.ap[:, 96:97], sg.ap[:, 0:512], ALU.mult, ALU.mult,
               R=[pbo.tok, pv.tok, sg.tok], W=[self.merged.tok])

    def l0_hg2(self, after_proj=None):
        kb = self.kb
        d = self.dram
        pv, pd = self.pv0, self.pd0
        f, b, sm = self.f, self.b, self.sm
        st = self.st0
        psf = self.psf
        A = dict(q=f[1], sg=f[2], sf=f[3], lf=f[4], kv=f[5], g=f[6], gm=f[7], Ep=f[8], Em=f[9], SpF=[f[10], f[11]],
                 qt=b[1], kt=b[2], vsb=b[3], ktT=b[4], scT=b[5], on=b[7], kte=b[8], SpA=[b[9], b[10]], junk=b[0],
                 emid=sm[6], ech=sm[7], ss2=sm[8], rms2=sm[9], rstd2=sm[10],
                 pbt=self.psb[0], dl=[psf[2], psf[3]], po=psf[0])
        Bf, Bb, Bs = self.hgB_f, self.hgB_b, self.hgB_sm
        B = dict(q=Bf[0], sg=Bf[1], sf=Bf[2], lf=Bf[3], kv=Bf[4], g=Bf[5], gm=Bf[6], Ep=Bf[7], Em=Bf[8], SpF=[Bf[9], Bf[10]],
                 qt=Bb[0], kt=Bb[1], vsb=Bb[2], ktT=Bb[3], scT=Bb[4], on=Bb[5], kte=Bb[6], SpA=[Bb[7], Bb[8]], junk=Bb[6],
                 emid=Bs[0], ech=Bs[1], ss2=Bs[2], rms2=Bs[3], rstd2=Bs[4],
                 pbt=self.psb[1], dl=[psf[4], psf[5]], po=psf[1])
        U = [A, B]
        pq, pf, pg, pvv = psf[2], psf[3], psf[4], psf[5]
        for u in range(2):
            X = U[u]
            self.l0_hg_proj(u)
            kb.act(X["q"].ap[:, 0:512], pq.ap[:, :], AF.Sigmoid, R=[pq.tok], W=[X["q"].tok])
            kb.act(X["sg"].ap[:, 0:512], pg.ap[:, :], AF.Sigmoid, R=[pg.tok], W=[X["sg"].tok])
            kb.act(X["sf"].ap[:, 0:512], pf.ap[:, :], AF.Sigmoid, R=[pf.tok], W=[X["sf"].tok])
            kb.act(X["vsb"].ap[:, 0:512], pvv.ap[:, :], AF.Copy, R=[pvv.tok], W=[X["vsb"].tok])
            kb.tt(X["q"].ap[:, 0:512], X["q"].ap[:, 0:512], pq.ap[:, :], ALU.mult, R=[X["q"].tok, pq.tok], W=[X["q"].tok])
            kb.tt(X["sg"].ap[:, 0:512], X["sg"].ap[:, 0:512], pg.ap[:, :], ALU.mult, R=[X["sg"].tok, pg.tok],
                  W=[X["sg"].tok])
        if after_proj is not None:
            after_proj()

        def each():
            return [(u, U[u]) for u in range(2)]
        for u, X in each():
            kb.ts(X["sf"].ap[:, 0:512], X["sf"].ap[:, 0:512], pd.ap[:, 24 + u:25 + u], pd.ap[:, 16 + u:17 + u], ALU.mult,
                  ALU.add, R=[X["sf"].tok, pd.tok], W=[X["sf"].tok])
        for u, X in each():
            kb.act(X["lf"].ap[:, 0:512], X["sf"].ap[:, 0:512], AF.Ln, R=[X["sf"].tok], W=[X["lf"].tok])
        for u, X in each():
            kb.ts(X["kv"].ap[:, 0:512], X["sf"].ap[:, 0:512], -1.0, 1.0, ALU.mult, ALU.add, R=[X["sf"].tok], W=[X["kv"].tok])
            kb.scan(X["g"].ap[:, 0:512], self.rmask.ap[:, :], X["lf"].ap[:, 0:512], 0.0, ALU.mult, ALU.add,
                    R=[self.rmask.tok, X["lf"].tok], W=[X["g"].tok])
        for u, X in each():
            g3 = X["g"].ap[:, 0:512].rearrange("p (c t) -> p c t", t=64)
            gm3 = X["gm"].ap[:, 0:512].rearrange("p (c t) -> p c t", t=64)
            kb.tt(gm3, g3, g3[:, :, 31:32].broadcast_to([128, NCH, 64]), ALU.subtract, R=[X["g"].tok], W=[X["gm"].tok])
        for u, X in each():
            g3 = X["g"].ap[:, 0:512].rearrange("p (c t) -> p c t", t=64)
            kb.act(X["Ep"].ap[:, 0:512], X["gm"].ap[:, 0:512], AF.Exp, R=[X["gm"].tok], W=[X["Ep"].tok])
            kb.act(X["Em"].ap[:, 0:512], X["gm"].ap[:, 0:512], AF.Exp, R=[X["gm"].tok], W=[X["Em"].tok], scale=-1.0)
            kb.act(X["emid"].ap[:, 0:NCH].unsqueeze(2), g3[:, :, 31:32], AF.Exp, R=[X["g"].tok], W=[X["emid"].tok])
        for u, X in each():
            Ep3 = X["Ep"].ap[:, 0:512].rearrange("p (c t) -> p c t", t=64)
            kb.tt(X["qt"].ap[:, 0:512], X["q"].ap[:, 0:512], X["Ep"].ap[:, 0:512], ALU.mult, R=[X["q"].tok, X["Ep"].tok],
                  W=[X["qt"].tok])
            kb.tt(X["kt"].ap[:, 0:512], X["kv"].ap[:, 0:512], X["Em"].ap[:, 0:512], ALU.mult, R=[X["kv"].tok, X["Em"].tok],
                  W=[X["kt"].tok])
            kb.tt(X["ech"].ap[:, 0:NCH - 1].unsqueeze(2), Ep3[:, 0:NCH - 1, 63:64], X["emid"].ap[:, 1:NCH].unsqueeze(2),
                  ALU.mult, R=[X["Ep"].tok, X["emid"].tok], W=[X["ech"].tok])
            kb.copy(X["ech"].ap[:, NCH - 1:NCH], X["Ep"].ap[:, 511:512], R=[X["Ep"].tok], W=[X["ech"].tok])
            kb.tt(X["kte"].ap[:, 0:512].rearrange("p (c t) -> p c t", t=64),
                  X["kt"].ap[:, 0:512].rearrange("p (c t) -> p c t", t=64),
                  X["ech"].ap[:, 0:NCH].unsqueeze(2).broadcast_to([128, NCH, 64]), ALU.mult, R=[X["kt"].tok, X["ech"].tok],
                  W=[X["kte"].tok])
        for u, X in each():
            pbt = X["pbt"]
            for sub in range(NSUB):
                kb.tr(pbt.ap[:, sub * 128:(sub + 1) * 128], X["kte"].ap[:, sub * 128:(sub + 1) * 128], self.ident.ap[:, :],
                      R=[X["kte"].tok, self.ident.tok], W=[pbt.tok])
        for u, X in each():
            kb.copy(X["ktT"].ap[:, 0:512], X["pbt"].ap[:, 0:512], R=[X["pbt"].tok], W=[X["ktT"].tok])
        for u, X in each():
            for c in range(NCH):
                sub, half = divmod(c, 2)
                p0 = 64 * half
                bk = X["dl"][half]
                kb.mm(bk.ap[:, sub * 128:(sub + 1) * 128], X["ktT"].ap[p0:p0 + 64, sub * 128:(sub + 1) * 128],
                      X["vsb"].ap[p0:p0 + 64, sub * 128:(sub + 1) * 128], True, True, R=[X["ktT"].tok, X["vsb"].tok],
                      W=[bk.tok])
        for u, X in each():
            S = st.ap[:, 32 + u * 128:32 + (u + 1) * 128]
            kb.act(X["SpF"][0].ap[:, 0:128], S, AF.Copy, R=[st.tok, X["emid"].tok], W=[X["SpF"][0].tok],
                   scale=X["emid"].ap[:, 0:1])
        for c in range(NCH):
            sub, half = divmod(c, 2)
            for u, X in each():
                S = st.ap[:, 32 + u * 128:32 + (u + 1) * 128]
                bk = X["dl"][half]
                dl = bk.ap[:, sub * 128:(sub + 1) * 128]
                ec = X["ech"].ap[:, c:c + 1]
                cf = X["SpF"][c // 4]
                cur = cf.ap[:, (c % 4) * 128:(c % 4 + 1) * 128]
                if c < NCH - 1:
                    n_ = c + 1
                    nf = X["SpF"][n_ // 4]
                    kb.stt(nf.ap[:, (n_ % 4) * 128:(n_ % 4 + 1) * 128], cur, ec, dl, ALU.mult, ALU.add,
                           R=[cf.tok, X["ech"].tok, bk.tok], W=[nf.tok])
                else:
                    kb.stt(S, cur, ec, dl, ALU.mult, ALU.add, R=[cf.tok, X["ech"].tok, bk.tok], W=[st.tok])
                if c % 4 == 3:
                    kb.act(X["SpA"][c // 4].ap[:, 0:512], cf.ap[:, 0:512], AF.Copy, R=[cf.tok], W=[X["SpA"][c // 4].tok])
        for sub in range(NSUB):
            for u, X in each():
                pmf = X["pbt"].ap[:, :].bitcast(F32)
                for half in range(2):
                    c = sub * 2 + half
                    p0 = 64 * half
                    kb.mm(pmf[p0:p0 + 64, 0:64], X["kt"].ap[:, c * 64:(c + 1) * 64], X["qt"].ap[:, c * 64:(c + 1) * 64], True,
                          True, R=[X["kt"].tok, X["qt"].tok], W=[X["pbt"].tok])
            for u, X in each():
                pmf = X["pbt"].ap[:, :].bitcast(F32)
                kb.tt(X["scT"].ap[:, 0:64], pmf[:, 0:64], self.maskU.ap[:, :], ALU.mult, R=[X["pbt"].tok, self.maskU.tok],
                      W=[X["scT"].tok])
            for u, X in each():
                po = X["po"]
                for half in range(2):
                    c = sub * 2 + half
                    p0 = 64 * half
                    kb.mm(po.ap[p0:p0 + 64, sub * 128:(sub + 1) * 128], X["scT"].ap[p0:p0 + 64, 0:64],
                          X["vsb"].ap[p0:p0 + 64, sub * 128:(sub + 1) * 128], True, False, R=[X["scT"].tok, X["vsb"].tok],
                          W=[po.tok])
                    kb.mm(po.ap[p0:p0 + 64, sub * 128:(sub + 1) * 128], X["qt"].ap[:, c * 64:(c + 1) * 64],
                          X["SpA"][c // 4].ap[:, (c % 4) * 128:(c % 4 + 1) * 128], False, True,
                          R=[X["qt"].tok, X["SpA"][c // 4].tok], W=[po.tok])
        for u, X in each():
            po = X["po"]
            for sub in range(NSUB):
                kb.act(X["junk"].ap[:, 0:128], po.ap[:, sub * 128:(sub + 1) * 128], AF.Square, R=[po.tok],
                       W=[X["junk"].tok, X["ss2"].tok], accum=X["ss2"].ap[:, sub:sub + 1])
        for u, X in each():
            kb.act(X["rms2"].ap[:, 0:4], X["ss2"].ap[:, 0:4], AF.Sqrt, R=[X["ss2"].tok, self.cst.tok], W=[X["rms2"].tok],
                   scale=1.0 / 128, bias=self.cst.ap[:, 0:1])
        for u, X in each():
            kb.recip(X["rstd2"].ap[:, 0:4], X["rms2"].ap[:, 0:4], R=[X["rms2"].tok], W=[X["rstd2"].tok])
            kb.tt(X["on"].ap[:, 0:512].rearrange("p (j t) -> p j t", t=128),
                  X["po"].ap[:, 0:512].rearrange("p (j t) -> p j t", t=128),
                  X["rstd2"].ap[:, 0:NSUB].unsqueeze(2).broadcast_to([128, NSUB, 128]), ALU.mult,
                  R=[X["po"].tok, X["rstd2"].tok], W=[X["on"].tok])
        for u, X in each():
            pbo = X["pbt"]
            for sub in range(NSUB):
                kb.tr(pbo.ap[:, sub * 128:(sub + 1) * 128], X["on"].ap[:, sub * 128:(sub + 1) * 128], self.ident.ap[:, :],
                      R=[X["on"].tok, self.ident.tok], W=[pbo.tok])
        for u, X in each():
            kb.stt(self.merged.ap[:, 8 + u, :], X["pbt"].ap[:, 0:512], pv.ap[:, 96:97], X["sg"].ap[:, 0:512], ALU.mult,
                   ALU.mult, R=[X["pbt"].tok, pv.tok, X["sg"].tok], W=[self.merged.tok])

    def l0_end(self):
        kb = self.kb
        kb.dma("sync", self.dram["st0_out"][:, :], self.st0.ap[:, :], self.st0.tok, R=[self.st0.tok])

    def layer0_tile(self, tile, p1=False):
        self.gain = self.pv0
        self.gain_off = 88
        self.norm_T(tile, 0)
        self.l0_rg_proj(0, p1)
        for cb in range(8):
            self.l0_rg(cb, p1, (lambda c=cb: self.l0_rg_proj(c + 1, p1)) if cb < 7 else None)
        self.l0_hg_proj(0, p1)
        for hh in range(8):
            self.l0_hg(hh, p1, (lambda h_=hh: self.l0_hg_proj(h_ + 1, p1)) if hh < 7 else None)
        if not p1:
            self.out_proj(tile, 16, self.wout0, self.postbc0)

    PV1_COLS = 112
    DECAY_C = -0.6065306597126334

    def l1_setup(self):
        kb = self.kb
        d = self.dram
        self.pv1 = self.sbuf("pv1", [128, self.PV1_COLS], F32)
        self.pd1 = self.sbuf("pd1", [128, 8], F32)
        self.mix = [self.sbuf(f"mix{i}", [128, 8, 512], BF16) for i in range(4)]
        self.w1b = self.sbuf("w1b", [128, 8, 64], BF16)
        self.a1b = self.sbuf("a1b", [128, 8, 64], BF16)
        self.w2b = self.sbuf("w2b", [64, 1024], BF16)
        self.a2b = self.sbuf("a2b", [64, 1024], BF16)
        self.hid = self.sbuf("hid", [64, 1024], BF16)
        self.M3 = self.sbuf("M3", [128, 384], BF16)
        self.M2 = self.sbuf("M2", [128, 256], BF16)
        self.onesbd = self.sbuf("onesbd", [128, 128], BF16)
        self.st1 = self.sbuf("st1", [128, 520], F32)
        self.SC = self.sbuf("SC", [128, 8, 384], BF16)
        self.PPh = [self.sbuf(f"PP{i}", [128, 512], BF16) for i in range(2)]
        self.smb = self.sbuf("smb", [128, 384], BF16)
        if not self.do_l0:
            self.rmask = self.sbuf("rmask", [128, 512], BF16)
            kb.dma("sync", self.rmask.ap[:, :], d["rmask"][:, :], self.rmask.tok, W=[self.rmask.tok])
        if self.comm:
            kb.memset(self.st1.ap[:, :], 0.0, [self.st1.tok])
        else:
            kb.dma("sync", self.st1.ap[:, :], d["st1_in"][:, :], self.st1.tok, W=[self.st1.tok])
        for (buf, nm) in ((self.pv1, "pv1"), (self.M3, "M3"), (self.M2, "M2"),
                          (self.postbc1, "post1_bc"), (self.onesbd, "onesbd")):
            kb.dma("sync", buf.ap[:, :], d[nm][:, :], buf.tok, W=[buf.tok])
        ncol = 256 if self.comm else D
        kb.dma("gpsimd", self.w2b.ap[:, 0:ncol], d["rw_w2"][:, :], self.w2b.tok, W=[self.w2b.tok])
        kb.dma("gpsimd", self.a2b.ap[:, 0:ncol], d["rw_a2"][:, :], self.a2b.tok, W=[self.a2b.tok])
        kb.dma("gpsimd", self.w1b.ap[:, :, :], d["rw_w1"].rearrange("(kc p) n -> p kc n", p=128), self.w1b.tok,
               W=[self.w1b.tok])
        kb.dma("gpsimd", self.a1b.ap[:, :, :], d["rw_a1"].rearrange("(kc p) n -> p kc n", p=128), self.a1b.tok,
               W=[self.a1b.tok])
        if not self.comm:
            for cb in range(8):
                self.load_wout1(cb)
        kb.memset(self.smb.ap[:, :], 0.0, [self.smb.tok])
        kb.ts(self.pd1.ap[:, 0:8], self.pv1.ap[:, 72:80], -1.0, 1.0, ALU.mult, ALU.add, R=[self.pv1.tok],
              W=[self.pd1.tok])

    def l1_end(self):
        kb = self.kb
        kb.dma("sync", self.dram["st1_out"][:, :], self.st1.ap[:, :], self.st1.tok, R=[self.st1.tok])

    def l1_mix(self, p, dst):
        for _ in self.l1_mix_g(p, dst):
            pass

    def l1_mix_g(self, p, dst):
        kb = self.kb
        diff = self.merged
        pool_set = tuple(int(c) for c in os.environ.get("MK_POOLMIX", "").split(",") if c)
        en = "gpsimd" if p in pool_set else "vector"
        for kc in range(8):
            kb.stt(dst.ap[:, kc, :], diff.ap[:, 8 + kc, :], self.pv1.ap[:, p * 8 + kc:p * 8 + kc + 1],
                   self.uT.ap[:, kc, 1:513], ALU.mult, ALU.add, R=[diff.tok, self.pv1.tok, self.uT.tok], W=[dst.tok],
                   en=en)
            yield

    def layer1_tile(self, tile, p1=False):
        kb = self.kb
        st1 = self.st1
        self.gain = self.pv1
        self.gain_off = 104
        for kc in range(8):
            kb.copy(self.uT.ap[:, kc, 0:1], st1.ap[:, 512 + kc:513 + kc], R=[st1.tok], W=[self.uT.tok])
        self.norm_T(tile, 1)
        for kc in range(8):
            kb.copy(st1.ap[:, 512 + kc:513 + kc], self.uT.ap[:, kc, 512:513], R=[self.uT.tok], W=[st1.tok])
        kb.tt(self.merged.ap[:, 8:16, :], self.uT.ap[:, :, 0:512], self.uT.ap[:, :, 1:513], ALU.subtract,
              R=[self.uT.tok], W=[self.merged.tok])
        self.l1_mix(4, self.mix[0])
        ph = self.psf[0]
        for kc in range(8):
            kb.mm(ph.ap[0:64, :], self.w1b.ap[:, kc, :], self.mix[0].ap[:, kc, :], kc == 0, kc == 7,
                  R=[self.w1b.tok, self.mix[0].tok], W=[ph.tok])
        kb.act(self.hid.ap[:, 0:512], ph.ap[0:64, :], AF.Tanh, R=[ph.tok], W=[self.hid.tok])
        self.l1_mix(5, self.mix[1])
        ph = self.psf[1]
        for kc in range(8):
            kb.mm(ph.ap[0:64, :], self.a1b.ap[:, kc, :], self.mix[1].ap[:, kc, :], kc == 0, kc == 7,
                  R=[self.a1b.tok, self.mix[1].tok], W=[ph.tok])
        kb.act(self.hid.ap[:, 512:1024], ph.ap[0:64, :], AF.Copy, R=[ph.tok], W=[self.hid.tok])
        for p_ in ((1, 2) if p1 else range(4)):
            self.l1_mix(p_, self.mix[p_])
        for hp in range(8):
            self.l1_hp(hp, p1)
        if not p1:
            self.out_proj(tile, 8, self.wout1, self.postbc1)

    def l1_hp(self, hp, p1=False, filler=None):
        kb = self.kb
        d = self.dram
        pv, pd = self.pv1, self.pd1
        f, b, sm = self.f, self.b, self.sm
        C = self.DECAY_C
        bq, r, k, sgm, icl, cs, Epos, Eneg, Eexc, kkn, t1, sg = (f[0], f[1], f[2], f[3], f[4], f[5], f[6], f[7],
                                                                 f[8], f[9], f[10], f[11])
        sq, at, rt, kt, bt, ktm, btm, vsb, rkr, yn, NN, NNT = (b[0], b[1], b[2], b[3], b[4], b[5], b[6], b[7],
                                                               b[8], b[9], b[10], b[11])
        psf = self.psf
        st1 = self.st1
        cols = slice(hp * 128, (hp + 1) * 128)
        pr, pk, pg, pvv, pw, pa = psf[2], psf[3], psf[4], psf[5], psf[0], psf[1]
        wk = self.load_w(d["rw_w_in"][1, :, cols])
        wv = self.load_w(d["rw_w_in"][2, :, cols])
        plist = [(wk, self.mix[1], pk)]
        if not p1:
            wr = self.load_w(d["rw_w_in"][0, :, cols])
            wg = self.load_w(d["rw_w_in"][3, :, cols])
            plist += [(wr, self.mix[0], pr), (wg, self.mix[3], pg)]
        for (w_, m_, p_) in plist:
            for kc in range(8):
                kb.mm(p_.ap[:, :], w_.ap[:, kc, :], m_.ap[:, kc, :], kc == 0, kc == 7, R=[w_.tok, m_.tok], W=[p_.tok])
        for sub in range(NSUB):
            for kc in range(8):
                kb.mm(pvv.ap[:, sub * 128:(sub + 1) * 128], self.mix[2].ap[:, kc, sub * 128:(sub + 1) * 128],
                      wv.ap[:, kc, :], kc == 0, kc == 7, R=[wv.tok, self.mix[2].tok], W=[pvv.tok])
        kb.mm(pw.ap[:, :], self.w2b.ap[0:64, cols], self.hid.ap[0:64, 0:512], True, True,
              R=[self.w2b.tok, self.hid.tok], W=[pw.tok])
        kb.mm(pa.ap[:, :], self.a2b.ap[0:64, cols], self.hid.ap[0:64, 512:1024], True, True,
              R=[self.a2b.tok, self.hid.tok], W=[pa.tok])
        if not p1:
            kb.act(r.ap[:, 0:512], pr.ap[:, :], AF.Copy, R=[pr.tok], W=[r.tok])
            kb.act(sg.ap[:, 0:512], pg.ap[:, :], AF.Silu, R=[pg.tok], W=[sg.tok])
        kb.act(k.ap[:, 0:512], pk.ap[:, :], AF.Copy, R=[pk.tok], W=[k.tok])
        kb.act(sq.ap[:, :], pk.ap[:, :], AF.Square, R=[pk.tok, pv.tok], W=[sq.tok], scale=pv.ap[:, 64 + hp:65 + hp])
        kb.act(vsb.ap[:, :], pvv.ap[:, :], AF.Copy, R=[pvv.tok], W=[vsb.tok])
        kb.act(sgm.ap[:, 0:512], pw.ap[:, :], AF.Sigmoid, R=[pw.tok, pv.tok], W=[sgm.tok], bias=pv.ap[:, 48 + hp:49 + hp])
        kb.act(icl.ap[:, 0:512], pa.ap[:, :], AF.Sigmoid, R=[pa.tok, pv.tok], W=[icl.tok], bias=pv.ap[:, 56 + hp:57 + hp])
        pn = psf[0]
        kb.mm(pn.ap[:, :], self.onesbd.ap[:, :], sq.ap[:, :], True, True, R=[self.onesbd.tok, sq.tok], W=[pn.tok])
        kb.act(t1.ap[:, 0:512], pn.ap[:, :], AF.Sqrt, R=[pn.tok], W=[t1.tok])
        kb.ts(t1.ap[:, 0:512], t1.ap[:, 0:512], 1e-12, None, ALU.max, None, R=[t1.tok], W=[t1.tok])
        kb.recip(t1.ap[:, 0:512], t1.ap[:, 0:512], R=[t1.tok], W=[t1.tok])
        kb.stt(kkn.ap[:, 0:512], k.ap[:, 0:512], pv.ap[:, 64 + hp:65 + hp], t1.ap[:, 0:512], ALU.mult, ALU.mult,
               R=[k.tok, pv.tok, t1.tok], W=[kkn.tok])
        kb.scan(cs.ap[:, 0:512], self.rmask.ap[:, :], sgm.ap[:, 0:512], 0.0, ALU.mult, ALU.add,
                R=[self.rmask.tok, sgm.tok], W=[cs.tok])
        kb.tt(Eexc.ap[:, 0:512], cs.ap[:, 0:512], sgm.ap[:, 0:512], ALU.subtract, R=[cs.tok, sgm.tok], W=[Eexc.tok])
        kb.act(Epos.ap[:, 0:512], cs.ap[:, 0:512], AF.Exp, R=[cs.tok], W=[Epos.tok], scale=C)
        kb.act(Eneg.ap[:, 0:512], cs.ap[:, 0:512], AF.Exp, R=[cs.tok], W=[Eneg.tok], scale=-C)
        kb.act(Eexc.ap[:, 0:512], Eexc.ap[:, 0:512], AF.Exp, R=[Eexc.tok], W=[Eexc.tok], scale=C)
        kb.ts(t1.ap[:, 0:512], icl.ap[:, 0:512], pv.ap[:, 72 + hp:73 + hp], pd.ap[:, hp:hp + 1], ALU.mult, ALU.add,
              R=[icl.tok, pv.tok, pd.tok], W=[t1.tok])
        kb.tt(k.ap[:, 0:512], k.ap[:, 0:512], t1.ap[:, 0:512], ALU.mult, R=[k.tok, t1.tok], W=[k.tok])
        kb.tt(bq.ap[:, 0:512], kkn.ap[:, 0:512], icl.ap[:, 0:512], ALU.mult, R=[kkn.tok, icl.tok], W=[bq.tok])
        kb.tt(kt.ap[:, :], k.ap[:, 0:512], Eneg.ap[:, 0:512], ALU.mult, R=[k.tok, Eneg.tok], W=[kt.tok])
        kb.tt(bt.ap[:, :], bq.ap[:, 0:512], Eneg.ap[:, 0:512], ALU.mult, R=[bq.tok, Eneg.tok], W=[bt.tok])
        kb.stt(at.ap[:, :], kkn.ap[:, 0:512], -1.0, Eexc.ap[:, 0:512], ALU.mult, ALU.mult, R=[kkn.tok, Eexc.tok],
               W=[at.tok])
        pbon = psf[1]
        if not p1:
            kb.tt(rt.ap[:, :], r.ap[:, 0:512], Epos.ap[:, 0:512], ALU.mult, R=[r.tok, Epos.tok], W=[rt.tok])
            kb.stt(rkr.ap[:, :], r.ap[:, 0:512], pv.ap[:, 80 + hp:81 + hp], k.ap[:, 0:512], ALU.mult, ALU.mult,
                   R=[r.tok, pv.tok, k.tok], W=[rkr.tok])
            kb.mm(pbon.ap[:, :], self.onesbd.ap[:, :], rkr.ap[:, :], True, True, R=[self.onesbd.tok, rkr.tok],
                  W=[pbon.tok])
            kb.act(r.ap[:, 0:512], pbon.ap[:, :], AF.Copy, R=[pbon.tok], W=[r.tok])
        Epos3 = Epos.ap[:, 0:512].rearrange("p (c t) -> p c t", t=64)
        wend_bc = Epos3[:, :, 63:64].broadcast_to([128, NCH, 64])
        kb.tt(yn.ap[:, :].rearrange("p (c t) -> p c t", t=64), kt.ap[:, :].rearrange("p (c t) -> p c t", t=64), wend_bc,
              ALU.mult, R=[kt.tok, Epos.tok], W=[yn.tok])
        for sub in range(NSUB):
            kb.tr(self.psb[0].ap[:, sub * 128:(sub + 1) * 128], yn.ap[:, sub * 128:(sub + 1) * 128], self.ident.ap[:, :],
                  R=[yn.tok, self.ident.tok], W=[self.psb[0].tok])
        kb.tt(yn.ap[:, :].rearrange("p (c t) -> p c t", t=64), bt.ap[:, :].rearrange("p (c t) -> p c t", t=64), wend_bc,
              ALU.mult, R=[bt.tok, Epos.tok], W=[yn.tok])
        for sub in range(NSUB):
            kb.tr(self.psb[1].ap[:, sub * 128:(sub + 1) * 128], yn.ap[:, sub * 128:(sub + 1) * 128], self.ident.ap[:, :],
                  R=[yn.tok, self.ident.tok], W=[self.psb[1].tok])
        kb.copy(ktm.ap[:, :], self.psb[0].ap[:, 0:512], R=[self.psb[0].tok], W=[ktm.tok])
        kb.act(btm.ap[:, :], self.psb[1].ap[:, 0:512], AF.Copy, R=[self.psb[1].tok], W=[btm.tok])
        NNs = [(NN, NNT), (rkr, sq)]
        for hl in range(2):
            rs = slice(64 * hl, 64 * hl + 64)
            nn, nnt = NNs[hl]
            for sub in range(NSUB):
                pi = hl * 4 + sub
                X, Y = (psf[0], psf[2]) if pi % 2 == 0 else (psf[3], psf[4])
                tc = slice(sub * 128, (sub + 1) * 128)
                kb.mm(X.ap[:, 0:128], kt.ap[rs, tc], at.ap[rs, tc], True, True, R=[kt.tok, at.tok], W=[X.tok])
                if not p1:
                    kb.mm(X.ap[:, 128:256], kt.ap[rs, tc], rt.ap[rs, tc], True, True, R=[kt.tok, rt.tok], W=[X.tok])
                    kb.mm(X.ap[:, 256:384], bt.ap[rs, tc], rt.ap[rs, tc], True, True, R=[bt.tok, rt.tok], W=[X.tok])
                kb.mm(Y.ap[:, 0:128], bt.ap[rs, tc], at.ap[rs, tc], True, True, R=[bt.tok, at.tok], W=[Y.tok])
                kb.mm(Y.ap[:, 128:256], at.ap[rs, tc], bt.ap[rs, tc], True, True, R=[bt.tok, at.tok], W=[Y.tok])
                nsc = 128 if p1 else 384
                kb.tt(self.SC.ap[:, pi, 0:nsc], X.ap[:, 0:nsc], self.M3.ap[:, 0:nsc], ALU.mult, R=[X.tok, self.M3.tok],
                      W=[self.SC.tok])
                kb.tt(nn.ap[:, tc], Y.ap[:, 0:128], self.M2.ap[:, 0:128], ALU.mult, R=[Y.tok, self.M2.tok], W=[nn.tok])
                kb.tt(nnt.ap[:, tc], Y.ap[:, 128:256], self.M2.ap[:, 128:256], ALU.mult, R=[Y.tok, self.M2.tok],
                      W=[nnt.tok])
            for j in range(4):
                kb.tt(self.PPh[hl].ap[:, j * 128:(j + 1) * 128], nn.ap[:, j * 128:(j + 1) * 128], self.ident.ap[:, :],
                      ALU.add, R=[nn.tok, self.ident.tok], W=[self.PPh[hl].tok])
        ABC = [(psf[3], psf[4], psf[5]), (psf[0], psf[2], psf[1])]
        for lvl in range(5):
            last = lvl == 4
            for hl in range(2):
                nn, nnt = NNs[hl]
                A_, B_, C_ = ABC[hl]
                for j in range(4):
                    tc = slice(j * 128, (j + 1) * 128)
                    if not last:
                        kb.mm(A_.ap[:, tc], nnt.ap[:, tc], nn.ap[:, tc], True, True, R=[nn.tok, nnt.tok], W=[A_.tok])
                    kb.mm(B_.ap[:, tc], nn.ap[:, tc], nnt.ap[:, tc], True, True, R=[nn.tok, nnt.tok], W=[B_.tok])
            for hl in range(2):
                nn, nnt = NNs[hl]
                A_, B_, C_ = ABC[hl]
                kb.act(nnt.ap[:, :], B_.ap[:, :], AF.Copy, R=[B_.tok], W=[nnt.tok])
                if not last:
                    kb.copy(nn.ap[:, :], A_.ap[:, :], R=[A_.tok], W=[nn.tok])
            for hl in range(2):
                nn, nnt = NNs[hl]
                A_, B_, C_ = ABC[hl]
                for j in range(4):
                    tc = slice(j * 128, (j + 1) * 128)
                    kb.mm(C_.ap[:, tc], nnt.ap[:, tc], self.PPh[hl].ap[:, tc], True, True, R=[nnt.tok, self.PPh[hl].tok],
                          W=[C_.tok])
            for hl in range(2):
                A_, B_, C_ = ABC[hl]
                kb.tt(self.PPh[hl].ap[:, :], C_.ap[:, :], self.PPh[hl].ap[:, :], ALU.add, R=[C_.tok, self.PPh[hl].tok],
                      W=[self.PPh[hl].tok])
        yps, XU, dH = psf[0], psf[2], psf[3]
        if p1:
            VW = 128
            smb, smbt = self.smbA, self.wout0.tok
            Hs, stt_ = self.stA[:, hp, :], self.wout0.tok
        else:
            VW = 64
            smb, smbt = self.smb.ap, self.smb.tok
            Hs, stt_ = st1.ap[:, hp * 64:(hp + 1) * 64], st1.tok
        ho = 4 * VW
        tmpH = t1
        for hl in range(2):
            rs = slice(64 * hl, 64 * hl + 64)
            kb.act(smb[rs, ho + hl * VW:ho + (hl + 1) * VW], Hs[rs, :], AF.Copy, R=[stt_], W=[smbt])
        for c in range(NCH):
            sub, half = divmod(c, 2)
            p0 = 64 * half
            ps_ = slice(p0, p0 + 64)
            cc = slice(c * 64, (c + 1) * 64)
            for hl in range(2):
                vv = slice(sub * 128 + hl * 64, sub * 128 + hl * 64 + 64)
                pi = hl * 4 + sub
                kb.mm(XU.ap[ps_, hl * VW:(hl + 1) * VW], at.ap[:, cc], smb[:, ho + hl * VW:ho + (hl + 1) * VW], True, False,
                      R=[at.tok, smbt], W=[XU.tok])
                kb.mm(XU.ap[ps_, hl * VW:hl * VW + 64], self.SC.ap[:, pi, p0:p0 + 64], vsb.ap[:, vv], False, True,
                      R=[self.SC.tok, vsb.tok], W=[XU.tok])
            kb.act(smb[ps_, 0:2 * VW], XU.ap[ps_, 0:2 * VW], AF.Copy, R=[XU.tok], W=[smbt])
            for hl in range(2):
                kb.mm(XU.ap[ps_, 2 * VW + hl * VW:2 * VW + (hl + 1) * VW],
                      self.PPh[hl].ap[:, sub * 128 + p0:sub * 128 + p0 + 64], smb[:, hl * VW:(hl + 1) * VW], True, True,
                      R=[self.PPh[hl].tok, smbt], W=[XU.tok])
            kb.copy(smb[ps_, 2 * VW:4 * VW], XU.ap[ps_, 2 * VW:4 * VW], R=[XU.tok], W=[smbt])
            for hl in range(2):
                rs = slice(64 * hl, 64 * hl + 64)
                vv = slice(sub * 128 + hl * 64, sub * 128 + hl * 64 + 64)
                kb.mm(dH.ap[rs, 0:VW], btm.ap[ps_, vv], smb[ps_, 2 * VW + hl * VW:2 * VW + (hl + 1) * VW], True, False,
                      R=[btm.tok, smbt], W=[dH.tok])
                kb.mm(dH.ap[rs, 0:64], ktm.ap[ps_, vv], vsb.ap[ps_, vv], False, True, R=[ktm.tok, vsb.tok], W=[dH.tok])
            if not p1:
                for hl in range(2):
                    us = slice(2 * VW + hl * VW, 2 * VW + hl * VW + 64)
                    vv = slice(sub * 128 + hl * 64, sub * 128 + hl * 64 + 64)
                    pi = hl * 4 + sub
                    kb.mm(yps.ap[ps_, vv], rt.ap[:, cc], smb[:, ho + hl * VW:ho + (hl + 1) * VW], True, False,
                          R=[rt.tok, smbt], W=[yps.tok])
                    kb.mm(yps.ap[ps_, vv], self.SC.ap[:, pi, 256 + p0:256 + p0 + 64], smb[:, us], False, False,
                          R=[self.SC.tok, smbt], W=[yps.tok])
                    kb.mm(yps.ap[ps_, vv], self.SC.ap[:, pi, 128 + p0:128 + p0 + 64], vsb.ap[:, vv], False, True,
                          R=[self.SC.tok, vsb.tok], W=[yps.tok])
            for hl in range(2):
                rs = slice(64 * hl, 64 * hl + 64)
                kb.stt(smb[rs, ho + hl * VW:ho + (hl + 1) * VW], Hs[rs, :], Epos.ap[rs, c * 64 + 63:c * 64 + 64],
                       dH.ap[rs, 0:VW], ALU.mult, ALU.add, R=[stt_, Epos.tok, dH.tok], W=[smbt])
            kb.stt(Hs, Hs, Epos.ap[:, c * 64 + 63:c * 64 + 64], dH.ap[:, 0:VW], ALU.mult, ALU.add,
                   R=[stt_, Epos.tok, dH.tok], W=[stt_])
            if filler is not None:
                filler()
        if p1:
            return
        s1, s2, mean, msq, var, rms, rstd = sm[6], sm[7], sm[8], sm[9], sm[10], sm[11], sm[12]
        sqy = t1
        y3 = yps.ap[:, :].rearrange("p (g v) -> p g v", v=64)
        kb.emit("vector", lambda e: e.reduce_sum(s1.ap[:, 0:8], y3, mybir.AxisListType.X), R=[yps.tok], W=[s1.tok])
        kb.act(sqy.ap[:, 0:512], yps.ap[:, :], AF.Square, R=[yps.tok], W=[sqy.tok])
        sq3 = sqy.ap[:, 0:512].rearrange("p (g v) -> p g v", v=64)
        kb.emit("vector", lambda e: e.reduce_sum(s2.ap[:, 0:8], sq3, mybir.AxisListType.X), R=[sqy.tok], W=[s2.tok])
        kb.ts(mean.ap[:, 0:8], s1.ap[:, 0:8], 1.0 / 64, None, ALU.mult, None, R=[s1.tok], W=[mean.tok])
        kb.tt(msq.ap[:, 0:8], mean.ap[:, 0:8], mean.ap[:, 0:8], ALU.mult, R=[mean.tok], W=[msq.tok])
        kb.stt(var.ap[:, 0:8], s2.ap[:, 0:8], 1.0 / 64, msq.ap[:, 0:8], ALU.mult, ALU.subtract, R=[s2.tok, msq.tok],
               W=[var.tok])
        kb.act(rms.ap[:, 0:8], var.ap[:, 0:8], AF.Sqrt, R=[var.tok, self.cst.tok], W=[rms.tok], bias=self.cst.ap[:, 2:3])
        kb.recip(rstd.ap[:, 0:8], rms.ap[:, 0:8], R=[rms.tok], W=[rstd.tok])
        for g in range(8):
            kb.ts(yn.ap[:, g * 64:(g + 1) * 64], yps.ap[:, g * 64:(g + 1) * 64], mean.ap[:, g:g + 1], rstd.ap[:, g:g + 1],
                  ALU.subtract, ALU.mult, R=[yps.tok, mean.tok, rstd.tok], W=[yn.tok])
        for sub in range(NSUB):
            kb.tr(self.psb[0].ap[:, sub * 128:(sub + 1) * 128], yn.ap[:, sub * 128:(sub + 1) * 128], self.ident.ap[:, :],
                  R=[yn.tok, self.ident.tok], W=[self.psb[0].tok])
        for sub in range(NSUB):
            kb.tr(self.psb[1].ap[:, sub * 128:(sub + 1) * 128], vsb.ap[:, sub * 128:(sub + 1) * 128], self.ident.ap[:, :],
                  R=[vsb.tok, self.ident.tok], W=[self.psb[1].tok])
        vT = sq
        kb.act(vT.ap[:, :], self.psb[1].ap[:, 0:512], AF.Copy, R=[self.psb[1].tok], W=[vT.tok])
        bv = k
        kb.tt(bv.ap[:, 0:512], r.ap[:, 0:512], vT.ap[:, :], ALU.mult, R=[r.tok, vT.tok], W=[bv.tok])
        kb.stt(sgm.ap[:, 0:512], self.psb[0].ap[:, 0:512], pv.ap[:, 88 + hp:89 + hp], bv.ap[:, 0:512], ALU.mult, ALU.add,
               R=[self.psb[0].tok, pv.tok, bv.tok], W=[sgm.tok])
        kb.stt(self.merged.ap[:, hp, :], sgm.ap[:, 0:512], pv.ap[:, 96 + hp:97 + hp], sg.ap[:, 0:512], ALU.add, ALU.mult,
               R=[sgm.tok, pv.tok, sg.tok], W=[self.merged.tok])

    def cc_gather(self, parts, W, name):
        kb = self.kb
        bin_ = self.nc.dram_tensor(name + "_i", [128, W], F32)
        bout = self.nc.dram_tensor(name + "_o", [512, W], F32)
        tin, tout = kb.tok(name + "_i"), kb.tok(name + "_o")
        off = 0
        for (ap, tok, w) in parts:
            kb.dma("gpsimd", bin_.ap()[:, off:off + w], ap, tin, R=[tok], W=[tin])
            off += w
        kb.collective(bin_, bout, tin, tout, name)
        G = self.merged.ap[:, :, :].rearrange("p a b -> p (a b)").bitcast(F32)
        for r in range(3):
            kb.dma("sync", G[:, r * W:(r + 1) * W], bout.ap()[r * 128:(r + 1) * 128, :], self.merged.tok, R=[tout],
                   W=[self.merged.tok])
        return G

    def comm_setup(self):
        kb = self.kb
        d = self.dram
        self.dec0 = self.sbuf("dec0", [128, 16], F32)
        self.msk = self.sbuf("msk", [128, 16], F32)
        self.eye2 = self.sbuf("eye2", [128, 64], F32)
        self.ul = self.sbuf("ul", [128, 8], F32)
        self.up0 = self.sbuf("up0", [128, 8], F32)
        self.dd = self.sbuf("dd", [128, 16], F32)
        kb.dma("sync", self.msk.ap[:, :], d["msk"][:, :], self.msk.tok, W=[self.msk.tok])
        kb.dma("sync", self.eye2.ap[:, :], d["eye2"][:, :], self.eye2.tok, W=[self.eye2.tok])
        kb.memset(self.dec0.ap[:, :], 0.0, [self.dec0.tok])
        w0f = self.wout0.ap[:, :, :].rearrange("p a b -> p (a b)")
        self.stA = w0f.bitcast(F32)[:, 0:1024].rearrange("p (a b) -> p a b", b=128)
        self.smbA = w0f[:, 2048:2048 + 768]
        self.rgB_f = []
        for i in range(4):
            flat = self.mix[i].ap[:, :, :].rearrange("p a b -> p (a b)").bitcast(F32)
            for j in range(3 if i < 3 else 1):
                self.rgB_f.append(Buf(flat[:, j * 516:(j + 1) * 516], kb.tok(f"rgB{i}_{j}")))
        flatb = self.mix[3].ap[:, :, :].rearrange("p a b -> p (a b)")
        self.rgB_xcb = Buf(flatb[:, 1032:1032 + 512], kb.tok("rgBx"))
        w1b_ = self.wout1.ap[:, :, :].rearrange("p a b -> p (a b)")
        self.hgB_b = [Buf(w1b_[:, i * 512:(i + 1) * 512], kb.tok(f"hgBb{i}")) for i in range(9)]
        self.hgB_f = list(self.rgB_f) + [Buf(w1b_.bitcast(F32)[:, 2304:2820], kb.tok("hgBf10"))]
        self.hgB_sm = [self.sbuf(f"smB{i}", [128, 16], F32) for i in range(5)]
        w0b = self.wout0.ap[:, :, :].rearrange("p a b -> p (a b)")
        w0f = w0b.bitcast(F32)
        self.l1B = dict(
            b=[Buf(w0b[:, i * 512:(i + 1) * 512], kb.tok(f"l1Bb{i}")) for i in range(5)],
            SC=Buf(w0b[:, 2560:5632].rearrange("p (a b) -> p a b", b=384), kb.tok("l1Bsc")),
            PPh=[Buf(w0b[:, 5632 + i * 512:5632 + (i + 1) * 512], kb.tok(f"l1Bpp{i}")) for i in range(2)],
            smb=Buf(w0b[:, 6656:7040], kb.tok("l1Bsmb")),
            f=[Buf(w0f[:, 3520 + i * 516:3520 + (i + 1) * 516], kb.tok(f"l1Bf{i}")) for i in range(3)],
        )
        self.l1E = dict(
            b=[Buf(w0b[:, 10136 + i * 512:10136 + (i + 1) * 512], kb.tok(f"l1Eb{i}")) for i in range(2)],
            f=[Buf(w0f[:, 5580 + i * 516:5580 + (i + 1) * 516], kb.tok(f"l1Ef{i}")) for i in range(3)],
            sm=[self.sbuf(f"smE{i}", [128, 16], F32) for i in range(7)],
        )
        self.st1h = [kb.tok("st1h0"), kb.tok("st1h1")]
        for t_ in self.st1h:
            t_.w = self.st1.tok.w
        self.x1buf = self.nc.dram_tensor("x1buf", [self.TS, D], F32)
        self.x1tok = [kb.tok(f"x1t{t}") for t in range(self.NT)]

    def comm_l0_prefix(self, G):
        kb = self.kb
        st, dd, msk, pd = self.st0, self.dd, self.msk, self.pd0
        mt = self.merged.tok
        W = 1072
        kb.memset(st.ap[:, :], 0.0, [st.tok])
        for j in range(3):
            Gj = G[:, j * W:(j + 1) * W]
            mj = msk.ap[:, j:j + 1]
            ej = msk.ap[:, 4 + j:5 + j]
            kb.tt(dd.ap[:, 0:8], Gj[:, 1056:1064], pd.ap[:, 0:8], ALU.mult, R=[mt, pd.tok], W=[dd.tok])
            kb.act(dd.ap[:, 0:8], dd.ap[:, 0:8], AF.Exp, R=[dd.tok], W=[dd.tok])
            kb.act(dd.ap[:, 8:16], Gj[:, 1064:1072], AF.Exp, R=[mt], W=[dd.tok])
            kb.ts(dd.ap[:, 0:16], dd.ap[:, 0:16], -1.0, None, ALU.add, None, R=[dd.tok], W=[dd.tok])
            kb.ts(dd.ap[:, 0:16], dd.ap[:, 0:16], mj, None, ALU.mult, None, R=[dd.tok, msk.tok], W=[dd.tok])
            kb.ts(dd.ap[:, 0:16], dd.ap[:, 0:16], 1.0, None, ALU.add, None, R=[dd.tok], W=[dd.tok])
            kb.stt(st.ap[:, 0:24], Gj[:, 0:24], ej, st.ap[:, 0:24], ALU.mult, ALU.add, R=[mt, msk.tok, st.tok], W=[st.tok])
            kb.tt(st.ap[:, 24:32], st.ap[:, 24:32], dd.ap[:, 0:8], ALU.mult, R=[st.tok, dd.tok], W=[st.tok])
            kb.stt(st.ap[:, 24:32], Gj[:, 24:32], mj, st.ap[:, 24:32], ALU.mult, ALU.add, R=[mt, msk.tok, st.tok],
                   W=[st.tok])
            for h in range(8):
                kb.ts(st.ap[:, 32 + h * 128:32 + (h + 1) * 128], st.ap[:, 32 + h * 128:32 + (h + 1) * 128],
                      dd.ap[:, 8 + h:9 + h], None, ALU.mult, None, R=[st.tok, dd.tok], W=[st.tok])
            kb.stt(st.ap[:, 32:1056], Gj[:, 32:1056], mj, st.ap[:, 32:1056], ALU.mult, ALU.add,
                   R=[mt, msk.tok, st.tok], W=[st.tok])

    def comm_uprev(self, G):
        kb = self.kb
        st1, msk = self.st1, self.msk
        mt = self.merged.tok
        kb.memset(self.up0.ap[:, :], 0.0, [self.up0.tok])
        for j in range(3):
            kb.stt(self.up0.ap[:, 0:8], G[:, j * 8:(j + 1) * 8], msk.ap[:, 4 + j:5 + j], self.up0.ap[:, 0:8], ALU.mult,
                   ALU.add, R=[mt, msk.tok, self.up0.tok], W=[self.up0.tok])

    def comm_l1_prefix(self, G):
        kb = self.kb
        st1, msk = self.st1, self.msk
        mt = self.merged.tok
        Mbd, Xb, Hb = self.b[0], self.b[1], self.b[2]
        t = self.f[1]
        ps = self.psf[0]
        kb.memset(st1.ap[:, 0:512], 0.0, [st1.tok])
        kb.memset(Mbd.ap[:, 0:128], 0.0, [Mbd.tok])
        for j in range(3):
            mj = msk.ap[:, j:j + 1]
            for hp in range(8):
                Gh = G[:, j * 1024 + hp * 128:j * 1024 + (hp + 1) * 128]
                H = st1.ap[:, hp * 64:(hp + 1) * 64]
                kb.copy(Mbd.ap[0:64, 0:64], Gh[0:64, 64:128], R=[mt], W=[Mbd.tok])
                kb.copy(Mbd.ap[64:128, 64:128], Gh[64:128, 64:128], R=[mt], W=[Mbd.tok])
                kb.tr(self.psb[0].ap[:, 0:128], Mbd.ap[:, 0:128], self.ident.ap[:, :], R=[Mbd.tok, self.ident.tok],
                      W=[self.psb[0].tok])
                kb.act(Xb.ap[:, 0:128], self.psb[0].ap[:, 0:128], AF.Copy, R=[self.psb[0].tok], W=[Xb.tok])
                kb.act(Hb.ap[:, 0:64], H, AF.Copy, R=[st1.tok], W=[Hb.tok])
                kb.mm(ps.ap[:, 0:64], Xb.ap[:, 0:128], Hb.ap[:, 0:64], True, True, R=[Xb.tok, Hb.tok], W=[ps.tok])
                kb.tt(t.ap[:, 0:64], ps.ap[:, 0:64], Gh[:, 0:64], ALU.add, R=[ps.tok, mt], W=[t.tok])
                kb.tt(t.ap[:, 0:64], t.ap[:, 0:64], H, ALU.subtract, R=[t.tok, st1.tok], W=[t.tok])
                kb.stt(H, t.ap[:, 0:64], mj, H, ALU.mult, ALU.add, R=[t.tok, msk.tok, st1.tok], W=[st1.tok])

    def cc_gather_dram(self, bin_, bout, tin, name):
        kb = self.kb
        tout = kb.tok(name + "_o")
        kb.collective(bin_, bout, tin, tout, name)
        return tout

    def l1_ctx(self, hp, setB):
        from types import SimpleNamespace as NS
        f, b = self.f, self.b
        cx = NS(hp=hp)
        (cx.bq, cx.r, cx.k, cx.sgm, cx.icl, cx.cs, cx.Epos, cx.Eneg, cx.Eexc, cx.kkn, cx.t1, cx.sg) = (
            f[0], f[1], f[2], f[3], f[4], f[5], f[6], f[7], f[8], f[9], f[10], f[11])
        (cx.sq, cx.at, cx.rt, cx.kt, cx.bt, cx.ktm, cx.btm, cx.vsb, cx.rkr, cx.yn, cx.NN, cx.NNT) = (
            b[0], b[1], b[2], b[3], b[4], b[5], b[6], b[7], b[8], b[9], b[10], b[11])
        cx.SC, cx.PPh, cx.smb = self.SC, self.PPh, self.smb
        cx.yps, cx.XU, cx.xo, cx.dH, cx.do = self.psf[0], self.psf[2], 0, self.psf[3], 0
        if setB:
            B = self.l1B
            cx.at, cx.rt, cx.ktm, cx.btm, cx.vsb = B["b"]
            cx.Epos, cx.sg, cx.r = B["f"]
            cx.SC, cx.PPh, cx.smb = B["SC"], B["PPh"], B["smb"]
            cx.yps, cx.xo, cx.do = self.psf[1], 256, 64
        cx.Hs = self.st1.ap[:, hp * 64:(hp + 1) * 64]
        cx.Ht = self.st1h[hp]
        sm = self.sm
        cx.e_sm = [sm[6], sm[7], sm[8], sm[9], sm[10], sm[11], sm[12]]
        cx.e_yn, cx.e_vT, cx.e_sqy, cx.e_bv, cx.e_tmp = cx.yn, cx.sq, cx.t1, cx.k, cx.sgm
        cx.e_pT = [Buf(self.psb[0].ap[:, 0:512], self.psb[0].tok), Buf(self.psb[1].ap[:, 0:512], self.psb[1].tok)]
        if setB:
            E = self.l1E
            cx.e_sm = E["sm"]
            cx.e_yn, cx.e_vT = E["b"]
            cx.e_sqy, cx.e_bv, cx.e_tmp = E["f"]
            cx.e_pT = [Buf(self.psf[4].ap[:, :].bitcast(BF16)[:, 0:512], self.psf[4].tok),
                       Buf(self.psf[5].ap[:, :].bitcast(BF16)[:, 0:512], self.psf[5].tok)]
        return cx

    def l1_pre(self, cx):
        kb = self.kb
        d = self.dram
        pv, pd = self.pv1, self.pd1
        C = self.DECAY_C
        hp = cx.hp
        bq, r, k, sgm, icl, cs, Epos, Eneg, Eexc, kkn, t1, sg = (cx.bq, cx.r, cx.k, cx.sgm, cx.icl, cx.cs, cx.Epos,
                                                                 cx.Eneg, cx.Eexc, cx.kkn, cx.t1, cx.sg)
        sq, at, rt, kt, bt, ktm, btm, vsb, rkr, yn, NN, NNT = (cx.sq, cx.at, cx.rt, cx.kt, cx.bt, cx.ktm, cx.btm,
                                                               cx.vsb, cx.rkr, cx.yn, cx.NN, cx.NNT)
        psf = self.psf
        cols = slice(hp * 128, (hp + 1) * 128)
        pr, pk, pg, pvv, pw, pa = psf[2], psf[3], psf[4], psf[5], psf[0], psf[1]
        wk = self.load_w(d["rw_w_in"][1, :, cols])
        wv = self.load_w(d["rw_w_in"][2, :, cols])
        wr = self.load_w(d["rw_w_in"][0, :, cols])
        wg = self.load_w(d["rw_w_in"][3, :, cols])
        for (w_, m_, p_) in ((wk, self.mix[1], pk), (wr, self.mix[0], pr), (wg, self.mix[3], pg)):
            for kc in range(8):
                kb.mm(p_.ap[:, :], w_.ap[:, kc, :], m_.ap[:, kc, :], kc == 0, kc == 7, R=[w_.tok, m_.tok], W=[p_.tok])
        for sub in range(NSUB):
            for kc in range(8):
                kb.mm(pvv.ap[:, sub * 128:(sub + 1) * 128], self.mix[2].ap[:, kc, sub * 128:(sub + 1) * 128],
                      wv.ap[:, kc, :], kc == 0, kc == 7, R=[wv.tok, self.mix[2].tok], W=[pvv.tok])
        kb.mm(pw.ap[:, :], self.w2b.ap[0:64, cols], self.hid.ap[0:64, 0:512], True, True,
              R=[self.w2b.tok, self.hid.tok], W=[pw.tok])
        kb.mm(pa.ap[:, :], self.a2b.ap[0:64, cols], self.hid.ap[0:64, 512:1024], True, True,
              R=[self.a2b.tok, self.hid.tok], W=[pa.tok])
        kb.act(r.ap[:, 0:512], pr.ap[:, :], AF.Copy, R=[pr.tok], W=[r.tok])
        kb.act(sg.ap[:, 0:512], pg.ap[:, :], AF.Sigmoid, R=[pg.tok], W=[sg.tok])
        kb.tt(sg.ap[:, 0:512], sg.ap[:, 0:512], pg.ap[:, :], ALU.mult, R=[sg.tok, pg.tok], W=[sg.tok])
        kb.act(k.ap[:, 0:512], pk.ap[:, :], AF.Copy, R=[pk.tok], W=[k.tok])
        kb.act(sq.ap[:, :], pk.ap[:, :], AF.Square, R=[pk.tok, pv.tok], W=[sq.tok], scale=pv.ap[:, 64 + hp:65 + hp])
        kb.act(vsb.ap[:, 0:512], pvv.ap[:, :], AF.Copy, R=[pvv.tok], W=[vsb.tok])
        kb.act(sgm.ap[:, 0:512], pw.ap[:, :], AF.Sigmoid, R=[pw.tok, pv.tok], W=[sgm.tok], bias=pv.ap[:, 48 + hp:49 + hp])
        kb.act(icl.ap[:, 0:512], pa.ap[:, :], AF.Sigmoid, R=[pa.tok, pv.tok], W=[icl.tok], bias=pv.ap[:, 56 + hp:57 + hp])
        pn = psf[0]
        kb.mm(pn.ap[:, :], self.onesbd.ap[:, :], sq.ap[:, :], True, True, R=[self.onesbd.tok, sq.tok], W=[pn.tok])
        kb.act(t1.ap[:, 0:512], pn.ap[:, :], AF.Sqrt, R=[pn.tok], W=[t1.tok])
        kb.ts(t1.ap[:, 0:512], t1.ap[:, 0:512], 1e-12, None, ALU.max, None, R=[t1.tok], W=[t1.tok])
        kb.recip(t1.ap[:, 0:512], t1.ap[:, 0:512], R=[t1.tok], W=[t1.tok])
        kb.stt(kkn.ap[:, 0:512], k.ap[:, 0:512], pv.ap[:, 64 + hp:65 + hp], t1.ap[:, 0:512], ALU.mult, ALU.mult,
               R=[k.tok, pv.tok, t1.tok], W=[kkn.tok])
        kb.scan(cs.ap[:, 0:512], self.rmask.ap[:, :], sgm.ap[:, 0:512], 0.0, ALU.mult, ALU.add,
                R=[self.rmask.tok, sgm.tok], W=[cs.tok])
        kb.tt(Eexc.ap[:, 0:512], cs.ap[:, 0:512], sgm.ap[:, 0:512], ALU.subtract, R=[cs.tok, sgm.tok], W=[Eexc.tok])
        kb.act(Epos.ap[:, 0:512], cs.ap[:, 0:512], AF.Exp, R=[cs.tok], W=[Epos.tok], scale=C)
        kb.act(Eneg.ap[:, 0:512], cs.ap[:, 0:512], AF.Exp, R=[cs.tok], W=[Eneg.tok], scale=-C)
        kb.act(Eexc.ap[:, 0:512], Eexc.ap[:, 0:512], AF.Exp, R=[Eexc.tok], W=[Eexc.tok], scale=C)
        kb.ts(t1.ap[:, 0:512], icl.ap[:, 0:512], pv.ap[:, 72 + hp:73 + hp], pd.ap[:, hp:hp + 1], ALU.mult, ALU.add,
              R=[icl.tok, pv.tok, pd.tok], W=[t1.tok])
        kb.tt(k.ap[:, 0:512], k.ap[:, 0:512], t1.ap[:, 0:512], ALU.mult, R=[k.tok, t1.tok], W=[k.tok])
        kb.tt(bq.ap[:, 0:512], kkn.ap[:, 0:512], icl.ap[:, 0:512], ALU.mult, R=[kkn.tok, icl.tok], W=[bq.tok])
        kb.tt(kt.ap[:, :], k.ap[:, 0:512], Eneg.ap[:, 0:512], ALU.mult, R=[k.tok, Eneg.tok], W=[kt.tok])
        kb.tt(bt.ap[:, :], bq.ap[:, 0:512], Eneg.ap[:, 0:512], ALU.mult, R=[bq.tok, Eneg.tok], W=[bt.tok])
        kb.stt(at.ap[:, 0:512], kkn.ap[:, 0:512], -1.0, Eexc.ap[:, 0:512], ALU.mult, ALU.mult, R=[kkn.tok, Eexc.tok],
               W=[at.tok])
        pbon = psf[1]
        kb.tt(rt.ap[:, 0:512], r.ap[:, 0:512], Epos.ap[:, 0:512], ALU.mult, R=[r.tok, Epos.tok], W=[rt.tok])
        kb.stt(rkr.ap[:, :], r.ap[:, 0:512], pv.ap[:, 80 + hp:81 + hp], k.ap[:, 0:512], ALU.mult, ALU.mult,
               R=[r.tok, pv.tok, k.tok], W=[rkr.tok])
        kb.mm(pbon.ap[:, :], self.onesbd.ap[:, :], rkr.ap[:, :], True, True, R=[self.onesbd.tok, rkr.tok], W=[pbon.tok])
        kb.act(r.ap[:, 0:512], pbon.ap[:, :], AF.Copy, R=[pbon.tok], W=[r.tok])
        Epos3 = Epos.ap[:, 0:512].rearrange("p (c t) -> p c t", t=64)
        wend_bc = Epos3[:, :, 63:64].broadcast_to([128, NCH, 64])
        kb.tt(yn.ap[:, :].rearrange("p (c t) -> p c t", t=64), kt.ap[:, :].rearrange("p (c t) -> p c t", t=64), wend_bc,
              ALU.mult, R=[kt.tok, Epos.tok], W=[yn.tok])
        for sub in range(NSUB):
            kb.tr(self.psb[0].ap[:, sub * 128:(sub + 1) * 128], yn.ap[:, sub * 128:(sub + 1) * 128], self.ident.ap[:, :],
                  R=[yn.tok, self.ident.tok], W=[self.psb[0].tok])
        kb.tt(yn.ap[:, :].rearrange("p (c t) -> p c t", t=64), bt.ap[:, :].rearrange("p (c t) -> p c t", t=64), wend_bc,
              ALU.mult, R=[bt.tok, Epos.tok], W=[yn.tok])
        for sub in range(NSUB):
            kb.tr(self.psb[1].ap[:, sub * 128:(sub + 1) * 128], yn.ap[:, sub * 128:(sub + 1) * 128], self.ident.ap[:, :],
                  R=[yn.tok, self.ident.tok], W=[self.psb[1].tok])
        kb.copy(ktm.ap[:, 0:512], self.psb[0].ap[:, 0:512], R=[self.psb[0].tok], W=[ktm.tok])
        kb.act(btm.ap[:, 0:512], self.psb[1].ap[:, 0:512], AF.Copy, R=[self.psb[1].tok], W=[btm.tok])
        SC, PPh = cx.SC, cx.PPh
        NNs = [(NN, NNT), (rkr, sq)]
        for hl in range(2):
            rs = slice(64 * hl, 64 * hl + 64)
            nn, nnt = NNs[hl]
            for sub in range(NSUB):
                pi = hl * 4 + sub
                X, Y = (psf[0], psf[2]) if pi % 2 == 0 else (psf[3], psf[4])
                tc = slice(sub * 128, (sub + 1) * 128)
                kb.mm(X.ap[:, 0:128], kt.ap[rs, tc], at.ap[rs, tc], True, True, R=[kt.tok, at.tok], W=[X.tok])
                kb.mm(X.ap[:, 128:256], kt.ap[rs, tc], rt.ap[rs, tc], True, True, R=[kt.tok, rt.tok], W=[X.tok])
                kb.mm(X.ap[:, 256:384], bt.ap[rs, tc], rt.ap[rs, tc], True, True, R=[bt.tok, rt.tok], W=[X.tok])
                kb.mm(Y.ap[:, 0:128], bt.ap[rs, tc], at.ap[rs, tc], True, True, R=[bt.tok, at.tok], W=[Y.tok])
                kb.mm(Y.ap[:, 128:256], at.ap[rs, tc], bt.ap[rs, tc], True, True, R=[bt.tok, at.tok], W=[Y.tok])
                kb.tt(SC.ap[:, pi, 0:384], X.ap[:, 0:384], self.M3.ap[:, 0:384], ALU.mult, R=[X.tok, self.M3.tok],
                      W=[SC.tok])
                kb.tt(nn.ap[:, tc], Y.ap[:, 0:128], self.M2.ap[:, 0:128], ALU.mult, R=[Y.tok, self.M2.tok], W=[nn.tok])
                kb.tt(nnt.ap[:, tc], Y.ap[:, 128:256], self.M2.ap[:, 128:256], ALU.mult, R=[Y.tok, self.M2.tok],
                      W=[nnt.tok])
            kb.tt(PPh[hl].ap[:, 0:512].rearrange("p (j t) -> p j t", t=128),
                  nn.ap[:, 0:512].rearrange("p (j t) -> p j t", t=128),
                  self.ident.ap[:, :].unsqueeze(1).broadcast_to([128, 4, 128]), ALU.add,
                  R=[nn.tok, self.ident.tok], W=[PPh[hl].tok])
        ABC = [(psf[3], psf[4], psf[5]), (psf[0], psf[2], psf[1])]
        for lvl in range(5):
            last = lvl == 4
            for hl in range(2):
                nn, nnt = NNs[hl]
                A_, B_, C_ = ABC[hl]
                for j in range(4):
                    tc = slice(j * 128, (j + 1) * 128)
                    if not last:
                        kb.mm(A_.ap[:, tc], nnt.ap[:, tc], nn.ap[:, tc], True, True, R=[nn.tok, nnt.tok], W=[A_.tok])
                    kb.mm(B_.ap[:, tc], nn.ap[:, tc], nnt.ap[:, tc], True, True, R=[nn.tok, nnt.tok], W=[B_.tok])
            for hl in range(2):
                nn, nnt = NNs[hl]
                A_, B_, C_ = ABC[hl]
                kb.act(nnt.ap[:, :], B_.ap[:, :], AF.Copy, R=[B_.tok], W=[nnt.tok])
                if not last:
                    kb.copy(nn.ap[:, :], A_.ap[:, :], R=[A_.tok], W=[nn.tok])
            for hl in range(2):
                nn, nnt = NNs[hl]
                A_, B_, C_ = ABC[hl]
                for j in range(4):
                    tc = slice(j * 128, (j + 1) * 128)
                    kb.mm(C_.ap[:, tc], nnt.ap[:, tc], PPh[hl].ap[:, tc], True, True, R=[nnt.tok, PPh[hl].tok],
                          W=[C_.tok])
            for hl in range(2):
                A_, B_, C_ = ABC[hl]
                kb.tt(PPh[hl].ap[:, 0:512], C_.ap[:, :], PPh[hl].ap[:, 0:512], ALU.add, R=[C_.tok, PPh[hl].tok],
                      W=[PPh[hl].tok])

    def l1_chain(self, cxs):
        kb = self.kb
        VW = 64
        ho = 4 * VW
        for cx in cxs:
            for hl in range(2):
                rs = slice(64 * hl, 64 * hl + 64)
                kb.act(cx.smb.ap[rs, ho + hl * VW:ho + (hl + 1) * VW], cx.Hs[rs, :], AF.Copy, R=[cx.Ht], W=[cx.smb.tok])
        for c in range(NCH):
            sub, half = divmod(c, 2)
            p0 = 64 * half
            ps_ = slice(p0, p0 + 64)
            cc = slice(c * 64, (c + 1) * 64)
            for cx in cxs:
                smb, XU, xo = cx.smb, cx.XU, cx.xo
                for hl in range(2):
                    vv = slice(sub * 128 + hl * 64, sub * 128 + hl * 64 + 64)
                    pi = hl * 4 + sub
                    kb.mm(XU.ap[ps_, xo + hl * VW:xo + (hl + 1) * VW], cx.at.ap[:, cc],
                          smb.ap[:, ho + hl * VW:ho + (hl + 1) * VW], True, False, R=[cx.at.tok, smb.tok], W=[XU.tok])
                    kb.mm(XU.ap[ps_, xo + hl * VW:xo + hl * VW + 64], cx.SC.ap[:, pi, p0:p0 + 64], cx.vsb.ap[:, vv], False,
                          True, R=[cx.SC.tok, cx.vsb.tok], W=[XU.tok])
            for cx in cxs:
                kb.act(cx.smb.ap[ps_, 0:2 * VW], cx.XU.ap[ps_, cx.xo:cx.xo + 2 * VW], AF.Copy, R=[cx.XU.tok],
                       W=[cx.smb.tok])
            for cx in cxs:
                smb, XU, xo = cx.smb, cx.XU, cx.xo
                for hl in range(2):
                    kb.mm(XU.ap[ps_, xo + 2 * VW + hl * VW:xo + 2 * VW + (hl + 1) * VW],
                          cx.PPh[hl].ap[:, sub * 128 + p0:sub * 128 + p0 + 64], smb.ap[:, hl * VW:(hl + 1) * VW], True, True,
                          R=[cx.PPh[hl].tok, smb.tok], W=[XU.tok])
            for cx in cxs:
                kb.copy(cx.smb.ap[ps_, 2 * VW:4 * VW], cx.XU.ap[ps_, cx.xo + 2 * VW:cx.xo + 4 * VW], R=[cx.XU.tok],
                        W=[cx.smb.tok])
            for cx in cxs:
                smb, dH, do = cx.smb, cx.dH, cx.do
                for hl in range(2):
                    rs = slice(64 * hl, 64 * hl + 64)
                    vv = slice(sub * 128 + hl * 64, sub * 128 + hl * 64 + 64)
                    kb.mm(dH.ap[rs, do:do + VW], cx.btm.ap[ps_, vv], smb.ap[ps_, 2 * VW + hl * VW:2 * VW + (hl + 1) * VW], True,
                          False, R=[cx.btm.tok, smb.tok], W=[dH.tok])
                    kb.mm(dH.ap[rs, do:do + 64], cx.ktm.ap[ps_, vv], cx.vsb.ap[ps_, vv], False, True,
                          R=[cx.ktm.tok, cx.vsb.tok], W=[dH.tok])
            for cx in cxs:
                smb, yps = cx.smb, cx.yps
                for hl in range(2):
                    us = slice(2 * VW + hl * VW, 2 * VW + hl * VW + 64)
                    vv = slice(sub * 128 + hl * 64, sub * 128 + hl * 64 + 64)
                    pi = hl * 4 + sub
                    kb.mm(yps.ap[ps_, vv], cx.rt.ap[:, cc], smb.ap[:, ho + hl * VW:ho + (hl + 1) * VW], True, False,
                          R=[cx.rt.tok, smb.tok], W=[yps.tok])
                    kb.mm(yps.ap[ps_, vv], cx.SC.ap[:, pi, 256 + p0:256 + p0 + 64], smb.ap[:, us], False, False,
                          R=[cx.SC.tok, smb.tok], W=[yps.tok])
                    kb.mm(yps.ap[ps_, vv], cx.SC.ap[:, pi, 128 + p0:128 + p0 + 64], cx.vsb.ap[:, vv], False, True,
                          R=[cx.SC.tok, cx.vsb.tok], W=[yps.tok])
            for cx in cxs:
                smb, dH, do = cx.smb, cx.dH, cx.do
                for hl in range(2):
                    rs = slice(64 * hl, 64 * hl + 64)
                    kb.stt(smb.ap[rs, ho + hl * VW:ho + (hl + 1) * VW], cx.Hs[rs, :], cx.Epos.ap[rs, c * 64 + 63:c * 64 + 64],
                           dH.ap[rs, do:do + VW], ALU.mult, ALU.add, R=[cx.Ht, cx.Epos.tok, dH.tok], W=[smb.tok])
            for cx in cxs:
                kb.stt(cx.Hs, cx.Hs, cx.Epos.ap[:, c * 64 + 63:c * 64 + 64], cx.dH.ap[:, cx.do:cx.do + VW], ALU.mult,
                       ALU.add, R=[cx.Ht, cx.Epos.tok, cx.dH.tok], W=[cx.Ht])

    def l1_epi(self, cx):
        kb = self.kb
        pv = self.pv1
        sm = self.sm
        hp = cx.hp
        yps, yn, vsb, sg, r = cx.yps, cx.yn, cx.vsb, cx.sg, cx.r
        s1, s2, mean, msq, var, rms, rstd = sm[6], sm[7], sm[8], sm[9], sm[10], sm[11], sm[12]
        sqy = cx.t1
        y3 = yps.ap[:, :].rearrange("p (g v) -> p g v", v=64)
        kb.emit("vector", lambda e: e.reduce_sum(s1.ap[:, 0:8], y3, mybir.AxisListType.X), R=[yps.tok], W=[s1.tok])
        kb.act(sqy.ap[:, 0:512], yps.ap[:, :], AF.Square, R=[yps.tok], W=[sqy.tok])
        sq3 = sqy.ap[:, 0:512].rearrange("p (g v) -> p g v", v=64)
        kb.emit("vector", lambda e: e.reduce_sum(s2.ap[:, 0:8], sq3, mybir.AxisListType.X), R=[sqy.tok], W=[s2.tok])
        kb.ts(mean.ap[:, 0:8], s1.ap[:, 0:8], 1.0 / 64, None, ALU.mult, None, R=[s1.tok], W=[mean.tok])
        kb.tt(msq.ap[:, 0:8], mean.ap[:, 0:8], mean.ap[:, 0:8], ALU.mult, R=[mean.tok], W=[msq.tok])
        kb.stt(var.ap[:, 0:8], s2.ap[:, 0:8], 1.0 / 64, msq.ap[:, 0:8], ALU.mult, ALU.subtract, R=[s2.tok, msq.tok],
               W=[var.tok])
        kb.act(rms.ap[:, 0:8], var.ap[:, 0:8], AF.Sqrt, R=[var.tok, self.cst.tok], W=[rms.tok], bias=self.cst.ap[:, 2:3])
        kb.recip(rstd.ap[:, 0:8], rms.ap[:, 0:8], R=[rms.tok], W=[rstd.tok])
        for g in range(8):
            kb.ts(yn.ap[:, g * 64:(g + 1) * 64], yps.ap[:, g * 64:(g + 1) * 64], mean.ap[:, g:g + 1], rstd.ap[:, g:g + 1],
                  ALU.subtract, ALU.mult, R=[yps.tok, mean.tok, rstd.tok], W=[yn.tok])
        for sub in range(NSUB):
            kb.tr(self.psb[0].ap[:, sub * 128:(sub + 1) * 128], yn.ap[:, sub * 128:(sub + 1) * 128], self.ident.ap[:, :],
                  R=[yn.tok, self.ident.tok], W=[self.psb[0].tok])
        for sub in range(NSUB):
            kb.tr(self.psb[1].ap[:, sub * 128:(sub + 1) * 128], vsb.ap[:, sub * 128:(sub + 1) * 128], self.ident.ap[:, :],
                  R=[vsb.tok, self.ident.tok], W=[self.psb[1].tok])
        vT = cx.sq
        kb.act(vT.ap[:, :], self.psb[1].ap[:, 0:512], AF.Copy, R=[self.psb[1].tok], W=[vT.tok])
        bv = cx.k
        kb.tt(bv.ap[:, 0:512], r.ap[:, 0:512], vT.ap[:, :], ALU.mult, R=[r.tok, vT.tok], W=[bv.tok])
        kb.stt(cx.sgm.ap[:, 0:512], self.psb[0].ap[:, 0:512], pv.ap[:, 88 + hp:89 + hp], bv.ap[:, 0:512], ALU.mult, ALU.add,
               R=[self.psb[0].tok, pv.tok, bv.tok], W=[cx.sgm.tok])
        kb.stt(self.merged.ap[:, hp, :], cx.sgm.ap[:, 0:512], pv.ap[:, 96 + hp:97 + hp], sg.ap[:, 0:512], ALU.add, ALU.mult,
               R=[cx.sgm.tok, pv.tok, sg.tok], W=[self.merged.tok])

    def l1_epi2(self, cxs):
        kb = self.kb
        pv = self.pv1
        X = mybir.AxisListType.X
        for cx in cxs:
            s1 = cx.e_sm[0]
            y3 = cx.yps.ap[:, :].rearrange("p (g v) -> p g v", v=64)
            kb.emit("vector", lambda e, s1=s1, y3=y3: e.reduce_sum(s1.ap[:, 0:8], y3, X), R=[cx.yps.tok], W=[s1.tok])
            kb.act(cx.e_sqy.ap[:, 0:512], cx.yps.ap[:, :], AF.Square, R=[cx.yps.tok], W=[cx.e_sqy.tok])
        for cx in cxs:
            s1, s2, mean, msq, var, rms, rstd = cx.e_sm
            sq3 = cx.e_sqy.ap[:, 0:512].rearrange("p (g v) -> p g v", v=64)
            kb.emit("vector", lambda e, s2=s2, sq3=sq3: e.reduce_sum(s2.ap[:, 0:8], sq3, X), R=[cx.e_sqy.tok], W=[s2.tok])
            kb.ts(mean.ap[:, 0:8], s1.ap[:, 0:8], 1.0 / 64, None, ALU.mult, None, R=[s1.tok], W=[mean.tok])
            kb.tt(msq.ap[:, 0:8], mean.ap[:, 0:8], mean.ap[:, 0:8], ALU.mult, R=[mean.tok], W=[msq.tok])
            kb.stt(var.ap[:, 0:8], s2.ap[:, 0:8], 1.0 / 64, msq.ap[:, 0:8], ALU.mult, ALU.subtract, R=[s2.tok, msq.tok],
                   W=[var.tok])
        for cx in cxs:
            s1, s2, mean, msq, var, rms, rstd = cx.e_sm
            kb.act(rms.ap[:, 0:8], var.ap[:, 0:8], AF.Sqrt, R=[var.tok, self.cst.tok], W=[rms.tok], bias=self.cst.ap[:, 2:3])
        for cx in cxs:
            s1, s2, mean, msq, var, rms, rstd = cx.e_sm
            kb.recip(rstd.ap[:, 0:8], rms.ap[:, 0:8], R=[rms.tok], W=[rstd.tok])
            y3_ = cx.yps.ap[:, :].rearrange("p (g v) -> p g v", v=64)
            t3_ = cx.e_sqy.ap[:, 0:512].rearrange("p (g v) -> p g v", v=64)
            n3_ = cx.e_yn.ap[:, 0:512].rearrange("p (g v) -> p g v", v=64)
            kb.tt(t3_, y3_, mean.ap[:, 0:8].unsqueeze(2).broadcast_to([128, 8, 64]), ALU.subtract,
                  R=[cx.yps.tok, mean.tok, s2.tok], W=[cx.e_sqy.tok])
            kb.tt(n3_, t3_, rstd.ap[:, 0:8].unsqueeze(2).broadcast_to([128, 8, 64]), ALU.mult,
                  R=[cx.e_sqy.tok, rstd.tok], W=[cx.e_yn.tok])
        for cx in cxs:
            for sub in range(NSUB):
                kb.tr(cx.e_pT[0].ap[:, sub * 128:(sub + 1) * 128], cx.e_yn.ap[:, sub * 128:(sub + 1) * 128],
                      self.ident.ap[:, :], R=[cx.e_yn.tok, self.ident.tok], W=[cx.e_pT[0].tok])
            for sub in range(NSUB):
                kb.tr(cx.e_pT[1].ap[:, sub * 128:(sub + 1) * 128], cx.vsb.ap[:, sub * 128:(sub + 1) * 128],
                      self.ident.ap[:, :], R=[cx.vsb.tok, self.ident.tok], W=[cx.e_pT[1].tok])
        for cx in cxs:
            kb.act(cx.e_vT.ap[:, 0:512], cx.e_pT[1].ap[:, 0:512], AF.Copy, R=[cx.e_pT[1].tok], W=[cx.e_vT.tok])
        for cx in cxs:
            kb.tt(cx.e_bv.ap[:, 0:512], cx.r.ap[:, 0:512], cx.e_vT.ap[:, 0:512], ALU.mult, R=[cx.r.tok, cx.e_vT.tok],
                  W=[cx.e_bv.tok])
            kb.stt(cx.e_tmp.ap[:, 0:512], cx.e_pT[0].ap[:, 0:512], pv.ap[:, 88 + cx.hp:89 + cx.hp], cx.e_bv.ap[:, 0:512],
                   ALU.mult, ALU.add, R=[cx.e_pT[0].tok, pv.tok, cx.e_bv.tok], W=[cx.e_tmp.tok])
            kb.stt(self.merged.ap[:, cx.hp, :], cx.e_tmp.ap[:, 0:512], pv.ap[:, 96 + cx.hp:97 + cx.hp], cx.sg.ap[:, 0:512],
                   ALU.add, ALU.mult, R=[cx.e_tmp.tok, pv.tok, cx.sg.tok], W=[self.merged.tok])

    def l1_front_load(self, T, u1all, u1tok):
        kb = self.kb
        st1 = self.st1
        r, tl = divmod(T, self.NT)
        up = st1.ap[:, 512:520].unsqueeze(2)
        kb.copy(self.uT.ap[:, :, 0:1], up, R=[st1.tok], W=[self.uT.tok])
        src = u1all[tl].ap()[r * 128:(r + 1) * 128, :].rearrange("p (k t) -> p k t", k=8)
        kb.dma("sync", self.uT.ap[:, :, 1:513], src, self.uT.tok, R=[u1tok[tl]], W=[self.uT.tok])
        kb.copy(up, self.uT.ap[:, :, 512:513], R=[self.uT.tok], W=[st1.tok])

    def l1_front_rest(self, T):
        kb = self.kb
        diff = self.merged
        kb.tt(diff.ap[:, 8:16, :], self.uT.ap[:, :, 0:512], self.uT.ap[:, :, 1:513], ALU.subtract,
              R=[self.uT.tok], W=[diff.tok])
        for (wb, ws, ph, fn, lo) in ((self.w1b, self.w1s, self.psf[4], AF.Tanh, 0),
                                     (self.a1b, self.a1s, self.psf[5], AF.Copy, 512)):
            for kc in range(8):
                kb.mm(ph.ap[0:64, :], wb.ap[:, kc, :], self.uT.ap[:, kc, 1:513], kc == 0, False,
                      R=[wb.tok, self.uT.tok], W=[ph.tok])
            for kc in range(8):
                kb.mm(ph.ap[0:64, :], ws.ap[:, kc, :], diff.ap[:, 8 + kc, :], False, kc == 7,
                      R=[ws.tok, diff.tok], W=[ph.tok])
            if fn == AF.Tanh:
                tf = self.f[0]
                kb.act(tf.ap[0:64, 0:512], ph.ap[0:64, :], AF.Sigmoid, R=[ph.tok], W=[tf.tok], scale=2.0)
                kb.ts(self.hid.ap[:, lo:lo + 512], tf.ap[0:64, 0:512], 2.0, -1.0, ALU.mult, ALU.add, R=[tf.tok],
                      W=[self.hid.tok])
            else:
                kb.act(self.hid.ap[:, lo:lo + 512], ph.ap[0:64, :], fn, R=[ph.tok], W=[self.hid.tok])
        for p_ in range(4):
            self.l1_mix(p_, self.mix[p_])

    def layer1_tile_hp(self, T, NG, u1all, u1tok, mgloc, mgtok, after_pre=None):
        kb = self.kb
        NT = self.NT
        r, tl = divmod(T, NT)
        if T == 0:
            self.l1_front_load(0, u1all, u1tok)
            self.l1_front_rest(0)
            if NG > 1:
                self.l1_front_load(1, u1all, u1tok)
        cxA = self.l1_ctx(0, False)
        cxB = self.l1_ctx(1, True)
        self.l1_pre(cxA)
        self.l1_pre(cxB)
        if after_pre is not None:
            after_pre()
        self.l1_chain([cxA, cxB])
        self.l1_epi2([cxA, cxB])
        dst = mgloc[T].ap().rearrange("p (j t) -> p j t", j=2)
        kb.dma("sync", dst, self.merged.ap[:, 0:2, :], mgtok[T], R=[self.merged.tok], W=[mgtok[T]])
        if T + 1 < NG:
            self.l1_front_rest(T + 1)
            if T + 2 < NG:
                self.l1_front_load(T + 2, u1all, u1tok)

    def build_comm(self):
        kb = self.kb
        NT = self.NT
        NG = 4 * NT
        self.declare()
        self.alloc_common()
        self.consts()
        self.l0_setup()
        self.l0_begin()
        self.l1_setup()
        self.comm_setup()
        u1loc = [self.nc.dram_tensor(f"u1loc{t}", [128, 8 * 512], BF16) for t in range(NT)]
        u1all = [self.nc.dram_tensor(f"u1all{t}", [512, 8 * 512], BF16) for t in range(NT)]
        mgloc = [self.nc.dram_tensor(f"mgloc{q}", [128, 2 * 512], BF16) for q in range(NG)]
        mgall = [self.nc.dram_tensor(f"mgall{q}", [512, 2 * 512], BF16) for q in range(NG)]
        u1t = [kb.tok(f"u1loc{t}") for t in range(NT)]
        mgt = [kb.tok(f"mgloc{q}") for q in range(NG)]
        u1a = [None] * NT
        mga = [None] * NG
        u0loc = [self.nc.dram_tensor(f"u0loc{t}", [128, 8 * 512], BF16) for t in range(NT)]
        u0all = [self.nc.dram_tensor(f"u0all{t}", [512, 8 * 512], BF16) for t in range(NT)]
        NCK = NG
        m0loc = [self.nc.dram_tensor(f"m0loc{c}", [128, 4 * 512], BF16) for c in range(NCK)]
        m0all = [self.nc.dram_tensor(f"m0all{c}", [512, 4 * 512], BF16) for c in range(NCK)]
        u0t = [kb.tok(f"u0loc{t}") for t in range(NT)]
        m0t = [kb.tok(f"m0loc{c}") for c in range(NCK)]
        u0a = [None] * NT
        m0a = [None] * NCK
        w1f_ = self.wout1.ap[:, :, :].rearrange("p a b -> p (a b)").bitcast(F32)
        xsA = self.xs
        xsB = [Buf(w1f_[:, i * 1024:(i + 1) * 1024], kb.tok(f"xsB{i}")) for i in range(NSUB)]
        xbufs = [xsA, xsB]
        m0f_ = self.mix[0].ap[:, :, :].rearrange("p a b -> p (a b)")
        xn2_p0 = Buf(m0f_[:, 0:1024], kb.tok("xn2p0"))
        self.norm_xn2 = xn2_p0
        self.load_x(0, xsA)
        if NT > 1:
            self.load_x(1, xsB)
        for tile in range(NT):
            self.xs = xbufs[tile % 2]
            self.gain = self.pv0
            self.gain_off = 88
            self.norm_T(tile, 0)
            dst = u0loc[tile].ap().rearrange("p (k t) -> p k t", k=8)
            kb.dma("sync", dst, self.uT.ap[:, :, 0:512], u0t[tile], R=[self.uT.tok], W=[u0t[tile]])
            u0a[tile] = self.cc_gather_dram(u0loc[tile], u0all[tile], u0t[tile], f"g0u{tile}")
            if tile + 2 < NT:
                self.load_x(tile + 2, xbufs[tile % 2])
        self.xs = xsA
        self.norm_xn2 = None
        for bf in self.rgB_f[0:3]:
            kb.absorb(bf.tok, [xn2_p0.tok])
        for bf in self.hgB_b + [self.hgB_f[10]]:
            kb.absorb(bf.tok, [b_.tok for b_ in xsB])
        kb.absorb(self.wout1.tok, [b_.tok for b_ in xsB])
        scf = self.SC.ap[:, :, :].rearrange("p a b -> p (a b)")
        m3f = self.mix[3].ap[:, :, :].rearrange("p a b -> p (a b)")
        xslots = [Buf(scf[:, i * 1024:(i + 1) * 1024].rearrange("p (a b) -> p a b", b=128), kb.tok(f"wsx{i}"))
                  for i in range(3)]
        xslots += [Buf(m3f[:, 2048 + i * 1024:2048 + (i + 1) * 1024].rearrange("p (a b) -> p a b", b=128),
                       kb.tok(f"wsx{3 + i}")) for i in range(2)]
        self.wslots_active = self.wslot + xslots[0:4]
        pend = []
        def p1_load(T_):
            r_, tl_ = divmod(T_, NT)
            src_ = u0all[tl_].ap()[r_ * 128:(r_ + 1) * 128, :].rearrange("p (k t) -> p k t", k=8)
            kb.dma("sync", self.uT.ap[:, :, 0:512], src_, self.uT.tok, R=[u0a[tl_]], W=[self.uT.tok])

        p1_load(0)
        for T in range(NG):
            r, tl = divmod(T, NT)
            self.l0_rg2()
            self.l0_hg2((lambda T_=T: p1_load(T_ + 1)) if T + 1 < NG else None)
            nblk = -(-16 // NG)
            for cbw in range(T * nblk, min(16, (T + 1) * nblk)):
                self.load_wout0(cbw)
            ck = T
            dst = m0loc[ck].ap().rearrange("p (j t) -> p j t", j=4)
            kb.dma("sync", dst[:, 0:2, :], self.merged.ap[:, 0:2, :], m0t[ck], R=[self.merged.tok], W=[m0t[ck]])
            kb.dma("sync", dst[:, 2:4, :], self.merged.ap[:, 8:10, :], m0t[ck], R=[self.merged.tok], W=[m0t[ck]])
            for ck_ in pend:
                m0a[ck_] = self.cc_gather_dram(m0loc[ck_], m0all[ck_], m0t[ck_], f"g0m{ck_}")
            pend = [ck]
        for ck_ in pend:
            m0a[ck_] = self.cc_gather_dram(m0loc[ck_], m0all[ck_], m0t[ck_], f"g0m{ck_}")
        self.wslots_active = None
        kb.absorb(self.SC.tok, [b_.tok for b_ in xslots[0:3]])
        kb.absorb(self.mix[3].tok, [xslots[3].tok])
        for i in range(4):
            kb.absorb(self.mix[i].tok, [bf.tok for bf in self.rgB_f] + [self.rgB_xcb.tok])
        kb.absorb(self.wout1.tok, [bf.tok for bf in self.hgB_b] + [self.hgB_f[10].tok])
        Lp = [(self.mix[0], self.mix[1]), (self.mix[2], self.mix[3])]
        xn2_p2 = Buf(self.wout1.ap[:, :, :].rearrange("p a b -> p (a b)")[:, 0:1024], kb.tok("xn2p2"))
        xn2_p2.tok.r = dict(self.wout1.tok.r)
        xn2_p2.tok.w = self.wout1.tok.w
        self.norm_xn2 = xn2_p2

        def p2_load(tile, q):
            La, Lb = Lp[q % 2]
            ck = q * NT + tile
            for r in range(4):
                src = m0all[ck].ap()[r * 128:(r + 1) * 128, :].rearrange("p (j t) -> p j t", j=4)
                qn = "sync" if q % 2 == 0 else "scalar"
                kb.dma(qn, La.ap[:, 2 * r:2 * r + 2, :], src[:, 0:2, :], La.tok, R=[m0a[ck]], W=[La.tok])
                kb.dma(qn, Lb.ap[:, 2 * r:2 * r + 2, :], src[:, 2:4, :], Lb.tok, R=[m0a[ck]], W=[Lb.tok])

        def p2_select(tile, q):
            mq = self.msk.ap[:, 8 + q:9 + q]
            for (Lx, lo) in ((Lp[q % 2][0], 0), (Lp[q % 2][1], 8)):
                if q == 0:
                    kb.ts(self.merged.ap[:, lo:lo + 8, :], Lx.ap[:, :, :], mq, None, ALU.mult, None,
                          R=[Lx.tok, self.msk.tok], W=[self.merged.tok])
                else:
                    kb.stt(self.merged.ap[:, lo:lo + 8, :], Lx.ap[:, :, :], mq, self.merged.ap[:, lo:lo + 8, :], ALU.mult,
                           ALU.add, R=[Lx.tok, self.msk.tok, self.merged.tok], W=[self.merged.tok])

        p2_load(0, 0)
        p2_load(0, 1)
        for tile in range(NT):
            self.load_x(tile)
            p2_select(tile, 0)
            p2_load(tile, 2)
            p2_select(tile, 1)
            p2_load(tile, 3)
            p2_select(tile, 2)
            p2_select(tile, 3)
            if tile + 1 < NT:
                p2_load(tile + 1, 0)
                p2_load(tile + 1, 1)
            self.out_proj(tile, 16, self.wout0, self.postbc0)
            for sub in range(NSUB):
                g = tile * NSUB + sub
                kb.dma("sync", self.x1buf.ap()[g * 128:(g + 1) * 128, :], self.xs[sub].ap, self.x1tok[tile],
                       R=[self.xs[sub].tok], W=[self.x1tok[tile]])
            self.gain = self.pv1
            self.gain_off = 104
            self.norm_T(tile, 1)
            dst = u1loc[tile].ap().rearrange("p (k t) -> p k t", k=8)
            kb.dma("sync", dst, self.uT.ap[:, :, 1:513], u1t[tile], R=[self.uT.tok], W=[u1t[tile]])
            u1a[tile] = self.cc_gather_dram(u1loc[tile], u1all[tile], u1t[tile], f"gu{tile}")
        self.norm_xn2 = None
        kb.absorb(self.wout1.tok, [xn2_p2.tok])
        carved = (self.l1B["b"] + [self.l1B["SC"], self.l1B["smb"]] + self.l1B["PPh"] + self.l1B["f"]
                  + self.l1E["b"] + self.l1E["f"])
        for bf in carved:
            bf.tok.r = dict(self.wout0.tok.r)
            bf.tok.w = self.wout0.tok.w
        kb.memset(self.l1B["smb"].ap, 0.0, [self.l1B["smb"].tok])
        self.w1s = Buf(self.WA.ap[:, :, :].rearrange("p a b -> p (a b)")[:, 0:512].rearrange("p (a b) -> p a b", b=64),
                       kb.tok("w1s"))
        self.a1s = Buf(self.WX.ap[:, :, :].rearrange("p a b -> p (a b)")[:, 0:512].rearrange("p (a b) -> p a b", b=64),
                       kb.tok("a1s"))
        for (ws, wb, own, c0) in ((self.w1s, self.w1b, self.WA, 32), (self.a1s, self.a1b, self.WX, 40)):
            ws.tok.r = dict(own.tok.r)
            ws.tok.w = own.tok.w
            kb.tt(ws.ap, wb.ap[:, :, :], self.pv1.ap[:, c0:c0 + 8].unsqueeze(2).broadcast_to([128, 8, 64]), ALU.mult,
                  R=[wb.tok, self.pv1.tok], W=[ws.tok])
        def p4_load(tl, q):
            L = self.mix[2 * (q % 2)]
            for r in range(4):
                Tg = q * NT + tl
                src = mgall[Tg].ap()[r * 128:(r + 1) * 128, :].rearrange("p (j t) -> p j t", j=2)
                kb.dma("sync" if q % 2 == 0 else "scalar", L.ap[:, 2 * r:2 * r + 2, :], src, L.tok, R=[mga[Tg]],
                       W=[L.tok])

        def p4_first():
            p4_load(0, 0)
            p4_load(0, 1)

        pend = []
        for T in range(NG):
            self.layer1_tile_hp(T, NG, u1all, u1a, mgloc, mgt, p4_first if T == NG - 1 else None)
            nblk = -(-8 // NG)
            for cbw in range(T * nblk, min(8, (T + 1) * nblk)):
                self.load_wout1(cbw)
            for q_ in pend:
                mga[q_] = self.cc_gather_dram(mgloc[q_], mgall[q_], mgt[q_], f"gm{q_}")
            pend = [T]
        for q_ in pend:
            mga[q_] = self.cc_gather_dram(mgloc[q_], mgall[q_], mgt[q_], f"gm{q_}")
        def p4_select(tl, q):
            L = self.mix[2 * (q % 2)]
            mq = self.msk.ap[:, 8 + q:9 + q]
            if q == 0:
                kb.ts(self.merged.ap[:, 0:8, :], L.ap[:, :, :], mq, None, ALU.mult, None, R=[L.tok, self.msk.tok],
                      W=[self.merged.tok])
            else:
                kb.stt(self.merged.ap[:, 0:8, :], L.ap[:, :, :], mq, self.merged.ap[:, 0:8, :], ALU.mult, ALU.add,
                       R=[L.tok, self.msk.tok, self.merged.tok], W=[self.merged.tok])

        for tl in range(NT):
            self.load_x1(tl)
            p4_select(tl, 0)
            p4_load(tl, 2)
            p4_select(tl, 1)
            p4_load(tl, 3)
            p4_select(tl, 2)
            p4_select(tl, 3)
            if tl + 1 < NT:
                p4_load(tl + 1, 0)
                p4_load(tl + 1, 1)
            self.out_proj(tl, 8, self.wout1, self.postbc1)
            self.store_out(tl)
        toks = [b.tok for b in self.xs]
        self.kb.wait_all("sync", toks)
        self.kb.replay()

    def load_x1(self, tile):
        kb = self.kb
        for sub in range(NSUB):
            g = tile * NSUB + sub
            kb.dma("sync", self.xs[sub].ap, self.x1buf.ap()[g * 128:(g + 1) * 128, :], self.xs[sub].tok,
                   R=[self.x1tok[tile]], W=[self.xs[sub].tok])

    def declare(self):
        TS = self.TS
        self.din("x", [TS, D])
        self.dout("out", [TS, D])
        self.din("ident", [128, 128])
        self.din("maskU", [128, 64])
        if self.comm:
            self.din("msk", [128, 16])
            self.din("eye2", [128, 64])
        if self.do_l0:
            self.din("rmask", [128, 512])
            self.din("pv0", [128, self.PV0_COLS])
            self.din("wa_bd", [128, 8, 128])
            self.din("wx_bd", [128, 8, 128])
            self.din("w_in0", [D, 1536 if self.comm else 6144])
            self.din("w_out0", [2048, D])
            self.din("post0_bc", [128, D])
            if not self.comm:
                self.din("st0_in", [128, 1056])
                self.dout("st0_out", [128, 1056])
        if self.do_l1:
            if not self.do_l0:
                self.din("rmask", [128, 512])
            self.din("pv1", [128, self.PV1_COLS])
            self.din("M3", [128, 384])
            self.din("M2", [128, 256])
            self.din("onesbd", [128, 128])
            self.din("rw_w_in", [4, D, 256 if self.comm else D])
            self.din("w_out1", [D, D])
            self.din("rw_w1", [D, 64])
            self.din("rw_a1", [D, 64])
            self.din("rw_w2", [64, 256 if self.comm else D])
            self.din("rw_a2", [64, 256 if self.comm else D])
            self.din("post1_bc", [128, D])
            if not self.comm:
                self.din("st1_in", [128, 520])
                self.dout("st1_out", [128, 520])

    def build(self):
        self.declare()
        self.alloc_common()
        self.consts()
        if self.do_l0:
            self.l0_setup()
            self.l0_begin()
        if self.do_l1:
            self.l1_setup()
        for tile in range(self.NT):
            self.load_x(tile)
            if self.do_l0:
                self.layer0_tile(tile)
            if self.do_l1:
                self.layer1_tile(tile)
            self.store_out(tile)
        toks = [b.tok for b in self.xs]
        if self.do_l0:
            self.l0_end()
            toks.append(self.st0.tok)
        if self.do_l1:
            self.l1_end()
            toks.append(self.st1.tok)
        self.kb.wait_all("sync", toks)
        self.kb.replay()


def build_program(TS, do_l0=True, do_l1=True, comm=False):
    nc = bass.Bass("TRN2", target_bir_lowering=False)
    p = Prog(nc, TS, do_l0, do_l1, comm)
    if comm:
        p.build_comm()
    else:
        p.build()
    return nc


def fm(v):
    return np.ascontiguousarray(np.asarray(v, np.float32).reshape(8, 128).T)


def host_consts():
    p = np.arange(128)[:, None]
    t = np.arange(64)[None, :]
    c = {}
    c["ident"] = np.eye(128, dtype=np.float32)
    c["maskU"] = ((p % 64) <= t).astype(np.float32)
    rm = np.ones((128, 512), np.float32)
    rm[:, ::64] = 0.0
    c["rmask"] = rm
    return c


def host_l0(inp):
    o = {}
    pv = np.zeros((128, Prog.PV0_COLS), np.float32)
    cw = np.asarray(inp["rg_conv_w"][0], np.float32)
    for k in range(4):
        pv[:, k * 8:(k + 1) * 8] = fm(cw[k])
    pv[:, 32:40] = fm(inp["rg_conv_b"][0])
    pv[:, 40:48] = fm(inp["rg_b_a"][0])
    pv[:, 48:56] = fm(inp["rg_b_x"][0])
    pv[:, 56:64] = fm(inp["rg_lambda"][0])
    for r in range(3):
        pv[:, 64 + r * 8:64 + (r + 1) * 8] = fm(inp["hg_lower_bounds"][r])
    pv[:, 88:96] = fm(inp["pre_norm"][0])
    pv[:, 96] = np.asarray(inp["hg_out_norm"][0], np.float32)
    o["pv0"] = pv
    for nm, key in (("wa_bd", "rg_w_a"), ("wx_bd", "rg_w_x")):
        w = np.asarray(inp[key][0], np.float32)
        bd = np.zeros((128, 8, 128), np.float32)
        for cb in range(8):
            bd[0:64, cb, 0:64] = w[2 * cb]
            bd[64:128, cb, 64:128] = w[2 * cb + 1]
        o[nm] = bd
    o["w_in0"] = np.ascontiguousarray(np.asarray(inp["ab_w_in"][0], np.float32))
    o["w_out0"] = np.ascontiguousarray(np.asarray(inp["ab_w_out"][0], np.float32))
    o["post0_bc"] = np.ascontiguousarray(np.broadcast_to(np.asarray(inp["post_norm"][0], np.float32)[None, :], (128, D)))
    return o


def host_l1(inp):
    o = {}
    pv = np.zeros((128, Prog.PV1_COLS), np.float32)
    mu = np.asarray(inp["rw_mu"][0], np.float32)
    for p_ in range(6):
        pv[:, p_ * 8:(p_ + 1) * 8] = fm(mu[p_])
    pv[:, 48:56] = fm(inp["rw_w0"][0])
    pv[:, 56:64] = fm(inp["rw_a0"][0])
    pv[:, 64:72] = fm(inp["rw_k_k"][0])
    pv[:, 72:80] = fm(inp["rw_k_a"][0])
    pv[:, 80:88] = fm(np.asarray(inp["rw_r_k"][0], np.float32).reshape(-1))
    pv[:, 88:96] = fm(inp["rw_ln_w"][0])
    pv[:, 96:104] = fm(inp["rw_ln_b"][0])
    pv[:, 104:112] = fm(inp["pre_norm"][1])
    o["pv1"] = pv
    j = np.arange(128)[:, None]
    t = np.arange(128)[None, :]
    same = (j // 64) == (t // 64)
    MS = (same & (j < t)).astype(np.float32)
    MI = (same & (j <= t)).astype(np.float32)
    o["M3"] = np.ascontiguousarray(np.concatenate([MS, MI, MI], axis=1))
    o["M2"] = np.ascontiguousarray(np.concatenate([MS, MS.T], axis=1))
    o["onesbd"] = same.astype(np.float32)
    o["rw_w_in"] = np.ascontiguousarray(np.asarray(inp["rw_w_in"][0], np.float32))
    o["w_out1"] = np.ascontiguousarray(np.asarray(inp["rw_w_out"][0], np.float32))
    for nm in ("rw_w1", "rw_a1", "rw_w2", "rw_a2"):
        o[nm] = np.ascontiguousarray(np.asarray(inp[nm][0], np.float32))
    o["post1_bc"] = np.ascontiguousarray(np.broadcast_to(np.asarray(inp["post_norm"][1], np.float32)[None, :], (128, D)))
    return o


N_CORES = 8
SEG = 2048
MODE = os.environ.get("MK_MODE", "comm")
_NC_CACHE = {}


def _program(TS, comm=False):
    if (TS, comm) not in _NC_CACHE:
        _NC_CACHE[(TS, comm)] = build_program(TS, True, True, comm)
    return _NC_CACHE[(TS, comm)]


def host_comm(s, l1=None):
    m = np.zeros((128, 16), np.float32)
    for j in range(3):
        m[:, j] = 1.0 if j < s else 0.0
        m[:, 4 + j] = 1.0 if j == s - 1 else 0.0
    for q in range(4):
        m[:, 8 + q] = 1.0 if q == s else 0.0
    o = {"msk": m, "eye2": np.ascontiguousarray(np.tile(np.eye(64, dtype=np.float32), (2, 1)))}
    if l1 is not None:
        w = l1["w_in0"]
        o["w_in0"] = np.ascontiguousarray(np.concatenate(
            [w[:, g * 1024 + 256 * s:g * 1024 + 256 * s + 256] for g in range(6)], axis=1))
        pv0 = l1["pv0"].copy()
        for base in (0, 8, 16, 24, 32, 40, 48, 56, 64, 72, 80):
            pv0[:, base:base + 2] = l1["pv0"][:, base + 2 * s:base + 2 * s + 2]
        o["pv0"] = pv0
        for nm in ("wa_bd", "wx_bd"):
            bd = l1[nm].copy()
            bd[:, 0:2, :] = l1[nm][:, 2 * s:2 * s + 2, :]
            o[nm] = bd
        c0 = 256 * s
        o["rw_w_in"] = np.ascontiguousarray(l1["rw_w_in"][:, :, c0:c0 + 256])
        o["rw_w2"] = np.ascontiguousarray(l1["rw_w2"][:, c0:c0 + 256])
        o["rw_a2"] = np.ascontiguousarray(l1["rw_a2"][:, c0:c0 + 256])
        pv = l1["pv1"].copy()
        for base in (48, 56, 64, 72, 80, 88, 96):
            pv[:, base:base + 2] = l1["pv1"][:, base + 2 * s:base + 2 * s + 2]
        o["pv1"] = pv
    return o


def kernel(**inp):
    inp = {k: np.asarray(v) for k, v in inp.items()}
    x = np.asarray(inp["x"], np.float32)
    B, S, _ = x.shape
    base = {}
    base.update(host_consts())
    base.update(host_l0(inp))
    base.update(host_l1(inp))
    if MODE == "comm":
        nc = _program(SEG, True)
        maps = []
        for c in range(N_CORES):
            b, sgi = divmod(c, 4)
            m = dict(base)
            m.update(host_comm(sgi, base))
            m["x"] = np.ascontiguousarray(x[b, sgi * SEG:(sgi + 1) * SEG])
            maps.append(m)
        res = run_bass_kernel_spmd(nc, maps, core_ids=list(range(N_CORES)))
        out = np.empty((B, S, D), np.float32)
        for c in range(N_CORES):
            b, sgi = divmod(c, 4)
            out[b, sgi * SEG:(sgi + 1) * SEG] = res.results[c]["out"]
        return out
    if MODE == "fused":
        nc = _program(S)
        maps = []
        for c in range(N_CORES):
            m = dict(base)
            m["x"] = np.ascontiguousarray(x[c // 4])
            m["st0_in"] = np.zeros((128, 1056), np.float32)
            m["st1_in"] = np.zeros((128, 520), np.float32)
            maps.append(m)
        res = run_bass_kernel_spmd(nc, maps, core_ids=list(range(N_CORES)))
        out = np.empty((B, S, D), np.float32)
        for c in range(N_CORES):
            b, s = divmod(c, 4)
            out[b, s * SEG:(s + 1) * SEG] = res.results[c]["out"][s * SEG:(s + 1) * SEG]
        return out
    nc = _program(SEG)
    nseg = S // SEG
    st0 = [np.zeros((128, 1056), np.float32) for _ in range(N_CORES)]
    st1 = [np.zeros((128, 520), np.float32) for _ in range(N_CORES)]
    res = None
    for launch in range(nseg):
        maps = []
        for c in range(N_CORES):
            b, s = divmod(c, nseg)
            m = dict(base)
            m["x"] = np.ascontiguousarray(x[b, s * SEG:(s + 1) * SEG])
            m["st0_in"] = st0[c]
            m["st1_in"] = st1[c]
            maps.append(m)
        res = run_bass_kernel_spmd(nc, maps, core_ids=list(range(N_CORES)))
        for c in range(N_CORES):
            b, s = divmod(c, nseg)
            if s + 1 < nseg:
                st0[c + 1] = np.ascontiguousarray(res.results[c]["st0_out"])
                st1[c + 1] = np.ascontiguousarray(res.results[c]["st1_out"])
    out = np.empty((B, S, D), np.float32)
    for c in range(N_CORES):
        b, s = divmod(c, nseg)
        out[b, s * SEG:(s + 1) * SEG] = res.results[c]["out"]
    return out
```
